# Optimizing a Trainium2 kernel written in Bass

```python
import math
import jax, jax.numpy as jnp
from jax import lax
import numpy as np

D_MODEL = 2048
BATCH = 4
SEQ = 2048
DEPTH = 1
DEC_BATCH = 128
DEC_SEQ = 8
PAST_LEN = 16384
PAGE_SIZE = 128

POOL_WIDTH = D_MODEL // 2
POOL_WINDOWS = (2, 4, 8, 16)
POOL_GROUPS = len(POOL_WINDOWS)
POOL_GROUP_W = POOL_WIDTH // POOL_GROUPS
POOL_BUF = max(POOL_WINDOWS) - 1
SSM_WIDTH = D_MODEL // 2
SSM_GROUP_IN = 16
SSM_GROUPS = SSM_WIDTH // SSM_GROUP_IN
SSM_STATE = 64
DT_MIN = 1e-3
DT_MAX = 1e-1
XA_HEADS = 4
XA_WIDTH = D_MODEL // 2
XA_HEAD_DIM = XA_WIDTH // XA_HEADS
N_MEM = 256
N_BRANCH = 3
IN_WIDTH = POOL_WIDTH + SSM_WIDTH + XA_WIDTH + N_BRANCH * D_MODEL
D_FF = 4 * D_MODEL
EPS = 1e-6

kernel_name = "hybrid_pool_s5_memxattn_decoder_step"

F32 = jnp.float32


def rmsnorm(x, g):
    xf = x.astype(F32)
    r = lax.rsqrt(jnp.mean(xf * xf, axis=-1, keepdims=True) + EPS)
    return (xf * r).astype(x.dtype) * g


def pool_mixer(u, prev, n_prev, pool_w, pool_scale, pool_proj):
    bsz, L, P = u.shape
    ext = jnp.concatenate([prev.astype(u.dtype), u], axis=1).astype(F32)
    cs = jnp.concatenate([jnp.zeros((bsz, 1, P), F32), jnp.cumsum(ext, axis=1)], axis=1)
    cur = ext[:, POOL_BUF:]
    end = cs[:, POOL_BUF + 1:]
    t = jnp.arange(L)
    outs = []
    for k, wdw in enumerate(POOL_WINDOWS):
        lo, hi = k * POOL_GROUP_W, (k + 1) * POOL_GROUP_W
        start = cs[:, POOL_BUF + 1 - wdw: POOL_BUF + 1 - wdw + L, lo:hi]
        cnt = jnp.minimum(t + 1 + n_prev, wdw).astype(F32)[None, :, None]
        outs.append((end[..., lo:hi] - start) / cnt - cur[..., lo:hi])
    pooled = jnp.stack(outs, axis=2)
    mixed = jnp.einsum('blgc,gcd->blgd', pooled, pool_w.astype(F32)).reshape(bsz, L, P)
    mixed = (mixed * pool_scale.astype(F32)).astype(u.dtype)
    new_buf = ext[:, -POOL_BUF:].astype(prev.dtype)
    return mixed @ pool_proj, new_buf


def _ssm_combine(e1, e2):
    a1r, a1i, b1r, b1i = e1
    a2r, a2i, b2r, b2i = e2
    ar = a2r * a1r - a2i * a1i
    ai = a2r * a1i + a2i * a1r
    br = a2r * b1r - a2i * b1i + b2r
    bi = a2r * b1i + a2i * b1r + b2i
    return (ar, ai, br, bi)


def s5_mixer(u, h_re, h_im, A_re, A_im, log_dt, B_re, B_im, C_re, C_im, D_skip, glu_w, glu_b, ssm_proj):
    bsz, L, S = u.shape
    uf = u.astype(F32)
    ug = uf.reshape(bsz, L, SSM_GROUPS, SSM_GROUP_IN)
    dt = jnp.exp(log_dt.astype(F32))[:, None]
    ar, ai = A_re.astype(F32), A_im.astype(F32)
    mag = jnp.exp(dt * ar)
    abr, abi = mag * jnp.cos(dt * ai), mag * jnp.sin(dt * ai)
    den = ar * ar + ai * ai
    xr, xi = abr - 1.0, abi
    fr = (xr * ar + xi * ai) / den
    fi = (xi * ar - xr * ai) / den
    br_, bi_ = B_re.astype(F32), B_im.astype(F32)
    bbr = fr[..., None] * br_ - fi[..., None] * bi_
    bbi = fr[..., None] * bi_ + fi[..., None] * br_
    bu_r = jnp.einsum('blgp,gnp->blgn', ug, bbr)
    bu_i = jnp.einsum('blgp,gnp->blgn', ug, bbi)
    hr0, hi0 = h_re.astype(F32), h_im.astype(F32)
    bu_r = bu_r.at[:, 0].add(abr * hr0 - abi * hi0)
    bu_i = bu_i.at[:, 0].add(abr * hi0 + abi * hr0)
    a_r = jnp.broadcast_to(abr, (1, L, SSM_GROUPS, SSM_STATE))
    a_i = jnp.broadcast_to(abi, (1, L, SSM_GROUPS, SSM_STATE))
    _, _, hr, hi = lax.associative_scan(_ssm_combine, (a_r, a_i, bu_r, bu_i), axis=1)
    y = (jnp.einsum('blgn,gpn->blgp', hr, C_re.astype(F32))
         - jnp.einsum('blgn,gpn->blgp', hi, C_im.astype(F32))).reshape(bsz, L, S)
    y = y + D_skip.astype(F32) * uf
    zg = jax.nn.gelu(y)
    zg = zg * jax.nn.sigmoid(zg @ glu_w.astype(F32) + glu_b.astype(F32))
    out = zg.astype(u.dtype) @ ssm_proj
    return out, hr[:, -1].astype(h_re.dtype), hi[:, -1].astype(h_im.dtype)


def memory_kv(mem, mem_norm_g, xa_wk, xa_wv):
    bsz, M, _ = mem.shape
    mn = rmsnorm(mem, mem_norm_g)
    k = (mn @ xa_wk).reshape(bsz, M, XA_HEADS, XA_HEAD_DIM)
    v = (mn @ xa_wv).reshape(bsz, M, XA_HEADS, XA_HEAD_DIM)
    return k, v


def memory_cross_attn(q_flat, mem_k, mem_v, xa_wo):
    bsz, L, _ = q_flat.shape
    q = q_flat.reshape(bsz, L, XA_HEADS, XA_HEAD_DIM)
    s = jnp.einsum('blhd,bmhd->bhlm', q, mem_k.astype(q.dtype)).astype(F32) * (XA_HEAD_DIM ** -0.5)
    p = jax.nn.softmax(s, axis=-1)
    o = jnp.einsum('bhlm,bmhd->blhd', p.astype(q.dtype), mem_v.astype(q.dtype)).reshape(bsz, L, XA_WIDTH)
    return o @ xa_wo


def decoder_layer(x, pool_prev, n_prev, h_re, h_im, mem_k, mem_v,
                  norm1_g, w_in, b_gate, pool_w, pool_scale, pool_proj,
                  A_re, A_im, log_dt, B_re, B_im, C_re, C_im, D_skip, glu_w, glu_b, ssm_proj,
                  xa_wo, w_out, norm2_g, mlp_w1, mlp_w2):
    bsz, L, D = x.shape
    xn = rmsnorm(x, norm1_g)
    z = xn @ w_in
    o1 = POOL_WIDTH
    o2 = o1 + SSM_WIDTH
    o3 = o2 + XA_WIDTH
    u_pool, u_ssm, q = z[..., :o1], z[..., o1:o2], z[..., o2:o3]
    gates = jax.nn.sigmoid((z[..., o3:] + b_gate.reshape(-1)).astype(F32)).reshape(bsz, L, N_BRANCH, D)
    y_pool, pool_new = pool_mixer(u_pool, pool_prev, n_prev, pool_w, pool_scale, pool_proj)
    y_ssm, hr, hi = s5_mixer(u_ssm, h_re, h_im, A_re, A_im, log_dt, B_re, B_im, C_re, C_im,
                             D_skip, glu_w, glu_b, ssm_proj)
    y_xa = memory_cross_attn(q, mem_k, mem_v, xa_wo)
    merged = (gates[:, :, 0] * y_pool.astype(F32) + gates[:, :, 1] * y_ssm.astype(F32)
              + gates[:, :, 2] * y_xa.astype(F32)).astype(x.dtype)
    x = x + merged @ w_out
    xn2 = rmsnorm(x, norm2_g)
    x = x + jnp.square(jax.nn.relu(xn2 @ mlp_w1)) @ mlp_w2
    return x, pool_new, hr, hi


def setup_inputs(seed: int = 0) -> dict:
    key = jax.random.key(seed)
    ks = iter(list(jax.random.split(key, 48)))

    def nrm(shape, scale):
        return jax.random.normal(next(ks), shape, F32) * scale

    G, N, GI = SSM_GROUPS, SSM_STATE, SSM_GROUP_IN
    n_idx = jnp.arange(N, dtype=F32)[None, None, :]
    return {
        "x_prompt": nrm((BATCH, SEQ, D_MODEL), 1.0),
        "x_sample": nrm((DEC_BATCH, DEC_SEQ, D_MODEL), 1.0),
        "state_pool": nrm((DEPTH, DEC_BATCH, POOL_BUF, POOL_WIDTH), 1.0),
        "state_ssm_re": nrm((DEPTH, DEC_BATCH, G, N), 1.0),
        "state_ssm_im": nrm((DEPTH, DEC_BATCH, G, N), 1.0),
        "cache_mem_k": nrm((DEPTH, DEC_BATCH, N_MEM, XA_HEADS, XA_HEAD_DIM), 1.0),
        "cache_mem_v": nrm((DEPTH, DEC_BATCH, N_MEM, XA_HEADS, XA_HEAD_DIM), 1.0),
        "mem_prompt": nrm((BATCH, N_MEM, D_MODEL), 1.0),
        "norm1_g": 1.0 + nrm((DEPTH, D_MODEL), 0.02),
        "w_in": nrm((DEPTH, D_MODEL, IN_WIDTH), D_MODEL ** -0.5),
        "b_gate": nrm((DEPTH, N_BRANCH, D_MODEL), 0.01),
        "pool_w": nrm((DEPTH, POOL_GROUPS, POOL_GROUP_W, POOL_GROUP_W), POOL_GROUP_W ** -0.5),
        "pool_scale": 1.0 + nrm((DEPTH, POOL_WIDTH), 0.1),
        "pool_proj": nrm((DEPTH, POOL_WIDTH, D_MODEL), POOL_WIDTH ** -0.5),
        "ssm_A_re": -0.5 + nrm((DEPTH, G, N), 0.01),
        "ssm_A_im": jnp.pi * n_idx + nrm((DEPTH, G, N), 0.01),
        "ssm_log_dt": jax.random.uniform(next(ks), (DEPTH, G), F32, math.log(DT_MIN), math.log(DT_MAX)),
        "ssm_B_re": nrm((DEPTH, G, N, GI), (2 * GI) ** -0.5),
        "ssm_B_im": nrm((DEPTH, G, N, GI), (2 * GI) ** -0.5),
        "ssm_C_re": nrm((DEPTH, G, GI, N), (2 * N) ** -0.5),
        "ssm_C_im": nrm((DEPTH, G, GI, N), (2 * N) ** -0.5),
        "ssm_D": nrm((DEPTH, SSM_WIDTH), 1.0),
        "ssm_glu_w": nrm((DEPTH, SSM_WIDTH, SSM_WIDTH), SSM_WIDTH ** -0.5),
        "ssm_glu_b": nrm((DEPTH, SSM_WIDTH), 0.01),
        "ssm_proj": nrm((DEPTH, SSM_WIDTH, D_MODEL), SSM_WIDTH ** -0.5),
        "mem_norm_g": 1.0 + nrm((DEPTH, D_MODEL), 0.02),
        "xa_wk": nrm((DEPTH, D_MODEL, XA_WIDTH), D_MODEL ** -0.5),
        "xa_wv": nrm((DEPTH, D_MODEL, XA_WIDTH), D_MODEL ** -0.5),
        "xa_wo": nrm((DEPTH, XA_WIDTH, D_MODEL), XA_WIDTH ** -0.5),
        "w_out": nrm((DEPTH, D_MODEL, D_MODEL), D_MODEL ** -0.5),
        "norm2_g": 1.0 + nrm((DEPTH, D_MODEL), 0.02),
        "mlp_w1": nrm((DEPTH, D_MODEL, D_FF), D_MODEL ** -0.5),
        "mlp_w2": nrm((DEPTH, D_FF, D_MODEL), D_FF ** -0.5),
        "final_norm_g": 1.0 + nrm((D_MODEL,), 0.02),
    }


def reference(x_prompt, x_sample, state_pool, state_ssm_re, state_ssm_im, cache_mem_k, cache_mem_v,
              mem_prompt, norm1_g, w_in, b_gate, pool_w, pool_scale, pool_proj,
              ssm_A_re, ssm_A_im, ssm_log_dt, ssm_B_re, ssm_B_im, ssm_C_re, ssm_C_im, ssm_D,
              ssm_glu_w, ssm_glu_b, ssm_proj, mem_norm_g, xa_wk, xa_wv, xa_wo, w_out,
              norm2_g, mlp_w1, mlp_w2, final_norm_g):
    n_prev_sample = min(POOL_BUF, PAST_LEN)
    xp, xs = x_prompt, x_sample
    pool_p, re_p, im_p, mk_p, mv_p = [], [], [], [], []
    pool_s, re_s, im_s = [], [], []
    for l in range(DEPTH):
        lw = (norm1_g[l], w_in[l], b_gate[l], pool_w[l], pool_scale[l], pool_proj[l],
              ssm_A_re[l], ssm_A_im[l], ssm_log_dt[l], ssm_B_re[l], ssm_B_im[l], ssm_C_re[l], ssm_C_im[l],
              ssm_D[l], ssm_glu_w[l], ssm_glu_b[l], ssm_proj[l], xa_wo[l], w_out[l],
              norm2_g[l], mlp_w1[l], mlp_w2[l])
        mk, mv = memory_kv(mem_prompt, mem_norm_g[l], xa_wk[l], xa_wv[l])
        zero_pool = jnp.zeros((xp.shape[0], POOL_BUF, POOL_WIDTH), state_pool.dtype)
        zero_h = jnp.zeros((xp.shape[0], SSM_GROUPS, SSM_STATE), state_ssm_re.dtype)
        xp, pb, hr, hi = decoder_layer(xp, zero_pool, 0, zero_h, zero_h, mk, mv, *lw)
        pool_p.append(pb); re_p.append(hr); im_p.append(hi); mk_p.append(mk); mv_p.append(mv)
        xs, pb, hr, hi = decoder_layer(xs, state_pool[l], n_prev_sample, state_ssm_re[l], state_ssm_im[l],
                                       cache_mem_k[l], cache_mem_v[l], *lw)
        pool_s.append(pb); re_s.append(hr); im_s.append(hi)
    y_prompt = rmsnorm(xp, final_norm_g)
    y_sample = rmsnorm(xs, final_norm_g)
    return (y_prompt, y_sample,
            jnp.stack(pool_p), jnp.stack(re_p), jnp.stack(im_p), jnp.stack(mk_p), jnp.stack(mv_p),
            jnp.stack(pool_s), jnp.stack(re_s), jnp.stack(im_s))
```

```python
import math
import numpy as np
import concourse.bass as bass
import concourse.mybir as mybir
from concourse.bass_utils import run_bass_kernel_spmd
from contextlib import ExitStack

F32 = mybir.dt.float32
BF16 = mybir.dt.bfloat16
ACT = mybir.ActivationFunctionType
ALU = mybir.AluOpType

ENGS = ['pe', 'act', 'dve', 'pool', 'sp']
SAME_ENGINE_SYNC = True

D = 2048
NPR = 1024
NSM = 128
NT = NPR + NSM
NSEQ = 16
DFF = 8192
EPS = 1e-6
TG = [(0, 384), (384, 384), (768, 384)]
TGP = [(0, 512), (512, 512)]
PI = float(np.pi)


class Res:
    __slots__ = ('name', 'w', 'r', 'excl')

    def __init__(self, name='', excl=False):
        self.name = name
        self.w = None
        self.r = {}
        self.excl = excl


class Sched:
    def __init__(self, ndsem=48):
        self.ops = {e: [] for e in ENGS}
        self.seen = {e: {} for e in ENGS}
        self.ndsem = ndsem
        self.dsem_cnt = [0] * ndsem
        self.dsem_next = 0
        self.dsem_next_sw = 0

    def _deps(self, eng, reads, writes):
        deps = []
        for r in reads:
            if r.w is not None:
                deps.append(r.w)
            if r.excl:
                deps.extend(t for k, t in r.r.items() if k != eng)
        for w in writes:
            if w.w is not None:
                deps.append(w.w)
            deps.extend(w.r.values())
        return deps

    def _add_waits(self, eng, deps):
        seen = self.seen[eng]
        best = {}
        for d in deps:
            key = (d[0], d[1])
            if d[0] == 'e' and d[1] == eng and (eng == 'pe' or not SAME_ENGINE_SYNC):
                continue
            if seen.get(key, -1) >= d[2]:
                continue
            if best.get(key, -1) < d[2]:
                best[key] = d[2]
        waits = []
        for key, v in best.items():
            seen[key] = v
            waits.append((key[0], key[1], v))
            if key[0] == 'e':
                self.ops[key[1]][v][3] = True
        return waits

    def op(self, eng, fn, reads=(), writes=()):
        waits = self._add_waits(eng, self._deps(eng, reads, writes))
        idx = len(self.ops[eng])
        self.ops[eng].append(['c', fn, waits, False, None])
        tag = ('e', eng, idx)
        for r in reads:
            r.r[eng] = tag
        for w in writes:
            w.w = tag
            w.r = {}
        return tag

    def dma(self, q, fn, reads=(), writes=()):
        half = self.ndsem // 2
        if q == 'pool':
            s = half + self.dsem_next_sw
            self.dsem_next_sw = (self.dsem_next_sw + 1) % (self.ndsem - half)
        else:
            s = self.dsem_next
            self.dsem_next = (s + 1) % half
        deps = self._deps(q, reads, writes)
        if self.dsem_cnt[s] > 0:
            deps.append(('d', s, self.dsem_cnt[s]))
        waits = self._add_waits(q, deps)
        self.dsem_cnt[s] += 16
        tag = ('d', s, self.dsem_cnt[s])
        self.ops[q].append(['d', fn, waits, False, s])
        for r in reads:
            r.r[('dma', s)] = tag
        for w in writes:
            w.w = tag
            w.r = {}
        return tag

    def barrier(self):
        tags = []
        for e in ENGS:
            for idx in range(len(self.ops[e]) - 1, -1, -1):
                if self.ops[e][idx][0] == 'c':
                    tags.append(('e', e, idx))
                    break
        for s in range(self.ndsem):
            if self.dsem_cnt[s] > 0:
                tags.append(('d', s, self.dsem_cnt[s]))
        for e in ENGS:
            waits = self._add_waits(e, tags)
            if waits:
                self.ops[e].append(['w', None, waits, False, None])

    def finish(self):
        self.barrier()

    def emit(self, nc, es):
        SEMMAX = 30000
        rank = {}
        nsem = {}
        for e in ENGS:
            c = 0
            rk = []
            for o in self.ops[e]:
                if o[3]:
                    c += 1
                rk.append(c)
            rank[e] = rk
            nsem[e] = max(1, (c + SEMMAX - 1) // SEMMAX)
        esem = {e: [es.enter_context(nc.semaphore('es_%s_%d' % (e, i))) for i in range(nsem[e])] for e in ENGS}
        dsem = [es.enter_context(nc.semaphore('ds_%d' % i)) for i in range(self.ndsem)]
        ops = self.ops
        block = es.enter_context(nc.Block())

        def semval(a, r):
            return esem[a][(r - 1) // SEMMAX], (r - 1) % SEMMAX + 1

        def run(e, E):
            for i, (kind, fn, waits, marked, ds) in enumerate(ops[e]):
                for (k, a, v) in waits:
                    if k == 'e':
                        sm, val = semval(a, rank[a][v])
                        E.wait_ge(sm, val)
                    else:
                        E.wait_ge(dsem[a], v)
                if kind == 'w':
                    continue
                inst = fn(E)
                if kind == 'd':
                    inst.then_inc(dsem[ds], 16)
                elif marked:
                    sm, val = semval(e, rank[e][i])
                    inst.then_inc(sm, 1)

        @block.tensor
        def _(E):
            run('pe', E)

        @block.scalar
        def _(E):
            run('act', E)

        @block.vector
        def _(E):
            run('dve', E)

        @block.gpsimd
        def _(E):
            run('pool', E)

        @block.sync
        def _(E):
            run('sp', E)


class Buf:
    __slots__ = ('ap', 'res')

    def __init__(self, ap, res):
        self.ap = ap
        self.res = res


def bc(ap, n):
    return bass.AP(ap.tensor, ap.offset, [list(x) for x in ap.ap] + [[0, n]])


def bc_mid(ap, n):
    a = [list(x) for x in ap.ap]
    return bass.AP(ap.tensor, ap.offset, [a[0], [0, n]] + a[1:])


LAST_S = None


def build_program(stop=99, taps=()):
    global LAST_S
    nc = bass.Bass("TRN2", target_bir_lowering=False)
    S = Sched()
    LAST_S = S
    taps = set(taps)
    tap_out = {}

    def din(name, shape):
        return nc.dram_tensor(name, list(shape), F32, kind="ExternalInput").ap()

    def dout(name, shape):
        return nc.dram_tensor(name, list(shape), F32, kind="ExternalOutput").ap()

    I = {}
    for name, shape in [
        ('xm', [NT, D]), ('xp', [NPR, D]), ('spool', [NSEQ * 15, 1024]), ('sre', [NSEQ * 64, 64]), ('sim', [NSEQ * 64, 64]),
        ('ck', [NSEQ, 256, 1024]), ('cv', [NSEQ, 256, 1024]), ('mem', [256, D]),
        ('norm1_g', [1, D]), ('w_in', [D, 9216]), ('b_gate', [6144]), ('pool_w', [1024, 256]), ('pool_scale', [1024]),
        ('pool_proj', [1024, D]), ('A_re', [64, 64]), ('A_im', [64, 64]), ('log_dt', [1, 64]),
        ('B_re', [64, 64, 16]), ('B_im', [64, 64, 16]), ('C_re', [1024, 64]), ('C_im', [1024, 64]), ('ssm_D', [1024]),
        ('glu_w', [1024, 1024]), ('glu_b', [1024]), ('ssm_proj', [1024, D]), ('mem_norm_g', [1, D]),
        ('xa_wk', [D, 1024]), ('xa_wv', [D, 1024]), ('xa_wo', [1024, D]), ('w_out', [D, D]), ('norm2_g', [1, D]),
        ('w1', [D, DFF]), ('w2', [DFF, D]), ('final_g', [1, D]),
        ('ident', [128, 128]), ('sel', [64, 128, 128]), ('mask', [128, 128]), ('icnt', [4, 16]), ('ciota', [1, 128]),
    ]:
        I[name] = din(name, shape)
    O = {}
    for name, shape in [
        ('y', [NT, D]), ('pool_p', [16, 1024]), ('hp_re', [64, 64]), ('hp_im', [64, 64]), ('mk', [256, 1024]), ('mv', [256, 1024]),
        ('pool_s', [NSEQ, 15, 1024]), ('hs_re', [NSEQ * 64, 64]), ('hs_im', [NSEQ * 64, 64]),
    ]:
        O[name] = dout(name, shape)

    with ExitStack() as es:
        es.enter_context(nc.allow_non_contiguous_dma(reason="small strided param loads"))
        ARENA_BYTES = 207 * 1024 + 512
        arena_t = es.enter_context(nc.sbuf_tensor("arena", [128, ARENA_BYTES // 4], F32))
        ps_t = es.enter_context(nc.psum_tensor("ps", [128, 4096], F32))
        PB = [Res('bank%d' % i, excl=True) for i in range(8)]
        st = {'bank': 0, 'ev': 0, 'ring': 0, 'big': 0}

        KB = 256
        REG = {}

        def region(name, base_kb, size_kb):
            REG[name] = {'base': int(base_kb * KB), 'top': int(base_kb * KB), 'lim': int((base_kb + size_kb) * KB), 'peak': 0}
        region('A', 0, 44)
        region('B', 44, 9)
        region('C', 53, 72)
        region('D', 125, 36)
        region('E', 161, 36)
        region('F', 197, 9)
        region('S', 71, 136.5)
        cur = {'R': 'A'}

        def use(name):
            cur['R'] = name

        def alloc(shape, dtype=F32, parts=128, R=None):
            rg = REG[R or cur['R']]
            n = 1
            for s_ in shape:
                n *= s_
            nb_ = n * (2 if dtype == BF16 else 4)
            nw = (nb_ + 3) // 4
            nw = (nw + 7) // 8 * 8
            w0 = rg['top']
            st['last_w0'] = w0
            rg['top'] += nw
            rg['peak'] = max(rg['peak'], rg['top'])
            assert rg['top'] <= rg['lim'], "region %s overflow: need %d KiB more" % (R or cur['R'], (rg['top'] - rg['lim']) // KB + 1)
            ap = arena_t[0:parts, w0:w0 + nw]
            if dtype == BF16:
                ap = ap.bitcast(BF16)
            ap = ap[:, 0:n]
            if len(shape) == 2:
                ap = ap.rearrange("p (a b) -> p a b", a=shape[0], b=shape[1])
            elif len(shape) == 3:
                ap = ap.rearrange("p (a b c) -> p a b c", a=shape[0], b=shape[1], c=shape[2])
            elif len(shape) == 4:
                ap = ap.rearrange("p (a b c d) -> p a b c d", a=shape[0], b=shape[1], c=shape[2], d=shape[3])
            return Buf(ap, Res())

        def mark(R=None):
            return (R or cur['R'], REG[R or cur['R']]['top'])

        def release(m):
            S.barrier()
            REG[m[0]]['top'] = m[1]

        def reset(R):
            S.barrier()
            REG[R]['top'] = REG[R]['base']

        def bank(i):
            return ps_t[:, i * 512:(i + 1) * 512]

        def bankb(i):
            return ps_t[:, i * 512:(i + 1) * 512].bitcast(BF16)

        def nb():
            b = st['bank'] % st.get('nbanks', 8)
            st['bank'] = (b + 1) % st.get('nbanks', 8)
            return b

        def evq():
            st['ev'] ^= 1
            return 'act' if st['ev'] else 'dve'

        def mm(out, lhsT, rhs, start, stop, rd, wr):
            S.op('pe', lambda E: E.matmul(out, lhsT, rhs, start=start, stop=stop), reads=rd, writes=wr)

        def trp(out, in_, ident, rd, wr):
            S.op('pe', lambda E: E.transpose(out, in_, ident), reads=rd, writes=wr)

        def vtt(out, a, b, op, rd, wr, eng='dve'):
            S.op(eng, lambda E: E.tensor_tensor(out, a, b, op), reads=rd, writes=wr)

        def vts(out, a, s1, s2, op0, op1, rd, wr, eng='dve'):
            S.op(eng, lambda E: E.tensor_scalar(out, a, s1, s2, op0, op1), reads=rd, writes=wr)

        def vstt(out, in0, scalar, in1, op0, op1, rd, wr, eng='dve'):
            S.op(eng, lambda E: E.scalar_tensor_tensor(out, in0, scalar, in1, op0, op1), reads=rd, writes=wr)

        def vcopy(out, a, rd, wr, eng='dve'):
            if eng == 'act':
                S.op('act', lambda E: E.copy(out, a), reads=rd, writes=wr)
            else:
                S.op(eng, lambda E: E.tensor_copy(out, a), reads=rd, writes=wr)

        def vset(out, val, wr, eng='dve'):
            S.op(eng, lambda E: E.memset(out, val), writes=wr)

        def act(out, in_, func, rd, wr, bias=None, scale=None, accum_out=None):
            kw = {}
            if bias is not None:
                kw['bias'] = bias
            if scale is not None:
                kw['scale'] = scale
            if accum_out is not None:
                kw['accum_out'] = accum_out
            S.op('act', lambda E: E.activation(out, in_, func, **kw), reads=rd, writes=wr)

        def dma(q, out, in_, rd=(), wr=()):
            S.dma(q, lambda E: E.dma_start(out=out, in_=in_), reads=rd, writes=wr)

        def tap(name, ap, rd, shape):
            if name in taps:
                t = dout('tap_' + name, shape)
                tap_out[name] = shape
                dma('sp' if ap.dtype == F32 else 'pool', t, ap, rd=rd)

        def vsadd(out, a, c, rd, wr, eng='dve'):
            S.op(eng, lambda E: E.tensor_scalar_add(out, a, c), reads=rd, writes=wr)

        def vsmul(out, a, c, rd, wr, eng='dve'):
            S.op(eng, lambda E: E.tensor_scalar_mul(out, a, c), reads=rd, writes=wr)

        def vrecip(out, a, rd, wr):
            S.op('dve', lambda E: E.reciprocal(out, a), reads=rd, writes=wr)

        def done():
            import os
            if os.environ.get('PEAKS'):
                for k_, v_ in REG.items():
                    print('region', k_, 'peak KiB', (v_['peak'] - v_['base']) / KB, 'size', (v_['lim'] - v_['base']) / KB)
            S.finish()
            S.emit(nc, es)
            return nc, tap_out

        use('A')
        identf = alloc([128])
        dma('sp', identf.ap, I['ident'], wr=[identf.res])
        identb = alloc([128], BF16)
        dma('pool', identb.ap, I['ident'], wr=[identb.res])
        onesb = alloc([128], BF16)
        vset(onesb.ap, 1.0, [onesb.res])
        gbc = alloc([D])
        small = {}

        def load_cols(name, n):
            b = alloc([n // 128])
            dma('sp', b.ap, I[name].rearrange("(c p) -> p c", p=128), wr=[b.res])
            small[name] = b
        load_cols('b_gate', 6144)
        load_cols('pool_scale', 1024)
        load_cols('ssm_D', 1024)
        load_cols('glu_b', 1024)
        icnt = alloc([4, 16])
        dma('sp', icnt.ap, bass.AP(I['icnt'].tensor, 0, [[0, 128], [16, 4], [1, 16]]), wr=[icnt.res])
        ss_list = [alloc([8]) for _ in range(4)]
        ss = ss_list[0]
        ring_all = alloc([4, 4096], BF16)
        ring_res = [Res() for _ in range(4)]

        def load_g(name):
            dma('sp', gbc.ap, bass.AP(I[name].tensor, 0, [[0, 128], [1, D]]), wr=[gbc.res])

        def wpanel(w2d, r0, K, c0, ncols, slots=(0, 1, 2, 3), key='ring'):
            KC = K // 128
            assert KC * ncols <= 4096
            si = slots[st.setdefault(key, 0) % len(slots)]
            st[key] += 1
            view = ring_all.ap[:, si, 0:KC * ncols].rearrange("p (k m) -> p k m", k=KC, m=ncols)
            src = w2d[r0:r0 + K, c0:c0 + ncols].rearrange("(kc p) m -> p kc m", p=128)
            dma('pool', view, src, wr=[ring_res[si]])
            return Buf(view, ring_res[si]), [ring_res[si]]

        def wpanel_big(w2d, r0, K, c0, ncols):
            KC = K // 128
            assert KC * ncols <= 8192
            bi = st['big'] % 2
            st['big'] += 1
            flat = ring_all.ap[:, 2 * bi:2 * bi + 2, :].rearrange("p a b -> p (a b)")
            view = flat[:, 0:KC * ncols].rearrange("p (k m) -> p k m", k=KC, m=ncols)
            src = w2d[r0:r0 + K, c0:c0 + ncols].rearrange("(kc p) m -> p kc m", p=128)
            rl = [ring_res[2 * bi], ring_res[2 * bi + 1]]
            dma('pool', view, src, wr=rl)
            return Buf(view, None), rl

        def norm_ops(s, tt, xslots, xs_b, junk):
            if isinstance(s, Buf):
                xt = s
            else:
                xt = xslots[tt % len(xslots)]
                dma('sp', xt.ap, s, wr=[xt.res])
            ss = ss_list[tt % 4]
            jk = junk if junk is not None else xs_b
            vset(ss.ap[:, 0:1], 0.0, [ss.res])
            act(jk.ap, xt.ap, ACT.Square, [xt.res, ss.res], [jk.res, ss.res], accum_out=ss.ap[:, 0:1])
            vts(ss.ap[:, 1:2], ss.ap[:, 0:1], 1.0 / D, EPS, ALU.mult, ALU.add, [ss.res], [ss.res])
            act(ss.ap[:, 1:2], ss.ap[:, 1:2], ACT.Sqrt, [ss.res], [ss.res])
            vrecip(ss.ap[:, 2:3], ss.ap[:, 1:2], [ss.res], [ss.res])
            vstt(xs_b.ap, xt.ap, ss.ap[:, 2:3], gbc.ap, ALU.mult, ALU.mult, [xt.res, ss.res, gbc.res], [xs_b.res])

        def norm_T(xs_b, dst, c0):
            for h in range(2):
                b = nb()
                for j in range(8):
                    c = h * 8 + j
                    trp(bankb(b)[:, j * 128:(j + 1) * 128], xs_b.ap[:, c * 128:(c + 1) * 128], identb.ap,
                        [xs_b.res, identb.res], [PB[b]])
                vcopy(dst.ap[:, h * 8:(h + 1) * 8, c0:c0 + 128],
                      bankb(b).rearrange("p (j c) -> p j c", j=8, c=128), [PB[b]], [dst.res], eng=evq())

        def norm_transpose(src_fn, ntiles, dst, col0, xslots, xs_b_in, junk):
            for tt in range(ntiles):
                xs_b = xs_b_in[tt % len(xs_b_in)] if isinstance(xs_b_in, list) else xs_b_in
                norm_ops(src_fn(tt), tt, xslots, xs_b, junk)
                norm_T(xs_b, dst, col0 + tt * 128)

        def linear_fm(w2d, r0, K, c0, M, rhsT, tgroups, evac):
            KC = K // 128
            pc = min(M, (4096 // KC) // 128 * 128)
            for p0 in range(0, M, pc):
                panel, prl = wpanel(w2d, r0, K, c0 + p0, pc)
                for j in range(pc // 128):
                    mc = p0 // 128 + j
                    for (t0, n) in tgroups:
                        b = nb()
                        for kc in range(KC):
                            mm(bank(b)[:, 0:n], panel.ap[:, kc, j * 128:(j + 1) * 128], rhsT.ap[:, kc, t0:t0 + n],
                               kc == 0, kc == KC - 1, prl + [rhsT.res], [PB[b]])
                        evac(mc, t0, n, b)

        use('B')
        KT = alloc([8, 256], BF16)
        Vb = alloc([2, 1024], BF16)
        halo = alloc([8, 16])
        use('S')

        def sc():
            return alloc([64], F32, parts=64)
        maskf = alloc([128], R='B')
        dma('sp', maskf.ap, I['mask'], wr=[maskf.res])
        Pst_r = alloc([9, 64], F32, parts=64)
        Pst_i = alloc([9, 64], F32, parts=64)
        Qst_r = alloc([8, 64], F32, parts=64)
        Qst_i = alloc([8, 64], F32, parts=64)
        cst_r = alloc([8, 64], F32, parts=64)
        cst_i = alloc([8, 64], F32, parts=64)
        P_r = [Buf(Pst_r.ap[:, k, :], Pst_r.res) for k in range(9)]
        P_i = [Buf(Pst_i.ap[:, k, :], Pst_i.res) for k in range(9)]
        Q_r = [Buf(Qst_r.ap[:, 7 - k, :], Qst_r.res) for k in range(8)]
        Q_i = [Buf(Qst_i.ap[:, 7 - k, :], Qst_i.res) for k in range(8)]
        cs_r = [Buf(cst_r.ap[:, k, :], cst_r.res) for k in range(8)]
        cs_i = [Buf(cst_i.ap[:, k, :], cst_i.res) for k in range(8)]
        rho8 = sc()
        phin = sc()
        ciota = alloc([128], F32, parts=64)
        dma('sp', ciota.ap, bass.AP(I['ciota'].tensor, 0, [[0, 64], [1, 128]]), wr=[ciota.res])
        H0 = alloc([2, 64], F32, parts=64)
        HpF = alloc([2, 64], F32, parts=64)
        h0T = alloc([2, NSEQ, 64], F32, parts=64)
        sel = alloc([64, 128], BF16)
        dma('pool', sel.ap, I['sel'].rearrange("j p m -> p j m"), wr=[sel.res])
        upreT = alloc([8, NPR], BF16)
        umT = alloc([8, NT], BF16)
        m0 = mark()
        ar, ai, dtb = sc(), sc(), sc()
        tmpA = alloc([64], F32, parts=64)
        for (nm, dst) in (('A_re', ar), ('A_im', ai)):
            dma('sp', tmpA.ap, I[nm], wr=[tmpA.res])
            b = nb()
            trp(bank(b)[0:64, 0:64], tmpA.ap, identf.ap[0:64, 0:64], [tmpA.res, identf.res], [PB[b]])
            vcopy(dst.ap, bank(b)[0:64, 0:64], [PB[b]], [dst.res])
        dma('sp', dtb.ap, bass.AP(I['log_dt'].tensor, 0, [[0, 64], [1, 64]]), wr=[dtb.res])
        act(dtb.ap, dtb.ap, ACT.Exp, [dtb.res], [dtb.res])
        xsl = [alloc([D]) for _ in range(2)]
        xs_b = alloc([D], BF16)
        junk = alloc([D], BF16)
        mnT = alloc([16, 256], BF16)
        load_g('mem_norm_g')
        norm_transpose(lambda tt: I['mem'][tt * 128:(tt + 1) * 128, :], 2, mnT, 0, xsl, xs_b, junk)
        osb = alloc([512])
        for (wn, on, isk) in (('xa_wk', 'mk', True), ('xa_wv', 'mv', False)):
            for half in range(2):
                panel, prl = wpanel_big(I[wn], 0, D, half * 512, 512)
                if isk:
                    for j in range(4):
                        mc = half * 4 + j
                        b = nb()
                        for kc in range(16):
                            mm(bank(b)[:, 0:256], panel.ap[:, kc, j * 128:(j + 1) * 128], mnT.ap[:, kc, :], kc == 0, kc == 15,
                               prl + [mnT.res], [PB[b]])
                        vcopy(KT.ap[:, mc, :], bank(b)[:, 0:256], [PB[b]], [KT.res], eng='act')
                for mt in range(2):
                    b = nb()
                    for kc in range(16):
                        mm(bank(b), mnT.ap[:, kc, mt * 128:(mt + 1) * 128], panel.ap[:, kc, :], kc == 0, kc == 15,
                           prl + [mnT.res], [PB[b]])
                    vcopy(osb.ap, bank(b), [PB[b]], [osb.res], eng='act')
                    if not isk:
                        vcopy(Vb.ap[:, mt, half * 512:(half + 1) * 512], bank(b), [PB[b]], [Vb.res], eng='act')
                    dma('sp', O[on][mt * 128:(mt + 1) * 128, half * 512:(half + 1) * 512], osb.ap, rd=[osb.res])
        lr, th, Em, t1, t2, t3 = sc(), sc(), sc(), sc(), sc(), sc()
        vtt(lr.ap, dtb.ap, ar.ap, ALU.mult, [dtb.res, ar.res], [lr.res])
        vtt(th.ap, dtb.ap, ai.ap, ALU.mult, [dtb.res, ai.res], [th.res])
        act(Em.ap, lr.ap, ACT.Exp, [lr.res], [Em.res])

        def sin_of(dst, src, shift):
            vsadd(t1.ap, src.ap, shift, [src.res], [t1.res])
            vcopy(t2.ap, t1.ap, [t1.res], [t2.res])
            for kk in (1, 3, 5, 7, 9):
                vts(t3.ap, t1.ap, kk * PI, -2 * PI, ALU.is_ge, ALU.mult, [t1.res], [t3.res])
                vtt(t2.ap, t2.ap, t3.ap, ALU.add, [t2.res, t3.res], [t2.res])
            vts(t2.ap, t2.ap, -3.1415925, 3.1415925, ALU.max, ALU.min, [t2.res], [t2.res])
            act(dst.ap, t2.ap, ACT.Sin, [t2.res], [dst.res])
        sn, cs_ = sc(), sc()
        sin_of(sn, th, 0.0)
        sin_of(cs_, th, PI / 2)
        vset(P_r[0].ap, 1.0, [P_r[0].res])
        vset(P_i[0].ap, 0.0, [P_i[0].res])
        vset(Q_r[0].ap, 1.0, [Q_r[0].res])
        vset(Q_i[0].ap, 0.0, [Q_i[0].res])
        vtt(P_r[1].ap, Em.ap, cs_.ap, ALU.mult, [Em.res, cs_.res], [P_r[1].res])
        vtt(P_i[1].ap, Em.ap, sn.ap, ALU.mult, [Em.res, sn.res], [P_i[1].res])

        def cmul(o_r, o_i, a_r, a_i, b_r, b_i, rd, wr_r, wr_i, ta, tb, conj_b=False, eng='dve'):
            vtt(ta.ap, a_r, b_r, ALU.mult, rd, [ta.res], eng)
            vtt(tb.ap, a_i, b_i, ALU.mult, rd, [tb.res], eng)
            vtt(o_r, ta.ap, tb.ap, ALU.add if conj_b else ALU.subtract, [ta.res, tb.res], wr_r, eng)
            vtt(ta.ap, a_i, b_r, ALU.mult, rd, [ta.res], eng)
            vtt(tb.ap, a_r, b_i, ALU.mult, rd, [tb.res], eng)
            vtt(o_i, ta.ap, tb.ap, ALU.subtract if conj_b else ALU.add, [ta.res, tb.res], wr_i, eng)

        def cmul_s(o_r, o_i, a_r, a_i, b_r, b_i, conj_b=False):
            cmul(o_r.ap, o_i.ap, a_r.ap, a_i.ap, b_r.ap, b_i.ap, [a_r.res, a_i.res, b_r.res, b_i.res],
                 [o_r.res], [o_i.res], t1, t2, conj_b)
        for k in range(2, 9):
            cmul_s(P_r[k], P_i[k], P_r[k - 1], P_i[k - 1], P_r[1], P_i[1])
        den, fr, fi, xr = sc(), sc(), sc(), sc()
        vtt(den.ap, ar.ap, ar.ap, ALU.mult, [ar.res], [den.res])
        vtt(t3.ap, ai.ap, ai.ap, ALU.mult, [ai.res], [t3.res])
        vtt(den.ap, den.ap, t3.ap, ALU.add, [den.res, t3.res], [den.res])
        vrecip(den.ap, den.ap, [den.res], [den.res])
        vsadd(xr.ap, P_r[1].ap, -1.0, [P_r[1].res], [xr.res])
        cmul_s(fr, fi, xr, P_i[1], ar, ai, conj_b=True)
        vtt(fr.ap, fr.ap, den.ap, ALU.mult, [fr.res, den.res], [fr.res])
        vtt(fi.ap, fi.ap, den.ap, ALU.mult, [fi.res, den.res], [fi.res])
        einv, irho8 = sc(), sc()
        act(einv.ap, lr.ap, ACT.Exp, [lr.res], [einv.res], scale=-2.0)
        vtt(Q_r[1].ap, P_r[1].ap, einv.ap, ALU.mult, [P_r[1].res, einv.res], [Q_r[1].res])
        vstt(Q_i[1].ap, P_i[1].ap, -1.0, einv.ap, ALU.mult, ALU.mult, [P_i[1].res, einv.res], [Q_i[1].res])
        for k in range(2, 8):
            cmul_s(Q_r[k], Q_i[k], Q_r[k - 1], Q_i[k - 1], Q_r[1], Q_i[1])
        for s_ in range(8):
            cmul_s(cs_r[s_], cs_i[s_], P_r[7 - s_], P_i[7 - s_], fr, fi)
        act(rho8.ap, lr.ap, ACT.Exp, [lr.res], [rho8.res], scale=8.0)
        I32 = mybir.dt.int32
        vsmul(t1.ap, th.ap, 4.0 / PI, [th.res], [t1.res])
        vcopy(t2.ap.bitcast(I32), t1.ap, [t1.res], [t2.res])
        vcopy(t3.ap, t2.ap.bitcast(I32), [t2.res], [t3.res])
        vtt(phin.ap, t1.ap, t3.ap, ALU.subtract, [t1.res, t3.res], [phin.res])
        tap('P8r', P_r[8].ap, [P_r[8].res], [64, 64])
        tap('P8i', P_i[8].ap, [P_i[8].res], [64, 64])
        tap('fr', fr.ap, [fr.res], [64, 64])
        tap('fi', fi.ap, [fi.res], [64, 64])
        tmpS = alloc([64])
        for ri, nm in ((0, 'sre'), (1, 'sim')):
            for j in range(8):
                dma('sp', tmpS.ap, I[nm][j * 128:(j + 1) * 128, :], wr=[tmpS.res])
                b = nb()
                trp(bank(b)[0:64, 0:128], tmpS.ap, identf.ap, [tmpS.res, identf.res], [PB[b]])
                vcopy(h0T.ap[:, ri, 2 * j:2 * j + 2, :], bank(b)[0:64, 0:128].rearrange("p (s g) -> p s g", s=2, g=64),
                      [PB[b]], [h0T.res], eng='act')
        release(m0)
        if stop <= 1:
            return done()

        m2 = mark()
        xsl = [alloc([D]) for _ in range(2)]
        xsb4 = [alloc([D], BF16) for _ in range(4)]
        xg = alloc([16, 512], BF16)
        xhalo = alloc([16, 128], BF16)
        load_g('norm1_g')
        glist = [(I['xp'], upreT, 0, 512), (I['xp'], upreT, 512, 512)] + [(I['xm'], umT, t0, n) for (t0, n) in TG]
        wres = [wpanel(I['w_in'], 0, D, 1024 + p_ * 256, 256, slots=(p_,), key='p2r%d' % p_) for p_ in range(4)]
        tile_ctr = [0]

        def p2_norm_ops(i, k):
            src, dstT, t0, n = glist[i]
            norm_ops(src[t0 + k * 128:t0 + (k + 1) * 128, :], tile_ctr[0], xsl, xsb4[k], None)
            tile_ctr[0] += 1

        def p2_T(i):
            src, dstT, t0, n = glist[i]
            for k in range(n // 128):
                norm_T(xsb4[k], xg, k * 128)
            if i == 1:
                vcopy(xhalo.ap, xg.ap[:, :, 384:512], [xg.res], [xhalo.res], eng='act')

        def p2_unit(i, mc):
            src, dstT, t0, n = glist[i]
            panel, prl = wres[mc // 2]
            j = mc % 2
            b = nb()
            for kc in range(16):
                mm(bank(b)[:, 0:n], panel.ap[:, kc, j * 128:(j + 1) * 128], xg.ap[:, kc, 0:n], kc == 0, kc == 15,
                   prl + [xg.res], [PB[b]])
            vcopy(dstT.ap[:, mc, t0:t0 + n], bank(b)[:, 0:n], [PB[b]], [dstT.res], eng=evq())
        for k in range(glist[0][3] // 128):
            p2_norm_ops(0, k)
        p2_T(0)
        for i in range(len(glist)):
            nxt = glist[i + 1][3] // 128 if i + 1 < len(glist) else 0
            for mc in range(8):
                if mc % 2 == 0 and mc // 2 < nxt:
                    p2_norm_ops(i + 1, mc // 2)
                p2_unit(i, mc)
            if nxt:
                p2_T(i + 1)

        def evh(mc, tt0, nn, b):
            vcopy(halo.ap[:, mc, :], bank(b)[:, 112:128], [PB[b]], [halo.res], eng=evq())
        linear_fm(I['w_in'], 0, D, 0, 1024, xhalo, [(0, 128)], evh)
        release(m2)
        tap('umT', umT.ap[:, 0, :], [umT.res], [128, NT])
        tap('upreT', upreT.ap[:, 0, :], [upreT.res], [128, NPR])
        if stop <= 2:
            return done()

        zg = alloc([8, NT], BF16, R='C')
        G8 = 8
        Brb = alloc([G8, 16], F32, parts=64)
        Bib = alloc([G8, 16], F32, parts=64)
        Crb = alloc([G8, 16], F32, parts=64)
        Cib = alloc([G8, 16], F32, parts=64)
        tmpC = alloc([64], R='A')
        bufA = alloc([2, G8, 128], F32)
        bufB = alloc([2, G8, 128], F32)
        tq1 = alloc([G8, 128], F32, parts=64)
        tq2 = alloc([G8, 128], F32, parts=64)
        Wa = alloc([G8, 2, 64], BF16)
        w0_wa = st['last_w0']
        Wb = alloc([G8, 128], BF16)
        Wc = alloc([G8, 2, 128], BF16, parts=64)
        ytmp_ap = None
        YR = None
        Rr = alloc([G8, 128], F32, parts=64)
        Ri = alloc([G8, 128], F32, parts=64)
        rhot = alloc([G8, 128], F32, parts=64)
        um = [alloc([2, G8], F32, parts=64, R='A') for _ in range(2)]
        U = alloc([G8, 144], BF16)
        Sb = alloc([2, G8, 144], F32, parts=64)
        Hprev = alloc([2, G8, 144], BF16, parts=64)
        Ysb = U
        ts8a = Buf(tq1.ap[:, :, 0:16], tq1.res)
        ts8b = Buf(tq2.ap[:, :, 0:16], tq2.res)
        ts8c = alloc([2, G8, 16], F32, parts=64, R='A')
        Dcol = small['ssm_D']
        bAf = bufA.ap.rearrange("p r g c -> p (r g c)")
        bBf = bufB.ap.rearrange("p r g c -> p (r g c)")
        gt = Buf(bAf[:, 0:NT], bufA.res)
        sg = gt
        ytmp_ap = bBf[:, 0:NT]
        YR = [bufB.res]
        A64 = Buf(bufA.ap[0:64], bufA.res)
        B64 = Buf(bufB.ap[0:64], bufB.res)

        def ring64(si):
            return ring_all.ap[0:64, si, :]
        WaTp = Buf(ring64(0).bitcast(F32).rearrange("p (r g c) -> p r g c", r=2, g=G8, c=128), Res())
        Wcppp = Buf(ring64(1).bitcast(F32).rearrange("p (r g c) -> p r g c", r=2, g=G8, c=128), Res())
        Wc2 = [Buf(ring64(2)[:, hh * 2048:(hh + 1) * 2048].rearrange("p (g r c) -> p g r c", g=G8, r=2, c=128), Res()) for hh in range(2)]
        s3 = ring64(3).bitcast(F32)
        tq1p = Buf(s3[:, 0:1024].rearrange("p (g c) -> p g c", g=G8, c=128), Res())
        tq2p = Buf(s3[:, 1024:2048].rearrange("p (g c) -> p g c", g=G8, c=128), Res())
        negone = alloc([1], F32, parts=64)
        vset(negone.ap, -1.0, [negone.res], eng='pool')
        v4 = lambda ap3: ap3.rearrange("p g (s q) -> p g s q", s=8, q=16)
        St, Ht, Hun = A64, B64, A64
        qa = Buf(tq1.ap, tq1.res)
        qb = Buf(tq2.ap, tq2.res)
        GRP = [(0, 3), (3, 3), (6, 2)]

        def ssm_loads(ch):
            g0 = ch * G8
            dma('sp', Brb.ap, I['B_re'][g0:g0 + G8].rearrange("g n q -> n g q"), wr=[Brb.res])
            dma('sp', Bib.ap, I['B_im'][g0:g0 + G8].rearrange("g n q -> n g q"), wr=[Bib.res])
            for (nm, dst) in (('C_re', Crb), ('C_im', Cib)):
                dma('sp', tmpC.ap, I[nm][ch * 128:(ch + 1) * 128, :], wr=[tmpC.res])
                b = nb()
                trp(bank(b)[0:64, 0:128], tmpC.ap, identf.ap, [tmpC.res, identf.res], [PB[b]])
                vcopy(dst.ap.rearrange("p g q -> p (g q)"), bank(b)[0:64, 0:128], [PB[b]], [dst.res], eng='act')

        def ssm_matrices(ch, part=3):
            g0 = ch * G8
            Wc = Wc2[ch % 2]

            def gsq(buf, off_elems):
                a = buf.ap
                return bass.AP(a.tensor, a.offset + off_elems + g0, [list(a.ap[0]), [1, G8], [64, 8], [0, 16]])

            def gq_s(buf):
                a = buf.ap
                return bass.AP(a.tensor, a.offset, [list(a.ap[0]), [16, G8], [0, 8], [1, 16]])
            q4a = Buf(v4(tq1p.ap), tq1p.res)
            q4b = Buf(v4(tq2p.ap), tq2p.res)
            if part & 1:
                cmul(v4(WaTp.ap[:, 0]), v4(WaTp.ap[:, 1]), gsq(cst_r, 0), gsq(cst_i, 0), gq_s(Brb), gq_s(Bib),
                     [cst_r.res, cst_i.res, Brb.res, Bib.res], [WaTp.res], [WaTp.res], q4a, q4b, eng='pool')
            todo = ([(Wcppp, Qst_r, Qst_i, False)] if part & 1 else []) + ([(Wc, Pst_r, Pst_i, True)] if part & 2 else [])
            for (dstb, pw_r, pw_i, isWc) in todo:
                off = 64 if isWc else 0
                pr, pi_ = gsq(pw_r, off), gsq(pw_i, off)
                cr, ci = gq_s(Crb), gq_s(Cib)
                rdl = [pw_r.res, pw_i.res, Crb.res, Cib.res]
                if isWc:
                    o_re, o_im = v4(dstb.ap[:, :, 0, :]), v4(dstb.ap[:, :, 1, :])
                else:
                    o_re, o_im = v4(dstb.ap[:, 0]), v4(dstb.ap[:, 1])
                vtt(q4a.ap, pr, cr, ALU.mult, rdl, [q4a.res], 'pool')
                vtt(q4b.ap, pi_, ci, ALU.mult, rdl, [q4b.res], 'pool')
                vtt(o_re, q4a.ap, q4b.ap, ALU.subtract, [q4a.res, q4b.res], [dstb.res], 'pool')
                vtt(q4a.ap, pi_, cr, ALU.mult, rdl, [q4a.res], 'pool')
                vtt(q4b.ap, pr, ci, ALU.mult, rdl, [q4b.res], 'pool')
                vtt(q4a.ap, q4a.ap, q4b.ap, ALU.add, [q4a.res, q4b.res], [q4a.res], 'pool')
                na = negone.ap
                nbc = bass.AP(na.tensor, na.offset, [list(na.ap[0]), [0, G8], [0, 8], [0, 16]])
                vtt(o_im, q4a.ap, nbc, ALU.mult, [q4a.res, negone.res], [dstb.res], 'pool')

        def ssm_tables(ch):
            g0 = ch * G8
            gs = slice(g0, g0 + G8)
            I32 = mybir.dt.int32
            SC = 2.0 * PI * 0.999999
            T_, Ti_ = tq1, tq2
            vtt(T_.ap, bc(phin.ap[:, gs], 128), bc_mid(ciota.ap, G8), ALU.mult, [phin.res, ciota.res], [T_.res])
            vcopy(Ti_.ap.bitcast(I32), T_.ap, [T_.res], [Ti_.res])
            vcopy(Ri.ap, Ti_.ap.bitcast(I32), [Ti_.res], [Ri.res])
            vtt(Ri.ap, T_.ap, Ri.ap, ALU.subtract, [T_.res, Ri.res], [Ri.res])
            act(Ri.ap, Ri.ap, ACT.Sin, [Ri.res], [Ri.res], scale=-SC)
            vsadd(T_.ap, T_.ap, 0.25, [T_.res], [T_.res])
            vcopy(Ti_.ap.bitcast(I32), T_.ap, [T_.res], [Ti_.res])
            vcopy(Rr.ap, Ti_.ap.bitcast(I32), [Ti_.res], [Rr.res])
            vtt(Rr.ap, T_.ap, Rr.ap, ALU.subtract, [T_.res, Rr.res], [Rr.res])
            act(Rr.ap, Rr.ap, ACT.Sin, [Rr.res], [Rr.res], scale=SC)
            vcopy(rhot.ap, bc(rho8.ap[:, gs], 128), [rho8.res], [rhot.res], eng='pool')
            vset(rhot.ap[:, :, 0:1], 0.0, [rhot.res], eng='pool')

        def ssm_wa_wb(ch):
            for gq in range(2):
                b = nb()
                for gg in range(4):
                    g = gq * 4 + gg
                    for ri in range(2):
                        trp(bank(b)[:, gg * 128 + ri * 64:gg * 128 + (ri + 1) * 64], WaTp.ap[:, ri, g, :], identf.ap[0:64, 0:64],
                            [WaTp.res, identf.res], [PB[b]])
                vcopy(Wa.ap[:, gq * 4:gq * 4 + 4].rearrange("p g r n -> p (g r n)"), bank(b), [PB[b]], [Wa.res], eng='act')
            for gq in range(2):
                b = nb()
                for gg in range(4):
                    g = gq * 4 + gg
                    o = bank(b)[:, gg * 128:(gg + 1) * 128]
                    mm(o, WaTp.ap[:, 0, g, :], Wcppp.ap[:, 0, g, :], True, False, [WaTp.res, Wcppp.res], [PB[b]])
                    mm(o, WaTp.ap[:, 1, g, :], Wcppp.ap[:, 1, g, :], False, True, [WaTp.res, Wcppp.res], [PB[b]])
                vtt(Wb.ap[:, gq * 4:gq * 4 + 4, :], bank(b).rearrange("p (g m) -> p g m", g=4, m=128), bc_mid(maskf.ap, 4), ALU.mult,
                    [PB[b], maskf.res], [Wb.res])

        def rotate_scan():
            cmul(St.ap[:, 0], St.ap[:, 1], Sb.ap[:, 0, :, 0:128], Sb.ap[:, 1, :, 0:128], Rr.ap, Ri.ap,
                 [Sb.res, Rr.res, Ri.res], [St.res], [St.res], qa, qb)
            for ri in range(2):
                S.op('dve', lambda E, ri=ri: E.tensor_tensor_scan(
                    Ht.ap[:, ri].rearrange("p g c -> p (g c)"), rhot.ap.rearrange("p g c -> p (g c)"),
                    St.ap[:, ri].rearrange("p g c -> p (g c)"), 0.0, ALU.mult, ALU.add),
                    reads=[St.res, rhot.res], writes=[Ht.res])

        def shuffle_in(ch, uT, ncol):
            for (ga, gn) in GRP:
                b = nb()
                for gg in range(gn):
                    g = ga + gg
                    for s_ in range(8):
                        mm(bank(b)[:, gg * ncol:(gg + 1) * ncol], sel.ap[:, g * 8 + s_, :], uT.ap[:, ch, s_:8 * ncol:8], s_ == 0, s_ == 7,
                           [sel.res, uT.res], [PB[b]])
                vcopy(U.ap[:, ga:ga + gn, 0:ncol], bank(b)[:, 0:gn * ncol].rearrange("p (g c) -> p g c", g=gn, c=ncol),
                      [PB[b]], [U.res], eng='act')

        def map_a(ncol):
            for ri in range(2):
                for (ga, gn) in GRP:
                    b = nb()
                    for gg in range(gn):
                        g = ga + gg
                        mm(bank(b)[0:64, gg * ncol:(gg + 1) * ncol], Wa.ap[:, g, ri, :], U.ap[:, g, 0:ncol], True, True, [Wa.res, U.res], [PB[b]])
                    vcopy(Sb.ap[:, ri, ga:ga + gn, 0:ncol], bank(b)[0:64, 0:gn * ncol].rearrange("p (g c) -> p g c", g=gn, c=ncol),
                          [PB[b]], [Sb.res] + ([SbS_res] if ncol > 128 else []), eng='act')

        qs1 = Buf(ts8a.ap[:, :, 0], ts8a.res)
        qs2 = Buf(ts8b.ap[:, :, 0], ts8b.res)
        p8r, p8i = P_r[8], P_i[8]
        SbS_res = Res()
        h0T_ch = [Res() for _ in range(8)]
        YB = [5, 6, 7]

        def ssm_pre(ch):
            gs = slice(ch * G8, ch * G8 + G8)
            shuffle_in(ch, upreT, 128)
            map_a(128)

        def ssm_pre_dve(ch):
            gs = slice(ch * G8, ch * G8 + G8)
            rotate_scan()
            cmul(H0.ap[:, 0, gs], H0.ap[:, 1, gs], Ht.ap[:, 0, :, 127], Ht.ap[:, 1, :, 127], Rr.ap[:, :, 127], Ri.ap[:, :, 127],
                 [Ht.res, Rr.res, Ri.res], [H0.res], [H0.res], qs1, qs2, conj_b=True)

        def ssm_main(ch):
            g0 = ch * G8
            gs = slice(g0, g0 + G8)
            Wc = Wc2[ch % 2]
            shuffle_in(ch, umT, 144)
            map_a(144)
            for bi, (ga, gn) in enumerate(GRP):
                b = YB[bi]
                for gg in range(gn):
                    g = ga + gg
                    mm(bank(b)[:, gg * 144:(gg + 1) * 144], Wb.ap[:, g, :], U.ap[:, g, :], gg == 0, False, [Wb.res, U.res], [PB[b]])
            vcopy(Hprev.ap[:, :, :, 128:144], h0T.ap[:, :, :, gs].rearrange("p r s g -> p r g s"), [h0T.res], [Hprev.res, h0T_ch[ch]], eng='act')
            h0v_r = h0T.ap[:, 0, :, gs].rearrange("p s g -> p g s")
            h0v_i = h0T.ap[:, 1, :, gs].rearrange("p s g -> p g s")
            tsa = Buf(tq1p.ap[:, :, 0:16], tq1p.res)
            tsb = Buf(tq2p.ap[:, :, 0:16], tq2p.res)
            cmul(ts8c.ap[:, 0], ts8c.ap[:, 1], bc(p8r.ap[:, gs], 16), bc(p8i.ap[:, gs], 16), h0v_r, h0v_i,
                 [p8r.res, p8i.res, h0T_ch[ch]], [ts8c.res], [ts8c.res], tsa, tsb, eng='pool')
            vtt(h0v_r, ts8c.ap[:, 0], Sb.ap[:, 0, :, 128:144], ALU.add, [ts8c.res, SbS_res], [h0T_ch[ch]], 'pool')
            vtt(h0v_i, ts8c.ap[:, 1], Sb.ap[:, 1, :, 128:144], ALU.add, [ts8c.res, SbS_res], [h0T_ch[ch]], 'pool')
            cc = Buf(tq1.ap[:, :, 64:66].rearrange("p g r -> p r g"), tq1.res)
            cca = Buf(tq2.ap[:, :, 64], tq2.res)
            ccb = Buf(tq2.ap[:, :, 65], tq2.res)
            cmul(cc.ap[:, 0], cc.ap[:, 1], p8r.ap[:, gs], p8i.ap[:, gs], H0.ap[:, 0, gs], H0.ap[:, 1, gs],
                 [p8r.res, p8i.res, H0.res], [cc.res], [cc.res], cca, ccb)
            vtt(Sb.ap[:, :, :, 0], Sb.ap[:, :, :, 0], cc.ap, ALU.add, [Sb.res, cc.res], [Sb.res])
            rotate_scan()
            cmul(Hun.ap[:, 0], Hun.ap[:, 1], Ht.ap[:, 0], Ht.ap[:, 1], Rr.ap, Ri.ap,
                 [Ht.res, Rr.res, Ri.res], [Hun.res], [Hun.res], qa, qb, conj_b=True)
            vcopy(Hprev.ap[:, :, :, 1:128], Hun.ap[:, :, :, 0:127], [Hun.res], [Hprev.res])
            vcopy(Hprev.ap[:, :, :, 0], H0.ap[:, :, gs], [H0.res], [Hprev.res], eng='act')
            vcopy(HpF.ap[:, :, gs], Hun.ap[:, :, :, 127], [Hun.res], [HpF.res], eng='act')
            if ch < 7:
                ssm_wa_wb(ch + 1)
                if ch < 6:
                    ssm_loads(ch + 2)
                    ssm_matrices(ch + 2, part=1)
                ssm_pre(ch + 1)
            for bi, (ga, gn) in enumerate(GRP):
                b = YB[bi]
                for gg in range(gn):
                    g = ga + gg
                    o = bank(b)[:, gg * 144:(gg + 1) * 144]
                    mm(o, Wc.ap[:, g, 0, :], Hprev.ap[:, 0, g, :], False, False, [Wc.res, Hprev.res], [PB[b]])
                    mm(o, Wc.ap[:, g, 1, :], Hprev.ap[:, 1, g, :], False, True, [Wc.res, Hprev.res], [PB[b]])
                vcopy(Ysb.ap[:, ga:ga + gn, :], bank(b)[:, 0:gn * 144].rearrange("p (g c) -> p g c", g=gn, c=144), [PB[b]], [Ysb.res], eng='act')
            if ch < 6:
                ssm_matrices(ch + 2, part=2)
            if ch < 7:
                ssm_tables(ch + 1)
                ssm_pre_dve(ch + 1)
            ua = umT.ap[:, ch, :]
            for (sa, sn) in GRP:
                b = nb()
                for sj in range(sn):
                    s_ = sa + sj
                    for g in range(G8):
                        mm(bank(b)[:, sj * 144:(sj + 1) * 144], sel.ap[:, s_ * 8 + g, :], Ysb.ap[:, g, :], g == 0, g == G8 - 1, [sel.res, Ysb.res], [PB[b]])
                yv = bass.AP(ytmp_ap.tensor, ytmp_ap.offset + sa, [list(ytmp_ap.ap[0]), [1, sn], [8, 144]])
                uv = bass.AP(ua.tensor, ua.offset + sa, [list(ua.ap[0]), [1, sn], [8, 144]])
                vstt(yv, uv, Dcol.ap[:, ch:ch + 1], bank(b)[:, 0:sn * 144].rearrange("p (s c) -> p s c", s=sn, c=144), ALU.mult, ALU.add,
                     [umT.res, Dcol.res, PB[b]], YR)
            if ch == 0:
                tap('y0', ytmp_ap, YR, [128, NT])
            vtt(gt.ap, ytmp_ap, ytmp_ap, ALU.mult, YR, [gt.res])
            vts(gt.ap, gt.ap, 0.044715, 1.0, ALU.mult, ALU.add, [gt.res], [gt.res])
            vtt(gt.ap, gt.ap, ytmp_ap, ALU.mult, [gt.res] + YR, [gt.res])
            act(sg.ap, gt.ap, ACT.Sigmoid, [gt.res], [sg.res], scale=1.5957691216057308)
            vtt(zg.ap[:, ch, :], ytmp_ap, sg.ap, ALU.mult, YR + [sg.res], [zg.res])

        st['nbanks'] = 5
        st['bank'] = 0
        ssm_loads(0)
        ssm_matrices(0)
        ssm_tables(0)
        ssm_wa_wb(0)
        ssm_loads(1)
        ssm_matrices(1)
        ssm_pre(0)
        ssm_pre_dve(0)
        for ch in range(8):
            ssm_main(ch)
        st['nbanks'] = 8

        tap('HpF', HpF.ap.rearrange("p r g -> p (r g)"), [HpF.res], [64, 128])
        tap('H0', H0.ap.rearrange("p r g -> p (r g)"), [H0.res], [64, 128])
        tap('h0T', h0T.ap[:, 0, 0, :], [h0T.res], [64, 64])
        stg = [Buf(bAf[:, k * 64:(k + 1) * 64], Res()) for k in range(18)] + [Buf(bBf[:, k * 64:(k + 1) * 64], Res()) for k in range(18)]
        si_ = 0
        S.barrier()
        for ri, on in ((0, 'hp_re'), (1, 'hp_im')):
            b = nb()
            trp(bank(b)[0:64, 0:64], HpF.ap[:, ri, :], identf.ap[0:64, 0:64], [HpF.res, identf.res], [PB[b]])
            o_ = stg[si_]; si_ += 1
            vcopy(o_.ap[0:64, :], bank(b)[0:64, 0:64], [PB[b]], [o_.res], eng=evq())
            dma('sp', O[on], o_.ap[0:64, :], rd=[o_.res])
        for ri, on in ((0, 'hs_re'), (1, 'hs_im')):
            for j in range(8):
                b = nb()
                trp(bank(b)[:, 0:64], h0T.ap[:, ri, 2 * j:2 * j + 2, :].rearrange("p s g -> p (s g)"), identf.ap[0:64, 0:64],
                    [h0T.res, identf.res], [PB[b]])
                o_ = stg[si_]; si_ += 1
                vcopy(o_.ap, bank(b)[:, 0:64], [PB[b]], [o_.res], eng=evq())
                dma('sp', O[on][j * 128:(j + 1) * 128, :], o_.ap, rd=[o_.res])
        tap('zg', zg.ap[:, 0, :], [zg.res], [128, NT])
        reset('S')
        if stop <= 3:
            return done()

        xnT = alloc([16, NT], BF16, R='E')
        merged = alloc([16, NT], BF16, R='D')
        gsbs = [alloc([512], R='F') for _ in range(3)]
        gtmp = alloc([512], R='F')
        rtmp = alloc([512], BF16, R='F')
        gctr = [0]

        def next_gsb():
            gctr[0] += 1
            return gsbs[gctr[0] % 3]
        use('C')
        zg2 = alloc([8, NT], BF16)
        glub = small['glu_b']
        m4 = mark()
        xsl = [alloc([D]) for _ in range(2)]
        xsb3 = [alloc([D], BF16) for _ in range(3)]
        glu_panels = {}

        def glu_unit(ui):
            mc, gi_ = ui // 3, ui % 3
            t0, n = TG[gi_]
            pk = mc // 4
            if pk not in glu_panels:
                glu_panels[pk] = wpanel(I['glu_w'], 0, 1024, pk * 512, 512)
            panel, prl = glu_panels[pk]
            j = mc % 4
            b = nb()
            for kc in range(8):
                mm(bank(b)[:, 0:n], panel.ap[:, kc, j * 128:(j + 1) * 128], zg.ap[:, kc, t0:t0 + n], kc == 0, kc == 7,
                   prl + [zg.res], [PB[b]])
            gsb = next_gsb()
            act(gsb.ap[:, 0:n], bank(b)[:, 0:n], ACT.Sigmoid, [PB[b], glub.res], [gsb.res], bias=glub.ap[:, mc:mc + 1])
            vtt(zg2.ap[:, mc, t0:t0 + n], gsb.ap[:, 0:n], zg.ap[:, mc, t0:t0 + n], ALU.mult, [gsb.res, zg.res], [zg2.res])
        ui = 0
        for k in range(9 + 1):
            if k < 9:
                norm_ops(I['xm'][k * 128:(k + 1) * 128, :], k, xsl, xsb3[k % 3], None)
            for _ in range(3 if k < 6 else 2):
                if ui < 24:
                    glu_unit(ui)
                    ui += 1
            if k >= 1:
                norm_T(xsb3[(k - 1) % 3], xnT, (k - 1) * 128)
        while ui < 24:
            glu_unit(ui)
            ui += 1
        release(m4)
        bgate = small['b_gate']

        def gated_merge(w2d, K, rhsT, gi, first):
            KCg = 16
            KCb = K // 128
            gpc = 256
            bpc = min(D, (4096 // KCb) // 128 * 128)
            gpanel = bpanel = None
            for mc in range(16):
                if (mc * 128) % gpc == 0:
                    gpanel, grl = wpanel(I['w_in'], 0, D, 3072 + gi * D + mc * 128, gpc, slots=(0, 1), key='rg')
                if (mc * 128) % bpc == 0:
                    bpanel, brl = wpanel(w2d, 0, K, mc * 128, bpc, slots=(2, 3), key='rb')
                jg = (mc * 128 % gpc) // 128
                jb = (mc * 128 % bpc) // 128
                for (t0, n) in TG:
                    b1 = nb()
                    for kc in range(KCg):
                        mm(bank(b1)[:, 0:n], gpanel.ap[:, kc, jg * 128:(jg + 1) * 128], xnT.ap[:, kc, t0:t0 + n], kc == 0, kc == KCg - 1,
                           grl + [xnT.res], [PB[b1]])
                    gsb = next_gsb()
                    act(gsb.ap[:, 0:n], bank(b1)[:, 0:n], ACT.Sigmoid, [PB[b1], bgate.res], [gsb.res],
                        bias=bgate.ap[:, gi * 16 + mc:gi * 16 + mc + 1])
                    b2 = nb()
                    for kc in range(KCb):
                        mm(bank(b2)[:, 0:n], bpanel.ap[:, kc, jb * 128:(jb + 1) * 128], rhsT.ap[:, kc, t0:t0 + n], kc == 0, kc == KCb - 1,
                           brl + [rhsT.res], [PB[b2]])
                    if first:
                        vtt(merged.ap[:, mc, t0:t0 + n], gsb.ap[:, 0:n], bank(b2)[:, 0:n], ALU.mult, [gsb.res, PB[b2]], [merged.res])
                    else:
                        vtt(gtmp.ap[:, 0:n], gsb.ap[:, 0:n], bank(b2)[:, 0:n], ALU.mult, [gsb.res, PB[b2]], [gtmp.res])
                        vtt(merged.ap[:, mc, t0:t0 + n], merged.ap[:, mc, t0:t0 + n], gtmp.ap[:, 0:n], ALU.add,
                            [merged.res, gtmp.res], [merged.res])

        gated_merge(I['ssm_proj'], 1024, zg2, 1, True)
        tap('merged1', merged.ap[:, 0, :], [merged.res], [128, NT])
        reset('C')
        if stop <= 4:
            return done()

        HL = 16
        upP = [alloc([HL + NPR]) for _ in range(2)]
        upS = [alloc([NSEQ, HL + 8]) for _ in range(2)]
        wsP = [alloc([HL + NPR]) for _ in range(2)]
        wsS = [alloc([NSEQ, HL + 8]) for _ in range(2)]
        pooled = alloc([2, NT], BF16)
        mixed = alloc([8, NT], BF16)
        spT = alloc([8, NSEQ, 15])
        usc = [alloc([128]) for _ in range(2)]
        sp_t = alloc([1024])
        po = alloc([1024])
        po2 = alloc([1024])
        pscale = small['pool_scale']
        for half in range(2):
            dma('sp', sp_t.ap[0:120, :], I['spool'][half * 120:(half + 1) * 120, :], wr=[sp_t.res])
            for c in range(8):
                b = nb()
                trp(bank(b)[:, 0:120], sp_t.ap[0:120, c * 128:(c + 1) * 128], identf.ap[0:120, 0:120], [sp_t.res, identf.res], [PB[b]])
                vcopy(spT.ap[:, c, half * 8:(half + 1) * 8, :], bank(b)[:, 0:120].rearrange("p (s r) -> p s r", s=8, r=15),
                      [PB[b]], [spT.res], eng=evq())
        for sq in range(NSEQ):
            dma('sp', O['pool_s'][sq, 0:7, :], I['spool'][sq * 15 + 8:sq * 15 + 15, :])

        def pool_chunk(c):
            uP, uS = upP[c % 2], upS[c % 2]
            k = c // 2
            nlev = k + 1
            wdw = 2 ** nlev
            b = nb()
            trp(bank(b)[0:16, 0:128], uP.ap[:, HL + NPR - 16:HL + NPR], identf.ap, [uP.res, identf.res], [PB[b]])
            vcopy(po.ap[0:16, c * 128:(c + 1) * 128], bank(b)[0:16, 0:128], [PB[b]], [po.res], eng=evq())
            b = nb()
            trp(bank(b)[:, 0:128], usc[c % 2].ap, identf.ap, [usc[c % 2].res, identf.res], [PB[b]])
            vcopy(po2.ap[:, c * 128:(c + 1) * 128], bank(b)[:, 0:128], [PB[b]], [po2.res], eng=evq())
            srcP, srcS = uP.ap, uS.ap
            rdP, rdS = [uP.res], [uS.res]
            for lev in range(nlev):
                sh = 2 ** lev
                lo = 2 ** (lev + 1) - 1
                dP, dS = wsP[lev % 2], wsS[lev % 2]
                vtt(dP.ap[:, lo:], srcP[:, lo:], srcP[:, lo - sh:HL + NPR - sh], ALU.add, rdP, [dP.res])
                vtt(dS.ap[:, :, lo:], srcS[:, :, lo:], srcS[:, :, lo - sh:HL + 8 - sh], ALU.add, rdS, [dS.res])
                srcP, srcS, rdP, rdS = dP.ap, dS.ap, [dP.res], [dS.res]
            j = c % 2
            vstt(pooled.ap[:, j, 0:NPR], srcP[:, HL:], 1.0 / wdw, uP.ap[:, HL:], ALU.mult, ALU.subtract, rdP + [uP.res], [pooled.res])
            vtt(gtmp.ap[:, 0:16], srcP[:, HL:HL + 16], icnt.ap[:, k, :], ALU.mult, rdP + [icnt.res], [gtmp.res])
            vtt(pooled.ap[:, j, 0:16], gtmp.ap[:, 0:16], uP.ap[:, HL:HL + 16], ALU.subtract, [gtmp.res, uP.res], [pooled.res])
            vstt(pooled.ap[:, j, NPR:NT].rearrange("p (s t) -> p s t", s=NSEQ, t=8), srcS[:, :, HL:], 1.0 / wdw, uS.ap[:, :, HL:],
                 ALU.mult, ALU.subtract, rdS + [uS.res], [pooled.res])

        def ev_up(mc, t0, n, b):
            uP, uS = upP[mc % 2], upS[mc % 2]
            if t0 == 0:
                vcopy(uP.ap[:, 0:HL], halo.ap[:, mc, :], [halo.res], [uP.res])
                vset(uS.ap[:, :, 0:1], 0.0, [uS.res])
                vcopy(uS.ap[:, :, 1:16], spT.ap[:, mc, :, :], [spT.res], [uS.res])
            npr = max(0, min(n, NPR - t0))
            if npr > 0:
                vcopy(uP.ap[:, HL + t0:HL + t0 + npr], bank(b)[:, 0:npr], [PB[b]], [uP.res], eng=evq())
            if t0 + n > NPR:
                vcopy(uS.ap[:, :, HL:HL + 8], bank(b)[:, npr:n].rearrange("p (s t) -> p s t", s=NSEQ, t=8), [PB[b]], [uS.res], eng=evq())
                vcopy(usc[mc % 2].ap, bank(b)[:, npr:n], [PB[b]], [usc[mc % 2].res], eng=evq())
                pool_chunk(mc)
                if mc % 2 == 1:
                    g = mc // 2

                    def ev_mix(mc2, t0_, n_, b_, g=g):
                        c_ = g * 2 + mc2
                        vsmul(mixed.ap[:, c_, t0_:t0_ + n_], bank(b_)[:, 0:n_], pscale.ap[:, c_:c_ + 1], [PB[b_], pscale.res], [mixed.res])
                    linear_fm(I['pool_w'], g * 256, 256, 0, 256, pooled, TG, ev_mix)
        linear_fm(I['w_in'], 0, D, 0, 1024, xnT, TG, ev_up)
        dma('sp', O['pool_p'], po.ap[0:16, :], rd=[po.res])
        for sq in range(NSEQ):
            dma('sp', O['pool_s'][sq, 7:15, :], po2.ap[sq * 8:(sq + 1) * 8, :], rd=[po2.res])
        gated_merge(I['pool_proj'], 1024, mixed, 0, False)
        tap('merged2', merged.ap[:, 0, :], [merged.res], [128, NT])
        reset('C')
        if stop <= 5:
            return done()

        qT = alloc([8, NT], BF16)
        oT = alloc([8, NT], BF16)

        def ev_q(mc, t0, n, b):
            vcopy(qT.ap[:, mc, t0:t0 + n], bank(b)[:, 0:n], [PB[b]], [qT.res], eng=evq())
        linear_fm(I['w_in'], 0, D, 2048, 1024, xnT, TG, ev_q)
        eT = alloc([2, 512], BF16)
        rZ = alloc([512])
        SCL = 1.0 / 16.0
        for h in range(4):
            for (t0, n) in TGP:
                for mc in range(2):
                    b = nb()
                    for dc in range(2):
                        mm(bank(b), KT.ap[:, 2 * h + dc, mc * 128:(mc + 1) * 128], qT.ap[:, 2 * h + dc, t0:t0 + n], dc == 0, dc == 1,
                           [KT.res, qT.res], [PB[b]])
                    act(eT.ap[:, mc, :], bank(b), ACT.Exp, [PB[b]], [eT.res], scale=SCL)
                b = nb()
                for mc in range(2):
                    mm(bank(b), onesb.ap, eT.ap[:, mc, :], mc == 0, mc == 1, [onesb.res, eT.res], [PB[b]])
                vrecip(rZ.ap, bank(b), [PB[b]], [rZ.res])
                for dc in range(2):
                    b = nb()
                    for mc in range(2):
                        mm(bank(b), Vb.ap[:, mc, (2 * h + dc) * 128:(2 * h + dc + 1) * 128], eT.ap[:, mc, :], mc == 0, mc == 1,
                           [Vb.res, eT.res], [PB[b]])
                    vtt(oT.ap[:, 2 * h + dc, t0:t0 + n], bank(b), rZ.ap, ALU.mult, [PB[b], rZ.res], [oT.res])
        Ks = [alloc([2, 1024], BF16) for _ in range(2)]
        Vs = [alloc([2, 1024], BF16) for _ in range(2)]
        KTs = alloc([8, 256], BF16)
        eTs = alloc([4, 2, 8], BF16)
        rZs = alloc([4, 8])
        for sq in range(NSEQ):
            ks, vs = Ks[sq % 2], Vs[sq % 2]
            dma('pool', ks.ap, I['ck'][sq].rearrange("(mc p) d -> p mc d", p=128), wr=[ks.res])
            dma('pool', vs.ap, I['cv'][sq].rearrange("(mc p) d -> p mc d", p=128), wr=[vs.res])
            for mc in range(2):
                b = nb()
                for dch in range(8):
                    trp(bankb(b)[:, dch * 128:(dch + 1) * 128], ks.ap[:, mc, dch * 128:(dch + 1) * 128], identb.ap, [ks.res, identb.res], [PB[b]])
                vcopy(KTs.ap[:, :, mc * 128:(mc + 1) * 128], bankb(b).rearrange("p (j c) -> p j c", j=8, c=128), [PB[b]], [KTs.res], eng=evq())
            tk = slice(NPR + sq * 8, NPR + sq * 8 + 8)
            b = nb()
            for h in range(4):
                for mc in range(2):
                    o = bank(b)[:, (h * 2 + mc) * 8:(h * 2 + mc + 1) * 8]
                    for dc in range(2):
                        mm(o, KTs.ap[:, 2 * h + dc, mc * 128:(mc + 1) * 128], qT.ap[:, 2 * h + dc, tk], dc == 0, dc == 1, [KTs.res, qT.res], [PB[b]])
            act(eTs.ap.rearrange("p h m t -> p (h m t)"), bank(b)[:, 0:64], ACT.Exp, [PB[b]], [eTs.res], scale=SCL)
            b = nb()
            for h in range(4):
                for mc in range(2):
                    mm(bank(b)[:, h * 8:(h + 1) * 8], onesb.ap, eTs.ap[:, h, mc, :], mc == 0, mc == 1, [onesb.res, eTs.res], [PB[b]])
            vrecip(rZs.ap.rearrange("p h t -> p (h t)"), bank(b)[:, 0:32], [PB[b]], [rZs.res])
            b = nb()
            for h in range(4):
                for dc in range(2):
                    o = bank(b)[:, (h * 2 + dc) * 8:(h * 2 + dc + 1) * 8]
                    for mc in range(2):
                        mm(o, vs.ap[:, mc, (2 * h + dc) * 128:(2 * h + dc + 1) * 128], eTs.ap[:, h, mc, :], mc == 0, mc == 1, [vs.res, eTs.res], [PB[b]])
            rz4 = rZs.ap
            rzb = bass.AP(rz4.tensor, rz4.offset, [list(rz4.ap[0]), list(rz4.ap[1]), [0, 2], list(rz4.ap[2])])
            vtt(oT.ap[:, :, tk].rearrange("p (h c) t -> p h c t", h=4, c=2), bank(b)[:, 0:64].rearrange("p (h c t) -> p h c t", h=4, c=2, t=8),
                rzb, ALU.mult, [PB[b], rZs.res], [oT.res])
        tap('oT', oT.ap[:, 0, :], [oT.res], [128, NT])
        gated_merge(I['xa_wo'], 1024, oT, 2, False)
        tap('merged3', merged.ap[:, 0, :], [merged.res], [128, NT])
        reset('C')
        if stop <= 6:
            return done()

        reset('E')
        reset('B')
        reset('F')
        x1 = alloc([9, D], R='C')
        xn2T = alloc([16, NT], BF16, R='E')
        xr_t = [alloc([512], R='B') for _ in range(2)]
        rtmps = [alloc([512], BF16, R='B') for _ in range(3)]
        xsb2 = [alloc([D], BF16, R='F') for _ in range(2)]
        load_g('norm2_g')
        for cg in range(4):
            panel, prl = wpanel_big(I['w_out'], 0, D, cg * 512, 512)
            for tt in range(9):
                xrt = xr_t[(cg * 9 + tt) % 2]
                dma('sp', xrt.ap, I['xm'][tt * 128:(tt + 1) * 128, cg * 512:(cg + 1) * 512], wr=[xrt.res])
                b = nb()
                for kc in range(16):
                    mm(bank(b), merged.ap[:, kc, tt * 128:(tt + 1) * 128], panel.ap[:, kc, :], kc == 0, kc == 15, [merged.res] + prl, [PB[b]])
                vtt(x1.ap[:, tt, cg * 512:(cg + 1) * 512], bank(b), xrt.ap, ALU.add, [PB[b], xrt.res], [x1.res])
                if cg == 3:
                    norm_ops(Buf(x1.ap[:, tt, :], x1.res), tt, None, xsb2[tt % 2], None)
                    if tt >= 1:
                        norm_T(xsb2[(tt - 1) % 2], xn2T, (tt - 1) * 128)
        norm_T(xsb2[8 % 2], xn2T, 8 * 128)
        tap('x1', x1.ap[:, 0, :], [x1.res], [128, D])
        reset('D')
        reset('F')
        if stop <= 7:
            return done()

        hT = alloc([16, NT], BF16, R='D')
        junkf = alloc([D], BF16, R='F')
        rctr = [0]
        for fq in range(4):
            def ev_h(mc, t0, n, b):
                rctr[0] += 1
                rt = rtmps[rctr[0] % 3]
                act(rt.ap[:, 0:n], bank(b)[:, 0:n], ACT.Relu, [PB[b]], [rt.res])
                vtt(hT.ap[:, mc, t0:t0 + n], rt.ap[:, 0:n], rt.ap[:, 0:n], ALU.mult, [rt.res], [hT.res])
            linear_fm(I['w1'], 0, D, fq * 2048, 2048, xn2T, TG, ev_h)
            if fq == 3:
                load_g('final_g')
            for cg in range(4):
                panel, prl = wpanel_big(I['w2'], fq * 2048, 2048, cg * 512, 512)
                for tt in range(9):
                    b = nb()
                    for kc in range(16):
                        mm(bank(b), hT.ap[:, kc, tt * 128:(tt + 1) * 128], panel.ap[:, kc, :], kc == 0, kc == 15, [hT.res] + prl, [PB[b]])
                    last = (fq == 3 and cg == 3)
                    xres = Res() if last else x1.res
                    vtt(x1.ap[:, tt, cg * 512:(cg + 1) * 512], x1.ap[:, tt, cg * 512:(cg + 1) * 512], bank(b), ALU.add, [PB[b], x1.res], [xres])
                    if last:
                        ss = ss_list[tt % 4]
                        xt_ap = x1.ap[:, tt, :]
                        vset(ss.ap[:, 0:1], 0.0, [ss.res])
                        act(junkf.ap, xt_ap, ACT.Square, [xres, ss.res], [junkf.res, ss.res], accum_out=ss.ap[:, 0:1])
                        vts(ss.ap[:, 1:2], ss.ap[:, 0:1], 1.0 / D, EPS, ALU.mult, ALU.add, [ss.res], [ss.res])
                        act(ss.ap[:, 1:2], ss.ap[:, 1:2], ACT.Sqrt, [ss.res], [ss.res])
                        vrecip(ss.ap[:, 2:3], ss.ap[:, 1:2], [ss.res], [ss.res])
                        vstt(xt_ap, xt_ap, ss.ap[:, 2:3], gbc.ap, ALU.mult, ALU.mult, [xres, ss.res, gbc.res], [xres])
                        dma('sp', O['y'][tt * 128:(tt + 1) * 128, :], xt_ap, rd=[xres])
        return done()


_WIN = (2, 4, 8, 16)


def _consts():
    ident = np.eye(128, dtype=np.float32)
    sel = np.zeros((64, 128, 128), np.float32)
    for a in range(8):
        for b in range(8):
            for q in range(16):
                sel[a * 8 + b, a * 16 + q, b * 16 + q] = 1.0
    mask = np.zeros((128, 128), np.float32)
    for s in range(8):
        for sp in range(s, 8):
            mask[s * 16:(s + 1) * 16, sp * 16:(sp + 1) * 16] = 1.0
    return ident, sel, mask


def make_in_maps(inp):
    f = lambda a: np.ascontiguousarray(np.asarray(a, dtype=np.float32))
    ident, sel, mask = _consts()
    shared = {
        'norm1_g': f(inp['norm1_g']).reshape(1, D), 'w_in': f(inp['w_in'][0]), 'b_gate': f(inp['b_gate']).reshape(6144),
        'pool_w': f(inp['pool_w']).reshape(1024, 256), 'pool_scale': f(inp['pool_scale']).reshape(1024),
        'pool_proj': f(inp['pool_proj'][0]), 'A_re': f(inp['ssm_A_re'][0]), 'A_im': f(inp['ssm_A_im'][0]),
        'log_dt': f(inp['ssm_log_dt']).reshape(1, 64), 'B_re': f(inp['ssm_B_re'][0]), 'B_im': f(inp['ssm_B_im'][0]),
        'C_re': f(inp['ssm_C_re']).reshape(1024, 64), 'C_im': f(inp['ssm_C_im']).reshape(1024, 64),
        'ssm_D': f(inp['ssm_D']).reshape(1024), 'glu_w': f(inp['ssm_glu_w'][0]), 'glu_b': f(inp['ssm_glu_b']).reshape(1024),
        'ssm_proj': f(inp['ssm_proj'][0]), 'mem_norm_g': f(inp['mem_norm_g']).reshape(1, D),
        'xa_wk': f(inp['xa_wk'][0]), 'xa_wv': f(inp['xa_wv'][0]), 'xa_wo': f(inp['xa_wo'][0]), 'w_out': f(inp['w_out'][0]),
        'norm2_g': f(inp['norm2_g']).reshape(1, D), 'w1': f(inp['mlp_w1'][0]), 'w2': f(inp['mlp_w2'][0]),
        'final_g': f(inp['final_norm_g']).reshape(1, D), 'ident': ident, 'sel': sel, 'mask': mask,
        'ciota': np.arange(128, dtype=np.float32).reshape(1, 128),
    }
    xpr, xsm = inp['x_prompt'], inp['x_sample']
    maps = []
    for c in range(8):
        b, h = c // 2, c % 2
        sq = slice(16 * c, 16 * c + 16)
        m = dict(shared)
        m['xm'] = f(np.concatenate([xpr[b, h * NPR:(h + 1) * NPR], xsm[sq].reshape(NSM, D)], axis=0))
        m['xp'] = f(xpr[b, 0:NPR]) if h == 1 else np.zeros((NPR, D), np.float32)
        m['spool'] = f(inp['state_pool'][0, sq]).reshape(NSEQ * 15, 1024)
        m['sre'] = f(inp['state_ssm_re'][0, sq]).reshape(NSEQ * 64, 64)
        m['sim'] = f(inp['state_ssm_im'][0, sq]).reshape(NSEQ * 64, 64)
        m['ck'] = f(inp['cache_mem_k'][0, sq]).reshape(NSEQ, 256, 1024)
        m['cv'] = f(inp['cache_mem_v'][0, sq]).reshape(NSEQ, 256, 1024)
        m['mem'] = f(inp['mem_prompt'][b])
        ic = np.zeros((4, 16), np.float32)
        for k, w in enumerate(_WIN):
            for t in range(16):
                ic[k, t] = 1.0 / (min(t + 1, w) if h == 0 else w)
        m['icnt'] = ic
        maps.append(m)
    return maps


def assemble(res):
    y_prompt = np.zeros((4, 2048, D), np.float32)
    y_sample = np.zeros((128, 8, D), np.float32)
    pool_p = np.zeros((1, 4, 15, 1024), np.float32)
    re_p = np.zeros((1, 4, 64, 64), np.float32)
    im_p = np.zeros((1, 4, 64, 64), np.float32)
    mk_p = np.zeros((1, 4, 256, 4, 256), np.float32)
    mv_p = np.zeros((1, 4, 256, 4, 256), np.float32)
    pool_s = np.zeros((1, 128, 15, 1024), np.float32)
    re_s = np.zeros((1, 128, 64, 64), np.float32)
    im_s = np.zeros((1, 128, 64, 64), np.float32)
    for c in range(8):
        r = res[c]
        b, h = c // 2, c % 2
        sq = slice(16 * c, 16 * c + 16)
        y_prompt[b, h * NPR:(h + 1) * NPR] = r['y'][0:NPR]
        y_sample[sq] = r['y'][NPR:NT].reshape(NSEQ, 8, D)
        pool_s[0, sq] = r['pool_s']
        re_s[0, sq] = r['hs_re'].reshape(NSEQ, 64, 64)
        im_s[0, sq] = r['hs_im'].reshape(NSEQ, 64, 64)
        if h == 1:
            pool_p[0, b] = r['pool_p'][1:16]
            re_p[0, b] = r['hp_re']
            im_p[0, b] = r['hp_im']
        else:
            mk_p[0, b] = r['mk'].reshape(256, 4, 256)
            mv_p[0, b] = r['mv'].reshape(256, 4, 256)
    return (y_prompt, y_sample, pool_p, re_p, im_p, mk_p, mv_p, pool_s, re_s, im_s)


_NC_CACHE = {}


def kernel(**inputs):
    if 'nc' not in _NC_CACHE:
        _NC_CACHE['nc'] = build_program()[0]
    nc = _NC_CACHE['nc']
    in_maps = make_in_maps(inputs)
    res = run_bass_kernel_spmd(nc, in_maps, core_ids=list(range(8)))
    return assemble(res.results)
```

```python
import math
import numpy as np
import concourse.bass as bass
import concourse.mybir as mybir
from concourse.bass_utils import run_bass_kernel_spmd
from contextlib import ExitStack

F32 = mybir.dt.float32
BF16 = mybir.dt.bfloat16
ACT = mybir.ActivationFunctionType
ALU = mybir.AluOpType

ENGS = ['pe', 'act', 'dve', 'pool', 'sp']
SAME_ENGINE_SYNC = True

D = 2048
NPR = 1024
NSM = 128
NT = NPR + NSM
NSEQ = 16
DFF = 8192
EPS = 1e-6
TG = [(0, 512), (512, 512), (1024, 128)]
TGP = [(0, 512), (512, 512)]
PI = float(np.pi)


class Res:
    __slots__ = ('name', 'w', 'r', 'excl')

    def __init__(self, name='', excl=False):
        self.name = name
        self.w = None
        self.r = {}
        self.excl = excl


class Sched:
    def __init__(self, ndsem=48):
        self.ops = {e: [] for e in ENGS}
        self.seen = {e: {} for e in ENGS}
        self.ndsem = ndsem
        self.dsem_cnt = [0] * ndsem
        self.dsem_next = 0
        self.dsem_next_sw = 0

    def _deps(self, eng, reads, writes):
        deps = []
        for r in reads:
            if r.w is not None:
                deps.append(r.w)
            if r.excl:
                deps.extend(t for k, t in r.r.items() if k != eng)
        for w in writes:
            if w.w is not None:
                deps.append(w.w)
            deps.extend(w.r.values())
        return deps

    def _add_waits(self, eng, deps):
        seen = self.seen[eng]
        best = {}
        for d in deps:
            key = (d[0], d[1])
            if d[0] == 'e' and d[1] == eng and (eng == 'pe' or not SAME_ENGINE_SYNC):
                continue
            if seen.get(key, -1) >= d[2]:
                continue
            if best.get(key, -1) < d[2]:
                best[key] = d[2]
        waits = []
        for key, v in best.items():
            seen[key] = v
            waits.append((key[0], key[1], v))
            if key[0] == 'e':
                self.ops[key[1]][v][3] = True
        return waits

    def op(self, eng, fn, reads=(), writes=()):
        waits = self._add_waits(eng, self._deps(eng, reads, writes))
        idx = len(self.ops[eng])
        self.ops[eng].append(['c', fn, waits, False, None])
        tag = ('e', eng, idx)
        for r in reads:
            r.r[eng] = tag
        for w in writes:
            w.w = tag
            w.r = {}
        return tag

    def dma(self, q, fn, reads=(), writes=()):
        half = self.ndsem // 2
        if q == 'pool':
            s = half + self.dsem_next_sw
            self.dsem_next_sw = (self.dsem_next_sw + 1) % (self.ndsem - half)
        else:
            s = self.dsem_next
            self.dsem_next = (s + 1) % half
        deps = self._deps(q, reads, writes)
        if self.dsem_cnt[s] > 0:
            deps.append(('d', s, self.dsem_cnt[s]))
        waits = self._add_waits(q, deps)
        self.dsem_cnt[s] += 16
        tag = ('d', s, self.dsem_cnt[s])
        self.ops[q].append(['d', fn, waits, False, s])
        for r in reads:
            r.r[('dma', s)] = tag
        for w in writes:
            w.w = tag
            w.r = {}
        return tag

    def barrier(self):
        tags = []
        for e in ENGS:
            for idx in range(len(self.ops[e]) - 1, -1, -1):
                if self.ops[e][idx][0] == 'c':
                    tags.append(('e', e, idx))
                    break
        for s in range(self.ndsem):
            if self.dsem_cnt[s] > 0:
                tags.append(('d', s, self.dsem_cnt[s]))
        for e in ENGS:
            waits = self._add_waits(e, tags)
            if waits:
                self.ops[e].append(['w', None, waits, False, None])

    def finish(self):
        self.barrier()

    def emit(self, nc, es):
        SEMMAX = 30000
        rank = {}
        nsem = {}
        for e in ENGS:
            c = 0
            rk = []
            for o in self.ops[e]:
                if o[3]:
                    c += 1
                rk.append(c)
            rank[e] = rk
            nsem[e] = max(1, (c + SEMMAX - 1) // SEMMAX)
        esem = {e: [es.enter_context(nc.semaphore('es_%s_%d' % (e, i))) for i in range(nsem[e])] for e in ENGS}
        dsem = [es.enter_context(nc.semaphore('ds_%d' % i)) for i in range(self.ndsem)]
        ops = self.ops
        block = es.enter_context(nc.Block())

        def semval(a, r):
            return esem[a][(r - 1) // SEMMAX], (r - 1) % SEMMAX + 1

        def run(e, E):
            for i, (kind, fn, waits, marked, ds) in enumerate(ops[e]):
                for (k, a, v) in waits:
                    if k == 'e':
                        sm, val = semval(a, rank[a][v])
                        E.wait_ge(sm, val)
                    else:
                        E.wait_ge(dsem[a], v)
                if kind == 'w':
                    continue
                inst = fn(E)
                if kind == 'd':
                    inst.then_inc(dsem[ds], 16)
                elif marked:
                    sm, val = semval(e, rank[e][i])
                    inst.then_inc(sm, 1)

        @block.tensor
        def _(E):
            run('pe', E)

        @block.scalar
        def _(E):
            run('act', E)

        @block.vector
        def _(E):
            run('dve', E)

        @block.gpsimd
        def _(E):
            run('pool', E)

        @block.sync
        def _(E):
            run('sp', E)


class Buf:
    __slots__ = ('ap', 'res')

    def __init__(self, ap, res):
        self.ap = ap
        self.res = res


def bc(ap, n):
    return bass.AP(ap.tensor, ap.offset, [list(x) for x in ap.ap] + [[0, n]])


def bc_mid(ap, n):
    a = [list(x) for x in ap.ap]
    return bass.AP(ap.tensor, ap.offset, [a[0], [0, n]] + a[1:])


LAST_S = None


def build_program(stop=99, taps=()):
    global LAST_S
    nc = bass.Bass("TRN2", target_bir_lowering=False)
    S = Sched()
    LAST_S = S
    taps = set(taps)
    tap_out = {}

    def din(name, shape):
        return nc.dram_tensor(name, list(shape), F32, kind="ExternalInput").ap()

    def dout(name, shape):
        return nc.dram_tensor(name, list(shape), F32, kind="ExternalOutput").ap()

    I = {}
    for name, shape in [
        ('xm', [NT, D]), ('xp', [NPR, D]), ('spool', [NSEQ * 15, 1024]), ('sre', [NSEQ * 64, 64]), ('sim', [NSEQ * 64, 64]),
        ('ck', [NSEQ, 256, 1024]), ('cv', [NSEQ, 256, 1024]), ('mem', [256, D]),
        ('norm1_g', [1, D]), ('w_in', [D, 9216]), ('b_gate', [6144]), ('pool_w', [1024, 256]), ('pool_scale', [1024]),
        ('pool_proj', [1024, D]), ('A_re', [64, 64]), ('A_im', [64, 64]), ('log_dt', [1, 64]),
        ('B_re', [64, 64, 16]), ('B_im', [64, 64, 16]), ('C_re', [1024, 64]), ('C_im', [1024, 64]), ('ssm_D', [1024]),
        ('glu_w', [1024, 1024]), ('glu_b', [1024]), ('ssm_proj', [1024, D]), ('mem_norm_g', [1, D]),
        ('xa_wk', [D, 1024]), ('xa_wv', [D, 1024]), ('xa_wo', [1024, D]), ('w_out', [D, D]), ('norm2_g', [1, D]),
        ('w1', [D, DFF]), ('w2', [DFF, D]), ('final_g', [1, D]),
        ('ident', [128, 128]), ('sel', [64, 128, 128]), ('mask', [128, 128]), ('icnt', [4, 16]), ('ciota', [1, 128]),
    ]:
        I[name] = din(name, shape)
    O = {}
    for name, shape in [
        ('y', [NT, D]), ('pool_p', [16, 1024]), ('hp_re', [64, 64]), ('hp_im', [64, 64]), ('mk', [256, 1024]), ('mv', [256, 1024]),
        ('pool_s', [NSEQ, 15, 1024]), ('hs_re', [NSEQ * 64, 64]), ('hs_im', [NSEQ * 64, 64]),
    ]:
        O[name] = dout(name, shape)

    with ExitStack() as es:
        es.enter_context(nc.allow_non_contiguous_dma(reason="small strided param loads"))
        ARENA_BYTES = 207 * 1024 + 512
        arena_t = es.enter_context(nc.sbuf_tensor("arena", [128, ARENA_BYTES // 4], F32))
        ps_t = es.enter_context(nc.psum_tensor("ps", [128, 4096], F32))
        PB = [Res('bank%d' % i, excl=True) for i in range(8)]
        st = {'bank': 0, 'ev': 0, 'ring': 0, 'big': 0}

        KB = 256
        REG = {}

        def region(name, base_kb, size_kb):
            REG[name] = {'base': int(base_kb * KB), 'top': int(base_kb * KB), 'lim': int((base_kb + size_kb) * KB), 'peak': 0}
        region('A', 0, 44)
        region('B', 44, 9)
        region('C', 53, 72)
        region('D', 125, 36)
        region('E', 161, 36)
        region('F', 197, 9)
        region('S', 71, 136.5)
        cur = {'R': 'A'}

        def use(name):
            cur['R'] = name

        def alloc(shape, dtype=F32, parts=128, R=None):
            rg = REG[R or cur['R']]
            n = 1
            for s_ in shape:
                n *= s_
            nb_ = n * (2 if dtype == BF16 else 4)
            nw = (nb_ + 3) // 4
            nw = (nw + 7) // 8 * 8
            w0 = rg['top']
            st['last_w0'] = w0
            rg['top'] += nw
            rg['peak'] = max(rg['peak'], rg['top'])
            assert rg['top'] <= rg['lim'], "region %s overflow: need %d KiB more" % (R or cur['R'], (rg['top'] - rg['lim']) // KB + 1)
            ap = arena_t[0:parts, w0:w0 + nw]
            if dtype == BF16:
                ap = ap.bitcast(BF16)
            ap = ap[:, 0:n]
            if len(shape) == 2:
                ap = ap.rearrange("p (a b) -> p a b", a=shape[0], b=shape[1])
            elif len(shape) == 3:
                ap = ap.rearrange("p (a b c) -> p a b c", a=shape[0], b=shape[1], c=shape[2])
            elif len(shape) == 4:
                ap = ap.rearrange("p (a b c d) -> p a b c d", a=shape[0], b=shape[1], c=shape[2], d=shape[3])
            return Buf(ap, Res())

        def mark(R=None):
            return (R or cur['R'], REG[R or cur['R']]['top'])

        def release(m):
            S.barrier()
            REG[m[0]]['top'] = m[1]

        def reset(R):
            S.barrier()
            REG[R]['top'] = REG[R]['base']

        def bank(i):
            return ps_t[:, i * 512:(i + 1) * 512]

        def bankb(i):
            return ps_t[:, i * 512:(i + 1) * 512].bitcast(BF16)

        def nb():
            b = st['bank'] % st.get('nbanks', 8)
            st['bank'] = (b + 1) % st.get('nbanks', 8)
            return b

        def evq():
            st['ev'] ^= 1
            return 'act' if st['ev'] else 'dve'

        def mm(out, lhsT, rhs, start, stop, rd, wr):
            S.op('pe', lambda E: E.matmul(out, lhsT, rhs, start=start, stop=stop), reads=rd, writes=wr)

        def trp(out, in_, ident, rd, wr):
            S.op('pe', lambda E: E.transpose(out, in_, ident), reads=rd, writes=wr)

        def vtt(out, a, b, op, rd, wr, eng='dve'):
            S.op(eng, lambda E: E.tensor_tensor(out, a, b, op), reads=rd, writes=wr)

        def vts(out, a, s1, s2, op0, op1, rd, wr, eng='dve'):
            S.op(eng, lambda E: E.tensor_scalar(out, a, s1, s2, op0, op1), reads=rd, writes=wr)

        def vstt(out, in0, scalar, in1, op0, op1, rd, wr, eng='dve'):
            S.op(eng, lambda E: E.scalar_tensor_tensor(out, in0, scalar, in1, op0, op1), reads=rd, writes=wr)

        def vcopy(out, a, rd, wr, eng='dve'):
            if eng == 'act':
                S.op('act', lambda E: E.copy(out, a), reads=rd, writes=wr)
            else:
                S.op(eng, lambda E: E.tensor_copy(out, a), reads=rd, writes=wr)

        def vset(out, val, wr, eng='dve'):
            S.op(eng, lambda E: E.memset(out, val), writes=wr)

        def act(out, in_, func, rd, wr, bias=None, scale=None, accum_out=None):
            kw = {}
            if bias is not None:
                kw['bias'] = bias
            if scale is not None:
                kw['scale'] = scale
            if accum_out is not None:
                kw['accum_out'] = accum_out
            S.op('act', lambda E: E.activation(out, in_, func, **kw), reads=rd, writes=wr)

        def dma(q, out, in_, rd=(), wr=()):
            S.dma(q, lambda E: E.dma_start(out=out, in_=in_), reads=rd, writes=wr)

        def tap(name, ap, rd, shape):
            if name in taps:
                t = dout('tap_' + name, shape)
                tap_out[name] = shape
                dma('sp' if ap.dtype == F32 else 'pool', t, ap, rd=rd)

        def vsadd(out, a, c, rd, wr, eng='dve'):
            S.op(eng, lambda E: E.tensor_scalar_add(out, a, c), reads=rd, writes=wr)

        def vsmul(out, a, c, rd, wr, eng='dve'):
            S.op(eng, lambda E: E.tensor_scalar_mul(out, a, c), reads=rd, writes=wr)

        def vrecip(out, a, rd, wr):
            S.op('dve', lambda E: E.reciprocal(out, a), reads=rd, writes=wr)

        def done():
            import os
            if os.environ.get('PEAKS'):
                for k_, v_ in REG.items():
                    print('region', k_, 'peak KiB', (v_['peak'] - v_['base']) / KB, 'size', (v_['lim'] - v_['base']) / KB)
            S.finish()
            S.emit(nc, es)
            return nc, tap_out

        use('A')
        identf = alloc([128])
        dma('sp', identf.ap, I['ident'], wr=[identf.res])
        identb = alloc([128], BF16)
        dma('pool', identb.ap, I['ident'], wr=[identb.res])
        onesb = alloc([128], BF16)
        vset(onesb.ap, 1.0, [onesb.res])
        gbc = alloc([D])
        small = {}

        def load_cols(name, n):
            b = alloc([n // 128])
            dma('sp', b.ap, I[name].rearrange("(c p) -> p c", p=128), wr=[b.res])
            small[name] = b
        load_cols('b_gate', 6144)
        load_cols('pool_scale', 1024)
        load_cols('ssm_D', 1024)
        load_cols('glu_b', 1024)
        icnt = alloc([4, 16])
        dma('sp', icnt.ap, bass.AP(I['icnt'].tensor, 0, [[0, 128], [16, 4], [1, 16]]), wr=[icnt.res])
        ss_list = [alloc([8]) for _ in range(4)]
        ss = ss_list[0]
        ring_all = alloc([4, 4096], BF16)
        ring_res = [Res() for _ in range(4)]

        def load_g(name):
            dma('sp', gbc.ap, bass.AP(I[name].tensor, 0, [[0, 128], [1, D]]), wr=[gbc.res])

        def wpanel(w2d, r0, K, c0, ncols, slots=(0, 1, 2, 3), key='ring'):
            KC = K // 128
            assert KC * ncols <= 4096
            si = slots[st.setdefault(key, 0) % len(slots)]
            st[key] += 1
            view = ring_all.ap[:, si, 0:KC * ncols].rearrange("p (k m) -> p k m", k=KC, m=ncols)
            src = w2d[r0:r0 + K, c0:c0 + ncols].rearrange("(kc p) m -> p kc m", p=128)
            dma('pool', view, src, wr=[ring_res[si]])
            return Buf(view, ring_res[si]), [ring_res[si]]

        def wpanel_big(w2d, r0, K, c0, ncols):
            KC = K // 128
            assert KC * ncols <= 8192
            bi = st['big'] % 2
            st['big'] += 1
            flat = ring_all.ap[:, 2 * bi:2 * bi + 2, :].rearrange("p a b -> p (a b)")
            view = flat[:, 0:KC * ncols].rearrange("p (k m) -> p k m", k=KC, m=ncols)
            src = w2d[r0:r0 + K, c0:c0 + ncols].rearrange("(kc p) m -> p kc m", p=128)
            rl = [ring_res[2 * bi], ring_res[2 * bi + 1]]
            dma('pool', view, src, wr=rl)
            return Buf(view, None), rl

        def norm_ops(s, tt, xslots, xs_b, junk):
            if isinstance(s, Buf):
                xt = s
            else:
                xt = xslots[tt % len(xslots)]
                dma('sp', xt.ap, s, wr=[xt.res])
            ss = ss_list[tt % 4]
            jk = junk if junk is not None else xs_b
            vset(ss.ap[:, 0:1], 0.0, [ss.res])
            act(jk.ap, xt.ap, ACT.Square, [xt.res, ss.res], [jk.res, ss.res], accum_out=ss.ap[:, 0:1])
            vts(ss.ap[:, 1:2], ss.ap[:, 0:1], 1.0 / D, EPS, ALU.mult, ALU.add, [ss.res], [ss.res])
            act(ss.ap[:, 1:2], ss.ap[:, 1:2], ACT.Sqrt, [ss.res], [ss.res])
            vrecip(ss.ap[:, 2:3], ss.ap[:, 1:2], [ss.res], [ss.res])
            vstt(xs_b.ap, xt.ap, ss.ap[:, 2:3], gbc.ap, ALU.mult, ALU.mult, [xt.res, ss.res, gbc.res], [xs_b.res])

        def norm_T(xs_b, dst, c0):
            for h in range(2):
                b = nb()
                for j in range(8):
                    c = h * 8 + j
                    trp(bankb(b)[:, j * 128:(j + 1) * 128], xs_b.ap[:, c * 128:(c + 1) * 128], identb.ap,
                        [xs_b.res, identb.res], [PB[b]])
                vcopy(dst.ap[:, h * 8:(h + 1) * 8, c0:c0 + 128],
                      bankb(b).rearrange("p (j c) -> p j c", j=8, c=128), [PB[b]], [dst.res], eng=evq())

        def norm_transpose(src_fn, ntiles, dst, col0, xslots, xs_b_in, junk):
            for tt in range(ntiles):
                xs_b = xs_b_in[tt % len(xs_b_in)] if isinstance(xs_b_in, list) else xs_b_in
                norm_ops(src_fn(tt), tt, xslots, xs_b, junk)
                norm_T(xs_b, dst, col0 + tt * 128)

        def linear_fm(w2d, r0, K, c0, M, rhsT, tgroups, evac):
            KC = K // 128
            pc = min(M, (4096 // KC) // 128 * 128)
            for p0 in range(0, M, pc):
                panel, prl = wpanel(w2d, r0, K, c0 + p0, pc)
                for j in range(pc // 128):
                    mc = p0 // 128 + j
                    for (t0, n) in tgroups:
                        b = nb()
                        for kc in range(KC):
                            mm(bank(b)[:, 0:n], panel.ap[:, kc, j * 128:(j + 1) * 128], rhsT.ap[:, kc, t0:t0 + n],
                               kc == 0, kc == KC - 1, prl + [rhsT.res], [PB[b]])
                        evac(mc, t0, n, b)

        use('B')
        KT = alloc([8, 256], BF16)
        Vb = alloc([2, 1024], BF16)
        halo = alloc([8, 16])
        use('S')

        def sc():
            return alloc([64], F32, parts=64)
        maskf = alloc([128], R='B')
        dma('sp', maskf.ap, I['mask'], wr=[maskf.res])
        Pst_r = alloc([9, 64], F32, parts=64)
        Pst_i = alloc([9, 64], F32, parts=64)
        Qst_r = alloc([8, 64], F32, parts=64)
        Qst_i = alloc([8, 64], F32, parts=64)
        cst_r = alloc([8, 64], F32, parts=64)
        cst_i = alloc([8, 64], F32, parts=64)
        P_r = [Buf(Pst_r.ap[:, k, :], Pst_r.res) for k in range(9)]
        P_i = [Buf(Pst_i.ap[:, k, :], Pst_i.res) for k in range(9)]
        Q_r = [Buf(Qst_r.ap[:, 7 - k, :], Qst_r.res) for k in range(8)]
        Q_i = [Buf(Qst_i.ap[:, 7 - k, :], Qst_i.res) for k in range(8)]
        cs_r = [Buf(cst_r.ap[:, k, :], cst_r.res) for k in range(8)]
        cs_i = [Buf(cst_i.ap[:, k, :], cst_i.res) for k in range(8)]
        rho8 = sc()
        phin = sc()
        ciota = alloc([128], F32, parts=64)
        dma('sp', ciota.ap, bass.AP(I['ciota'].tensor, 0, [[0, 64], [1, 128]]), wr=[ciota.res])
        H0 = alloc([2, 64], F32, parts=64)
        HpF = alloc([2, 64], F32, parts=64)
        h0T = alloc([2, NSEQ, 64], F32, parts=64)
        sel = alloc([64, 128], BF16)
        dma('pool', sel.ap, I['sel'].rearrange("j p m -> p j m"), wr=[sel.res])
        upreT = alloc([8, NPR], BF16)
        umT = alloc([8, NT], BF16)
        m0 = mark()
        ar, ai, dtb = sc(), sc(), sc()
        tmpA = alloc([64], F32, parts=64)
        for (nm, dst) in (('A_re', ar), ('A_im', ai)):
            dma('sp', tmpA.ap, I[nm], wr=[tmpA.res])
            b = nb()
            trp(bank(b)[0:64, 0:64], tmpA.ap, identf.ap[0:64, 0:64], [tmpA.res, identf.res], [PB[b]])
            vcopy(dst.ap, bank(b)[0:64, 0:64], [PB[b]], [dst.res])
        dma('sp', dtb.ap, bass.AP(I['log_dt'].tensor, 0, [[0, 64], [1, 64]]), wr=[dtb.res])
        act(dtb.ap, dtb.ap, ACT.Exp, [dtb.res], [dtb.res])
        xsl = [alloc([D]) for _ in range(2)]
        xs_b = alloc([D], BF16)
        junk = alloc([D], BF16)
        mnT = alloc([16, 256], BF16)
        load_g('mem_norm_g')
        norm_transpose(lambda tt: I['mem'][tt * 128:(tt + 1) * 128, :], 2, mnT, 0, xsl, xs_b, junk)
        osb = alloc([512])
        for (wn, on, isk) in (('xa_wk', 'mk', True), ('xa_wv', 'mv', False)):
            for half in range(2):
                panel, prl = wpanel_big(I[wn], 0, D, half * 512, 512)
                if isk:
                    for j in range(4):
                        mc = half * 4 + j
                        b = nb()
                        for kc in range(16):
                            mm(bank(b)[:, 0:256], panel.ap[:, kc, j * 128:(j + 1) * 128], mnT.ap[:, kc, :], kc == 0, kc == 15,
                               prl + [mnT.res], [PB[b]])
                        vcopy(KT.ap[:, mc, :], bank(b)[:, 0:256], [PB[b]], [KT.res], eng='act')
                for mt in range(2):
                    b = nb()
                    for kc in range(16):
                        mm(bank(b), mnT.ap[:, kc, mt * 128:(mt + 1) * 128], panel.ap[:, kc, :], kc == 0, kc == 15,
                           prl + [mnT.res], [PB[b]])
                    vcopy(osb.ap, bank(b), [PB[b]], [osb.res], eng='act')
                    if not isk:
                        vcopy(Vb.ap[:, mt, half * 512:(half + 1) * 512], bank(b), [PB[b]], [Vb.res], eng='act')
                    dma('sp', O[on][mt * 128:(mt + 1) * 128, half * 512:(half + 1) * 512], osb.ap, rd=[osb.res])
        lr, th, Em, t1, t2, t3 = sc(), sc(), sc(), sc(), sc(), sc()
        vtt(lr.ap, dtb.ap, ar.ap, ALU.mult, [dtb.res, ar.res], [lr.res])
        vtt(th.ap, dtb.ap, ai.ap, ALU.mult, [dtb.res, ai.res], [th.res])
        act(Em.ap, lr.ap, ACT.Exp, [lr.res], [Em.res])

        def sin_of(dst, src, shift):
            vsadd(t1.ap, src.ap, shift, [src.res], [t1.res])
            vcopy(t2.ap, t1.ap, [t1.res], [t2.res])
            for kk in (1, 3, 5, 7, 9):
                vts(t3.ap, t1.ap, kk * PI, -2 * PI, ALU.is_ge, ALU.mult, [t1.res], [t3.res])
                vtt(t2.ap, t2.ap, t3.ap, ALU.add, [t2.res, t3.res], [t2.res])
            vts(t2.ap, t2.ap, -3.1415925, 3.1415925, ALU.max, ALU.min, [t2.res], [t2.res])
            act(dst.ap, t2.ap, ACT.Sin, [t2.res], [dst.res])
        sn, cs_ = sc(), sc()
        sin_of(sn, th, 0.0)
        sin_of(cs_, th, PI / 2)
        vset(P_r[0].ap, 1.0, [P_r[0].res])
        vset(P_i[0].ap, 0.0, [P_i[0].res])
        vset(Q_r[0].ap, 1.0, [Q_r[0].res])
        vset(Q_i[0].ap, 0.0, [Q_i[0].res])
        vtt(P_r[1].ap, Em.ap, cs_.ap, ALU.mult, [Em.res, cs_.res], [P_r[1].res])
        vtt(P_i[1].ap, Em.ap, sn.ap, ALU.mult, [Em.res, sn.res], [P_i[1].res])

        def cmul(o_r, o_i, a_r, a_i, b_r, b_i, rd, wr_r, wr_i, ta, tb, conj_b=False, eng='dve'):
            vtt(ta.ap, a_r, b_r, ALU.mult, rd, [ta.res], eng)
            vtt(tb.ap, a_i, b_i, ALU.mult, rd, [tb.res], eng)
            vtt(o_r, ta.ap, tb.ap, ALU.add if conj_b else ALU.subtract, [ta.res, tb.res], wr_r, eng)
            vtt(ta.ap, a_i, b_r, ALU.mult, rd, [ta.res], eng)
            vtt(tb.ap, a_r, b_i, ALU.mult, rd, [tb.res], eng)
            vtt(o_i, ta.ap, tb.ap, ALU.subtract if conj_b else ALU.add, [ta.res, tb.res], wr_i, eng)

        def cmul_s(o_r, o_i, a_r, a_i, b_r, b_i, conj_b=False):
            cmul(o_r.ap, o_i.ap, a_r.ap, a_i.ap, b_r.ap, b_i.ap, [a_r.res, a_i.res, b_r.res, b_i.res],
                 [o_r.res], [o_i.res], t1, t2, conj_b)
        for k in range(2, 9):
            cmul_s(P_r[k], P_i[k], P_r[k - 1], P_i[k - 1], P_r[1], P_i[1])
        den, fr, fi, xr = sc(), sc(), sc(), sc()
        vtt(den.ap, ar.ap, ar.ap, ALU.mult, [ar.res], [den.res])
        vtt(t3.ap, ai.ap, ai.ap, ALU.mult, [ai.res], [t3.res])
        vtt(den.ap, den.ap, t3.ap, ALU.add, [den.res, t3.res], [den.res])
        vrecip(den.ap, den.ap, [den.res], [den.res])
        vsadd(xr.ap, P_r[1].ap, -1.0, [P_r[1].res], [xr.res])
        cmul_s(fr, fi, xr, P_i[1], ar, ai, conj_b=True)
        vtt(fr.ap, fr.ap, den.ap, ALU.mult, [fr.res, den.res], [fr.res])
        vtt(fi.ap, fi.ap, den.ap, ALU.mult, [fi.res, den.res], [fi.res])
        einv, irho8 = sc(), sc()
        act(einv.ap, lr.ap, ACT.Exp, [lr.res], [einv.res], scale=-2.0)
        vtt(Q_r[1].ap, P_r[1].ap, einv.ap, ALU.mult, [P_r[1].res, einv.res], [Q_r[1].res])
        vstt(Q_i[1].ap, P_i[1].ap, -1.0, einv.ap, ALU.mult, ALU.mult, [P_i[1].res, einv.res], [Q_i[1].res])
        for k in range(2, 8):
            cmul_s(Q_r[k], Q_i[k], Q_r[k - 1], Q_i[k - 1], Q_r[1], Q_i[1])
        for s_ in range(8):
            cmul_s(cs_r[s_], cs_i[s_], P_r[7 - s_], P_i[7 - s_], fr, fi)
        act(rho8.ap, lr.ap, ACT.Exp, [lr.res], [rho8.res], scale=8.0)
        I32 = mybir.dt.int32
        vsmul(t1.ap, th.ap, 4.0 / PI, [th.res], [t1.res])
        vcopy(t2.ap.bitcast(I32), t1.ap, [t1.res], [t2.res])
        vcopy(t3.ap, t2.ap.bitcast(I32), [t2.res], [t3.res])
        vtt(phin.ap, t1.ap, t3.ap, ALU.subtract, [t1.res, t3.res], [phin.res])
        tap('P8r', P_r[8].ap, [P_r[8].res], [64, 64])
        tap('P8i', P_i[8].ap, [P_i[8].res], [64, 64])
        tap('fr', fr.ap, [fr.res], [64, 64])
        tap('fi', fi.ap, [fi.res], [64, 64])
        tmpS = alloc([64])
        for ri, nm in ((0, 'sre'), (1, 'sim')):
            for j in range(8):
                dma('sp', tmpS.ap, I[nm][j * 128:(j + 1) * 128, :], wr=[tmpS.res])
                b = nb()
                trp(bank(b)[0:64, 0:128], tmpS.ap, identf.ap, [tmpS.res, identf.res], [PB[b]])
                vcopy(h0T.ap[:, ri, 2 * j:2 * j + 2, :], bank(b)[0:64, 0:128].rearrange("p (s g) -> p s g", s=2, g=64),
                      [PB[b]], [h0T.res], eng='act')
        release(m0)
        if stop <= 1:
            return done()

        m2 = mark()
        xsl = [alloc([D]) for _ in range(2)]
        xsb4 = [alloc([D], BF16) for _ in range(4)]
        xg = alloc([16, 512], BF16)
        xhalo = alloc([16, 128], BF16)
        load_g('norm1_g')
        glist = [(I['xp'], upreT, 0, 512), (I['xp'], upreT, 512, 512)] + [(I['xm'], umT, t0, n) for (t0, n) in TG]
        wres = [wpanel(I['w_in'], 0, D, 1024 + p_ * 256, 256, slots=(p_,), key='p2r%d' % p_) for p_ in range(4)]
        tile_ctr = [0]

        def p2_norm_ops(i, k):
            src, dstT, t0, n = glist[i]
            norm_ops(src[t0 + k * 128:t0 + (k + 1) * 128, :], tile_ctr[0], xsl, xsb4[k], None)
            tile_ctr[0] += 1

        def p2_T(i):
            src, dstT, t0, n = glist[i]
            for k in range(n // 128):
                norm_T(xsb4[k], xg, k * 128)
            if i == 1:
                vcopy(xhalo.ap, xg.ap[:, :, 384:512], [xg.res], [xhalo.res], eng='act')

        def p2_unit(i, mc):
            src, dstT, t0, n = glist[i]
            panel, prl = wres[mc // 2]
            j = mc % 2
            b = nb()
            for kc in range(16):
                mm(bank(b)[:, 0:n], panel.ap[:, kc, j * 128:(j + 1) * 128], xg.ap[:, kc, 0:n], kc == 0, kc == 15,
                   prl + [xg.res], [PB[b]])
            vcopy(dstT.ap[:, mc, t0:t0 + n], bank(b)[:, 0:n], [PB[b]], [dstT.res], eng=evq())
        for k in range(glist[0][3] // 128):
            p2_norm_ops(0, k)
        p2_T(0)
        for i in range(len(glist)):
            nxt = glist[i + 1][3] // 128 if i + 1 < len(glist) else 0
            for mc in range(8):
                if mc % 2 == 0 and mc // 2 < nxt:
                    p2_norm_ops(i + 1, mc // 2)
                p2_unit(i, mc)
            if nxt:
                p2_T(i + 1)

        def evh(mc, tt0, nn, b):
            vcopy(halo.ap[:, mc, :], bank(b)[:, 112:128], [PB[b]], [halo.res], eng=evq())
        linear_fm(I['w_in'], 0, D, 0, 1024, xhalo, [(0, 128)], evh)
        release(m2)
        tap('umT', umT.ap[:, 0, :], [umT.res], [128, NT])
        tap('upreT', upreT.ap[:, 0, :], [upreT.res], [128, NPR])
        if stop <= 2:
            return done()

        zg = alloc([8, NT], BF16, R='C')
        G8 = 8
        Brb = alloc([G8, 16], F32, parts=64)
        Bib = alloc([G8, 16], F32, parts=64)
        Crb = alloc([G8, 16], F32, parts=64)
        Cib = alloc([G8, 16], F32, parts=64)
        tmpC = alloc([64], R='A')
        bufA = alloc([2, G8, 128], F32)
        bufB = alloc([2, G8, 128], F32)
        tq1 = alloc([G8, 128], F32, parts=64)
        tq2 = alloc([G8, 128], F32, parts=64)
        Wa = alloc([G8, 2, 64], BF16)
        w0_wa = st['last_w0']
        Wb = alloc([G8, 128], BF16)
        Wc = alloc([G8, 2, 128], BF16, parts=64)
        ytmp_ap = None
        YR = None
        Rr = alloc([G8, 128], F32, parts=64)
        Ri = alloc([G8, 128], F32, parts=64)
        rhot = alloc([G8, 128], F32, parts=64)
        um = [alloc([2, G8], F32, parts=64, R='A') for _ in range(2)]
        U = alloc([G8, 144], BF16)
        Sb = alloc([2, G8, 144], F32, parts=64)
        Hprev = alloc([2, G8, 144], BF16, parts=64)
        Ysb = U
        ts8a = Buf(tq1.ap[:, :, 0:16], tq1.res)
        ts8b = Buf(tq2.ap[:, :, 0:16], tq2.res)
        ts8c = alloc([2, G8, 16], F32, parts=64, R='A')
        Dcol = small['ssm_D']
        bAf = bufA.ap.rearrange("p r g c -> p (r g c)")
        bBf = bufB.ap.rearrange("p r g c -> p (r g c)")
        gt = Buf(bAf[:, 0:NT], bufA.res)
        sg = gt
        ytmp_ap = bBf[:, 0:NT]
        YR = [bufB.res]
        A64 = Buf(bufA.ap[0:64], bufA.res)
        B64 = Buf(bufB.ap[0:64], bufB.res)

        def ring64(si):
            return ring_all.ap[0:64, si, :]
        WaTp = Buf(ring64(0).bitcast(F32).rearrange("p (r g c) -> p r g c", r=2, g=G8, c=128), Res())
        Wcppp = Buf(ring64(1).bitcast(F32).rearrange("p (r g c) -> p r g c", r=2, g=G8, c=128), Res())
        Wc2 = [Buf(ring64(2)[:, hh * 2048:(hh + 1) * 2048].rearrange("p (g r c) -> p g r c", g=G8, r=2, c=128), Res()) for hh in range(2)]
        s3 = ring64(3).bitcast(F32)
        tq1p = Buf(s3[:, 0:1024].rearrange("p (g c) -> p g c", g=G8, c=128), Res())
        tq2p = Buf(s3[:, 1024:2048].rearrange("p (g c) -> p g c", g=G8, c=128), Res())
        negone = alloc([1], F32, parts=64)
        vset(negone.ap, -1.0, [negone.res], eng='pool')
        v4 = lambda ap3: ap3.rearrange("p g (s q) -> p g s q", s=8, q=16)
        St, Ht, Hun = A64, B64, A64
        qa = Buf(tq1.ap, tq1.res)
        qb = Buf(tq2.ap, tq2.res)
        GRP = [(0, 3), (3, 3), (6, 2)]

        def ssm_loads(ch):
            g0 = ch * G8
            dma('sp', Brb.ap, I['B_re'][g0:g0 + G8].rearrange("g n q -> n g q"), wr=[Brb.res])
            dma('sp', Bib.ap, I['B_im'][g0:g0 + G8].rearrange("g n q -> n g q"), wr=[Bib.res])
            for (nm, dst) in (('C_re', Crb), ('C_im', Cib)):
                dma('sp', tmpC.ap, I[nm][ch * 128:(ch + 1) * 128, :], wr=[tmpC.res])
                b = nb()
                trp(bank(b)[0:64, 0:128], tmpC.ap, identf.ap, [tmpC.res, identf.res], [PB[b]])
                vcopy(dst.ap.rearrange("p g q -> p (g q)"), bank(b)[0:64, 0:128], [PB[b]], [dst.res], eng='act')

        def ssm_matrices(ch, part=3):
            g0 = ch * G8
            Wc = Wc2[ch % 2]

            def gsq(buf, off_elems):
                a = buf.ap
                return bass.AP(a.tensor, a.offset + off_elems + g0, [list(a.ap[0]), [1, G8], [64, 8], [0, 16]])

            def gq_s(buf):
                a = buf.ap
                return bass.AP(a.tensor, a.offset, [list(a.ap[0]), [16, G8], [0, 8], [1, 16]])
            q4a = Buf(v4(tq1p.ap), tq1p.res)
            q4b = Buf(v4(tq2p.ap), tq2p.res)
            if part & 1:
                cmul(v4(WaTp.ap[:, 0]), v4(WaTp.ap[:, 1]), gsq(cst_r, 0), gsq(cst_i, 0), gq_s(Brb), gq_s(Bib),
                     [cst_r.res, cst_i.res, Brb.res, Bib.res], [WaTp.res], [WaTp.res], q4a, q4b, eng='pool')
            todo = ([(Wcppp, Qst_r, Qst_i, False)] if part & 1 else []) + ([(Wc, Pst_r, Pst_i, True)] if part & 2 else [])
            for (dstb, pw_r, pw_i, isWc) in todo:
                off = 64 if isWc else 0
                pr, pi_ = gsq(pw_r, off), gsq(pw_i, off)
                cr, ci = gq_s(Crb), gq_s(Cib)
                rdl = [pw_r.res, pw_i.res, Crb.res, Cib.res]
                if isWc:
                    o_re, o_im = v4(dstb.ap[:, :, 0, :]), v4(dstb.ap[:, :, 1, :])
                else:
                    o_re, o_im = v4(dstb.ap[:, 0]), v4(dstb.ap[:, 1])
                vtt(q4a.ap, pr, cr, ALU.mult, rdl, [q4a.res], 'pool')
                vtt(q4b.ap, pi_, ci, ALU.mult, rdl, [q4b.res], 'pool')
                vtt(o_re, q4a.ap, q4b.ap, ALU.subtract, [q4a.res, q4b.res], [dstb.res], 'pool')
                vtt(q4a.ap, pi_, cr, ALU.mult, rdl, [q4a.res], 'pool')
                vtt(q4b.ap, pr, ci, ALU.mult, rdl, [q4b.res], 'pool')
                vtt(q4a.ap, q4a.ap, q4b.ap, ALU.add, [q4a.res, q4b.res], [q4a.res], 'pool')
                na = negone.ap
                nbc = bass.AP(na.tensor, na.offset, [list(na.ap[0]), [0, G8], [0, 8], [0, 16]])
                vtt(o_im, q4a.ap, nbc, ALU.mult, [q4a.res, negone.res], [dstb.res], 'pool')

        def ssm_tables(ch):
            g0 = ch * G8
            gs = slice(g0, g0 + G8)
            I32 = mybir.dt.int32
            SC = 2.0 * PI * 0.999999
            T_, Ti_ = tq1, tq2
            vtt(T_.ap, bc(phin.ap[:, gs], 128), bc_mid(ciota.ap, G8), ALU.mult, [phin.res, ciota.res], [T_.res])
            vcopy(Ti_.ap.bitcast(I32), T_.ap, [T_.res], [Ti_.res])
            vcopy(Ri.ap, Ti_.ap.bitcast(I32), [Ti_.res], [Ri.res])
            vtt(Ri.ap, T_.ap, Ri.ap, ALU.subtract, [T_.res, Ri.res], [Ri.res])
            act(Ri.ap, Ri.ap, ACT.Sin, [Ri.res], [Ri.res], scale=-SC)
            vsadd(T_.ap, T_.ap, 0.25, [T_.res], [T_.res])
            vcopy(Ti_.ap.bitcast(I32), T_.ap, [T_.res], [Ti_.res])
            vcopy(Rr.ap, Ti_.ap.bitcast(I32), [Ti_.res], [Rr.res])
            vtt(Rr.ap, T_.ap, Rr.ap, ALU.subtract, [T_.res, Rr.res], [Rr.res])
            act(Rr.ap, Rr.ap, ACT.Sin, [Rr.res], [Rr.res], scale=SC)
            vcopy(rhot.ap, bc(rho8.ap[:, gs], 128), [rho8.res], [rhot.res], eng='pool')
            vset(rhot.ap[:, :, 0:1], 0.0, [rhot.res], eng='pool')

        def ssm_wa_wb(ch):
            for gq in range(2):
                b = nb()
                for gg in range(4):
                    g = gq * 4 + gg
                    for ri in range(2):
                        trp(bank(b)[:, gg * 128 + ri * 64:gg * 128 + (ri + 1) * 64], WaTp.ap[:, ri, g, :], identf.ap[0:64, 0:64],
                            [WaTp.res, identf.res], [PB[b]])
                vcopy(Wa.ap[:, gq * 4:gq * 4 + 4].rearrange("p g r n -> p (g r n)"), bank(b), [PB[b]], [Wa.res], eng='act')
            for gq in range(2):
                b = nb()
                for gg in range(4):
                    g = gq * 4 + gg
                    o = bank(b)[:, gg * 128:(gg + 1) * 128]
                    mm(o, WaTp.ap[:, 0, g, :], Wcppp.ap[:, 0, g, :], True, False, [WaTp.res, Wcppp.res], [PB[b]])
                    mm(o, WaTp.ap[:, 1, g, :], Wcppp.ap[:, 1, g, :], False, True, [WaTp.res, Wcppp.res], [PB[b]])
                vtt(Wb.ap[:, gq * 4:gq * 4 + 4, :], bank(b).rearrange("p (g m) -> p g m", g=4, m=128), bc_mid(maskf.ap, 4), ALU.mult,
                    [PB[b], maskf.res], [Wb.res])

        def rotate_scan():
            cmul(St.ap[:, 0], St.ap[:, 1], Sb.ap[:, 0, :, 0:128], Sb.ap[:, 1, :, 0:128], Rr.ap, Ri.ap,
                 [Sb.res, Rr.res, Ri.res], [St.res], [St.res], qa, qb)
            for ri in range(2):
                S.op('dve', lambda E, ri=ri: E.tensor_tensor_scan(
                    Ht.ap[:, ri].rearrange("p g c -> p (g c)"), rhot.ap.rearrange("p g c -> p (g c)"),
                    St.ap[:, ri].rearrange("p g c -> p (g c)"), 0.0, ALU.mult, ALU.add),
                    reads=[St.res, rhot.res], writes=[Ht.res])

        def shuffle_in(ch, uT, ncol):
            for (ga, gn) in GRP:
                b = nb()
                for gg in range(gn):
                    g = ga + gg
                    for s_ in range(8):
                        mm(bank(b)[:, gg * ncol:(gg + 1) * ncol], sel.ap[:, g * 8 + s_, :], uT.ap[:, ch, s_:8 * ncol:8], s_ == 0, s_ == 7,
                           [sel.res, uT.res], [PB[b]])
                vcopy(U.ap[:, ga:ga + gn, 0:ncol], bank(b)[:, 0:gn * ncol].rearrange("p (g c) -> p g c", g=gn, c=ncol),
                      [PB[b]], [U.res], eng='act')

        def map_a(ncol):
            for ri in range(2):
                for (ga, gn) in GRP:
                    b = nb()
                    for gg in range(gn):
                        g = ga + gg
                        mm(bank(b)[0:64, gg * ncol:(gg + 1) * ncol], Wa.ap[:, g, ri, :], U.ap[:, g, 0:ncol], True, True, [Wa.res, U.res], [PB[b]])
                    vcopy(Sb.ap[:, ri, ga:ga + gn, 0:ncol], bank(b)[0:64, 0:gn * ncol].rearrange("p (g c) -> p g c", g=gn, c=ncol),
                          [PB[b]], [Sb.res] + ([SbS_res] if ncol > 128 else []), eng='act')

        qs1 = Buf(ts8a.ap[:, :, 0], ts8a.res)
        qs2 = Buf(ts8b.ap[:, :, 0], ts8b.res)
        p8r, p8i = P_r[8], P_i[8]
        CC = alloc([2, 64], F32, parts=64)
        SbS_res = Res()
        h0T_ch = [Res() for _ in range(8)]
        YB = [5, 6, 7]

        def ssm_pre(ch):
            gs = slice(ch * G8, ch * G8 + G8)
            shuffle_in(ch, upreT, 128)
            map_a(128)

        def ssm_pre_dve(ch):
            gs = slice(ch * G8, ch * G8 + G8)
            rotate_scan()
            cmul(H0.ap[:, 0, gs], H0.ap[:, 1, gs], Ht.ap[:, 0, :, 127], Ht.ap[:, 1, :, 127], Rr.ap[:, :, 127], Ri.ap[:, :, 127],
                 [Ht.res, Rr.res, Ri.res], [H0.res], [H0.res], qs1, qs2, conj_b=True)
            ca = Buf(tq1p.ap[:, :, 16], tq1p.res)
            cb = Buf(tq2p.ap[:, :, 16], tq2p.res)
            cmul(CC.ap[:, 0, gs], CC.ap[:, 1, gs], p8r.ap[:, gs], p8i.ap[:, gs], H0.ap[:, 0, gs], H0.ap[:, 1, gs],
                 [p8r.res, p8i.res, H0.res], [CC.res], [CC.res], ca, cb, eng='pool')

        def ssm_main(ch):
            g0 = ch * G8
            gs = slice(g0, g0 + G8)
            Wc = Wc2[ch % 2]
            shuffle_in(ch, umT, 144)
            map_a(144)
            for bi, (ga, gn) in enumerate(GRP):
                b = YB[bi]
                for gg in range(gn):
                    g = ga + gg
                    mm(bank(b)[:, gg * 144:(gg + 1) * 144], Wb.ap[:, g, :], U.ap[:, g, :], gg == 0, False, [Wb.res, U.res], [PB[b]])
            vcopy(Hprev.ap[:, :, :, 128:144], h0T.ap[:, :, :, gs].rearrange("p r s g -> p r g s"), [h0T.res], [Hprev.res, h0T_ch[ch]], eng='act')
            h0v_r = h0T.ap[:, 0, :, gs].rearrange("p s g -> p g s")
            h0v_i = h0T.ap[:, 1, :, gs].rearrange("p s g -> p g s")
            tsa = Buf(tq1p.ap[:, :, 0:16], tq1p.res)
            tsb = Buf(tq2p.ap[:, :, 0:16], tq2p.res)
            cmul(ts8c.ap[:, 0], ts8c.ap[:, 1], bc(p8r.ap[:, gs], 16), bc(p8i.ap[:, gs], 16), h0v_r, h0v_i,
                 [p8r.res, p8i.res, h0T_ch[ch]], [ts8c.res], [ts8c.res], tsa, tsb, eng='pool')
            vtt(h0v_r, ts8c.ap[:, 0], Sb.ap[:, 0, :, 128:144], ALU.add, [ts8c.res, SbS_res], [h0T_ch[ch]], 'pool')
            vtt(h0v_i, ts8c.ap[:, 1], Sb.ap[:, 1, :, 128:144], ALU.add, [ts8c.res, SbS_res], [h0T_ch[ch]], 'pool')
            vtt(Sb.ap[:, :, :, 0], Sb.ap[:, :, :, 0], CC.ap[:, :, gs], ALU.add, [Sb.res, CC.res], [Sb.res])
            rotate_scan()
            cmul(Hun.ap[:, 0], Hun.ap[:, 1], Ht.ap[:, 0], Ht.ap[:, 1], Rr.ap, Ri.ap,
                 [Ht.res, Rr.res, Ri.res], [Hun.res], [Hun.res], qa, qb, conj_b=True)
            vcopy(Hprev.ap[:, :, :, 1:128], Hun.ap[:, :, :, 0:127], [Hun.res], [Hprev.res], eng='act')
            vcopy(Hprev.ap[:, :, :, 0], H0.ap[:, :, gs], [H0.res], [Hprev.res], eng='act')
            vcopy(HpF.ap[:, :, gs], Hun.ap[:, :, :, 127], [Hun.res], [HpF.res], eng='act')
            if ch < 7:
                ssm_wa_wb(ch + 1)
                if ch < 6:
                    ssm_loads(ch + 2)
                    ssm_matrices(ch + 2, part=1)
                ssm_pre(ch + 1)
            for bi, (ga, gn) in enumerate(GRP):
                b = YB[bi]
                for gg in range(gn):
                    g = ga + gg
                    o = bank(b)[:, gg * 144:(gg + 1) * 144]
                    mm(o, Wc.ap[:, g, 0, :], Hprev.ap[:, 0, g, :], False, False, [Wc.res, Hprev.res], [PB[b]])
                    mm(o, Wc.ap[:, g, 1, :], Hprev.ap[:, 1, g, :], False, True, [Wc.res, Hprev.res], [PB[b]])
                vcopy(Ysb.ap[:, ga:ga + gn, :], bank(b)[:, 0:gn * 144].rearrange("p (g c) -> p g c", g=gn, c=144), [PB[b]], [Ysb.res], eng='act')
            if ch < 6:
                ssm_matrices(ch + 2, part=2)
            if ch < 7:
                ssm_tables(ch + 1)
                ssm_pre_dve(ch + 1)
            ua = umT.ap[:, ch, :]
            for (sa, sn) in GRP:
                b = nb()
                for sj in range(sn):
                    s_ = sa + sj
                    for g in range(G8):
                        mm(bank(b)[:, sj * 144:(sj + 1) * 144], sel.ap[:, s_ * 8 + g, :], Ysb.ap[:, g, :], g == 0, g == G8 - 1, [sel.res, Ysb.res], [PB[b]])
                yv = bass.AP(ytmp_ap.tensor, ytmp_ap.offset + sa, [list(ytmp_ap.ap[0]), [1, sn], [8, 144]])
                uv = bass.AP(ua.tensor, ua.offset + sa, [list(ua.ap[0]), [1, sn], [8, 144]])
                vstt(yv, uv, Dcol.ap[:, ch:ch + 1], bank(b)[:, 0:sn * 144].rearrange("p (s c) -> p s c", s=sn, c=144), ALU.mult, ALU.add,
                     [umT.res, Dcol.res, PB[b]], YR)
            if ch == 0:
                tap('y0', ytmp_ap, YR, [128, NT])
            vtt(gt.ap, ytmp_ap, ytmp_ap, ALU.mult, YR, [gt.res])
            vts(gt.ap, gt.ap, 0.044715, 1.0, ALU.mult, ALU.add, [gt.res], [gt.res])
            vtt(gt.ap, gt.ap, ytmp_ap, ALU.mult, [gt.res] + YR, [gt.res])
            act(sg.ap, gt.ap, ACT.Sigmoid, [gt.res], [sg.res], scale=1.5957691216057308)
            vtt(zg.ap[:, ch, :], ytmp_ap, sg.ap, ALU.mult, YR + [sg.res], [zg.res])

        st['nbanks'] = 5
        st['bank'] = 0
        ssm_loads(0)
        ssm_matrices(0)
        ssm_tables(0)
        ssm_wa_wb(0)
        ssm_loads(1)
        ssm_matrices(1)
        ssm_pre(0)
        ssm_pre_dve(0)
        for ch in range(8):
            ssm_main(ch)
        st['nbanks'] = 8

        tap('HpF', HpF.ap.rearrange("p r g -> p (r g)"), [HpF.res], [64, 128])
        tap('H0', H0.ap.rearrange("p r g -> p (r g)"), [H0.res], [64, 128])
        tap('h0T', h0T.ap[:, 0, 0, :], [h0T.res], [64, 64])
        stg = [Buf(bAf[:, k * 64:(k + 1) * 64], Res()) for k in range(18)] + [Buf(bBf[:, k * 64:(k + 1) * 64], Res()) for k in range(18)]
        si_ = 0
        S.barrier()
        for ri, on in ((0, 'hp_re'), (1, 'hp_im')):
            b = nb()
            trp(bank(b)[0:64, 0:64], HpF.ap[:, ri, :], identf.ap[0:64, 0:64], [HpF.res, identf.res], [PB[b]])
            o_ = stg[si_]; si_ += 1
            vcopy(o_.ap[0:64, :], bank(b)[0:64, 0:64], [PB[b]], [o_.res], eng=evq())
            dma('sp', O[on], o_.ap[0:64, :], rd=[o_.res])
        for ri, on in ((0, 'hs_re'), (1, 'hs_im')):
            for j in range(8):
                b = nb()
                trp(bank(b)[:, 0:64], h0T.ap[:, ri, 2 * j:2 * j + 2, :].rearrange("p s g -> p (s g)"), identf.ap[0:64, 0:64],
                    [h0T.res, identf.res], [PB[b]])
                o_ = stg[si_]; si_ += 1
                vcopy(o_.ap, bank(b)[:, 0:64], [PB[b]], [o_.res], eng=evq())
                dma('sp', O[on][j * 128:(j + 1) * 128, :], o_.ap, rd=[o_.res])
        tap('zg', zg.ap[:, 0, :], [zg.res], [128, NT])
        reset('S')
        if stop <= 3:
            return done()

        xnT = alloc([16, NT], BF16, R='E')
        merged = alloc([16, NT], BF16, R='D')
        gsbs = [alloc([512], R='F') for _ in range(3)]
        gtmp = alloc([512], R='F')
        rtmp = alloc([512], BF16, R='F')
        gctr = [0]

        def next_gsb():
            gctr[0] += 1
            return gsbs[gctr[0] % 3]
        use('C')
        zg2 = alloc([8, NT], BF16)
        glub = small['glu_b']
        m4 = mark()
        xsl = [alloc([D]) for _ in range(2)]
        xsb3 = [alloc([D], BF16) for _ in range(3)]
        glu_panels = {}

        def glu_unit(ui):
            mc, gi_ = ui // 3, ui % 3
            t0, n = TG[gi_]
            pk = mc // 4
            if pk not in glu_panels:
                glu_panels[pk] = wpanel(I['glu_w'], 0, 1024, pk * 512, 512)
            panel, prl = glu_panels[pk]
            j = mc % 4
            b = nb()
            for kc in range(8):
                mm(bank(b)[:, 0:n], panel.ap[:, kc, j * 128:(j + 1) * 128], zg.ap[:, kc, t0:t0 + n], kc == 0, kc == 7,
                   prl + [zg.res], [PB[b]])
            gsb = next_gsb()
            act(gsb.ap[:, 0:n], bank(b)[:, 0:n], ACT.Sigmoid, [PB[b], glub.res], [gsb.res], bias=glub.ap[:, mc:mc + 1])
            vtt(zg2.ap[:, mc, t0:t0 + n], gsb.ap[:, 0:n], zg.ap[:, mc, t0:t0 + n], ALU.mult, [gsb.res, zg.res], [zg2.res])
        ui = 0
        for k in range(9 + 1):
            if k < 9:
                norm_ops(I['xm'][k * 128:(k + 1) * 128, :], k, xsl, xsb3[k % 3], None)
            for _ in range(3 if k < 6 else 2):
                if ui < 24:
                    glu_unit(ui)
                    ui += 1
            if k >= 1:
                norm_T(xsb3[(k - 1) % 3], xnT, (k - 1) * 128)
        while ui < 24:
            glu_unit(ui)
            ui += 1
        release(m4)
        bgate = small['b_gate']

        def gated_merge(w2d, K, rhsT, gi, first):
            KCg = 16
            KCb = K // 128
            gpc = 256
            bpc = min(D, (4096 // KCb) // 128 * 128)
            gpanel = bpanel = None
            for mc in range(16):
                if (mc * 128) % gpc == 0:
                    gpanel, grl = wpanel(I['w_in'], 0, D, 3072 + gi * D + mc * 128, gpc, slots=(0, 1), key='rg')
                if (mc * 128) % bpc == 0:
                    bpanel, brl = wpanel(w2d, 0, K, mc * 128, bpc, slots=(2, 3), key='rb')
                jg = (mc * 128 % gpc) // 128
                jb = (mc * 128 % bpc) // 128
                for (t0, n) in TG:
                    b1 = nb()
                    for kc in range(KCg):
                        mm(bank(b1)[:, 0:n], gpanel.ap[:, kc, jg * 128:(jg + 1) * 128], xnT.ap[:, kc, t0:t0 + n], kc == 0, kc == KCg - 1,
                           grl + [xnT.res], [PB[b1]])
                    gsb = next_gsb()
                    act(gsb.ap[:, 0:n], bank(b1)[:, 0:n], ACT.Sigmoid, [PB[b1], bgate.res], [gsb.res],
                        bias=bgate.ap[:, gi * 16 + mc:gi * 16 + mc + 1])
                    b2 = nb()
                    for kc in range(KCb):
                        mm(bank(b2)[:, 0:n], bpanel.ap[:, kc, jb * 128:(jb + 1) * 128], rhsT.ap[:, kc, t0:t0 + n], kc == 0, kc == KCb - 1,
                           brl + [rhsT.res], [PB[b2]])
                    if first:
                        vtt(merged.ap[:, mc, t0:t0 + n], gsb.ap[:, 0:n], bank(b2)[:, 0:n], ALU.mult, [gsb.res, PB[b2]], [merged.res])
                    else:
                        vtt(gtmp.ap[:, 0:n], gsb.ap[:, 0:n], bank(b2)[:, 0:n], ALU.mult, [gsb.res, PB[b2]], [gtmp.res])
                        vtt(merged.ap[:, mc, t0:t0 + n], merged.ap[:, mc, t0:t0 + n], gtmp.ap[:, 0:n], ALU.add,
                            [merged.res, gtmp.res], [merged.res])

        gated_merge(I['ssm_proj'], 1024, zg2, 1, True)
        tap('merged1', merged.ap[:, 0, :], [merged.res], [128, NT])
        reset('C')
        if stop <= 4:
            return done()

        HL = 16
        upP = [alloc([HL + NPR]) for _ in range(2)]
        upS = [alloc([NSEQ, HL + 8]) for _ in range(2)]
        wsP = [alloc([HL + NPR]) for _ in range(2)]
        wsS = [alloc([NSEQ, HL + 8]) for _ in range(2)]
        pooled = alloc([2, NT], BF16)
        mixed = alloc([8, NT], BF16)
        spT = alloc([8, NSEQ, 15])
        usc = [alloc([128]) for _ in range(2)]
        sp_t = alloc([1024])
        po = alloc([1024])
        po2 = alloc([1024])
        pscale = small['pool_scale']
        for half in range(2):
            dma('sp', sp_t.ap[0:120, :], I['spool'][half * 120:(half + 1) * 120, :], wr=[sp_t.res])
            for c in range(8):
                b = nb()
                trp(bank(b)[:, 0:120], sp_t.ap[0:120, c * 128:(c + 1) * 128], identf.ap[0:120, 0:120], [sp_t.res, identf.res], [PB[b]])
                vcopy(spT.ap[:, c, half * 8:(half + 1) * 8, :], bank(b)[:, 0:120].rearrange("p (s r) -> p s r", s=8, r=15),
                      [PB[b]], [spT.res], eng=evq())
        for sq in range(NSEQ):
            dma('sp', O['pool_s'][sq, 0:7, :], I['spool'][sq * 15 + 8:sq * 15 + 15, :])

        def pool_chunk(c):
            uP, uS = upP[c % 2], upS[c % 2]
            k = c // 2
            nlev = k + 1
            wdw = 2 ** nlev
            b = nb()
            trp(bank(b)[0:16, 0:128], uP.ap[:, HL + NPR - 16:HL + NPR], identf.ap, [uP.res, identf.res], [PB[b]])
            vcopy(po.ap[0:16, c * 128:(c + 1) * 128], bank(b)[0:16, 0:128], [PB[b]], [po.res], eng=evq())
            b = nb()
            trp(bank(b)[:, 0:128], usc[c % 2].ap, identf.ap, [usc[c % 2].res, identf.res], [PB[b]])
            vcopy(po2.ap[:, c * 128:(c + 1) * 128], bank(b)[:, 0:128], [PB[b]], [po2.res], eng=evq())
            srcP, srcS = uP.ap, uS.ap
            rdP, rdS = [uP.res], [uS.res]
            for lev in range(nlev):
                sh = 2 ** lev
                lo = 2 ** (lev + 1) - 1
                dP, dS = wsP[lev % 2], wsS[lev % 2]
                vtt(dP.ap[:, lo:], srcP[:, lo:], srcP[:, lo - sh:HL + NPR - sh], ALU.add, rdP, [dP.res])
                vtt(dS.ap[:, :, lo:], srcS[:, :, lo:], srcS[:, :, lo - sh:HL + 8 - sh], ALU.add, rdS, [dS.res])
                srcP, srcS, rdP, rdS = dP.ap, dS.ap, [dP.res], [dS.res]
            j = c % 2
            vstt(pooled.ap[:, j, 0:NPR], srcP[:, HL:], 1.0 / wdw, uP.ap[:, HL:], ALU.mult, ALU.subtract, rdP + [uP.res], [pooled.res])
            vtt(gtmp.ap[:, 0:16], srcP[:, HL:HL + 16], icnt.ap[:, k, :], ALU.mult, rdP + [icnt.res], [gtmp.res])
            vtt(pooled.ap[:, j, 0:16], gtmp.ap[:, 0:16], uP.ap[:, HL:HL + 16], ALU.subtract, [gtmp.res, uP.res], [pooled.res])
            vstt(pooled.ap[:, j, NPR:NT].rearrange("p (s t) -> p s t", s=NSEQ, t=8), srcS[:, :, HL:], 1.0 / wdw, uS.ap[:, :, HL:],
                 ALU.mult, ALU.subtract, rdS + [uS.res], [pooled.res])

        def ev_up(mc, t0, n, b):
            uP, uS = upP[mc % 2], upS[mc % 2]
            if t0 == 0:
                vcopy(uP.ap[:, 0:HL], halo.ap[:, mc, :], [halo.res], [uP.res])
                vset(uS.ap[:, :, 0:1], 0.0, [uS.res])
                vcopy(uS.ap[:, :, 1:16], spT.ap[:, mc, :, :], [spT.res], [uS.res])
            npr = max(0, min(n, NPR - t0))
            if npr > 0:
                vcopy(uP.ap[:, HL + t0:HL + t0 + npr], bank(b)[:, 0:npr], [PB[b]], [uP.res], eng=evq())
            if t0 + n > NPR:
                vcopy(uS.ap[:, :, HL:HL + 8], bank(b)[:, npr:n].rearrange("p (s t) -> p s t", s=NSEQ, t=8), [PB[b]], [uS.res], eng=evq())
                vcopy(usc[mc % 2].ap, bank(b)[:, npr:n], [PB[b]], [usc[mc % 2].res], eng=evq())
                pool_chunk(mc)
                if mc % 2 == 1:
                    g = mc // 2

                    def ev_mix(mc2, t0_, n_, b_, g=g):
                        c_ = g * 2 + mc2
                        vsmul(mixed.ap[:, c_, t0_:t0_ + n_], bank(b_)[:, 0:n_], pscale.ap[:, c_:c_ + 1], [PB[b_], pscale.res], [mixed.res])
                    linear_fm(I['pool_w'], g * 256, 256, 0, 256, pooled, TG, ev_mix)
        linear_fm(I['w_in'], 0, D, 0, 1024, xnT, TG, ev_up)
        dma('sp', O['pool_p'], po.ap[0:16, :], rd=[po.res])
        for sq in range(NSEQ):
            dma('sp', O['pool_s'][sq, 7:15, :], po2.ap[sq * 8:(sq + 1) * 8, :], rd=[po2.res])
        gated_merge(I['pool_proj'], 1024, mixed, 0, False)
        tap('merged2', merged.ap[:, 0, :], [merged.res], [128, NT])
        reset('C')
        if stop <= 5:
            return done()

        qT = alloc([8, NT], BF16)
        oT = alloc([8, NT], BF16)

        def ev_q(mc, t0, n, b):
            vcopy(qT.ap[:, mc, t0:t0 + n], bank(b)[:, 0:n], [PB[b]], [qT.res], eng=evq())
        linear_fm(I['w_in'], 0, D, 2048, 1024, xnT, TG, ev_q)
        eT = alloc([2, 512], BF16)
        rZ = alloc([512])
        SCL = 1.0 / 16.0
        for h in range(4):
            for (t0, n) in TGP:
                for mc in range(2):
                    b = nb()
                    for dc in range(2):
                        mm(bank(b), KT.ap[:, 2 * h + dc, mc * 128:(mc + 1) * 128], qT.ap[:, 2 * h + dc, t0:t0 + n], dc == 0, dc == 1,
                           [KT.res, qT.res], [PB[b]])
                    act(eT.ap[:, mc, :], bank(b), ACT.Exp, [PB[b]], [eT.res], scale=SCL)
                b = nb()
                for mc in range(2):
                    mm(bank(b), onesb.ap, eT.ap[:, mc, :], mc == 0, mc == 1, [onesb.res, eT.res], [PB[b]])
                vrecip(rZ.ap, bank(b), [PB[b]], [rZ.res])
                for dc in range(2):
                    b = nb()
                    for mc in range(2):
                        mm(bank(b), Vb.ap[:, mc, (2 * h + dc) * 128:(2 * h + dc + 1) * 128], eT.ap[:, mc, :], mc == 0, mc == 1,
                           [Vb.res, eT.res], [PB[b]])
                    vtt(oT.ap[:, 2 * h + dc, t0:t0 + n], bank(b), rZ.ap, ALU.mult, [PB[b], rZ.res], [oT.res])
        Ks = [alloc([2, 1024], BF16) for _ in range(2)]
        Vs = [alloc([2, 1024], BF16) for _ in range(2)]
        KTs = alloc([8, 256], BF16)
        eTs = alloc([4, 2, 8], BF16)
        rZs = alloc([4, 8])
        for sq in range(NSEQ):
            ks, vs = Ks[sq % 2], Vs[sq % 2]
            dma('pool', ks.ap, I['ck'][sq].rearrange("(mc p) d -> p mc d", p=128), wr=[ks.res])
            dma('pool', vs.ap, I['cv'][sq].rearrange("(mc p) d -> p mc d", p=128), wr=[vs.res])
            for mc in range(2):
                b = nb()
                for dch in range(8):
                    trp(bankb(b)[:, dch * 128:(dch + 1) * 128], ks.ap[:, mc, dch * 128:(dch + 1) * 128], identb.ap, [ks.res, identb.res], [PB[b]])
                vcopy(KTs.ap[:, :, mc * 128:(mc + 1) * 128], bankb(b).rearrange("p (j c) -> p j c", j=8, c=128), [PB[b]], [KTs.res], eng=evq())
            tk = slice(NPR + sq * 8, NPR + sq * 8 + 8)
            b = nb()
            for h in range(4):
                for mc in range(2):
                    o = bank(b)[:, (h * 2 + mc) * 8:(h * 2 + mc + 1) * 8]
                    for dc in range(2):
                        mm(o, KTs.ap[:, 2 * h + dc, mc * 128:(mc + 1) * 128], qT.ap[:, 2 * h + dc, tk], dc == 0, dc == 1, [KTs.res, qT.res], [PB[b]])
            act(eTs.ap.rearrange("p h m t -> p (h m t)"), bank(b)[:, 0:64], ACT.Exp, [PB[b]], [eTs.res], scale=SCL)
            b = nb()
            for h in range(4):
                for mc in range(2):
                    mm(bank(b)[:, h * 8:(h + 1) * 8], onesb.ap, eTs.ap[:, h, mc, :], mc == 0, mc == 1, [onesb.res, eTs.res], [PB[b]])
            vrecip(rZs.ap.rearrange("p h t -> p (h t)"), bank(b)[:, 0:32], [PB[b]], [rZs.res])
            b = nb()
            for h in range(4):
                for dc in range(2):
                    o = bank(b)[:, (h * 2 + dc) * 8:(h * 2 + dc + 1) * 8]
                    for mc in range(2):
                        mm(o, vs.ap[:, mc, (2 * h + dc) * 128:(2 * h + dc + 1) * 128], eTs.ap[:, h, mc, :], mc == 0, mc == 1, [vs.res, eTs.res], [PB[b]])
            rz4 = rZs.ap
            rzb = bass.AP(rz4.tensor, rz4.offset, [list(rz4.ap[0]), list(rz4.ap[1]), [0, 2], list(rz4.ap[2])])
            vtt(oT.ap[:, :, tk].rearrange("p (h c) t -> p h c t", h=4, c=2), bank(b)[:, 0:64].rearrange("p (h c t) -> p h c t", h=4, c=2, t=8),
                rzb, ALU.mult, [PB[b], rZs.res], [oT.res])
        tap('oT', oT.ap[:, 0, :], [oT.res], [128, NT])
        gated_merge(I['xa_wo'], 1024, oT, 2, False)
        tap('merged3', merged.ap[:, 0, :], [merged.res], [128, NT])
        reset('C')
        if stop <= 6:
            return done()

        reset('E')
        reset('B')
        reset('F')
        x1 = alloc([9, D], R='C')
        xn2T = alloc([16, NT], BF16, R='E')
        xr_t = [alloc([512], R='B') for _ in range(2)]
        rtmps = [alloc([512], BF16, R='B') for _ in range(3)]
        xsb2 = [alloc([D], BF16, R='F') for _ in range(2)]
        load_g('norm2_g')
        for cg in range(4):
            panel, prl = wpanel_big(I['w_out'], 0, D, cg * 512, 512)
            for tt in range(9):
                xrt = xr_t[(cg * 9 + tt) % 2]
                dma('sp', xrt.ap, I['xm'][tt * 128:(tt + 1) * 128, cg * 512:(cg + 1) * 512], wr=[xrt.res])
                b = nb()
                for kc in range(16):
                    mm(bank(b), merged.ap[:, kc, tt * 128:(tt + 1) * 128], panel.ap[:, kc, :], kc == 0, kc == 15, [merged.res] + prl, [PB[b]])
                vtt(x1.ap[:, tt, cg * 512:(cg + 1) * 512], bank(b), xrt.ap, ALU.add, [PB[b], xrt.res], [x1.res])
                if cg == 3:
                    norm_ops(Buf(x1.ap[:, tt, :], x1.res), tt, None, xsb2[tt % 2], None)
                    if tt >= 1:
                        norm_T(xsb2[(tt - 1) % 2], xn2T, (tt - 1) * 128)
        norm_T(xsb2[8 % 2], xn2T, 8 * 128)
        tap('x1', x1.ap[:, 0, :], [x1.res], [128, D])
        reset('D')
        reset('F')
        if stop <= 7:
            return done()

        hT = alloc([16, NT], BF16, R='D')
        junkf = alloc([D], BF16, R='F')
        rctr = [0]
        for fq in range(4):
            def ev_h(mc, t0, n, b):
                rctr[0] += 1
                rt = rtmps[rctr[0] % 3]
                act(rt.ap[:, 0:n], bank(b)[:, 0:n], ACT.Relu, [PB[b]], [rt.res])
                vtt(hT.ap[:, mc, t0:t0 + n], rt.ap[:, 0:n], rt.ap[:, 0:n], ALU.mult, [rt.res], [hT.res])
            linear_fm(I['w1'], 0, D, fq * 2048, 2048, xn2T, TG, ev_h)
            if fq == 3:
                load_g('final_g')
            for cg in range(4):
                panel, prl = wpanel_big(I['w2'], fq * 2048, 2048, cg * 512, 512)
                for tt in range(9):
                    b = nb()
                    for kc in range(16):
                        mm(bank(b), hT.ap[:, kc, tt * 128:(tt + 1) * 128], panel.ap[:, kc, :], kc == 0, kc == 15, [hT.res] + prl, [PB[b]])
                    last = (fq == 3 and cg == 3)
                    xres = Res() if last else x1.res
                    vtt(x1.ap[:, tt, cg * 512:(cg + 1) * 512], x1.ap[:, tt, cg * 512:(cg + 1) * 512], bank(b), ALU.add, [PB[b], x1.res], [xres])
                    if last:
                        ss = ss_list[tt % 4]
                        xt_ap = x1.ap[:, tt, :]
                        vset(ss.ap[:, 0:1], 0.0, [ss.res])
                        act(junkf.ap, xt_ap, ACT.Square, [xres, ss.res], [junkf.res, ss.res], accum_out=ss.ap[:, 0:1])
                        vts(ss.ap[:, 1:2], ss.ap[:, 0:1], 1.0 / D, EPS, ALU.mult, ALU.add, [ss.res], [ss.res])
                        act(ss.ap[:, 1:2], ss.ap[:, 1:2], ACT.Sqrt, [ss.res], [ss.res])
                        vrecip(ss.ap[:, 2:3], ss.ap[:, 1:2], [ss.res], [ss.res])
                        vstt(xt_ap, xt_ap, ss.ap[:, 2:3], gbc.ap, ALU.mult, ALU.mult, [xres, ss.res, gbc.res], [xres])
                        dma('sp', O['y'][tt * 128:(tt + 1) * 128, :], xt_ap, rd=[xres])
        return done()


_WIN = (2, 4, 8, 16)


def _consts():
    ident = np.eye(128, dtype=np.float32)
    sel = np.zeros((64, 128, 128), np.float32)
    for a in range(8):
        for b in range(8):
            for q in range(16):
                sel[a * 8 + b, a * 16 + q, b * 16 + q] = 1.0
    mask = np.zeros((128, 128), np.float32)
    for s in range(8):
        for sp in range(s, 8):
            mask[s * 16:(s + 1) * 16, sp * 16:(sp + 1) * 16] = 1.0
    return ident, sel, mask


def make_in_maps(inp):
    f = lambda a: np.ascontiguousarray(np.asarray(a, dtype=np.float32))
    ident, sel, mask = _consts()
    shared = {
        'norm1_g': f(inp['norm1_g']).reshape(1, D), 'w_in': f(inp['w_in'][0]), 'b_gate': f(inp['b_gate']).reshape(6144),
        'pool_w': f(inp['pool_w']).reshape(1024, 256), 'pool_scale': f(inp['pool_scale']).reshape(1024),
        'pool_proj': f(inp['pool_proj'][0]), 'A_re': f(inp['ssm_A_re'][0]), 'A_im': f(inp['ssm_A_im'][0]),
        'log_dt': f(inp['ssm_log_dt']).reshape(1, 64), 'B_re': f(inp['ssm_B_re'][0]), 'B_im': f(inp['ssm_B_im'][0]),
        'C_re': f(inp['ssm_C_re']).reshape(1024, 64), 'C_im': f(inp['ssm_C_im']).reshape(1024, 64),
        'ssm_D': f(inp['ssm_D']).reshape(1024), 'glu_w': f(inp['ssm_glu_w'][0]), 'glu_b': f(inp['ssm_glu_b']).reshape(1024),
        'ssm_proj': f(inp['ssm_proj'][0]), 'mem_norm_g': f(inp['mem_norm_g']).reshape(1, D),
        'xa_wk': f(inp['xa_wk'][0]), 'xa_wv': f(inp['xa_wv'][0]), 'xa_wo': f(inp['xa_wo'][0]), 'w_out': f(inp['w_out'][0]),
        'norm2_g': f(inp['norm2_g']).reshape(1, D), 'w1': f(inp['mlp_w1'][0]), 'w2': f(inp['mlp_w2'][0]),
        'final_g': f(inp['final_norm_g']).reshape(1, D), 'ident': ident, 'sel': sel, 'mask': mask,
        'ciota': np.arange(128, dtype=np.float32).reshape(1, 128),
    }
    xpr, xsm = inp['x_prompt'], inp['x_sample']
    maps = []
    for c in range(8):
        b, h = c // 2, c % 2
        sq = slice(16 * c, 16 * c + 16)
        m = dict(shared)
        m['xm'] = f(np.concatenate([xpr[b, h * NPR:(h + 1) * NPR], xsm[sq].reshape(NSM, D)], axis=0))
        m['xp'] = f(xpr[b, 0:NPR]) if h == 1 else np.zeros((NPR, D), np.float32)
        m['spool'] = f(inp['state_pool'][0, sq]).reshape(NSEQ * 15, 1024)
        m['sre'] = f(inp['state_ssm_re'][0, sq]).reshape(NSEQ * 64, 64)
        m['sim'] = f(inp['state_ssm_im'][0, sq]).reshape(NSEQ * 64, 64)
        m['ck'] = f(inp['cache_mem_k'][0, sq]).reshape(NSEQ, 256, 1024)
        m['cv'] = f(inp['cache_mem_v'][0, sq]).reshape(NSEQ, 256, 1024)
        m['mem'] = f(inp['mem_prompt'][b])
        ic = np.zeros((4, 16), np.float32)
        for k, w in enumerate(_WIN):
            for t in range(16):
                ic[k, t] = 1.0 / (min(t + 1, w) if h == 0 else w)
        m['icnt'] = ic
        maps.append(m)
    return maps


def assemble(res):
    y_prompt = np.zeros((4, 2048, D), np.float32)
    y_sample = np.zeros((128, 8, D), np.float32)
    pool_p = np.zeros((1, 4, 15, 1024), np.float32)
    re_p = np.zeros((1, 4, 64, 64), np.float32)
    im_p = np.zeros((1, 4, 64, 64), np.float32)
    mk_p = np.zeros((1, 4, 256, 4, 256), np.float32)
    mv_p = np.zeros((1, 4, 256, 4, 256), np.float32)
    pool_s = np.zeros((1, 128, 15, 1024), np.float32)
    re_s = np.zeros((1, 128, 64, 64), np.float32)
    im_s = np.zeros((1, 128, 64, 64), np.float32)
    for c in range(8):
        r = res[c]
        b, h = c // 2, c % 2
        sq = slice(16 * c, 16 * c + 16)
        y_prompt[b, h * NPR:(h + 1) * NPR] = r['y'][0:NPR]
        y_sample[sq] = r['y'][NPR:NT].reshape(NSEQ, 8, D)
        pool_s[0, sq] = r['pool_s']
        re_s[0, sq] = r['hs_re'].reshape(NSEQ, 64, 64)
        im_s[0, sq] = r['hs_im'].reshape(NSEQ, 64, 64)
        if h == 1:
            pool_p[0, b] = r['pool_p'][1:16]
            re_p[0, b] = r['hp_re']
            im_p[0, b] = r['hp_im']
        else:
            mk_p[0, b] = r['mk'].reshape(256, 4, 256)
            mv_p[0, b] = r['mv'].reshape(256, 4, 256)
    return (y_prompt, y_sample, pool_p, re_p, im_p, mk_p, mv_p, pool_s, re_s, im_s)


_NC_CACHE = {}


def kernel(**inputs):
    if 'nc' not in _NC_CACHE:
        _NC_CACHE['nc'] = build_program()[0]
    nc = _NC_CACHE['nc']
    in_maps = make_in_maps(inputs)
    res = run_bass_kernel_spmd(nc, in_maps, core_ids=list(range(8)))
    return assemble(res.results)
```

```python
import math
import numpy as np
import concourse.bass as bass
import concourse.mybir as mybir
from concourse.bass_utils import run_bass_kernel_spmd
from contextlib import ExitStack

F32 = mybir.dt.float32
BF16 = mybir.dt.bfloat16
ACT = mybir.ActivationFunctionType
ALU = mybir.AluOpType

ENGS = ['pe', 'act', 'dve', 'pool', 'sp']
SAME_ENGINE_SYNC = True

D = 2048
NPR = 1024
NSM = 128
NT = NPR + NSM
NSEQ = 16
DFF = 8192
EPS = 1e-6
TG = [(0, 512), (512, 512), (1024, 128)]
TGP = [(0, 512), (512, 512)]
PI = float(np.pi)


class Res:
    __slots__ = ('name', 'w', 'r', 'excl')

    def __init__(self, name='', excl=False):
        self.name = name
        self.w = None
        self.r = {}
        self.excl = excl


class Sched:
    def __init__(self, ndsem=48):
        self.ops = {e: [] for e in ENGS}
        self.seen = {e: {} for e in ENGS}
        self.ndsem = ndsem
        self.dsem_cnt = [0] * ndsem
        self.dsem_next = 0
        self.dsem_next_sw = 0

    def _deps(self, eng, reads, writes):
        deps = []
        for r in reads:
            if r.w is not None:
                deps.append(r.w)
            if r.excl:
                deps.extend(t for k, t in r.r.items() if k != eng)
        for w in writes:
            if w.w is not None:
                deps.append(w.w)
            deps.extend(w.r.values())
        return deps

    def _add_waits(self, eng, deps):
        seen = self.seen[eng]
        best = {}
        for d in deps:
            key = (d[0], d[1])
            if d[0] == 'e' and d[1] == eng and (eng == 'pe' or not SAME_ENGINE_SYNC):
                continue
            if seen.get(key, -1) >= d[2]:
                continue
            if best.get(key, -1) < d[2]:
                best[key] = d[2]
        waits = []
        for key, v in best.items():
            seen[key] = v
            waits.append((key[0], key[1], v))
            if key[0] == 'e':
                self.ops[key[1]][v][3] = True
        return waits

    def op(self, eng, fn, reads=(), writes=()):
        waits = self._add_waits(eng, self._deps(eng, reads, writes))
        idx = len(self.ops[eng])
        self.ops[eng].append(['c', fn, waits, False, None])
        tag = ('e', eng, idx)
        for r in reads:
            r.r[eng] = tag
        for w in writes:
            w.w = tag
            w.r = {}
        return tag

    def dma(self, q, fn, reads=(), writes=()):
        half = self.ndsem // 2
        if q == 'pool':
            s = half + self.dsem_next_sw
            self.dsem_next_sw = (self.dsem_next_sw + 1) % (self.ndsem - half)
        else:
            s = self.dsem_next
            self.dsem_next = (s + 1) % half
        deps = self._deps(q, reads, writes)
        if self.dsem_cnt[s] > 0:
            deps.append(('d', s, self.dsem_cnt[s]))
        waits = self._add_waits(q, deps)
        self.dsem_cnt[s] += 16
        tag = ('d', s, self.dsem_cnt[s])
        self.ops[q].append(['d', fn, waits, False, s])
        for r in reads:
            r.r[('dma', s)] = tag
        for w in writes:
            w.w = tag
            w.r = {}
        return tag

    def barrier(self):
        tags = []
        for e in ENGS:
            for idx in range(len(self.ops[e]) - 1, -1, -1):
                if self.ops[e][idx][0] == 'c':
                    tags.append(('e', e, idx))
                    break
        for s in range(self.ndsem):
            if self.dsem_cnt[s] > 0:
                tags.append(('d', s, self.dsem_cnt[s]))
        for e in ENGS:
            waits = self._add_waits(e, tags)
            if waits:
                self.ops[e].append(['w', None, waits, False, None])

    def finish(self):
        self.barrier()

    def emit(self, nc, es):
        SEMMAX = 30000
        rank = {}
        nsem = {}
        for e in ENGS:
            c = 0
            rk = []
            for o in self.ops[e]:
                if o[3]:
                    c += 1
                rk.append(c)
            rank[e] = rk
            nsem[e] = max(1, (c + SEMMAX - 1) // SEMMAX)
        esem = {e: [es.enter_context(nc.semaphore('es_%s_%d' % (e, i))) for i in range(nsem[e])] for e in ENGS}
        dsem = [es.enter_context(nc.semaphore('ds_%d' % i)) for i in range(self.ndsem)]
        ops = self.ops
        block = es.enter_context(nc.Block())

        def semval(a, r):
            return esem[a][(r - 1) // SEMMAX], (r - 1) % SEMMAX + 1

        def run(e, E):
            for i, (kind, fn, waits, marked, ds) in enumerate(ops[e]):
                for (k, a, v) in waits:
                    if k == 'e':
                        sm, val = semval(a, rank[a][v])
                        E.wait_ge(sm, val)
                    else:
                        E.wait_ge(dsem[a], v)
                if kind == 'w':
                    continue
                inst = fn(E)
                if kind == 'd':
                    inst.then_inc(dsem[ds], 16)
                elif marked:
                    sm, val = semval(e, rank[e][i])
                    inst.then_inc(sm, 1)

        @block.tensor
        def _(E):
            run('pe', E)

        @block.scalar
        def _(E):
            run('act', E)

        @block.vector
        def _(E):
            run('dve', E)

        @block.gpsimd
        def _(E):
            run('pool', E)

        @block.sync
        def _(E):
            run('sp', E)


class Buf:
    __slots__ = ('ap', 'res')

    def __init__(self, ap, res):
        self.ap = ap
        self.res = res


def bc(ap, n):
    return bass.AP(ap.tensor, ap.offset, [list(x) for x in ap.ap] + [[0, n]])


def bc_mid(ap, n):
    a = [list(x) for x in ap.ap]
    return bass.AP(ap.tensor, ap.offset, [a[0], [0, n]] + a[1:])


LAST_S = None


def build_program(stop=99, taps=()):
    global LAST_S
    nc = bass.Bass("TRN2", target_bir_lowering=False)
    S = Sched()
    LAST_S = S
    taps = set(taps)
    tap_out = {}

    def din(name, shape):
        return nc.dram_tensor(name, list(shape), F32, kind="ExternalInput").ap()

    def dout(name, shape):
        return nc.dram_tensor(name, list(shape), F32, kind="ExternalOutput").ap()

    I = {}
    for name, shape in [
        ('xm', [NT, D]), ('xp', [NPR, D]), ('spool', [NSEQ * 15, 1024]), ('sre', [NSEQ * 64, 64]), ('sim', [NSEQ * 64, 64]),
        ('ck', [NSEQ, 256, 1024]), ('cv', [NSEQ, 256, 1024]), ('mem', [256, D]),
        ('norm1_g', [1, D]), ('w_in', [D, 9216]), ('b_gate', [6144]), ('pool_w', [1024, 256]), ('pool_scale', [1024]),
        ('pool_proj', [1024, D]), ('A_re', [64, 64]), ('A_im', [64, 64]), ('log_dt', [1, 64]),
        ('B_re', [64, 64, 16]), ('B_im', [64, 64, 16]), ('C_re', [1024, 64]), ('C_im', [1024, 64]), ('ssm_D', [1024]),
        ('glu_w', [1024, 1024]), ('glu_b', [1024]), ('ssm_proj', [1024, D]), ('mem_norm_g', [1, D]),
        ('xa_wk', [D, 1024]), ('xa_wv', [D, 1024]), ('xa_wo', [1024, D]), ('w_out', [D, D]), ('norm2_g', [1, D]),
        ('w1', [D, DFF]), ('w2', [DFF, D]), ('final_g', [1, D]),
        ('ident', [128, 128]), ('sel', [64, 128, 128]), ('mask', [128, 128]), ('icnt', [4, 16]), ('ciota', [1, 128]),
    ]:
        I[name] = din(name, shape)
    O = {}
    for name, shape in [
        ('y', [NT, D]), ('pool_p', [16, 1024]), ('hp_re', [64, 64]), ('hp_im', [64, 64]), ('mk', [256, 1024]), ('mv', [256, 1024]),
        ('pool_s', [NSEQ, 15, 1024]), ('hs_re', [NSEQ * 64, 64]), ('hs_im', [NSEQ * 64, 64]),
    ]:
        O[name] = dout(name, shape)

    with ExitStack() as es:
        es.enter_context(nc.allow_non_contiguous_dma(reason="small strided param loads"))
        ARENA_BYTES = 207 * 1024 + 512
        arena_t = es.enter_context(nc.sbuf_tensor("arena", [128, ARENA_BYTES // 4], F32))
        ps_t = es.enter_context(nc.psum_tensor("ps", [128, 4096], F32))
        PB = [Res('bank%d' % i, excl=True) for i in range(8)]
        st = {'bank': 0, 'ev': 0, 'ring': 0, 'big': 0}

        KB = 256
        REG = {}

        def region(name, base_kb, size_kb):
            REG[name] = {'base': int(base_kb * KB), 'top': int(base_kb * KB), 'lim': int((base_kb + size_kb) * KB), 'peak': 0}
        region('A', 0, 44)
        region('B', 44, 9)
        region('C', 53, 72)
        region('D', 125, 36)
        region('E', 161, 36)
        region('F', 197, 9)
        region('S', 71, 136.5)
        cur = {'R': 'A'}

        def use(name):
            cur['R'] = name

        def alloc(shape, dtype=F32, parts=128, R=None):
            rg = REG[R or cur['R']]
            n = 1
            for s_ in shape:
                n *= s_
            nb_ = n * (2 if dtype == BF16 else 4)
            nw = (nb_ + 3) // 4
            nw = (nw + 7) // 8 * 8
            w0 = rg['top']
            st['last_w0'] = w0
            rg['top'] += nw
            rg['peak'] = max(rg['peak'], rg['top'])
            assert rg['top'] <= rg['lim'], "region %s overflow: need %d KiB more" % (R or cur['R'], (rg['top'] - rg['lim']) // KB + 1)
            ap = arena_t[0:parts, w0:w0 + nw]
            if dtype == BF16:
                ap = ap.bitcast(BF16)
            ap = ap[:, 0:n]
            if len(shape) == 2:
                ap = ap.rearrange("p (a b) -> p a b", a=shape[0], b=shape[1])
            elif len(shape) == 3:
                ap = ap.rearrange("p (a b c) -> p a b c", a=shape[0], b=shape[1], c=shape[2])
            elif len(shape) == 4:
                ap = ap.rearrange("p (a b c d) -> p a b c d", a=shape[0], b=shape[1], c=shape[2], d=shape[3])
            return Buf(ap, Res())

        def mark(R=None):
            return (R or cur['R'], REG[R or cur['R']]['top'])

        def release(m):
            S.barrier()
            REG[m[0]]['top'] = m[1]

        def reset(R):
            S.barrier()
            REG[R]['top'] = REG[R]['base']

        def bank(i):
            return ps_t[:, i * 512:(i + 1) * 512]

        def bankb(i):
            return ps_t[:, i * 512:(i + 1) * 512].bitcast(BF16)

        def nb():
            b = st['bank'] % st.get('nbanks', 8)
            st['bank'] = (b + 1) % st.get('nbanks', 8)
            return b

        def evq():
            st['ev'] ^= 1
            return 'act' if st['ev'] else 'dve'

        def mm(out, lhsT, rhs, start, stop, rd, wr):
            S.op('pe', lambda E: E.matmul(out, lhsT, rhs, start=start, stop=stop), reads=rd, writes=wr)

        def trp(out, in_, ident, rd, wr):
            S.op('pe', lambda E: E.transpose(out, in_, ident), reads=rd, writes=wr)

        def vtt(out, a, b, op, rd, wr, eng='dve'):
            S.op(eng, lambda E: E.tensor_tensor(out, a, b, op), reads=rd, writes=wr)

        def vts(out, a, s1, s2, op0, op1, rd, wr, eng='dve'):
            S.op(eng, lambda E: E.tensor_scalar(out, a, s1, s2, op0, op1), reads=rd, writes=wr)

        def vstt(out, in0, scalar, in1, op0, op1, rd, wr, eng='dve'):
            S.op(eng, lambda E: E.scalar_tensor_tensor(out, in0, scalar, in1, op0, op1), reads=rd, writes=wr)

        def vcopy(out, a, rd, wr, eng='dve'):
            if eng == 'act':
                S.op('act', lambda E: E.copy(out, a), reads=rd, writes=wr)
            else:
                S.op(eng, lambda E: E.tensor_copy(out, a), reads=rd, writes=wr)

        def vset(out, val, wr, eng='dve'):
            S.op(eng, lambda E: E.memset(out, val), writes=wr)

        def act(out, in_, func, rd, wr, bias=None, scale=None, accum_out=None):
            kw = {}
            if bias is not None:
                kw['bias'] = bias
            if scale is not None:
                kw['scale'] = scale
            if accum_out is not None:
                kw['accum_out'] = accum_out
            S.op('act', lambda E: E.activation(out, in_, func, **kw), reads=rd, writes=wr)

        def dma(q, out, in_, rd=(), wr=()):
            S.dma(q, lambda E: E.dma_start(out=out, in_=in_), reads=rd, writes=wr)

        def tap(name, ap, rd, shape):
            if name in taps:
                t = dout('tap_' + name, shape)
                tap_out[name] = shape
                dma('sp' if ap.dtype == F32 else 'pool', t, ap, rd=rd)

        def vsadd(out, a, c, rd, wr, eng='dve'):
            S.op(eng, lambda E: E.tensor_scalar_add(out, a, c), reads=rd, writes=wr)

        def vsmul(out, a, c, rd, wr, eng='dve'):
            S.op(eng, lambda E: E.tensor_scalar_mul(out, a, c), reads=rd, writes=wr)

        def vrecip(out, a, rd, wr):
            S.op('dve', lambda E: E.reciprocal(out, a), reads=rd, writes=wr)

        def done():
            import os
            if os.environ.get('PEAKS'):
                for k_, v_ in REG.items():
                    print('region', k_, 'peak KiB', (v_['peak'] - v_['base']) / KB, 'size', (v_['lim'] - v_['base']) / KB)
            S.finish()
            S.emit(nc, es)
            return nc, tap_out

        use('A')
        identf = alloc([128])
        dma('sp', identf.ap, I['ident'], wr=[identf.res])
        identb = alloc([128], BF16)
        dma('pool', identb.ap, I['ident'], wr=[identb.res])
        onesb = alloc([128], BF16)
        vset(onesb.ap, 1.0, [onesb.res])
        gbc = alloc([D])
        small = {}

        def load_cols(name, n):
            b = alloc([n // 128])
            dma('sp', b.ap, I[name].rearrange("(c p) -> p c", p=128), wr=[b.res])
            small[name] = b
        load_cols('b_gate', 6144)
        load_cols('pool_scale', 1024)
        load_cols('ssm_D', 1024)
        load_cols('glu_b', 1024)
        icnt = alloc([4, 16])
        dma('sp', icnt.ap, bass.AP(I['icnt'].tensor, 0, [[0, 128], [16, 4], [1, 16]]), wr=[icnt.res])
        ss_list = [alloc([8]) for _ in range(4)]
        ss = ss_list[0]
        ring_all = alloc([4, 4096], BF16)
        ring_res = [Res() for _ in range(4)]

        def load_g(name):
            dma('sp', gbc.ap, bass.AP(I[name].tensor, 0, [[0, 128], [1, D]]), wr=[gbc.res])

        def wpanel(w2d, r0, K, c0, ncols, slots=(0, 1, 2, 3), key='ring'):
            KC = K // 128
            assert KC * ncols <= 4096
            si = slots[st.setdefault(key, 0) % len(slots)]
            st[key] += 1
            view = ring_all.ap[:, si, 0:KC * ncols].rearrange("p (k m) -> p k m", k=KC, m=ncols)
            src = w2d[r0:r0 + K, c0:c0 + ncols].rearrange("(kc p) m -> p kc m", p=128)
            dma('pool', view, src, wr=[ring_res[si]])
            return Buf(view, ring_res[si]), [ring_res[si]]

        def wpanel_big(w2d, r0, K, c0, ncols):
            KC = K // 128
            assert KC * ncols <= 8192
            bi = st['big'] % 2
            st['big'] += 1
            flat = ring_all.ap[:, 2 * bi:2 * bi + 2, :].rearrange("p a b -> p (a b)")
            view = flat[:, 0:KC * ncols].rearrange("p (k m) -> p k m", k=KC, m=ncols)
            src = w2d[r0:r0 + K, c0:c0 + ncols].rearrange("(kc p) m -> p kc m", p=128)
            rl = [ring_res[2 * bi], ring_res[2 * bi + 1]]
            dma('pool', view, src, wr=rl)
            return Buf(view, None), rl

        def norm_ops(s, tt, xslots, xs_b, junk, g=None):
            gb = g if g is not None else gbc
            if isinstance(s, Buf):
                xt = s
            else:
                xt = xslots[tt % len(xslots)]
                dma('sp', xt.ap, s, wr=[xt.res])
            ss = ss_list[tt % 4]
            jk = junk if junk is not None else xs_b
            vset(ss.ap[:, 0:1], 0.0, [ss.res])
            act(jk.ap, xt.ap, ACT.Square, [xt.res, ss.res], [jk.res, ss.res], accum_out=ss.ap[:, 0:1])
            vts(ss.ap[:, 1:2], ss.ap[:, 0:1], 1.0 / D, EPS, ALU.mult, ALU.add, [ss.res], [ss.res])
            act(ss.ap[:, 1:2], ss.ap[:, 1:2], ACT.Sqrt, [ss.res], [ss.res])
            vrecip(ss.ap[:, 2:3], ss.ap[:, 1:2], [ss.res], [ss.res])
            vstt(xs_b.ap, xt.ap, ss.ap[:, 2:3], gb.ap, ALU.mult, ALU.mult, [xt.res, ss.res, gb.res], [xs_b.res])

        def norm_T(xs_b, dst, c0):
            for h in range(2):
                b = nb()
                for j in range(8):
                    c = h * 8 + j
                    trp(bankb(b)[:, j * 128:(j + 1) * 128], xs_b.ap[:, c * 128:(c + 1) * 128], identb.ap,
                        [xs_b.res, identb.res], [PB[b]])
                vcopy(dst.ap[:, h * 8:(h + 1) * 8, c0:c0 + 128],
                      bankb(b).rearrange("p (j c) -> p j c", j=8, c=128), [PB[b]], [dst.res], eng=evq())

        def norm_transpose(src_fn, ntiles, dst, col0, xslots, xs_b_in, junk):
            for tt in range(ntiles):
                xs_b = xs_b_in[tt % len(xs_b_in)] if isinstance(xs_b_in, list) else xs_b_in
                norm_ops(src_fn(tt), tt, xslots, xs_b, junk)
                norm_T(xs_b, dst, col0 + tt * 128)

        def linear_fm(w2d, r0, K, c0, M, rhsT, tgroups, evac):
            KC = K // 128
            pc = min(M, (4096 // KC) // 128 * 128)
            for p0 in range(0, M, pc):
                panel, prl = wpanel(w2d, r0, K, c0 + p0, pc)
                for j in range(pc // 128):
                    mc = p0 // 128 + j
                    for (t0, n) in tgroups:
                        b = nb()
                        for kc in range(KC):
                            mm(bank(b)[:, 0:n], panel.ap[:, kc, j * 128:(j + 1) * 128], rhsT.ap[:, kc, t0:t0 + n],
                               kc == 0, kc == KC - 1, prl + [rhsT.res], [PB[b]])
                        evac(mc, t0, n, b)

        use('B')
        KT = alloc([8, 256], BF16)
        Vb = alloc([2, 1024], BF16)
        halo = alloc([8, 16])
        use('S')

        def sc():
            return alloc([64], F32, parts=64)
        maskf = alloc([128], R='B')
        dma('sp', maskf.ap, I['mask'], wr=[maskf.res])
        Pst_r = alloc([9, 64], F32, parts=64)
        Pst_i = alloc([9, 64], F32, parts=64)
        Qst_r = alloc([8, 64], F32, parts=64)
        Qst_i = alloc([8, 64], F32, parts=64)
        cst_r = alloc([8, 64], F32, parts=64)
        cst_i = alloc([8, 64], F32, parts=64)
        P_r = [Buf(Pst_r.ap[:, k, :], Pst_r.res) for k in range(9)]
        P_i = [Buf(Pst_i.ap[:, k, :], Pst_i.res) for k in range(9)]
        Q_r = [Buf(Qst_r.ap[:, 7 - k, :], Qst_r.res) for k in range(8)]
        Q_i = [Buf(Qst_i.ap[:, 7 - k, :], Qst_i.res) for k in range(8)]
        cs_r = [Buf(cst_r.ap[:, k, :], cst_r.res) for k in range(8)]
        cs_i = [Buf(cst_i.ap[:, k, :], cst_i.res) for k in range(8)]
        rho8 = sc()
        phin = sc()
        ciota = alloc([128], F32, parts=64)
        dma('sp', ciota.ap, bass.AP(I['ciota'].tensor, 0, [[0, 64], [1, 128]]), wr=[ciota.res])
        H0 = alloc([2, 64], F32, parts=64)
        HpF = alloc([2, 64], F32, parts=64)
        h0T = alloc([2, NSEQ, 64], F32, parts=64)
        sel = alloc([64, 128], BF16)
        dma('pool', sel.ap, I['sel'].rearrange("j p m -> p j m"), wr=[sel.res])
        upreT = alloc([8, NPR], BF16)
        umT = alloc([8, NT], BF16)
        m0 = mark()
        ar, ai, dtb = sc(), sc(), sc()
        tmpA = alloc([64], F32, parts=64)
        for (nm, dst) in (('A_re', ar), ('A_im', ai)):
            dma('sp', tmpA.ap, I[nm], wr=[tmpA.res])
            b = nb()
            trp(bank(b)[0:64, 0:64], tmpA.ap, identf.ap[0:64, 0:64], [tmpA.res, identf.res], [PB[b]])
            vcopy(dst.ap, bank(b)[0:64, 0:64], [PB[b]], [dst.res])
        dma('sp', dtb.ap, bass.AP(I['log_dt'].tensor, 0, [[0, 64], [1, 64]]), wr=[dtb.res])
        act(dtb.ap, dtb.ap, ACT.Exp, [dtb.res], [dtb.res])
        xsl = [alloc([D]) for _ in range(2)]
        xsb4 = [alloc([D], BF16) for _ in range(4)]
        xg = alloc([16, 512], BF16)
        xhalo = alloc([16, 128], BF16)
        gmem = alloc([D], R='C')
        mnT = alloc([16, 256], BF16, R='C')
        dma('sp', gmem.ap, bass.AP(I['mem_norm_g'].tensor, 0, [[0, 128], [1, D]]), wr=[gmem.res])
        for tt in range(2):
            norm_ops(I['mem'][tt * 128:(tt + 1) * 128, :], tt, xsl, xsb4[tt], None, g=gmem)
            norm_T(xsb4[tt], mnT, tt * 128)
        osb = alloc([512], R='C')
        for (wn, on, isk) in (('xa_wk', 'mk', True), ('xa_wv', 'mv', False)):
            for half in range(2):
                panel, prl = wpanel_big(I[wn], 0, D, half * 512, 512)
                if isk:
                    for j in range(4):
                        mc = half * 4 + j
                        b = nb()
                        for kc in range(16):
                            mm(bank(b)[:, 0:256], panel.ap[:, kc, j * 128:(j + 1) * 128], mnT.ap[:, kc, :], kc == 0, kc == 15,
                               prl + [mnT.res], [PB[b]])
                        vcopy(KT.ap[:, mc, :], bank(b)[:, 0:256], [PB[b]], [KT.res], eng='act')
                for mt in range(2):
                    b = nb()
                    for kc in range(16):
                        mm(bank(b), mnT.ap[:, kc, mt * 128:(mt + 1) * 128], panel.ap[:, kc, :], kc == 0, kc == 15,
                           prl + [mnT.res], [PB[b]])
                    vcopy(osb.ap, bank(b), [PB[b]], [osb.res], eng='act')
                    if not isk:
                        vcopy(Vb.ap[:, mt, half * 512:(half + 1) * 512], bank(b), [PB[b]], [Vb.res], eng='act')
                    dma('sp', O[on][mt * 128:(mt + 1) * 128, half * 512:(half + 1) * 512], osb.ap, rd=[osb.res])
        load_g('norm1_g')
        glist = [(I['xp'], upreT, 0, 512), (I['xp'], upreT, 512, 512)] + [(I['xm'], umT, t0, n) for (t0, n) in TG]
        wres = [wpanel(I['w_in'], 0, D, 1024 + p_ * 256, 256, slots=(p_,), key='p2r%d' % p_) for p_ in range(4)]
        tile_ctr = [0]

        def p2_norm_ops(i, k):
            src, dstT, t0, n = glist[i]
            norm_ops(src[t0 + k * 128:t0 + (k + 1) * 128, :], tile_ctr[0], xsl, xsb4[k], None)
            tile_ctr[0] += 1

        def p2_T(i):
            src, dstT, t0, n = glist[i]
            for k in range(n // 128):
                norm_T(xsb4[k], xg, k * 128)
            if i == 1:
                vcopy(xhalo.ap, xg.ap[:, :, 384:512], [xg.res], [xhalo.res], eng='act')

        def p2_unit(i, mc):
            src, dstT, t0, n = glist[i]
            panel, prl = wres[mc // 2]
            j = mc % 2
            b = nb()
            for kc in range(16):
                mm(bank(b)[:, 0:n], panel.ap[:, kc, j * 128:(j + 1) * 128], xg.ap[:, kc, 0:n], kc == 0, kc == 15,
                   prl + [xg.res], [PB[b]])
            vcopy(dstT.ap[:, mc, t0:t0 + n], bank(b)[:, 0:n], [PB[b]], [dstT.res], eng=evq())
        for k in range(glist[0][3] // 128):
            p2_norm_ops(0, k)
        lr, th, Em, t1, t2, t3 = sc(), sc(), sc(), sc(), sc(), sc()
        vtt(lr.ap, dtb.ap, ar.ap, ALU.mult, [dtb.res, ar.res], [lr.res])
        vtt(th.ap, dtb.ap, ai.ap, ALU.mult, [dtb.res, ai.res], [th.res])
        act(Em.ap, lr.ap, ACT.Exp, [lr.res], [Em.res])

        def sin_of(dst, src, shift):
            vsadd(t1.ap, src.ap, shift, [src.res], [t1.res])
            vcopy(t2.ap, t1.ap, [t1.res], [t2.res])
            for kk in (1, 3, 5, 7, 9):
                vts(t3.ap, t1.ap, kk * PI, -2 * PI, ALU.is_ge, ALU.mult, [t1.res], [t3.res])
                vtt(t2.ap, t2.ap, t3.ap, ALU.add, [t2.res, t3.res], [t2.res])
            vts(t2.ap, t2.ap, -3.1415925, 3.1415925, ALU.max, ALU.min, [t2.res], [t2.res])
            act(dst.ap, t2.ap, ACT.Sin, [t2.res], [dst.res])
        sn, cs_ = sc(), sc()
        sin_of(sn, th, 0.0)
        sin_of(cs_, th, PI / 2)
        vset(P_r[0].ap, 1.0, [P_r[0].res])
        vset(P_i[0].ap, 0.0, [P_i[0].res])
        vset(Q_r[0].ap, 1.0, [Q_r[0].res])
        vset(Q_i[0].ap, 0.0, [Q_i[0].res])
        vtt(P_r[1].ap, Em.ap, cs_.ap, ALU.mult, [Em.res, cs_.res], [P_r[1].res])
        vtt(P_i[1].ap, Em.ap, sn.ap, ALU.mult, [Em.res, sn.res], [P_i[1].res])

        def cmul(o_r, o_i, a_r, a_i, b_r, b_i, rd, wr_r, wr_i, ta, tb, conj_b=False, eng='dve'):
            vtt(ta.ap, a_r, b_r, ALU.mult, rd, [ta.res], eng)
            vtt(tb.ap, a_i, b_i, ALU.mult, rd, [tb.res], eng)
            vtt(o_r, ta.ap, tb.ap, ALU.add if conj_b else ALU.subtract, [ta.res, tb.res], wr_r, eng)
            vtt(ta.ap, a_i, b_r, ALU.mult, rd, [ta.res], eng)
            vtt(tb.ap, a_r, b_i, ALU.mult, rd, [tb.res], eng)
            vtt(o_i, ta.ap, tb.ap, ALU.subtract if conj_b else ALU.add, [ta.res, tb.res], wr_i, eng)

        def cmul_s(o_r, o_i, a_r, a_i, b_r, b_i, conj_b=False):
            cmul(o_r.ap, o_i.ap, a_r.ap, a_i.ap, b_r.ap, b_i.ap, [a_r.res, a_i.res, b_r.res, b_i.res],
                 [o_r.res], [o_i.res], t1, t2, conj_b)
        for k in range(2, 9):
            cmul_s(P_r[k], P_i[k], P_r[k - 1], P_i[k - 1], P_r[1], P_i[1])
        den, fr, fi, xr = sc(), sc(), sc(), sc()
        vtt(den.ap, ar.ap, ar.ap, ALU.mult, [ar.res], [den.res])
        vtt(t3.ap, ai.ap, ai.ap, ALU.mult, [ai.res], [t3.res])
        vtt(den.ap, den.ap, t3.ap, ALU.add, [den.res, t3.res], [den.res])
        vrecip(den.ap, den.ap, [den.res], [den.res])
        vsadd(xr.ap, P_r[1].ap, -1.0, [P_r[1].res], [xr.res])
        cmul_s(fr, fi, xr, P_i[1], ar, ai, conj_b=True)
        vtt(fr.ap, fr.ap, den.ap, ALU.mult, [fr.res, den.res], [fr.res])
        vtt(fi.ap, fi.ap, den.ap, ALU.mult, [fi.res, den.res], [fi.res])
        einv, irho8 = sc(), sc()
        act(einv.ap, lr.ap, ACT.Exp, [lr.res], [einv.res], scale=-2.0)
        vtt(Q_r[1].ap, P_r[1].ap, einv.ap, ALU.mult, [P_r[1].res, einv.res], [Q_r[1].res])
        vstt(Q_i[1].ap, P_i[1].ap, -1.0, einv.ap, ALU.mult, ALU.mult, [P_i[1].res, einv.res], [Q_i[1].res])
        for k in range(2, 8):
            cmul_s(Q_r[k], Q_i[k], Q_r[k - 1], Q_i[k - 1], Q_r[1], Q_i[1])
        for s_ in range(8):
            cmul_s(cs_r[s_], cs_i[s_], P_r[7 - s_], P_i[7 - s_], fr, fi)
        act(rho8.ap, lr.ap, ACT.Exp, [lr.res], [rho8.res], scale=8.0)
        I32 = mybir.dt.int32
        vsmul(t1.ap, th.ap, 4.0 / PI, [th.res], [t1.res])
        vcopy(t2.ap.bitcast(I32), t1.ap, [t1.res], [t2.res])
        vcopy(t3.ap, t2.ap.bitcast(I32), [t2.res], [t3.res])
        vtt(phin.ap, t1.ap, t3.ap, ALU.subtract, [t1.res, t3.res], [phin.res])
        tap('P8r', P_r[8].ap, [P_r[8].res], [64, 64])
        tap('P8i', P_i[8].ap, [P_i[8].res], [64, 64])
        tap('fr', fr.ap, [fr.res], [64, 64])
        tap('fi', fi.ap, [fi.res], [64, 64])
        tmpS = alloc([64])
        for ri, nm in ((0, 'sre'), (1, 'sim')):
            for j in range(8):
                dma('sp', tmpS.ap, I[nm][j * 128:(j + 1) * 128, :], wr=[tmpS.res])
                b = nb()
                trp(bank(b)[0:64, 0:128], tmpS.ap, identf.ap, [tmpS.res, identf.res], [PB[b]])
                vcopy(h0T.ap[:, ri, 2 * j:2 * j + 2, :], bank(b)[0:64, 0:128].rearrange("p (s g) -> p s g", s=2, g=64),
                      [PB[b]], [h0T.res], eng='act')

        p2_T(0)
        for i in range(len(glist)):
            nxt = glist[i + 1][3] // 128 if i + 1 < len(glist) else 0
            for mc in range(8):
                if mc % 2 == 0 and mc // 2 < nxt:
                    p2_norm_ops(i + 1, mc // 2)
                p2_unit(i, mc)
            if nxt:
                p2_T(i + 1)

        def evh(mc, tt0, nn, b):
            vcopy(halo.ap[:, mc, :], bank(b)[:, 112:128], [PB[b]], [halo.res], eng=evq())
        linear_fm(I['w_in'], 0, D, 0, 1024, xhalo, [(0, 128)], evh)
        release(m0)
        REG['C']['top'] = REG['C']['base']
        tap('umT', umT.ap[:, 0, :], [umT.res], [128, NT])
        tap('upreT', upreT.ap[:, 0, :], [upreT.res], [128, NPR])
        if stop <= 2:
            return done()

        zg = alloc([8, NT], BF16, R='C')
        G8 = 8
        Brb = alloc([G8, 16], F32, parts=64)
        Bib = alloc([G8, 16], F32, parts=64)
        Crb = alloc([G8, 16], F32, parts=64)
        Cib = alloc([G8, 16], F32, parts=64)
        tmpC = alloc([64], R='A')
        bufA = alloc([2, G8, 128], F32)
        bufB = alloc([2, G8, 128], F32)
        tq1 = alloc([G8, 128], F32, parts=64)
        tq2 = alloc([G8, 128], F32, parts=64)
        Wa = alloc([G8, 2, 64], BF16)
        w0_wa = st['last_w0']
        Wb = alloc([G8, 128], BF16)
        Wc = alloc([G8, 2, 128], BF16, parts=64)
        ytmp_ap = None
        YR = None
        Rr = alloc([G8, 128], F32, parts=64)
        Ri = alloc([G8, 128], F32, parts=64)
        rhot = alloc([G8, 128], F32, parts=64)
        um = [alloc([2, G8], F32, parts=64, R='A') for _ in range(2)]
        U = alloc([G8, 144], BF16)
        Sb = alloc([2, G8, 144], F32, parts=64)
        Hprev = alloc([2, G8, 144], BF16, parts=64)
        Ysb = U
        ts8a = Buf(tq1.ap[:, :, 0:16], tq1.res)
        ts8b = Buf(tq2.ap[:, :, 0:16], tq2.res)
        ts8c = alloc([2, G8, 16], F32, parts=64, R='A')
        Dcol = small['ssm_D']
        bAf = bufA.ap.rearrange("p r g c -> p (r g c)")
        bBf = bufB.ap.rearrange("p r g c -> p (r g c)")
        gt = Buf(bAf[:, 0:NT], bufA.res)
        sg = gt
        ytmp_ap = bBf[:, 0:NT]
        YR = [bufB.res]
        A64 = Buf(bufA.ap[0:64], bufA.res)
        B64 = Buf(bufB.ap[0:64], bufB.res)

        def ring64(si):
            return ring_all.ap[0:64, si, :]
        WaTp = Buf(ring64(0).bitcast(F32).rearrange("p (r g c) -> p r g c", r=2, g=G8, c=128), Res())
        Wcppp = Buf(ring64(1).bitcast(F32).rearrange("p (r g c) -> p r g c", r=2, g=G8, c=128), Res())
        Wc2 = [Buf(ring64(2)[:, hh * 2048:(hh + 1) * 2048].rearrange("p (g r c) -> p g r c", g=G8, r=2, c=128), Res()) for hh in range(2)]
        s3 = ring64(3).bitcast(F32)
        tq1p = Buf(s3[:, 0:1024].rearrange("p (g c) -> p g c", g=G8, c=128), Res())
        tq2p = Buf(s3[:, 1024:2048].rearrange("p (g c) -> p g c", g=G8, c=128), Res())
        negone = alloc([1], F32, parts=64)
        vset(negone.ap, -1.0, [negone.res], eng='pool')
        v4 = lambda ap3: ap3.rearrange("p g (s q) -> p g s q", s=8, q=16)
        St, Ht, Hun = A64, B64, A64
        qa = Buf(tq1.ap, tq1.res)
        qb = Buf(tq2.ap, tq2.res)
        GRP = [(0, 3), (3, 3), (6, 2)]

        def ssm_loads(ch):
            g0 = ch * G8
            dma('sp', Brb.ap, I['B_re'][g0:g0 + G8].rearrange("g n q -> n g q"), wr=[Brb.res])
            dma('sp', Bib.ap, I['B_im'][g0:g0 + G8].rearrange("g n q -> n g q"), wr=[Bib.res])
            for (nm, dst) in (('C_re', Crb), ('C_im', Cib)):
                dma('sp', tmpC.ap, I[nm][ch * 128:(ch + 1) * 128, :], wr=[tmpC.res])
                b = nb()
                trp(bank(b)[0:64, 0:128], tmpC.ap, identf.ap, [tmpC.res, identf.res], [PB[b]])
                vcopy(dst.ap.rearrange("p g q -> p (g q)"), bank(b)[0:64, 0:128], [PB[b]], [dst.res], eng='act')

        def ssm_matrices(ch, part=3):
            g0 = ch * G8
            Wc = Wc2[ch % 2]

            def gsq(buf, off_elems):
                a = buf.ap
                return bass.AP(a.tensor, a.offset + off_elems + g0, [list(a.ap[0]), [1, G8], [64, 8], [0, 16]])

            def gq_s(buf):
                a = buf.ap
                return bass.AP(a.tensor, a.offset, [list(a.ap[0]), [16, G8], [0, 8], [1, 16]])
            q4a = Buf(v4(tq1p.ap), tq1p.res)
            q4b = Buf(v4(tq2p.ap), tq2p.res)
            if part & 1:
                cmul(v4(WaTp.ap[:, 0]), v4(WaTp.ap[:, 1]), gsq(cst_r, 0), gsq(cst_i, 0), gq_s(Brb), gq_s(Bib),
                     [cst_r.res, cst_i.res, Brb.res, Bib.res], [WaTp.res], [WaTp.res], q4a, q4b, eng='pool')
            todo = ([(Wcppp, Qst_r, Qst_i, False)] if part & 1 else []) + ([(Wc, Pst_r, Pst_i, True)] if part & 2 else [])
            for (dstb, pw_r, pw_i, isWc) in todo:
                off = 64 if isWc else 0
                pr, pi_ = gsq(pw_r, off), gsq(pw_i, off)
                cr, ci = gq_s(Crb), gq_s(Cib)
                rdl = [pw_r.res, pw_i.res, Crb.res, Cib.res]
                if isWc:
                    o_re, o_im = v4(dstb.ap[:, :, 0, :]), v4(dstb.ap[:, :, 1, :])
                else:
                    o_re, o_im = v4(dstb.ap[:, 0]), v4(dstb.ap[:, 1])
                vtt(q4a.ap, pr, cr, ALU.mult, rdl, [q4a.res], 'pool')
                vtt(q4b.ap, pi_, ci, ALU.mult, rdl, [q4b.res], 'pool')
                vtt(o_re, q4a.ap, q4b.ap, ALU.subtract, [q4a.res, q4b.res], [dstb.res], 'pool')
                vtt(q4a.ap, pi_, cr, ALU.mult, rdl, [q4a.res], 'pool')
                vtt(q4b.ap, pr, ci, ALU.mult, rdl, [q4b.res], 'pool')
                vtt(q4a.ap, q4a.ap, q4b.ap, ALU.add, [q4a.res, q4b.res], [q4a.res], 'pool')
                na = negone.ap
                nbc = bass.AP(na.tensor, na.offset, [list(na.ap[0]), [0, G8], [0, 8], [0, 16]])
                vtt(o_im, q4a.ap, nbc, ALU.mult, [q4a.res, negone.res], [dstb.res], 'pool')

        def ssm_tables(ch):
            g0 = ch * G8
            gs = slice(g0, g0 + G8)
            I32 = mybir.dt.int32
            SC = 2.0 * PI * 0.999999
            T_, Ti_ = tq1, tq2
            vtt(T_.ap, bc(phin.ap[:, gs], 128), bc_mid(ciota.ap, G8), ALU.mult, [phin.res, ciota.res], [T_.res])
            vcopy(Ti_.ap.bitcast(I32), T_.ap, [T_.res], [Ti_.res])
            vcopy(Ri.ap, Ti_.ap.bitcast(I32), [Ti_.res], [Ri.res])
            vtt(Ri.ap, T_.ap, Ri.ap, ALU.subtract, [T_.res, Ri.res], [Ri.res])
            act(Ri.ap, Ri.ap, ACT.Sin, [Ri.res], [Ri.res], scale=-SC)
            vsadd(T_.ap, T_.ap, 0.25, [T_.res], [T_.res])
            vcopy(Ti_.ap.bitcast(I32), T_.ap, [T_.res], [Ti_.res])
            vcopy(Rr.ap, Ti_.ap.bitcast(I32), [Ti_.res], [Rr.res])
            vtt(Rr.ap, T_.ap, Rr.ap, ALU.subtract, [T_.res, Rr.res], [Rr.res])
            act(Rr.ap, Rr.ap, ACT.Sin, [Rr.res], [Rr.res], scale=SC)
            vcopy(rhot.ap, bc(rho8.ap[:, gs], 128), [rho8.res], [rhot.res], eng='pool')
            vset(rhot.ap[:, :, 0:1], 0.0, [rhot.res], eng='pool')

        def ssm_wa_wb(ch):
            for gq in range(2):
                b = nb()
                for gg in range(4):
                    g = gq * 4 + gg
                    for ri in range(2):
                        trp(bank(b)[:, gg * 128 + ri * 64:gg * 128 + (ri + 1) * 64], WaTp.ap[:, ri, g, :], identf.ap[0:64, 0:64],
                            [WaTp.res, identf.res], [PB[b]])
                vcopy(Wa.ap[:, gq * 4:gq * 4 + 4].rearrange("p g r n -> p (g r n)"), bank(b), [PB[b]], [Wa.res], eng='act')
            for gq in range(2):
                b = nb()
                for gg in range(4):
                    g = gq * 4 + gg
                    o = bank(b)[:, gg * 128:(gg + 1) * 128]
                    mm(o, WaTp.ap[:, 0, g, :], Wcppp.ap[:, 0, g, :], True, False, [WaTp.res, Wcppp.res], [PB[b]])
                    mm(o, WaTp.ap[:, 1, g, :], Wcppp.ap[:, 1, g, :], False, True, [WaTp.res, Wcppp.res], [PB[b]])
                vtt(Wb.ap[:, gq * 4:gq * 4 + 4, :], bank(b).rearrange("p (g m) -> p g m", g=4, m=128), bc_mid(maskf.ap, 4), ALU.mult,
                    [PB[b], maskf.res], [Wb.res])

        def rotate_scan():
            cmul(St.ap[:, 0], St.ap[:, 1], Sb.ap[:, 0, :, 0:128], Sb.ap[:, 1, :, 0:128], Rr.ap, Ri.ap,
                 [Sb.res, Rr.res, Ri.res], [St.res], [St.res], qa, qb)
            for ri in range(2):
                S.op('dve', lambda E, ri=ri: E.tensor_tensor_scan(
                    Ht.ap[:, ri].rearrange("p g c -> p (g c)"), rhot.ap.rearrange("p g c -> p (g c)"),
                    St.ap[:, ri].rearrange("p g c -> p (g c)"), 0.0, ALU.mult, ALU.add),
                    reads=[St.res, rhot.res], writes=[Ht.res])

        def shuffle_in(ch, uT, ncol):
            for (ga, gn) in GRP:
                b = nb()
                for gg in range(gn):
                    g = ga + gg
                    for s_ in range(8):
                        mm(bank(b)[:, gg * ncol:(gg + 1) * ncol], sel.ap[:, g * 8 + s_, :], uT.ap[:, ch, s_:8 * ncol:8], s_ == 0, s_ == 7,
                           [sel.res, uT.res], [PB[b]])
                vcopy(U.ap[:, ga:ga + gn, 0:ncol], bank(b)[:, 0:gn * ncol].rearrange("p (g c) -> p g c", g=gn, c=ncol),
                      [PB[b]], [U.res], eng='act')

        def map_a(ncol):
            for ri in range(2):
                for (ga, gn) in GRP:
                    b = nb()
                    for gg in range(gn):
                        g = ga + gg
                        mm(bank(b)[0:64, gg * ncol:(gg + 1) * ncol], Wa.ap[:, g, ri, :], U.ap[:, g, 0:ncol], True, True, [Wa.res, U.res], [PB[b]])
                    vcopy(Sb.ap[:, ri, ga:ga + gn, 0:ncol], bank(b)[0:64, 0:gn * ncol].rearrange("p (g c) -> p g c", g=gn, c=ncol),
                          [PB[b]], [Sb.res] + ([SbS_res] if ncol > 128 else []), eng='act')

        qs1 = Buf(ts8a.ap[:, :, 0], ts8a.res)
        qs2 = Buf(ts8b.ap[:, :, 0], ts8b.res)
        p8r, p8i = P_r[8], P_i[8]
        CC = alloc([2, 64], F32, parts=64)
        SbS_res = Res()
        h0T_ch = [Res() for _ in range(8)]
        YB = [5, 6, 7]

        def ssm_pre(ch):
            gs = slice(ch * G8, ch * G8 + G8)
            shuffle_in(ch, upreT, 128)
            map_a(128)

        def ssm_pre_dve(ch):
            gs = slice(ch * G8, ch * G8 + G8)
            rotate_scan()
            cmul(H0.ap[:, 0, gs], H0.ap[:, 1, gs], Ht.ap[:, 0, :, 127], Ht.ap[:, 1, :, 127], Rr.ap[:, :, 127], Ri.ap[:, :, 127],
                 [Ht.res, Rr.res, Ri.res], [H0.res], [H0.res], qs1, qs2, conj_b=True)
            ca = Buf(tq1p.ap[:, :, 16], tq1p.res)
            cb = Buf(tq2p.ap[:, :, 16], tq2p.res)
            cmul(CC.ap[:, 0, gs], CC.ap[:, 1, gs], p8r.ap[:, gs], p8i.ap[:, gs], H0.ap[:, 0, gs], H0.ap[:, 1, gs],
                 [p8r.res, p8i.res, H0.res], [CC.res], [CC.res], ca, cb, eng='pool')

        def ssm_main(ch):
            g0 = ch * G8
            gs = slice(g0, g0 + G8)
            Wc = Wc2[ch % 2]
            shuffle_in(ch, umT, 144)
            map_a(144)
            for bi, (ga, gn) in enumerate(GRP):
                b = YB[bi]
                for gg in range(gn):
                    g = ga + gg
                    mm(bank(b)[:, gg * 144:(gg + 1) * 144], Wb.ap[:, g, :], U.ap[:, g, :], gg == 0, False, [Wb.res, U.res], [PB[b]])
            vcopy(Hprev.ap[:, :, :, 128:144], h0T.ap[:, :, :, gs].rearrange("p r s g -> p r g s"), [h0T.res], [Hprev.res, h0T_ch[ch]], eng='act')
            h0v_r = h0T.ap[:, 0, :, gs].rearrange("p s g -> p g s")
            h0v_i = h0T.ap[:, 1, :, gs].rearrange("p s g -> p g s")
            tsa = Buf(tq1p.ap[:, :, 0:16], tq1p.res)
            tsb = Buf(tq2p.ap[:, :, 0:16], tq2p.res)
            cmul(ts8c.ap[:, 0], ts8c.ap[:, 1], bc(p8r.ap[:, gs], 16), bc(p8i.ap[:, gs], 16), h0v_r, h0v_i,
                 [p8r.res, p8i.res, h0T_ch[ch]], [ts8c.res], [ts8c.res], tsa, tsb, eng='pool')
            vtt(h0v_r, ts8c.ap[:, 0], Sb.ap[:, 0, :, 128:144], ALU.add, [ts8c.res, SbS_res], [h0T_ch[ch]], 'pool')
            vtt(h0v_i, ts8c.ap[:, 1], Sb.ap[:, 1, :, 128:144], ALU.add, [ts8c.res, SbS_res], [h0T_ch[ch]], 'pool')
            vtt(Sb.ap[:, :, :, 0], Sb.ap[:, :, :, 0], CC.ap[:, :, gs], ALU.add, [Sb.res, CC.res], [Sb.res])
            rotate_scan()
            cmul(Hun.ap[:, 0], Hun.ap[:, 1], Ht.ap[:, 0], Ht.ap[:, 1], Rr.ap, Ri.ap,
                 [Ht.res, Rr.res, Ri.res], [Hun.res], [Hun.res], qa, qb, conj_b=True)
            vcopy(Hprev.ap[:, :, :, 1:128], Hun.ap[:, :, :, 0:127], [Hun.res], [Hprev.res], eng='act')
            vcopy(Hprev.ap[:, :, :, 0], H0.ap[:, :, gs], [H0.res], [Hprev.res], eng='act')
            vcopy(HpF.ap[:, :, gs], Hun.ap[:, :, :, 127], [Hun.res], [HpF.res], eng='act')
            if ch < 7:
                ssm_wa_wb(ch + 1)
                if ch < 6:
                    ssm_loads(ch + 2)
                    ssm_matrices(ch + 2, part=1)
                ssm_pre(ch + 1)
            for bi, (ga, gn) in enumerate(GRP):
                b = YB[bi]
                for gg in range(gn):
                    g = ga + gg
                    o = bank(b)[:, gg * 144:(gg + 1) * 144]
                    mm(o, Wc.ap[:, g, 0, :], Hprev.ap[:, 0, g, :], False, False, [Wc.res, Hprev.res], [PB[b]])
                    mm(o, Wc.ap[:, g, 1, :], Hprev.ap[:, 1, g, :], False, True, [Wc.res, Hprev.res], [PB[b]])
                vcopy(Ysb.ap[:, ga:ga + gn, :], bank(b)[:, 0:gn * 144].rearrange("p (g c) -> p g c", g=gn, c=144), [PB[b]], [Ysb.res], eng='act')
            if ch < 6:
                ssm_matrices(ch + 2, part=2)
            if ch < 7:
                ssm_tables(ch + 1)
                ssm_pre_dve(ch + 1)
            ua = umT.ap[:, ch, :]
            for (sa, sn) in GRP:
                b = nb()
                for sj in range(sn):
                    s_ = sa + sj
                    for g in range(G8):
                        mm(bank(b)[:, sj * 144:(sj + 1) * 144], sel.ap[:, s_ * 8 + g, :], Ysb.ap[:, g, :], g == 0, g == G8 - 1, [sel.res, Ysb.res], [PB[b]])
                yv = bass.AP(ytmp_ap.tensor, ytmp_ap.offset + sa, [list(ytmp_ap.ap[0]), [1, sn], [8, 144]])
                uv = bass.AP(ua.tensor, ua.offset + sa, [list(ua.ap[0]), [1, sn], [8, 144]])
                vstt(yv, uv, Dcol.ap[:, ch:ch + 1], bank(b)[:, 0:sn * 144].rearrange("p (s c) -> p s c", s=sn, c=144), ALU.mult, ALU.add,
                     [umT.res, Dcol.res, PB[b]], YR)
            if ch == 0:
                tap('y0', ytmp_ap, YR, [128, NT])
            vtt(gt.ap, ytmp_ap, ytmp_ap, ALU.mult, YR, [gt.res])
            vts(gt.ap, gt.ap, 0.044715, 1.0, ALU.mult, ALU.add, [gt.res], [gt.res])
            vtt(gt.ap, gt.ap, ytmp_ap, ALU.mult, [gt.res] + YR, [gt.res])
            act(sg.ap, gt.ap, ACT.Sigmoid, [gt.res], [sg.res], scale=1.5957691216057308)
            vtt(zg.ap[:, ch, :], ytmp_ap, sg.ap, ALU.mult, YR + [sg.res], [zg.res])

        st['nbanks'] = 5
        st['bank'] = 0
        ssm_loads(0)
        ssm_matrices(0)
        ssm_tables(0)
        ssm_wa_wb(0)
        ssm_loads(1)
        ssm_matrices(1)
        ssm_pre(0)
        ssm_pre_dve(0)
        for ch in range(8):
            ssm_main(ch)
        st['nbanks'] = 8

        tap('HpF', HpF.ap.rearrange("p r g -> p (r g)"), [HpF.res], [64, 128])
        tap('H0', H0.ap.rearrange("p r g -> p (r g)"), [H0.res], [64, 128])
        tap('h0T', h0T.ap[:, 0, 0, :], [h0T.res], [64, 64])
        stg = [Buf(bAf[:, k * 64:(k + 1) * 64], Res()) for k in range(18)] + [Buf(bBf[:, k * 64:(k + 1) * 64], Res()) for k in range(18)]
        si_ = 0
        S.barrier()
        for ri, on in ((0, 'hp_re'), (1, 'hp_im')):
            b = nb()
            trp(bank(b)[0:64, 0:64], HpF.ap[:, ri, :], identf.ap[0:64, 0:64], [HpF.res, identf.res], [PB[b]])
            o_ = stg[si_]; si_ += 1
            vcopy(o_.ap[0:64, :], bank(b)[0:64, 0:64], [PB[b]], [o_.res], eng=evq())
            dma('sp', O[on], o_.ap[0:64, :], rd=[o_.res])
        for ri, on in ((0, 'hs_re'), (1, 'hs_im')):
            for j in range(8):
                b = nb()
                trp(bank(b)[:, 0:64], h0T.ap[:, ri, 2 * j:2 * j + 2, :].rearrange("p s g -> p (s g)"), identf.ap[0:64, 0:64],
                    [h0T.res, identf.res], [PB[b]])
                o_ = stg[si_]; si_ += 1
                vcopy(o_.ap, bank(b)[:, 0:64], [PB[b]], [o_.res], eng=evq())
                dma('sp', O[on][j * 128:(j + 1) * 128, :], o_.ap, rd=[o_.res])
        tap('zg', zg.ap[:, 0, :], [zg.res], [128, NT])
        reset('S')
        if stop <= 3:
            return done()

        xnT = alloc([16, NT], BF16, R='E')
        merged = alloc([16, NT], BF16, R='D')
        gsbs = [alloc([512], R='F') for _ in range(3)]
        gtmp = alloc([512], R='F')
        rtmp = alloc([512], BF16, R='F')
        gctr = [0]

        def next_gsb():
            gctr[0] += 1
            return gsbs[gctr[0] % 3]
        use('C')
        zg2 = alloc([8, NT], BF16)
        glub = small['glu_b']
        m4 = mark()
        xsl = [alloc([D]) for _ in range(2)]
        xsb3 = [alloc([D], BF16) for _ in range(3)]
        glu_panels = {}

        def glu_unit(ui):
            mc, gi_ = ui // 3, ui % 3
            t0, n = TG[gi_]
            pk = mc // 4
            if pk not in glu_panels:
                glu_panels[pk] = wpanel(I['glu_w'], 0, 1024, pk * 512, 512)
            panel, prl = glu_panels[pk]
            j = mc % 4
            b = nb()
            for kc in range(8):
                mm(bank(b)[:, 0:n], panel.ap[:, kc, j * 128:(j + 1) * 128], zg.ap[:, kc, t0:t0 + n], kc == 0, kc == 7,
                   prl + [zg.res], [PB[b]])
            gsb = next_gsb()
            act(gsb.ap[:, 0:n], bank(b)[:, 0:n], ACT.Sigmoid, [PB[b], glub.res], [gsb.res], bias=glub.ap[:, mc:mc + 1])
            vtt(zg2.ap[:, mc, t0:t0 + n], gsb.ap[:, 0:n], zg.ap[:, mc, t0:t0 + n], ALU.mult, [gsb.res, zg.res], [zg2.res])
        ui = 0
        for k in range(9 + 1):
            if k < 9:
                norm_ops(I['xm'][k * 128:(k + 1) * 128, :], k, xsl, xsb3[k % 3], None)
            for _ in range(3 if k < 6 else 2):
                if ui < 24:
                    glu_unit(ui)
                    ui += 1
            if k >= 1:
                norm_T(xsb3[(k - 1) % 3], xnT, (k - 1) * 128)
        while ui < 24:
            glu_unit(ui)
            ui += 1
        release(m4)
        bgate = small['b_gate']

        def gated_merge(w2d, K, rhsT, gi, first):
            KCg = 16
            KCb = K // 128
            gpc = 256
            bpc = min(D, (4096 // KCb) // 128 * 128)
            gpanel = bpanel = None
            for mc in range(16):
                if (mc * 128) % gpc == 0:
                    gpanel, grl = wpanel(I['w_in'], 0, D, 3072 + gi * D + mc * 128, gpc, slots=(0, 1), key='rg')
                if (mc * 128) % bpc == 0:
                    bpanel, brl = wpanel(w2d, 0, K, mc * 128, bpc, slots=(2, 3), key='rb')
                jg = (mc * 128 % gpc) // 128
                jb = (mc * 128 % bpc) // 128
                for (t0, n) in TG:
                    b1 = nb()
                    for kc in range(KCg):
                        mm(bank(b1)[:, 0:n], gpanel.ap[:, kc, jg * 128:(jg + 1) * 128], xnT.ap[:, kc, t0:t0 + n], kc == 0, kc == KCg - 1,
                           grl + [xnT.res], [PB[b1]])
                    gsb = next_gsb()
                    act(gsb.ap[:, 0:n], bank(b1)[:, 0:n], ACT.Sigmoid, [PB[b1], bgate.res], [gsb.res],
                        bias=bgate.ap[:, gi * 16 + mc:gi * 16 + mc + 1])
                    b2 = nb()
                    for kc in range(KCb):
                        mm(bank(b2)[:, 0:n], bpanel.ap[:, kc, jb * 128:(jb + 1) * 128], rhsT.ap[:, kc, t0:t0 + n], kc == 0, kc == KCb - 1,
                           brl + [rhsT.res], [PB[b2]])
                    if first:
                        vtt(merged.ap[:, mc, t0:t0 + n], gsb.ap[:, 0:n], bank(b2)[:, 0:n], ALU.mult, [gsb.res, PB[b2]], [merged.res])
                    else:
                        vtt(gtmp.ap[:, 0:n], gsb.ap[:, 0:n], bank(b2)[:, 0:n], ALU.mult, [gsb.res, PB[b2]], [gtmp.res])
                        vtt(merged.ap[:, mc, t0:t0 + n], merged.ap[:, mc, t0:t0 + n], gtmp.ap[:, 0:n], ALU.add,
                            [merged.res, gtmp.res], [merged.res])

        gated_merge(I['ssm_proj'], 1024, zg2, 1, True)
        tap('merged1', merged.ap[:, 0, :], [merged.res], [128, NT])
        reset('C')
        if stop <= 4:
            return done()

        HL = 16
        upP = [alloc([HL + NPR]) for _ in range(2)]
        upS = [alloc([NSEQ, HL + 8]) for _ in range(2)]
        wsP = [alloc([HL + NPR]) for _ in range(2)]
        wsS = [alloc([NSEQ, HL + 8]) for _ in range(2)]
        pooled = alloc([2, NT], BF16)
        mixed = alloc([8, NT], BF16)
        spT = alloc([8, NSEQ, 15])
        usc = [alloc([128]) for _ in range(2)]
        sp_t = alloc([1024])
        po = alloc([1024])
        po2 = alloc([1024])
        pscale = small['pool_scale']
        for half in range(2):
            dma('sp', sp_t.ap[0:120, :], I['spool'][half * 120:(half + 1) * 120, :], wr=[sp_t.res])
            for c in range(8):
                b = nb()
                trp(bank(b)[:, 0:120], sp_t.ap[0:120, c * 128:(c + 1) * 128], identf.ap[0:120, 0:120], [sp_t.res, identf.res], [PB[b]])
                vcopy(spT.ap[:, c, half * 8:(half + 1) * 8, :], bank(b)[:, 0:120].rearrange("p (s r) -> p s r", s=8, r=15),
                      [PB[b]], [spT.res], eng=evq())
        for sq in range(NSEQ):
            dma('sp', O['pool_s'][sq, 0:7, :], I['spool'][sq * 15 + 8:sq * 15 + 15, :])

        def pool_chunk(c):
            uP, uS = upP[c % 2], upS[c % 2]
            k = c // 2
            nlev = k + 1
            wdw = 2 ** nlev
            b = nb()
            trp(bank(b)[0:16, 0:128], uP.ap[:, HL + NPR - 16:HL + NPR], identf.ap, [uP.res, identf.res], [PB[b]])
            vcopy(po.ap[0:16, c * 128:(c + 1) * 128], bank(b)[0:16, 0:128], [PB[b]], [po.res], eng=evq())
            b = nb()
            trp(bank(b)[:, 0:128], usc[c % 2].ap, identf.ap, [usc[c % 2].res, identf.res], [PB[b]])
            vcopy(po2.ap[:, c * 128:(c + 1) * 128], bank(b)[:, 0:128], [PB[b]], [po2.res], eng=evq())
            srcP, srcS = uP.ap, uS.ap
            rdP, rdS = [uP.res], [uS.res]
            for lev in range(nlev):
                sh = 2 ** lev
                lo = 2 ** (lev + 1) - 1
                dP, dS = wsP[lev % 2], wsS[lev % 2]
                vtt(dP.ap[:, lo:], srcP[:, lo:], srcP[:, lo - sh:HL + NPR - sh], ALU.add, rdP, [dP.res])
                vtt(dS.ap[:, :, lo:], srcS[:, :, lo:], srcS[:, :, lo - sh:HL + 8 - sh], ALU.add, rdS, [dS.res])
                srcP, srcS, rdP, rdS = dP.ap, dS.ap, [dP.res], [dS.res]
            j = c % 2
            vstt(pooled.ap[:, j, 0:NPR], srcP[:, HL:], 1.0 / wdw, uP.ap[:, HL:], ALU.mult, ALU.subtract, rdP + [uP.res], [pooled.res])
            vtt(gtmp.ap[:, 0:16], srcP[:, HL:HL + 16], icnt.ap[:, k, :], ALU.mult, rdP + [icnt.res], [gtmp.res])
            vtt(pooled.ap[:, j, 0:16], gtmp.ap[:, 0:16], uP.ap[:, HL:HL + 16], ALU.subtract, [gtmp.res, uP.res], [pooled.res])
            vstt(pooled.ap[:, j, NPR:NT].rearrange("p (s t) -> p s t", s=NSEQ, t=8), srcS[:, :, HL:], 1.0 / wdw, uS.ap[:, :, HL:],
                 ALU.mult, ALU.subtract, rdS + [uS.res], [pooled.res])

        def ev_up(mc, t0, n, b):
            uP, uS = upP[mc % 2], upS[mc % 2]
            if t0 == 0:
                vcopy(uP.ap[:, 0:HL], halo.ap[:, mc, :], [halo.res], [uP.res])
                vset(uS.ap[:, :, 0:1], 0.0, [uS.res])
                vcopy(uS.ap[:, :, 1:16], spT.ap[:, mc, :, :], [spT.res], [uS.res])
            npr = max(0, min(n, NPR - t0))
            if npr > 0:
                vcopy(uP.ap[:, HL + t0:HL + t0 + npr], bank(b)[:, 0:npr], [PB[b]], [uP.res], eng=evq())
            if t0 + n > NPR:
                vcopy(uS.ap[:, :, HL:HL + 8], bank(b)[:, npr:n].rearrange("p (s t) -> p s t", s=NSEQ, t=8), [PB[b]], [uS.res], eng=evq())
                vcopy(usc[mc % 2].ap, bank(b)[:, npr:n], [PB[b]], [usc[mc % 2].res], eng=evq())
                pool_chunk(mc)
                if mc % 2 == 1:
                    g = mc // 2

                    def ev_mix(mc2, t0_, n_, b_, g=g):
                        c_ = g * 2 + mc2
                        vsmul(mixed.ap[:, c_, t0_:t0_ + n_], bank(b_)[:, 0:n_], pscale.ap[:, c_:c_ + 1], [PB[b_], pscale.res], [mixed.res])
                    linear_fm(I['pool_w'], g * 256, 256, 0, 256, pooled, TG, ev_mix)
        linear_fm(I['w_in'], 0, D, 0, 1024, xnT, TG, ev_up)
        dma('sp', O['pool_p'], po.ap[0:16, :], rd=[po.res])
        for sq in range(NSEQ):
            dma('sp', O['pool_s'][sq, 7:15, :], po2.ap[sq * 8:(sq + 1) * 8, :], rd=[po2.res])
        gated_merge(I['pool_proj'], 1024, mixed, 0, False)
        tap('merged2', merged.ap[:, 0, :], [merged.res], [128, NT])
        reset('C')
        if stop <= 5:
            return done()

        qT = alloc([8, NT], BF16)
        oT = alloc([8, NT], BF16)

        def ev_q(mc, t0, n, b):
            vcopy(qT.ap[:, mc, t0:t0 + n], bank(b)[:, 0:n], [PB[b]], [qT.res], eng=evq())
        linear_fm(I['w_in'], 0, D, 2048, 1024, xnT, TG, ev_q)
        eT = alloc([2, 512], BF16)
        rZ = alloc([512])
        SCL = 1.0 / 16.0
        for h in range(4):
            for (t0, n) in TGP:
                for mc in range(2):
                    b = nb()
                    for dc in range(2):
                        mm(bank(b), KT.ap[:, 2 * h + dc, mc * 128:(mc + 1) * 128], qT.ap[:, 2 * h + dc, t0:t0 + n], dc == 0, dc == 1,
                           [KT.res, qT.res], [PB[b]])
                    act(eT.ap[:, mc, :], bank(b), ACT.Exp, [PB[b]], [eT.res], scale=SCL)
                b = nb()
                for mc in range(2):
                    mm(bank(b), onesb.ap, eT.ap[:, mc, :], mc == 0, mc == 1, [onesb.res, eT.res], [PB[b]])
                vrecip(rZ.ap, bank(b), [PB[b]], [rZ.res])
                for dc in range(2):
                    b = nb()
                    for mc in range(2):
                        mm(bank(b), Vb.ap[:, mc, (2 * h + dc) * 128:(2 * h + dc + 1) * 128], eT.ap[:, mc, :], mc == 0, mc == 1,
                           [Vb.res, eT.res], [PB[b]])
                    vtt(oT.ap[:, 2 * h + dc, t0:t0 + n], bank(b), rZ.ap, ALU.mult, [PB[b], rZ.res], [oT.res])
        Ks = [alloc([2, 1024], BF16) for _ in range(2)]
        Vs = [alloc([2, 1024], BF16) for _ in range(2)]
        KTs = alloc([8, 256], BF16)
        eTs = alloc([4, 2, 8], BF16)
        rZs = alloc([4, 8])
        for sq in range(NSEQ):
            ks, vs = Ks[sq % 2], Vs[sq % 2]
            dma('pool', ks.ap, I['ck'][sq].rearrange("(mc p) d -> p mc d", p=128), wr=[ks.res])
            dma('pool', vs.ap, I['cv'][sq].rearrange("(mc p) d -> p mc d", p=128), wr=[vs.res])
            for mc in range(2):
                b = nb()
                for dch in range(8):
                    trp(bankb(b)[:, dch * 128:(dch + 1) * 128], ks.ap[:, mc, dch * 128:(dch + 1) * 128], identb.ap, [ks.res, identb.res], [PB[b]])
                vcopy(KTs.ap[:, :, mc * 128:(mc + 1) * 128], bankb(b).rearrange("p (j c) -> p j c", j=8, c=128), [PB[b]], [KTs.res], eng=evq())
            tk = slice(NPR + sq * 8, NPR + sq * 8 + 8)
            b = nb()
            for h in range(4):
                for mc in range(2):
                    o = bank(b)[:, (h * 2 + mc) * 8:(h * 2 + mc + 1) * 8]
                    for dc in range(2):
                        mm(o, KTs.ap[:, 2 * h + dc, mc * 128:(mc + 1) * 128], qT.ap[:, 2 * h + dc, tk], dc == 0, dc == 1, [KTs.res, qT.res], [PB[b]])
            act(eTs.ap.rearrange("p h m t -> p (h m t)"), bank(b)[:, 0:64], ACT.Exp, [PB[b]], [eTs.res], scale=SCL)
            b = nb()
            for h in range(4):
                for mc in range(2):
                    mm(bank(b)[:, h * 8:(h + 1) * 8], onesb.ap, eTs.ap[:, h, mc, :], mc == 0, mc == 1, [onesb.res, eTs.res], [PB[b]])
            vrecip(rZs.ap.rearrange("p h t -> p (h t)"), bank(b)[:, 0:32], [PB[b]], [rZs.res])
            b = nb()
            for h in range(4):
                for dc in range(2):
                    o = bank(b)[:, (h * 2 + dc) * 8:(h * 2 + dc + 1) * 8]
                    for mc in range(2):
                        mm(o, vs.ap[:, mc, (2 * h + dc) * 128:(2 * h + dc + 1) * 128], eTs.ap[:, h, mc, :], mc == 0, mc == 1, [vs.res, eTs.res], [PB[b]])
            rz4 = rZs.ap
            rzb = bass.AP(rz4.tensor, rz4.offset, [list(rz4.ap[0]), list(rz4.ap[1]), [0, 2], list(rz4.ap[2])])
            vtt(oT.ap[:, :, tk].rearrange("p (h c) t -> p h c t", h=4, c=2), bank(b)[:, 0:64].rearrange("p (h c t) -> p h c t", h=4, c=2, t=8),
                rzb, ALU.mult, [PB[b], rZs.res], [oT.res])
        tap('oT', oT.ap[:, 0, :], [oT.res], [128, NT])
        gated_merge(I['xa_wo'], 1024, oT, 2, False)
        tap('merged3', merged.ap[:, 0, :], [merged.res], [128, NT])
        reset('C')
        if stop <= 6:
            return done()

        reset('E')
        reset('B')
        reset('F')
        x1 = alloc([9, D], R='C')
        xn2T = alloc([16, NT], BF16, R='E')
        xr_t = [alloc([512], R='B') for _ in range(2)]
        rtmps = [alloc([512], BF16, R='B') for _ in range(3)]
        xsb2 = [alloc([D], BF16, R='F') for _ in range(2)]
        load_g('norm2_g')
        for cg in range(4):
            panel, prl = wpanel_big(I['w_out'], 0, D, cg * 512, 512)
            for tt in range(9):
                xrt = xr_t[(cg * 9 + tt) % 2]
                dma('sp', xrt.ap, I['xm'][tt * 128:(tt + 1) * 128, cg * 512:(cg + 1) * 512], wr=[xrt.res])
                b = nb()
                for kc in range(16):
                    mm(bank(b), merged.ap[:, kc, tt * 128:(tt + 1) * 128], panel.ap[:, kc, :], kc == 0, kc == 15, [merged.res] + prl, [PB[b]])
                vtt(x1.ap[:, tt, cg * 512:(cg + 1) * 512], bank(b), xrt.ap, ALU.add, [PB[b], xrt.res], [x1.res])
                if cg == 3:
                    norm_ops(Buf(x1.ap[:, tt, :], x1.res), tt, None, xsb2[tt % 2], None)
                    if tt >= 1:
                        norm_T(xsb2[(tt - 1) % 2], xn2T, (tt - 1) * 128)
        norm_T(xsb2[8 % 2], xn2T, 8 * 128)
        tap('x1', x1.ap[:, 0, :], [x1.res], [128, D])
        reset('D')
        reset('F')
        if stop <= 7:
            return done()

        hT = alloc([16, NT], BF16, R='D')
        junkf = alloc([D], BF16, R='F')
        rctr = [0]
        for fq in range(4):
            def ev_h(mc, t0, n, b):
                rctr[0] += 1
                rt = rtmps[rctr[0] % 3]
                act(rt.ap[:, 0:n], bank(b)[:, 0:n], ACT.Relu, [PB[b]], [rt.res])
                vtt(hT.ap[:, mc, t0:t0 + n], rt.ap[:, 0:n], rt.ap[:, 0:n], ALU.mult, [rt.res], [hT.res])
            linear_fm(I['w1'], 0, D, fq * 2048, 2048, xn2T, TG, ev_h)
            if fq == 3:
                load_g('final_g')
            for cg in range(4):
                panel, prl = wpanel_big(I['w2'], fq * 2048, 2048, cg * 512, 512)
                for tt in range(9):
                    b = nb()
                    for kc in range(16):
                        mm(bank(b), hT.ap[:, kc, tt * 128:(tt + 1) * 128], panel.ap[:, kc, :], kc == 0, kc == 15, [hT.res] + prl, [PB[b]])
                    last = (fq == 3 and cg == 3)
                    xres = Res() if last else x1.res
                    vtt(x1.ap[:, tt, cg * 512:(cg + 1) * 512], x1.ap[:, tt, cg * 512:(cg + 1) * 512], bank(b), ALU.add, [PB[b], x1.res], [xres])
                    if last:
                        ss = ss_list[tt % 4]
                        xt_ap = x1.ap[:, tt, :]
                        vset(ss.ap[:, 0:1], 0.0, [ss.res])
                        act(junkf.ap, xt_ap, ACT.Square, [xres, ss.res], [junkf.res, ss.res], accum_out=ss.ap[:, 0:1])
                        vts(ss.ap[:, 1:2], ss.ap[:, 0:1], 1.0 / D, EPS, ALU.mult, ALU.add, [ss.res], [ss.res])
                        act(ss.ap[:, 1:2], ss.ap[:, 1:2], ACT.Sqrt, [ss.res], [ss.res])
                        vrecip(ss.ap[:, 2:3], ss.ap[:, 1:2], [ss.res], [ss.res])
                        vstt(xt_ap, xt_ap, ss.ap[:, 2:3], gbc.ap, ALU.mult, ALU.mult, [xres, ss.res, gbc.res], [xres])
                        dma('sp', O['y'][tt * 128:(tt + 1) * 128, :], xt_ap, rd=[xres])
        return done()


_WIN = (2, 4, 8, 16)


def _consts():
    ident = np.eye(128, dtype=np.float32)
    sel = np.zeros((64, 128, 128), np.float32)
    for a in range(8):
        for b in range(8):
            for q in range(16):
                sel[a * 8 + b, a * 16 + q, b * 16 + q] = 1.0
    mask = np.zeros((128, 128), np.float32)
    for s in range(8):
        for sp in range(s, 8):
            mask[s * 16:(s + 1) * 16, sp * 16:(sp + 1) * 16] = 1.0
    return ident, sel, mask


def make_in_maps(inp):
    f = lambda a: np.ascontiguousarray(np.asarray(a, dtype=np.float32))
    ident, sel, mask = _consts()
    shared = {
        'norm1_g': f(inp['norm1_g']).reshape(1, D), 'w_in': f(inp['w_in'][0]), 'b_gate': f(inp['b_gate']).reshape(6144),
        'pool_w': f(inp['pool_w']).reshape(1024, 256), 'pool_scale': f(inp['pool_scale']).reshape(1024),
        'pool_proj': f(inp['pool_proj'][0]), 'A_re': f(inp['ssm_A_re'][0]), 'A_im': f(inp['ssm_A_im'][0]),
        'log_dt': f(inp['ssm_log_dt']).reshape(1, 64), 'B_re': f(inp['ssm_B_re'][0]), 'B_im': f(inp['ssm_B_im'][0]),
        'C_re': f(inp['ssm_C_re']).reshape(1024, 64), 'C_im': f(inp['ssm_C_im']).reshape(1024, 64),
        'ssm_D': f(inp['ssm_D']).reshape(1024), 'glu_w': f(inp['ssm_glu_w'][0]), 'glu_b': f(inp['ssm_glu_b']).reshape(1024),
        'ssm_proj': f(inp['ssm_proj'][0]), 'mem_norm_g': f(inp['mem_norm_g']).reshape(1, D),
        'xa_wk': f(inp['xa_wk'][0]), 'xa_wv': f(inp['xa_wv'][0]), 'xa_wo': f(inp['xa_wo'][0]), 'w_out': f(inp['w_out'][0]),
        'norm2_g': f(inp['norm2_g']).reshape(1, D), 'w1': f(inp['mlp_w1'][0]), 'w2': f(inp['mlp_w2'][0]),
        'final_g': f(inp['final_norm_g']).reshape(1, D), 'ident': ident, 'sel': sel, 'mask': mask,
        'ciota': np.arange(128, dtype=np.float32).reshape(1, 128),
    }
    xpr, xsm = inp['x_prompt'], inp['x_sample']
    maps = []
    for c in range(8):
        b, h = c // 2, c % 2
        sq = slice(16 * c, 16 * c + 16)
        m = dict(shared)
        m['xm'] = f(np.concatenate([xpr[b, h * NPR:(h + 1) * NPR], xsm[sq].reshape(NSM, D)], axis=0))
        m['xp'] = f(xpr[b, 0:NPR]) if h == 1 else np.zeros((NPR, D), np.float32)
        m['spool'] = f(inp['state_pool'][0, sq]).reshape(NSEQ * 15, 1024)
        m['sre'] = f(inp['state_ssm_re'][0, sq]).reshape(NSEQ * 64, 64)
        m['sim'] = f(inp['state_ssm_im'][0, sq]).reshape(NSEQ * 64, 64)
        m['ck'] = f(inp['cache_mem_k'][0, sq]).reshape(NSEQ, 256, 1024)
        m['cv'] = f(inp['cache_mem_v'][0, sq]).reshape(NSEQ, 256, 1024)
        m['mem'] = f(inp['mem_prompt'][b])
        ic = np.zeros((4, 16), np.float32)
        for k, w in enumerate(_WIN):
            for t in range(16):
                ic[k, t] = 1.0 / (min(t + 1, w) if h == 0 else w)
        m['icnt'] = ic
        maps.append(m)
    return maps


def assemble(res):
    y_prompt = np.zeros((4, 2048, D), np.float32)
    y_sample = np.zeros((128, 8, D), np.float32)
    pool_p = np.zeros((1, 4, 15, 1024), np.float32)
    re_p = np.zeros((1, 4, 64, 64), np.float32)
    im_p = np.zeros((1, 4, 64, 64), np.float32)
    mk_p = np.zeros((1, 4, 256, 4, 256), np.float32)
    mv_p = np.zeros((1, 4, 256, 4, 256), np.float32)
    pool_s = np.zeros((1, 128, 15, 1024), np.float32)
    re_s = np.zeros((1, 128, 64, 64), np.float32)
    im_s = np.zeros((1, 128, 64, 64), np.float32)
    for c in range(8):
        r = res[c]
        b, h = c // 2, c % 2
        sq = slice(16 * c, 16 * c + 16)
        y_prompt[b, h * NPR:(h + 1) * NPR] = r['y'][0:NPR]
        y_sample[sq] = r['y'][NPR:NT].reshape(NSEQ, 8, D)
        pool_s[0, sq] = r['pool_s']
        re_s[0, sq] = r['hs_re'].reshape(NSEQ, 64, 64)
        im_s[0, sq] = r['hs_im'].reshape(NSEQ, 64, 64)
        if h == 1:
            pool_p[0, b] = r['pool_p'][1:16]
            re_p[0, b] = r['hp_re']
            im_p[0, b] = r['hp_im']
        else:
            mk_p[0, b] = r['mk'].reshape(256, 4, 256)
            mv_p[0, b] = r['mv'].reshape(256, 4, 256)
    return (y_prompt, y_sample, pool_p, re_p, im_p, mk_p, mv_p, pool_s, re_s, im_s)


_NC_CACHE = {}


def kernel(**inputs):
    if 'nc' not in _NC_CACHE:
        _NC_CACHE['nc'] = build_program()[0]
    nc = _NC_CACHE['nc']
    in_maps = make_in_maps(inputs)
    res = run_bass_kernel_spmd(nc, in_maps, core_ids=list(range(8)))
    return assemble(res.results)
```

```python
import math
import numpy as np
import concourse.bass as bass
import concourse.mybir as mybir
from concourse.bass_utils import run_bass_kernel_spmd
from contextlib import ExitStack

F32 = mybir.dt.float32
BF16 = mybir.dt.bfloat16
ACT = mybir.ActivationFunctionType
ALU = mybir.AluOpType

ENGS = ['pe', 'act', 'dve', 'pool', 'sp']
SAME_ENGINE_SYNC = True

D = 2048
NPR = 1024
NSM = 128
NT = NPR + NSM
NSEQ = 16
DFF = 8192
EPS = 1e-6
TG = [(0, 512), (512, 512), (1024, 128)]
TGP = [(0, 512), (512, 512)]
PI = float(np.pi)


class Res:
    __slots__ = ('name', 'w', 'r', 'excl')

    def __init__(self, name='', excl=False):
        self.name = name
        self.w = None
        self.r = {}
        self.excl = excl


class Sched:
    def __init__(self, ndsem=48):
        self.ops = {e: [] for e in ENGS}
        self.seen = {e: {} for e in ENGS}
        self.ndsem = ndsem
        self.dsem_cnt = [0] * ndsem
        self.dsem_next = 0
        self.dsem_next_sw = 0

    def _deps(self, eng, reads, writes):
        deps = []
        for r in reads:
            if r.w is not None:
                deps.append(r.w)
            if r.excl:
                deps.extend(t for k, t in r.r.items() if k != eng)
        for w in writes:
            if w.w is not None:
                deps.append(w.w)
            deps.extend(w.r.values())
        return deps

    def _add_waits(self, eng, deps):
        seen = self.seen[eng]
        best = {}
        for d in deps:
            key = (d[0], d[1])
            if d[0] == 'e' and d[1] == eng and (eng == 'pe' or not SAME_ENGINE_SYNC):
                continue
            if seen.get(key, -1) >= d[2]:
                continue
            if best.get(key, -1) < d[2]:
                best[key] = d[2]
        waits = []
        for key, v in best.items():
            seen[key] = v
            waits.append((key[0], key[1], v))
            if key[0] == 'e':
                self.ops[key[1]][v][3] = True
        return waits

    def op(self, eng, fn, reads=(), writes=()):
        waits = self._add_waits(eng, self._deps(eng, reads, writes))
        idx = len(self.ops[eng])
        self.ops[eng].append(['c', fn, waits, False, None])
        tag = ('e', eng, idx)
        for r in reads:
            r.r[eng] = tag
        for w in writes:
            w.w = tag
            w.r = {}
        return tag

    def dma(self, q, fn, reads=(), writes=()):
        half = self.ndsem // 2
        if q == 'pool':
            s = half + self.dsem_next_sw
            self.dsem_next_sw = (self.dsem_next_sw + 1) % (self.ndsem - half)
        else:
            s = self.dsem_next
            self.dsem_next = (s + 1) % half
        deps = self._deps(q, reads, writes)
        if self.dsem_cnt[s] > 0:
            deps.append(('d', s, self.dsem_cnt[s]))
        waits = self._add_waits(q, deps)
        self.dsem_cnt[s] += 16
        tag = ('d', s, self.dsem_cnt[s])
        self.ops[q].append(['d', fn, waits, False, s])
        for r in reads:
            r.r[('dma', s)] = tag
        for w in writes:
            w.w = tag
            w.r = {}
        return tag

    def barrier(self):
        tags = []
        for e in ENGS:
            for idx in range(len(self.ops[e]) - 1, -1, -1):
                if self.ops[e][idx][0] == 'c':
                    tags.append(('e', e, idx))
                    break
        for s in range(self.ndsem):
            if self.dsem_cnt[s] > 0:
                tags.append(('d', s, self.dsem_cnt[s]))
        for e in ENGS:
            waits = self._add_waits(e, tags)
            if waits:
                self.ops[e].append(['w', None, waits, False, None])

    def finish(self):
        self.barrier()

    def emit(self, nc, es):
        SEMMAX = 30000
        rank = {}
        nsem = {}
        for e in ENGS:
            c = 0
            rk = []
            for o in self.ops[e]:
                if o[3]:
                    c += 1
                rk.append(c)
            rank[e] = rk
            nsem[e] = max(1, (c + SEMMAX - 1) // SEMMAX)
        esem = {e: [es.enter_context(nc.semaphore('es_%s_%d' % (e, i))) for i in range(nsem[e])] for e in ENGS}
        dsem = [es.enter_context(nc.semaphore('ds_%d' % i)) for i in range(self.ndsem)]
        ops = self.ops
        block = es.enter_context(nc.Block())

        def semval(a, r):
            return esem[a][(r - 1) // SEMMAX], (r - 1) % SEMMAX + 1

        def run(e, E):
            for i, (kind, fn, waits, marked, ds) in enumerate(ops[e]):
                for (k, a, v) in waits:
                    if k == 'e':
                        sm, val = semval(a, rank[a][v])
                        E.wait_ge(sm, val)
                    else:
                        E.wait_ge(dsem[a], v)
                if kind == 'w':
                    continue
                inst = fn(E)
                if kind == 'd':
                    inst.then_inc(dsem[ds], 16)
                elif marked:
                    sm, val = semval(e, rank[e][i])
                    inst.then_inc(sm, 1)

        @block.tensor
        def _(E):
            run('pe', E)

        @block.scalar
        def _(E):
            run('act', E)

        @block.vector
        def _(E):
            run('dve', E)

        @block.gpsimd
        def _(E):
            run('pool', E)

        @block.sync
        def _(E):
            run('sp', E)


class Buf:
    __slots__ = ('ap', 'res')

    def __init__(self, ap, res):
        self.ap = ap
        self.res = res


def bc(ap, n):
    return bass.AP(ap.tensor, ap.offset, [list(x) for x in ap.ap] + [[0, n]])


def bc_mid(ap, n):
    a = [list(x) for x in ap.ap]
    return bass.AP(ap.tensor, ap.offset, [a[0], [0, n]] + a[1:])


LAST_S = None


def build_program(stop=99, taps=()):
    global LAST_S
    nc = bass.Bass("TRN2", target_bir_lowering=False)
    S = Sched()
    LAST_S = S
    taps = set(taps)
    tap_out = {}

    def din(name, shape):
        return nc.dram_tensor(name, list(shape), F32, kind="ExternalInput").ap()

    def dout(name, shape):
        return nc.dram_tensor(name, list(shape), F32, kind="ExternalOutput").ap()

    I = {}
    for name, shape in [
        ('xm', [NT, D]), ('xp', [NPR, D]), ('spool', [NSEQ * 15, 1024]), ('sre', [NSEQ * 64, 64]), ('sim', [NSEQ * 64, 64]),
        ('ck', [NSEQ, 256, 1024]), ('cv', [NSEQ, 256, 1024]), ('mem', [256, D]),
        ('norm1_g', [1, D]), ('w_in', [D, 9216]), ('b_gate', [6144]), ('pool_w', [1024, 256]), ('pool_scale', [1024]),
        ('pool_proj', [1024, D]), ('A_re', [64, 64]), ('A_im', [64, 64]), ('log_dt', [1, 64]),
        ('B_re', [64, 64, 16]), ('B_im', [64, 64, 16]), ('C_re', [1024, 64]), ('C_im', [1024, 64]), ('ssm_D', [1024]),
        ('glu_w', [1024, 1024]), ('glu_b', [1024]), ('ssm_proj', [1024, D]), ('mem_norm_g', [1, D]),
        ('xa_wk', [D, 1024]), ('xa_wv', [D, 1024]), ('xa_wo', [1024, D]), ('w_out', [D, D]), ('norm2_g', [1, D]),
        ('w1', [D, DFF]), ('w2', [DFF, D]), ('final_g', [1, D]),
        ('ident', [128, 128]), ('sel', [64, 128, 128]), ('mask', [128, 128]), ('icnt', [4, 16]), ('ciota', [1, 128]),
    ]:
        I[name] = din(name, shape)
    O = {}
    for name, shape in [
        ('y', [NT, D]), ('pool_p', [16, 1024]), ('hp_re', [64, 64]), ('hp_im', [64, 64]), ('mk', [256, 1024]), ('mv', [256, 1024]),
        ('pool_s', [NSEQ, 15, 1024]), ('hs_re', [NSEQ * 64, 64]), ('hs_im', [NSEQ * 64, 64]),
    ]:
        O[name] = dout(name, shape)

    with ExitStack() as es:
        es.enter_context(nc.allow_non_contiguous_dma(reason="small strided param loads"))
        ARENA_BYTES = 207 * 1024 + 512
        arena_t = es.enter_context(nc.sbuf_tensor("arena", [128, ARENA_BYTES // 4], F32))
        ps_t = es.enter_context(nc.psum_tensor("ps", [128, 4096], F32))
        PB = [Res('bank%d' % i, excl=True) for i in range(8)]
        st = {'bank': 0, 'ev': 0, 'ring': 0, 'big': 0}

        KB = 256
        REG = {}

        def region(name, base_kb, size_kb):
            REG[name] = {'base': int(base_kb * KB), 'top': int(base_kb * KB), 'lim': int((base_kb + size_kb) * KB), 'peak': 0}
        region('A', 0, 44)
        region('B', 44, 9)
        region('C', 53, 72)
        region('D', 125, 36)
        region('E', 161, 36)
        region('F', 197, 9)
        region('S', 71, 136.5)
        cur = {'R': 'A'}

        def use(name):
            cur['R'] = name

        def alloc(shape, dtype=F32, parts=128, R=None):
            rg = REG[R or cur['R']]
            n = 1
            for s_ in shape:
                n *= s_
            nb_ = n * (2 if dtype == BF16 else 4)
            nw = (nb_ + 3) // 4
            nw = (nw + 7) // 8 * 8
            w0 = rg['top']
            st['last_w0'] = w0
            rg['top'] += nw
            rg['peak'] = max(rg['peak'], rg['top'])
            assert rg['top'] <= rg['lim'], "region %s overflow: need %d KiB more" % (R or cur['R'], (rg['top'] - rg['lim']) // KB + 1)
            ap = arena_t[0:parts, w0:w0 + nw]
            if dtype == BF16:
                ap = ap.bitcast(BF16)
            ap = ap[:, 0:n]
            if len(shape) == 2:
                ap = ap.rearrange("p (a b) -> p a b", a=shape[0], b=shape[1])
            elif len(shape) == 3:
                ap = ap.rearrange("p (a b c) -> p a b c", a=shape[0], b=shape[1], c=shape[2])
            elif len(shape) == 4:
                ap = ap.rearrange("p (a b c d) -> p a b c d", a=shape[0], b=shape[1], c=shape[2], d=shape[3])
            return Buf(ap, Res())

        def mark(R=None):
            return (R or cur['R'], REG[R or cur['R']]['top'])

        def release(m):
            S.barrier()
            REG[m[0]]['top'] = m[1]

        def reset(R):
            S.barrier()
            REG[R]['top'] = REG[R]['base']

        def bank(i):
            return ps_t[:, i * 512:(i + 1) * 512]

        def bankb(i):
            return ps_t[:, i * 512:(i + 1) * 512].bitcast(BF16)

        def nb():
            b = st['bank'] % st.get('nbanks', 8)
            st['bank'] = (b + 1) % st.get('nbanks', 8)
            return b

        def evq():
            st['ev'] ^= 1
            return 'act' if st['ev'] else 'dve'

        def mm(out, lhsT, rhs, start, stop, rd, wr):
            S.op('pe', lambda E: E.matmul(out, lhsT, rhs, start=start, stop=stop), reads=rd, writes=wr)

        def trp(out, in_, ident, rd, wr):
            S.op('pe', lambda E: E.transpose(out, in_, ident), reads=rd, writes=wr)

        def vtt(out, a, b, op, rd, wr, eng='dve'):
            S.op(eng, lambda E: E.tensor_tensor(out, a, b, op), reads=rd, writes=wr)

        def vts(out, a, s1, s2, op0, op1, rd, wr, eng='dve'):
            S.op(eng, lambda E: E.tensor_scalar(out, a, s1, s2, op0, op1), reads=rd, writes=wr)

        def vstt(out, in0, scalar, in1, op0, op1, rd, wr, eng='dve'):
            S.op(eng, lambda E: E.scalar_tensor_tensor(out, in0, scalar, in1, op0, op1), reads=rd, writes=wr)

        def vcopy(out, a, rd, wr, eng='dve'):
            if eng == 'act':
                S.op('act', lambda E: E.copy(out, a), reads=rd, writes=wr)
            else:
                S.op(eng, lambda E: E.tensor_copy(out, a), reads=rd, writes=wr)

        def vset(out, val, wr, eng='dve'):
            S.op(eng, lambda E: E.memset(out, val), writes=wr)

        def act(out, in_, func, rd, wr, bias=None, scale=None, accum_out=None):
            kw = {}
            if bias is not None:
                kw['bias'] = bias
            if scale is not None:
                kw['scale'] = scale
            if accum_out is not None:
                kw['accum_out'] = accum_out
            S.op('act', lambda E: E.activation(out, in_, func, **kw), reads=rd, writes=wr)

        def dma(q, out, in_, rd=(), wr=()):
            S.dma(q, lambda E: E.dma_start(out=out, in_=in_), reads=rd, writes=wr)

        def tap(name, ap, rd, shape):
            if name in taps:
                t = dout('tap_' + name, shape)
                tap_out[name] = shape
                dma('sp' if ap.dtype == F32 else 'pool', t, ap, rd=rd)

        def vsadd(out, a, c, rd, wr, eng='dve'):
            S.op(eng, lambda E: E.tensor_scalar_add(out, a, c), reads=rd, writes=wr)

        def vsmul(out, a, c, rd, wr, eng='dve'):
            S.op(eng, lambda E: E.tensor_scalar_mul(out, a, c), reads=rd, writes=wr)

        def vrecip(out, a, rd, wr):
            S.op('dve', lambda E: E.reciprocal(out, a), reads=rd, writes=wr)

        def done():
            import os
            if os.environ.get('PEAKS'):
                for k_, v_ in REG.items():
                    print('region', k_, 'peak KiB', (v_['peak'] - v_['base']) / KB, 'size', (v_['lim'] - v_['base']) / KB)
            S.finish()
            S.emit(nc, es)
            return nc, tap_out

        use('A')
        identf = alloc([128])
        dma('sp', identf.ap, I['ident'], wr=[identf.res])
        identb = alloc([128], BF16)
        dma('pool', identb.ap, I['ident'], wr=[identb.res])
        onesb = alloc([128], BF16)
        vset(onesb.ap, 1.0, [onesb.res])
        gbc = alloc([D])
        small = {}

        def load_cols(name, n):
            b = alloc([n // 128])
            dma('sp', b.ap, I[name].rearrange("(c p) -> p c", p=128), wr=[b.res])
            small[name] = b
        load_cols('b_gate', 6144)
        load_cols('pool_scale', 1024)
        load_cols('ssm_D', 1024)
        load_cols('glu_b', 1024)
        icnt = alloc([4, 16])
        dma('sp', icnt.ap, bass.AP(I['icnt'].tensor, 0, [[0, 128], [16, 4], [1, 16]]), wr=[icnt.res])
        ss_list = [alloc([8]) for _ in range(4)]
        ss = ss_list[0]
        ring_all = alloc([4, 4096], BF16)
        ring_res = [Res() for _ in range(4)]

        def load_g(name):
            dma('sp', gbc.ap, bass.AP(I[name].tensor, 0, [[0, 128], [1, D]]), wr=[gbc.res])

        def wpanel(w2d, r0, K, c0, ncols, slots=(0, 1, 2, 3), key='ring'):
            KC = K // 128
            assert KC * ncols <= 4096
            si = slots[st.setdefault(key, 0) % len(slots)]
            st[key] += 1
            view = ring_all.ap[:, si, 0:KC * ncols].rearrange("p (k m) -> p k m", k=KC, m=ncols)
            src = w2d[r0:r0 + K, c0:c0 + ncols].rearrange("(kc p) m -> p kc m", p=128)
            dma('pool', view, src, wr=[ring_res[si]])
            return Buf(view, ring_res[si]), [ring_res[si]]

        def wpanel_big(w2d, r0, K, c0, ncols):
            KC = K // 128
            assert KC * ncols <= 8192
            bi = st['big'] % 2
            st['big'] += 1
            flat = ring_all.ap[:, 2 * bi:2 * bi + 2, :].rearrange("p a b -> p (a b)")
            view = flat[:, 0:KC * ncols].rearrange("p (k m) -> p k m", k=KC, m=ncols)
            src = w2d[r0:r0 + K, c0:c0 + ncols].rearrange("(kc p) m -> p kc m", p=128)
            rl = [ring_res[2 * bi], ring_res[2 * bi + 1]]
            dma('pool', view, src, wr=rl)
            return Buf(view, None), rl

        def norm_ops(s, tt, xslots, xs_b, junk, g=None):
            gb = g if g is not None else gbc
            if isinstance(s, Buf):
                xt = s
            else:
                xt = xslots[tt % len(xslots)]
                dma('sp', xt.ap, s, wr=[xt.res])
            ss = ss_list[tt % 4]
            jk = junk if junk is not None else xs_b
            vset(ss.ap[:, 0:1], 0.0, [ss.res])
            act(jk.ap, xt.ap, ACT.Square, [xt.res, ss.res], [jk.res, ss.res], accum_out=ss.ap[:, 0:1])
            vts(ss.ap[:, 1:2], ss.ap[:, 0:1], 1.0 / D, EPS, ALU.mult, ALU.add, [ss.res], [ss.res])
            act(ss.ap[:, 1:2], ss.ap[:, 1:2], ACT.Sqrt, [ss.res], [ss.res])
            vrecip(ss.ap[:, 2:3], ss.ap[:, 1:2], [ss.res], [ss.res])
            vstt(xs_b.ap, xt.ap, ss.ap[:, 2:3], gb.ap, ALU.mult, ALU.mult, [xt.res, ss.res, gb.res], [xs_b.res])

        def norm_T(xs_b, dst, c0):
            for h in range(2):
                b = nb()
                for j in range(8):
                    c = h * 8 + j
                    trp(bankb(b)[:, j * 128:(j + 1) * 128], xs_b.ap[:, c * 128:(c + 1) * 128], identb.ap,
                        [xs_b.res, identb.res], [PB[b]])
                vcopy(dst.ap[:, h * 8:(h + 1) * 8, c0:c0 + 128],
                      bankb(b).rearrange("p (j c) -> p j c", j=8, c=128), [PB[b]], [dst.res], eng=evq())

        def norm_transpose(src_fn, ntiles, dst, col0, xslots, xs_b_in, junk):
            for tt in range(ntiles):
                xs_b = xs_b_in[tt % len(xs_b_in)] if isinstance(xs_b_in, list) else xs_b_in
                norm_ops(src_fn(tt), tt, xslots, xs_b, junk)
                norm_T(xs_b, dst, col0 + tt * 128)

        def linear_fm(w2d, r0, K, c0, M, rhsT, tgroups, evac):
            KC = K // 128
            pc = min(M, (4096 // KC) // 128 * 128)
            for p0 in range(0, M, pc):
                panel, prl = wpanel(w2d, r0, K, c0 + p0, pc)
                for j in range(pc // 128):
                    mc = p0 // 128 + j
                    for (t0, n) in tgroups:
                        b = nb()
                        for kc in range(KC):
                            mm(bank(b)[:, 0:n], panel.ap[:, kc, j * 128:(j + 1) * 128], rhsT.ap[:, kc, t0:t0 + n],
                               kc == 0, kc == KC - 1, prl + [rhsT.res], [PB[b]])
                        evac(mc, t0, n, b)

        use('B')
        KT = alloc([8, 256], BF16)
        Vb = alloc([2, 1024], BF16)
        halo = alloc([8, 16])
        use('S')

        def sc():
            return alloc([64], F32, parts=64)
        maskf = alloc([128], R='B')
        dma('sp', maskf.ap, I['mask'], wr=[maskf.res])
        Pst_r = alloc([9, 64], F32, parts=64)
        Pst_i = alloc([9, 64], F32, parts=64)
        Qst_r = alloc([8, 64], F32, parts=64)
        Qst_i = alloc([8, 64], F32, parts=64)
        cst_r = alloc([8, 64], F32, parts=64)
        cst_i = alloc([8, 64], F32, parts=64)
        P_r = [Buf(Pst_r.ap[:, k, :], Pst_r.res) for k in range(9)]
        P_i = [Buf(Pst_i.ap[:, k, :], Pst_i.res) for k in range(9)]
        Q_r = [Buf(Qst_r.ap[:, 7 - k, :], Qst_r.res) for k in range(8)]
        Q_i = [Buf(Qst_i.ap[:, 7 - k, :], Qst_i.res) for k in range(8)]
        cs_r = [Buf(cst_r.ap[:, k, :], cst_r.res) for k in range(8)]
        cs_i = [Buf(cst_i.ap[:, k, :], cst_i.res) for k in range(8)]
        rho8 = sc()
        phin = sc()
        ciota = alloc([128], F32, parts=64)
        dma('sp', ciota.ap, bass.AP(I['ciota'].tensor, 0, [[0, 64], [1, 128]]), wr=[ciota.res])
        H0 = alloc([2, 64], F32, parts=64)
        HpF = alloc([2, 64], F32, parts=64)
        h0T = alloc([2, NSEQ, 64], F32, parts=64)
        sel = alloc([64, 128], BF16)
        dma('pool', sel.ap, I['sel'].rearrange("j p m -> p j m"), wr=[sel.res])
        upreT = alloc([8, NPR], BF16)
        umT = alloc([8, NT], BF16)
        m0 = mark()
        ar, ai, dtb = sc(), sc(), sc()
        tmpA = alloc([64], F32, parts=64)
        for (nm, dst) in (('A_re', ar), ('A_im', ai)):
            dma('sp', tmpA.ap, I[nm], wr=[tmpA.res])
            b = nb()
            trp(bank(b)[0:64, 0:64], tmpA.ap, identf.ap[0:64, 0:64], [tmpA.res, identf.res], [PB[b]])
            vcopy(dst.ap, bank(b)[0:64, 0:64], [PB[b]], [dst.res])
        dma('sp', dtb.ap, bass.AP(I['log_dt'].tensor, 0, [[0, 64], [1, 64]]), wr=[dtb.res])
        act(dtb.ap, dtb.ap, ACT.Exp, [dtb.res], [dtb.res])
        xsl = [alloc([D]) for _ in range(2)]
        xsb4 = [alloc([D], BF16) for _ in range(4)]
        xg = alloc([16, 512], BF16)
        xhalo = alloc([16, 128], BF16)
        gmem = alloc([D], R='C')
        mnT = alloc([16, 256], BF16, R='C')
        dma('sp', gmem.ap, bass.AP(I['mem_norm_g'].tensor, 0, [[0, 128], [1, D]]), wr=[gmem.res])
        for tt in range(2):
            norm_ops(I['mem'][tt * 128:(tt + 1) * 128, :], tt, xsl, xsb4[tt], None, g=gmem)
            norm_T(xsb4[tt], mnT, tt * 128)
        osb = alloc([512], R='C')
        for (wn, on, isk) in (('xa_wk', 'mk', True), ('xa_wv', 'mv', False)):
            for half in range(2):
                panel, prl = wpanel_big(I[wn], 0, D, half * 512, 512)
                if isk:
                    for j in range(4):
                        mc = half * 4 + j
                        b = nb()
                        for kc in range(16):
                            mm(bank(b)[:, 0:256], panel.ap[:, kc, j * 128:(j + 1) * 128], mnT.ap[:, kc, :], kc == 0, kc == 15,
                               prl + [mnT.res], [PB[b]])
                        vcopy(KT.ap[:, mc, :], bank(b)[:, 0:256], [PB[b]], [KT.res], eng='act')
                for mt in range(2):
                    b = nb()
                    for kc in range(16):
                        mm(bank(b), mnT.ap[:, kc, mt * 128:(mt + 1) * 128], panel.ap[:, kc, :], kc == 0, kc == 15,
                           prl + [mnT.res], [PB[b]])
                    vcopy(osb.ap, bank(b), [PB[b]], [osb.res], eng='act')
                    if not isk:
                        vcopy(Vb.ap[:, mt, half * 512:(half + 1) * 512], bank(b), [PB[b]], [Vb.res], eng='act')
                    dma('act', O[on][mt * 128:(mt + 1) * 128, half * 512:(half + 1) * 512], osb.ap, rd=[osb.res])
        load_g('norm1_g')
        glist = [(I['xp'], upreT, 0, 512), (I['xp'], upreT, 512, 512)] + [(I['xm'], umT, t0, n) for (t0, n) in TG]
        wres = [wpanel(I['w_in'], 0, D, 1024 + p_ * 256, 256, slots=(p_,), key='p2r%d' % p_) for p_ in range(4)]
        tile_ctr = [0]

        def p2_norm_ops(i, k):
            src, dstT, t0, n = glist[i]
            norm_ops(src[t0 + k * 128:t0 + (k + 1) * 128, :], tile_ctr[0], xsl, xsb4[k], None)
            tile_ctr[0] += 1

        def p2_T(i):
            src, dstT, t0, n = glist[i]
            for k in range(n // 128):
                norm_T(xsb4[k], xg, k * 128)
            if i == 1:
                vcopy(xhalo.ap, xg.ap[:, :, 384:512], [xg.res], [xhalo.res], eng='act')

        def p2_unit(i, mc):
            src, dstT, t0, n = glist[i]
            panel, prl = wres[mc // 2]
            j = mc % 2
            b = nb()
            for kc in range(16):
                mm(bank(b)[:, 0:n], panel.ap[:, kc, j * 128:(j + 1) * 128], xg.ap[:, kc, 0:n], kc == 0, kc == 15,
                   prl + [xg.res], [PB[b]])
            vcopy(dstT.ap[:, mc, t0:t0 + n], bank(b)[:, 0:n], [PB[b]], [dstT.res], eng=evq())
        for k in range(glist[0][3] // 128):
            p2_norm_ops(0, k)
        lr, th, Em, t1, t2, t3 = sc(), sc(), sc(), sc(), sc(), sc()
        vtt(lr.ap, dtb.ap, ar.ap, ALU.mult, [dtb.res, ar.res], [lr.res])
        vtt(th.ap, dtb.ap, ai.ap, ALU.mult, [dtb.res, ai.res], [th.res])
        act(Em.ap, lr.ap, ACT.Exp, [lr.res], [Em.res])

        def sin_of(dst, src, shift):
            vsadd(t1.ap, src.ap, shift, [src.res], [t1.res])
            vcopy(t2.ap, t1.ap, [t1.res], [t2.res])
            for kk in (1, 3, 5, 7, 9):
                vts(t3.ap, t1.ap, kk * PI, -2 * PI, ALU.is_ge, ALU.mult, [t1.res], [t3.res])
                vtt(t2.ap, t2.ap, t3.ap, ALU.add, [t2.res, t3.res], [t2.res])
            vts(t2.ap, t2.ap, -3.1415925, 3.1415925, ALU.max, ALU.min, [t2.res], [t2.res])
            act(dst.ap, t2.ap, ACT.Sin, [t2.res], [dst.res])
        sn, cs_ = sc(), sc()
        sin_of(sn, th, 0.0)
        sin_of(cs_, th, PI / 2)
        vset(P_r[0].ap, 1.0, [P_r[0].res])
        vset(P_i[0].ap, 0.0, [P_i[0].res])
        vset(Q_r[0].ap, 1.0, [Q_r[0].res])
        vset(Q_i[0].ap, 0.0, [Q_i[0].res])
        vtt(P_r[1].ap, Em.ap, cs_.ap, ALU.mult, [Em.res, cs_.res], [P_r[1].res])
        vtt(P_i[1].ap, Em.ap, sn.ap, ALU.mult, [Em.res, sn.res], [P_i[1].res])

        def cmul(o_r, o_i, a_r, a_i, b_r, b_i, rd, wr_r, wr_i, ta, tb, conj_b=False, eng='dve'):
            vtt(ta.ap, a_r, b_r, ALU.mult, rd, [ta.res], eng)
            vtt(tb.ap, a_i, b_i, ALU.mult, rd, [tb.res], eng)
            vtt(o_r, ta.ap, tb.ap, ALU.add if conj_b else ALU.subtract, [ta.res, tb.res], wr_r, eng)
            vtt(ta.ap, a_i, b_r, ALU.mult, rd, [ta.res], eng)
            vtt(tb.ap, a_r, b_i, ALU.mult, rd, [tb.res], eng)
            vtt(o_i, ta.ap, tb.ap, ALU.subtract if conj_b else ALU.add, [ta.res, tb.res], wr_i, eng)

        def cmul_s(o_r, o_i, a_r, a_i, b_r, b_i, conj_b=False):
            cmul(o_r.ap, o_i.ap, a_r.ap, a_i.ap, b_r.ap, b_i.ap, [a_r.res, a_i.res, b_r.res, b_i.res],
                 [o_r.res], [o_i.res], t1, t2, conj_b)
        for k in range(2, 9):
            cmul_s(P_r[k], P_i[k], P_r[k - 1], P_i[k - 1], P_r[1], P_i[1])
        den, fr, fi, xr = sc(), sc(), sc(), sc()
        vtt(den.ap, ar.ap, ar.ap, ALU.mult, [ar.res], [den.res])
        vtt(t3.ap, ai.ap, ai.ap, ALU.mult, [ai.res], [t3.res])
        vtt(den.ap, den.ap, t3.ap, ALU.add, [den.res, t3.res], [den.res])
        vrecip(den.ap, den.ap, [den.res], [den.res])
        vsadd(xr.ap, P_r[1].ap, -1.0, [P_r[1].res], [xr.res])
        cmul_s(fr, fi, xr, P_i[1], ar, ai, conj_b=True)
        vtt(fr.ap, fr.ap, den.ap, ALU.mult, [fr.res, den.res], [fr.res])
        vtt(fi.ap, fi.ap, den.ap, ALU.mult, [fi.res, den.res], [fi.res])
        einv, irho8 = sc(), sc()
        act(einv.ap, lr.ap, ACT.Exp, [lr.res], [einv.res], scale=-2.0)
        vtt(Q_r[1].ap, P_r[1].ap, einv.ap, ALU.mult, [P_r[1].res, einv.res], [Q_r[1].res])
        vstt(Q_i[1].ap, P_i[1].ap, -1.0, einv.ap, ALU.mult, ALU.mult, [P_i[1].res, einv.res], [Q_i[1].res])
        for k in range(2, 8):
            cmul_s(Q_r[k], Q_i[k], Q_r[k - 1], Q_i[k - 1], Q_r[1], Q_i[1])
        for s_ in range(8):
            cmul_s(cs_r[s_], cs_i[s_], P_r[7 - s_], P_i[7 - s_], fr, fi)
        act(rho8.ap, lr.ap, ACT.Exp, [lr.res], [rho8.res], scale=8.0)
        I32 = mybir.dt.int32
        vsmul(t1.ap, th.ap, 4.0 / PI, [th.res], [t1.res])
        vcopy(t2.ap.bitcast(I32), t1.ap, [t1.res], [t2.res])
        vcopy(t3.ap, t2.ap.bitcast(I32), [t2.res], [t3.res])
        vtt(phin.ap, t1.ap, t3.ap, ALU.subtract, [t1.res, t3.res], [phin.res])
        tap('P8r', P_r[8].ap, [P_r[8].res], [64, 64])
        tap('P8i', P_i[8].ap, [P_i[8].res], [64, 64])
        tap('fr', fr.ap, [fr.res], [64, 64])
        tap('fi', fi.ap, [fi.res], [64, 64])
        tmpS = alloc([64])
        for ri, nm in ((0, 'sre'), (1, 'sim')):
            for j in range(8):
                dma('sp', tmpS.ap, I[nm][j * 128:(j + 1) * 128, :], wr=[tmpS.res])
                b = nb()
                trp(bank(b)[0:64, 0:128], tmpS.ap, identf.ap, [tmpS.res, identf.res], [PB[b]])
                vcopy(h0T.ap[:, ri, 2 * j:2 * j + 2, :], bank(b)[0:64, 0:128].rearrange("p (s g) -> p s g", s=2, g=64),
                      [PB[b]], [h0T.res], eng='act')

        p2_T(0)
        for i in range(len(glist)):
            nxt = glist[i + 1][3] // 128 if i + 1 < len(glist) else 0
            for mc in range(8):
                if mc % 2 == 0 and mc // 2 < nxt:
                    p2_norm_ops(i + 1, mc // 2)
                p2_unit(i, mc)
            if nxt:
                p2_T(i + 1)

        def evh(mc, tt0, nn, b):
            vcopy(halo.ap[:, mc, :], bank(b)[:, 112:128], [PB[b]], [halo.res], eng=evq())
        linear_fm(I['w_in'], 0, D, 0, 1024, xhalo, [(0, 128)], evh)
        release(m0)
        REG['C']['top'] = REG['C']['base']
        tap('umT', umT.ap[:, 0, :], [umT.res], [128, NT])
        tap('upreT', upreT.ap[:, 0, :], [upreT.res], [128, NPR])
        if stop <= 2:
            return done()

        zg = alloc([8, NT], BF16, R='C')
        G8 = 8
        Brb = alloc([G8, 16], F32, parts=64)
        Bib = alloc([G8, 16], F32, parts=64)
        Crb = alloc([G8, 16], F32, parts=64)
        Cib = alloc([G8, 16], F32, parts=64)
        tmpC = alloc([64], R='A')
        bufA = alloc([2, G8, 128], F32)
        bufB = alloc([2, G8, 128], F32)
        tq1 = alloc([G8, 128], F32, parts=64)
        tq2 = alloc([G8, 128], F32, parts=64)
        Wa = alloc([G8, 2, 64], BF16)
        w0_wa = st['last_w0']
        Wb = alloc([G8, 128], BF16)
        Wc = alloc([G8, 2, 128], BF16, parts=64)
        ytmp_ap = None
        YR = None
        Rr = alloc([G8, 128], F32, parts=64)
        Ri = alloc([G8, 128], F32, parts=64)
        rhot = alloc([G8, 128], F32, parts=64)
        um = [alloc([2, G8], F32, parts=64, R='A') for _ in range(2)]
        U = alloc([G8, 144], BF16)
        Sb = alloc([2, G8, 144], F32, parts=64)
        Hprev = alloc([2, G8, 144], BF16, parts=64)
        Ysb = U
        ts8a = Buf(tq1.ap[:, :, 0:16], tq1.res)
        ts8b = Buf(tq2.ap[:, :, 0:16], tq2.res)
        ts8c = alloc([2, G8, 16], F32, parts=64, R='A')
        Dcol = small['ssm_D']
        bAf = bufA.ap.rearrange("p r g c -> p (r g c)")
        bBf = bufB.ap.rearrange("p r g c -> p (r g c)")
        gt = Buf(bAf[:, 0:NT], bufA.res)
        sg = gt
        ytmp_ap = bBf[:, 0:NT]
        YR = [bufB.res]
        A64 = Buf(bufA.ap[0:64], bufA.res)
        B64 = Buf(bufB.ap[0:64], bufB.res)

        def ring64(si):
            return ring_all.ap[0:64, si, :]
        WaTp = Buf(ring64(0).bitcast(F32).rearrange("p (r g c) -> p r g c", r=2, g=G8, c=128), Res())
        Wcppp = Buf(ring64(1).bitcast(F32).rearrange("p (r g c) -> p r g c", r=2, g=G8, c=128), Res())
        Wc2 = [Buf(ring64(2)[:, hh * 2048:(hh + 1) * 2048].rearrange("p (g r c) -> p g r c", g=G8, r=2, c=128), Res()) for hh in range(2)]
        s3 = ring64(3).bitcast(F32)
        tq1p = Buf(s3[:, 0:1024].rearrange("p (g c) -> p g c", g=G8, c=128), Res())
        tq2p = Buf(s3[:, 1024:2048].rearrange("p (g c) -> p g c", g=G8, c=128), Res())
        negone = alloc([1], F32, parts=64)
        vset(negone.ap, -1.0, [negone.res], eng='pool')
        v4 = lambda ap3: ap3.rearrange("p g (s q) -> p g s q", s=8, q=16)
        St, Ht, Hun = A64, B64, A64
        qa = Buf(tq1.ap, tq1.res)
        qb = Buf(tq2.ap, tq2.res)
        GRP = [(0, 3), (3, 3), (6, 2)]

        def ssm_loads(ch):
            g0 = ch * G8
            dma('sp', Brb.ap, I['B_re'][g0:g0 + G8].rearrange("g n q -> n g q"), wr=[Brb.res])
            dma('sp', Bib.ap, I['B_im'][g0:g0 + G8].rearrange("g n q -> n g q"), wr=[Bib.res])
            for (nm, dst) in (('C_re', Crb), ('C_im', Cib)):
                dma('sp', tmpC.ap, I[nm][ch * 128:(ch + 1) * 128, :], wr=[tmpC.res])
                b = nb()
                trp(bank(b)[0:64, 0:128], tmpC.ap, identf.ap, [tmpC.res, identf.res], [PB[b]])
                vcopy(dst.ap.rearrange("p g q -> p (g q)"), bank(b)[0:64, 0:128], [PB[b]], [dst.res], eng='act')

        def ssm_matrices(ch, part=3):
            g0 = ch * G8
            Wc = Wc2[ch % 2]

            def gsq(buf, off_elems):
                a = buf.ap
                return bass.AP(a.tensor, a.offset + off_elems + g0, [list(a.ap[0]), [1, G8], [64, 8], [0, 16]])

            def gq_s(buf):
                a = buf.ap
                return bass.AP(a.tensor, a.offset, [list(a.ap[0]), [16, G8], [0, 8], [1, 16]])
            q4a = Buf(v4(tq1p.ap), tq1p.res)
            q4b = Buf(v4(tq2p.ap), tq2p.res)
            if part & 1:
                cmul(v4(WaTp.ap[:, 0]), v4(WaTp.ap[:, 1]), gsq(cst_r, 0), gsq(cst_i, 0), gq_s(Brb), gq_s(Bib),
                     [cst_r.res, cst_i.res, Brb.res, Bib.res], [WaTp.res], [WaTp.res], q4a, q4b, eng='pool')
            todo = ([(Wcppp, Qst_r, Qst_i, False)] if part & 1 else []) + ([(Wc, Pst_r, Pst_i, True)] if part & 2 else [])
            for (dstb, pw_r, pw_i, isWc) in todo:
                off = 64 if isWc else 0
                pr, pi_ = gsq(pw_r, off), gsq(pw_i, off)
                cr, ci = gq_s(Crb), gq_s(Cib)
                rdl = [pw_r.res, pw_i.res, Crb.res, Cib.res]
                if isWc:
                    o_re, o_im = v4(dstb.ap[:, :, 0, :]), v4(dstb.ap[:, :, 1, :])
                else:
                    o_re, o_im = v4(dstb.ap[:, 0]), v4(dstb.ap[:, 1])
                vtt(q4a.ap, pr, cr, ALU.mult, rdl, [q4a.res], 'pool')
                vtt(q4b.ap, pi_, ci, ALU.mult, rdl, [q4b.res], 'pool')
                vtt(o_re, q4a.ap, q4b.ap, ALU.subtract, [q4a.res, q4b.res], [dstb.res], 'pool')
                vtt(q4a.ap, pi_, cr, ALU.mult, rdl, [q4a.res], 'pool')
                vtt(q4b.ap, pr, ci, ALU.mult, rdl, [q4b.res], 'pool')
                vtt(q4a.ap, q4a.ap, q4b.ap, ALU.add, [q4a.res, q4b.res], [q4a.res], 'pool')
                na = negone.ap
                nbc = bass.AP(na.tensor, na.offset, [list(na.ap[0]), [0, G8], [0, 8], [0, 16]])
                vtt(o_im, q4a.ap, nbc, ALU.mult, [q4a.res, negone.res], [dstb.res], 'pool')

        def ssm_tables(ch):
            g0 = ch * G8
            gs = slice(g0, g0 + G8)
            I32 = mybir.dt.int32
            SC = 2.0 * PI * 0.999999
            T_, Ti_ = tq1, tq2
            vtt(T_.ap, bc(phin.ap[:, gs], 128), bc_mid(ciota.ap, G8), ALU.mult, [phin.res, ciota.res], [T_.res])
            vcopy(Ti_.ap.bitcast(I32), T_.ap, [T_.res], [Ti_.res])
            vcopy(Ri.ap, Ti_.ap.bitcast(I32), [Ti_.res], [Ri.res])
            vtt(Ri.ap, T_.ap, Ri.ap, ALU.subtract, [T_.res, Ri.res], [Ri.res])
            act(Ri.ap, Ri.ap, ACT.Sin, [Ri.res], [Ri.res], scale=-SC)
            vsadd(T_.ap, T_.ap, 0.25, [T_.res], [T_.res])
            vcopy(Ti_.ap.bitcast(I32), T_.ap, [T_.res], [Ti_.res])
            vcopy(Rr.ap, Ti_.ap.bitcast(I32), [Ti_.res], [Rr.res])
            vtt(Rr.ap, T_.ap, Rr.ap, ALU.subtract, [T_.res, Rr.res], [Rr.res])
            act(Rr.ap, Rr.ap, ACT.Sin, [Rr.res], [Rr.res], scale=SC)
            vcopy(rhot.ap, bc(rho8.ap[:, gs], 128), [rho8.res], [rhot.res], eng='pool')
            vset(rhot.ap[:, :, 0:1], 0.0, [rhot.res], eng='pool')

        def ssm_wa_wb(ch):
            for gq in range(2):
                b = nb()
                for gg in range(4):
                    g = gq * 4 + gg
                    for ri in range(2):
                        trp(bank(b)[:, gg * 128 + ri * 64:gg * 128 + (ri + 1) * 64], WaTp.ap[:, ri, g, :], identf.ap[0:64, 0:64],
                            [WaTp.res, identf.res], [PB[b]])
                vcopy(Wa.ap[:, gq * 4:gq * 4 + 4].rearrange("p g r n -> p (g r n)"), bank(b), [PB[b]], [Wa.res], eng='act')
            for gq in range(2):
                b = nb()
                for gg in range(4):
                    g = gq * 4 + gg
                    o = bank(b)[:, gg * 128:(gg + 1) * 128]
                    mm(o, WaTp.ap[:, 0, g, :], Wcppp.ap[:, 0, g, :], True, False, [WaTp.res, Wcppp.res], [PB[b]])
                    mm(o, WaTp.ap[:, 1, g, :], Wcppp.ap[:, 1, g, :], False, True, [WaTp.res, Wcppp.res], [PB[b]])
                vtt(Wb.ap[:, gq * 4:gq * 4 + 4, :], bank(b).rearrange("p (g m) -> p g m", g=4, m=128), bc_mid(maskf.ap, 4), ALU.mult,
                    [PB[b], maskf.res], [Wb.res])

        def rotate_scan():
            cmul(St.ap[:, 0], St.ap[:, 1], Sb.ap[:, 0, :, 0:128], Sb.ap[:, 1, :, 0:128], Rr.ap, Ri.ap,
                 [Sb.res, Rr.res, Ri.res], [St.res], [St.res], qa, qb)
            for ri in range(2):
                S.op('dve', lambda E, ri=ri: E.tensor_tensor_scan(
                    Ht.ap[:, ri].rearrange("p g c -> p (g c)"), rhot.ap.rearrange("p g c -> p (g c)"),
                    St.ap[:, ri].rearrange("p g c -> p (g c)"), 0.0, ALU.mult, ALU.add),
                    reads=[St.res, rhot.res], writes=[Ht.res])

        def shuffle_in(ch, uT, ncol):
            for (ga, gn) in GRP:
                b = nb()
                for gg in range(gn):
                    g = ga + gg
                    for s_ in range(8):
                        mm(bank(b)[:, gg * ncol:(gg + 1) * ncol], sel.ap[:, g * 8 + s_, :], uT.ap[:, ch, s_:8 * ncol:8], s_ == 0, s_ == 7,
                           [sel.res, uT.res], [PB[b]])
                vcopy(U.ap[:, ga:ga + gn, 0:ncol], bank(b)[:, 0:gn * ncol].rearrange("p (g c) -> p g c", g=gn, c=ncol),
                      [PB[b]], [U.res], eng='act')

        def map_a(ncol):
            for ri in range(2):
                for (ga, gn) in GRP:
                    b = nb()
                    for gg in range(gn):
                        g = ga + gg
                        mm(bank(b)[0:64, gg * ncol:(gg + 1) * ncol], Wa.ap[:, g, ri, :], U.ap[:, g, 0:ncol], True, True, [Wa.res, U.res], [PB[b]])
                    vcopy(Sb.ap[:, ri, ga:ga + gn, 0:ncol], bank(b)[0:64, 0:gn * ncol].rearrange("p (g c) -> p g c", g=gn, c=ncol),
                          [PB[b]], [Sb.res] + ([SbS_res] if ncol > 128 else []), eng='act')

        qs1 = Buf(ts8a.ap[:, :, 0], ts8a.res)
        qs2 = Buf(ts8b.ap[:, :, 0], ts8b.res)
        p8r, p8i = P_r[8], P_i[8]
        CC = alloc([2, 64], F32, parts=64)
        SbS_res = Res()
        h0T_ch = [Res() for _ in range(8)]
        YB = [5, 6, 7]

        def ssm_pre(ch):
            gs = slice(ch * G8, ch * G8 + G8)
            shuffle_in(ch, upreT, 128)
            map_a(128)

        def ssm_pre_dve(ch):
            gs = slice(ch * G8, ch * G8 + G8)
            rotate_scan()
            cmul(H0.ap[:, 0, gs], H0.ap[:, 1, gs], Ht.ap[:, 0, :, 127], Ht.ap[:, 1, :, 127], Rr.ap[:, :, 127], Ri.ap[:, :, 127],
                 [Ht.res, Rr.res, Ri.res], [H0.res], [H0.res], qs1, qs2, conj_b=True)
            ca = Buf(tq1p.ap[:, :, 16], tq1p.res)
            cb = Buf(tq2p.ap[:, :, 16], tq2p.res)
            cmul(CC.ap[:, 0, gs], CC.ap[:, 1, gs], p8r.ap[:, gs], p8i.ap[:, gs], H0.ap[:, 0, gs], H0.ap[:, 1, gs],
                 [p8r.res, p8i.res, H0.res], [CC.res], [CC.res], ca, cb, eng='pool')

        def ssm_main(ch):
            g0 = ch * G8
            gs = slice(g0, g0 + G8)
            Wc = Wc2[ch % 2]
            shuffle_in(ch, umT, 144)
            map_a(144)
            for bi, (ga, gn) in enumerate(GRP):
                b = YB[bi]
                for gg in range(gn):
                    g = ga + gg
                    mm(bank(b)[:, gg * 144:(gg + 1) * 144], Wb.ap[:, g, :], U.ap[:, g, :], gg == 0, False, [Wb.res, U.res], [PB[b]])
            vcopy(Hprev.ap[:, :, :, 128:144], h0T.ap[:, :, :, gs].rearrange("p r s g -> p r g s"), [h0T.res], [Hprev.res, h0T_ch[ch]], eng='act')
            h0v_r = h0T.ap[:, 0, :, gs].rearrange("p s g -> p g s")
            h0v_i = h0T.ap[:, 1, :, gs].rearrange("p s g -> p g s")
            tsa = Buf(tq1p.ap[:, :, 0:16], tq1p.res)
            tsb = Buf(tq2p.ap[:, :, 0:16], tq2p.res)
            cmul(ts8c.ap[:, 0], ts8c.ap[:, 1], bc(p8r.ap[:, gs], 16), bc(p8i.ap[:, gs], 16), h0v_r, h0v_i,
                 [p8r.res, p8i.res, h0T_ch[ch]], [ts8c.res], [ts8c.res], tsa, tsb, eng='pool')
            vtt(h0v_r, ts8c.ap[:, 0], Sb.ap[:, 0, :, 128:144], ALU.add, [ts8c.res, SbS_res], [h0T_ch[ch]], 'pool')
            vtt(h0v_i, ts8c.ap[:, 1], Sb.ap[:, 1, :, 128:144], ALU.add, [ts8c.res, SbS_res], [h0T_ch[ch]], 'pool')
            vtt(Sb.ap[:, :, :, 0], Sb.ap[:, :, :, 0], CC.ap[:, :, gs], ALU.add, [Sb.res, CC.res], [Sb.res])
            rotate_scan()
            cmul(Hun.ap[:, 0], Hun.ap[:, 1], Ht.ap[:, 0], Ht.ap[:, 1], Rr.ap, Ri.ap,
                 [Ht.res, Rr.res, Ri.res], [Hun.res], [Hun.res], qa, qb, conj_b=True)
            vcopy(Hprev.ap[:, :, :, 1:128], Hun.ap[:, :, :, 0:127], [Hun.res], [Hprev.res], eng='act')
            vcopy(Hprev.ap[:, :, :, 0], H0.ap[:, :, gs], [H0.res], [Hprev.res], eng='act')
            vcopy(HpF.ap[:, :, gs], Hun.ap[:, :, :, 127], [Hun.res], [HpF.res], eng='act')
            if ch < 7:
                ssm_wa_wb(ch + 1)
                if ch < 6:
                    ssm_loads(ch + 2)
                    ssm_matrices(ch + 2, part=1)
                ssm_pre(ch + 1)
            for bi, (ga, gn) in enumerate(GRP):
                b = YB[bi]
                for gg in range(gn):
                    g = ga + gg
                    o = bank(b)[:, gg * 144:(gg + 1) * 144]
                    mm(o, Wc.ap[:, g, 0, :], Hprev.ap[:, 0, g, :], False, False, [Wc.res, Hprev.res], [PB[b]])
                    mm(o, Wc.ap[:, g, 1, :], Hprev.ap[:, 1, g, :], False, True, [Wc.res, Hprev.res], [PB[b]])
                vcopy(Ysb.ap[:, ga:ga + gn, :], bank(b)[:, 0:gn * 144].rearrange("p (g c) -> p g c", g=gn, c=144), [PB[b]], [Ysb.res], eng='act')
            if ch < 6:
                ssm_matrices(ch + 2, part=2)
            if ch < 7:
                ssm_tables(ch + 1)
                ssm_pre_dve(ch + 1)
            ua = umT.ap[:, ch, :]
            for (sa, sn) in GRP:
                b = nb()
                for sj in range(sn):
                    s_ = sa + sj
                    for g in range(G8):
                        mm(bank(b)[:, sj * 144:(sj + 1) * 144], sel.ap[:, s_ * 8 + g, :], Ysb.ap[:, g, :], g == 0, g == G8 - 1, [sel.res, Ysb.res], [PB[b]])
                yv = bass.AP(ytmp_ap.tensor, ytmp_ap.offset + sa, [list(ytmp_ap.ap[0]), [1, sn], [8, 144]])
                uv = bass.AP(ua.tensor, ua.offset + sa, [list(ua.ap[0]), [1, sn], [8, 144]])
                vstt(yv, uv, Dcol.ap[:, ch:ch + 1], bank(b)[:, 0:sn * 144].rearrange("p (s c) -> p s c", s=sn, c=144), ALU.mult, ALU.add,
                     [umT.res, Dcol.res, PB[b]], YR)
            if ch == 0:
                tap('y0', ytmp_ap, YR, [128, NT])
            vtt(gt.ap, ytmp_ap, ytmp_ap, ALU.mult, YR, [gt.res])
            vts(gt.ap, gt.ap, 0.044715, 1.0, ALU.mult, ALU.add, [gt.res], [gt.res])
            vtt(gt.ap, gt.ap, ytmp_ap, ALU.mult, [gt.res] + YR, [gt.res])
            act(sg.ap, gt.ap, ACT.Sigmoid, [gt.res], [sg.res], scale=1.5957691216057308)
            vtt(zg.ap[:, ch, :], ytmp_ap, sg.ap, ALU.mult, YR + [sg.res], [zg.res])

        st['nbanks'] = 5
        st['bank'] = 0
        ssm_loads(0)
        ssm_matrices(0)
        ssm_tables(0)
        ssm_wa_wb(0)
        ssm_loads(1)
        ssm_matrices(1)
        ssm_pre(0)
        ssm_pre_dve(0)
        for ch in range(8):
            ssm_main(ch)
        st['nbanks'] = 8

        tap('HpF', HpF.ap.rearrange("p r g -> p (r g)"), [HpF.res], [64, 128])
        tap('H0', H0.ap.rearrange("p r g -> p (r g)"), [H0.res], [64, 128])
        tap('h0T', h0T.ap[:, 0, 0, :], [h0T.res], [64, 64])
        stg = [Buf(bAf[:, k * 64:(k + 1) * 64], Res()) for k in range(18)] + [Buf(bBf[:, k * 64:(k + 1) * 64], Res()) for k in range(18)]
        si_ = 0
        S.barrier()
        for ri, on in ((0, 'hp_re'), (1, 'hp_im')):
            b = nb()
            trp(bank(b)[0:64, 0:64], HpF.ap[:, ri, :], identf.ap[0:64, 0:64], [HpF.res, identf.res], [PB[b]])
            o_ = stg[si_]; si_ += 1
            vcopy(o_.ap[0:64, :], bank(b)[0:64, 0:64], [PB[b]], [o_.res], eng=evq())
            dma('sp', O[on], o_.ap[0:64, :], rd=[o_.res])
        for ri, on in ((0, 'hs_re'), (1, 'hs_im')):
            for j in range(8):
                b = nb()
                trp(bank(b)[:, 0:64], h0T.ap[:, ri, 2 * j:2 * j + 2, :].rearrange("p s g -> p (s g)"), identf.ap[0:64, 0:64],
                    [h0T.res, identf.res], [PB[b]])
                o_ = stg[si_]; si_ += 1
                vcopy(o_.ap, bank(b)[:, 0:64], [PB[b]], [o_.res], eng=evq())
                dma('sp', O[on][j * 128:(j + 1) * 128, :], o_.ap, rd=[o_.res])
        tap('zg', zg.ap[:, 0, :], [zg.res], [128, NT])
        reset('S')
        if stop <= 3:
            return done()

        xnT = alloc([16, NT], BF16, R='E')
        merged = alloc([16, NT], BF16, R='D')
        gsbs = [alloc([512], R='F') for _ in range(3)]
        gtmp = alloc([512], R='F')
        rtmp = alloc([512], BF16, R='F')
        gctr = [0]

        def next_gsb():
            gctr[0] += 1
            return gsbs[gctr[0] % 3]
        use('C')
        zg2 = alloc([8, NT], BF16)
        glub = small['glu_b']
        m4 = mark()
        xsl = [alloc([D]) for _ in range(2)]
        xsb3 = [alloc([D], BF16) for _ in range(3)]
        glu_panels = {}

        def glu_unit(ui):
            mc, gi_ = ui // 3, ui % 3
            t0, n = TG[gi_]
            pk = mc // 4
            if pk not in glu_panels:
                glu_panels[pk] = wpanel(I['glu_w'], 0, 1024, pk * 512, 512)
            panel, prl = glu_panels[pk]
            j = mc % 4
            b = nb()
            for kc in range(8):
                mm(bank(b)[:, 0:n], panel.ap[:, kc, j * 128:(j + 1) * 128], zg.ap[:, kc, t0:t0 + n], kc == 0, kc == 7,
                   prl + [zg.res], [PB[b]])
            gsb = next_gsb()
            act(gsb.ap[:, 0:n], bank(b)[:, 0:n], ACT.Sigmoid, [PB[b], glub.res], [gsb.res], bias=glub.ap[:, mc:mc + 1])
            vtt(zg2.ap[:, mc, t0:t0 + n], gsb.ap[:, 0:n], zg.ap[:, mc, t0:t0 + n], ALU.mult, [gsb.res, zg.res], [zg2.res])
        ui = 0
        for k in range(9 + 1):
            if k < 9:
                norm_ops(I['xm'][k * 128:(k + 1) * 128, :], k, xsl, xsb3[k % 3], None)
            for _ in range(3 if k < 6 else 2):
                if ui < 24:
                    glu_unit(ui)
                    ui += 1
            if k >= 1:
                norm_T(xsb3[(k - 1) % 3], xnT, (k - 1) * 128)
        while ui < 24:
            glu_unit(ui)
            ui += 1
        release(m4)
        bgate = small['b_gate']

        def gated_merge(w2d, K, rhsT, gi, first):
            KCg = 16
            KCb = K // 128
            gpc = 256
            bpc = min(D, (4096 // KCb) // 128 * 128)
            gpanel = bpanel = None
            for mc in range(16):
                if (mc * 128) % gpc == 0:
                    gpanel, grl = wpanel(I['w_in'], 0, D, 3072 + gi * D + mc * 128, gpc, slots=(0, 1), key='rg')
                if (mc * 128) % bpc == 0:
                    bpanel, brl = wpanel(w2d, 0, K, mc * 128, bpc, slots=(2, 3), key='rb')
                jg = (mc * 128 % gpc) // 128
                jb = (mc * 128 % bpc) // 128
                for (t0, n) in TG:
                    b1 = nb()
                    for kc in range(KCg):
                        mm(bank(b1)[:, 0:n], gpanel.ap[:, kc, jg * 128:(jg + 1) * 128], xnT.ap[:, kc, t0:t0 + n], kc == 0, kc == KCg - 1,
                           grl + [xnT.res], [PB[b1]])
                    gsb = next_gsb()
                    act(gsb.ap[:, 0:n], bank(b1)[:, 0:n], ACT.Sigmoid, [PB[b1], bgate.res], [gsb.res],
                        bias=bgate.ap[:, gi * 16 + mc:gi * 16 + mc + 1])
                    b2 = nb()
                    for kc in range(KCb):
                        mm(bank(b2)[:, 0:n], bpanel.ap[:, kc, jb * 128:(jb + 1) * 128], rhsT.ap[:, kc, t0:t0 + n], kc == 0, kc == KCb - 1,
                           brl + [rhsT.res], [PB[b2]])
                    if first:
                        vtt(merged.ap[:, mc, t0:t0 + n], gsb.ap[:, 0:n], bank(b2)[:, 0:n], ALU.mult, [gsb.res, PB[b2]], [merged.res])
                    else:
                        vtt(gtmp.ap[:, 0:n], gsb.ap[:, 0:n], bank(b2)[:, 0:n], ALU.mult, [gsb.res, PB[b2]], [gtmp.res])
                        vtt(merged.ap[:, mc, t0:t0 + n], merged.ap[:, mc, t0:t0 + n], gtmp.ap[:, 0:n], ALU.add,
                            [merged.res, gtmp.res], [merged.res])

        gated_merge(I['ssm_proj'], 1024, zg2, 1, True)
        tap('merged1', merged.ap[:, 0, :], [merged.res], [128, NT])
        reset('C')
        if stop <= 4:
            return done()

        HL = 16
        upP = [alloc([HL + NPR]) for _ in range(2)]
        upS = [alloc([NSEQ, HL + 8]) for _ in range(2)]
        wsP = [alloc([HL + NPR]) for _ in range(2)]
        wsS = [alloc([NSEQ, HL + 8]) for _ in range(2)]
        pooled = alloc([2, NT], BF16)
        mixed = alloc([8, NT], BF16)
        spT = alloc([8, NSEQ, 15])
        usc = [alloc([128]) for _ in range(2)]
        sp_t = alloc([1024])
        po = alloc([1024])
        po2 = alloc([1024])
        pscale = small['pool_scale']
        for half in range(2):
            dma('sp', sp_t.ap[0:120, :], I['spool'][half * 120:(half + 1) * 120, :], wr=[sp_t.res])
            for c in range(8):
                b = nb()
                trp(bank(b)[:, 0:120], sp_t.ap[0:120, c * 128:(c + 1) * 128], identf.ap[0:120, 0:120], [sp_t.res, identf.res], [PB[b]])
                vcopy(spT.ap[:, c, half * 8:(half + 1) * 8, :], bank(b)[:, 0:120].rearrange("p (s r) -> p s r", s=8, r=15),
                      [PB[b]], [spT.res], eng=evq())
        for sq in range(NSEQ):
            dma('sp', O['pool_s'][sq, 0:7, :], I['spool'][sq * 15 + 8:sq * 15 + 15, :])

        def pool_chunk(c):
            uP, uS = upP[c % 2], upS[c % 2]
            k = c // 2
            nlev = k + 1
            wdw = 2 ** nlev
            b = nb()
            trp(bank(b)[0:16, 0:128], uP.ap[:, HL + NPR - 16:HL + NPR], identf.ap, [uP.res, identf.res], [PB[b]])
            vcopy(po.ap[0:16, c * 128:(c + 1) * 128], bank(b)[0:16, 0:128], [PB[b]], [po.res], eng=evq())
            b = nb()
            trp(bank(b)[:, 0:128], usc[c % 2].ap, identf.ap, [usc[c % 2].res, identf.res], [PB[b]])
            vcopy(po2.ap[:, c * 128:(c + 1) * 128], bank(b)[:, 0:128], [PB[b]], [po2.res], eng=evq())
            srcP, srcS = uP.ap, uS.ap
            rdP, rdS = [uP.res], [uS.res]
            for lev in range(nlev):
                sh = 2 ** lev
                lo = 2 ** (lev + 1) - 1
                dP, dS = wsP[lev % 2], wsS[lev % 2]
                vtt(dP.ap[:, lo:], srcP[:, lo:], srcP[:, lo - sh:HL + NPR - sh], ALU.add, rdP, [dP.res])
                vtt(dS.ap[:, :, lo:], srcS[:, :, lo:], srcS[:, :, lo - sh:HL + 8 - sh], ALU.add, rdS, [dS.res])
                srcP, srcS, rdP, rdS = dP.ap, dS.ap, [dP.res], [dS.res]
            j = c % 2
            vstt(pooled.ap[:, j, 0:NPR], srcP[:, HL:], 1.0 / wdw, uP.ap[:, HL:], ALU.mult, ALU.subtract, rdP + [uP.res], [pooled.res])
            vtt(gtmp.ap[:, 0:16], srcP[:, HL:HL + 16], icnt.ap[:, k, :], ALU.mult, rdP + [icnt.res], [gtmp.res])
            vtt(pooled.ap[:, j, 0:16], gtmp.ap[:, 0:16], uP.ap[:, HL:HL + 16], ALU.subtract, [gtmp.res, uP.res], [pooled.res])
            vstt(pooled.ap[:, j, NPR:NT].rearrange("p (s t) -> p s t", s=NSEQ, t=8), srcS[:, :, HL:], 1.0 / wdw, uS.ap[:, :, HL:],
                 ALU.mult, ALU.subtract, rdS + [uS.res], [pooled.res])

        def ev_up(mc, t0, n, b):
            uP, uS = upP[mc % 2], upS[mc % 2]
            if t0 == 0:
                vcopy(uP.ap[:, 0:HL], halo.ap[:, mc, :], [halo.res], [uP.res])
                vset(uS.ap[:, :, 0:1], 0.0, [uS.res])
                vcopy(uS.ap[:, :, 1:16], spT.ap[:, mc, :, :], [spT.res], [uS.res])
            npr = max(0, min(n, NPR - t0))
            if npr > 0:
                vcopy(uP.ap[:, HL + t0:HL + t0 + npr], bank(b)[:, 0:npr], [PB[b]], [uP.res], eng=evq())
            if t0 + n > NPR:
                vcopy(uS.ap[:, :, HL:HL + 8], bank(b)[:, npr:n].rearrange("p (s t) -> p s t", s=NSEQ, t=8), [PB[b]], [uS.res], eng=evq())
                vcopy(usc[mc % 2].ap, bank(b)[:, npr:n], [PB[b]], [usc[mc % 2].res], eng=evq())
                pool_chunk(mc)
                if mc % 2 == 1:
                    g = mc // 2

                    def ev_mix(mc2, t0_, n_, b_, g=g):
                        c_ = g * 2 + mc2
                        vsmul(mixed.ap[:, c_, t0_:t0_ + n_], bank(b_)[:, 0:n_], pscale.ap[:, c_:c_ + 1], [PB[b_], pscale.res], [mixed.res])
                    linear_fm(I['pool_w'], g * 256, 256, 0, 256, pooled, TG, ev_mix)
        linear_fm(I['w_in'], 0, D, 0, 1024, xnT, TG, ev_up)
        dma('sp', O['pool_p'], po.ap[0:16, :], rd=[po.res])
        for sq in range(NSEQ):
            dma('sp', O['pool_s'][sq, 7:15, :], po2.ap[sq * 8:(sq + 1) * 8, :], rd=[po2.res])
        gated_merge(I['pool_proj'], 1024, mixed, 0, False)
        tap('merged2', merged.ap[:, 0, :], [merged.res], [128, NT])
        reset('C')
        if stop <= 5:
            return done()

        qT = alloc([8, NT], BF16)
        oT = alloc([8, NT], BF16)

        def ev_q(mc, t0, n, b):
            vcopy(qT.ap[:, mc, t0:t0 + n], bank(b)[:, 0:n], [PB[b]], [qT.res], eng=evq())
        linear_fm(I['w_in'], 0, D, 2048, 1024, xnT, TG, ev_q)
        eT = alloc([2, 512], BF16)
        rZ = alloc([512])
        SCL = 1.0 / 16.0
        for h in range(4):
            for (t0, n) in TGP:
                for mc in range(2):
                    b = nb()
                    for dc in range(2):
                        mm(bank(b), KT.ap[:, 2 * h + dc, mc * 128:(mc + 1) * 128], qT.ap[:, 2 * h + dc, t0:t0 + n], dc == 0, dc == 1,
                           [KT.res, qT.res], [PB[b]])
                    act(eT.ap[:, mc, :], bank(b), ACT.Exp, [PB[b]], [eT.res], scale=SCL)
                b = nb()
                for mc in range(2):
                    mm(bank(b), onesb.ap, eT.ap[:, mc, :], mc == 0, mc == 1, [onesb.res, eT.res], [PB[b]])
                vrecip(rZ.ap, bank(b), [PB[b]], [rZ.res])
                for dc in range(2):
                    b = nb()
                    for mc in range(2):
                        mm(bank(b), Vb.ap[:, mc, (2 * h + dc) * 128:(2 * h + dc + 1) * 128], eT.ap[:, mc, :], mc == 0, mc == 1,
                           [Vb.res, eT.res], [PB[b]])
                    vtt(oT.ap[:, 2 * h + dc, t0:t0 + n], bank(b), rZ.ap, ALU.mult, [PB[b], rZ.res], [oT.res])
        Ks = [alloc([2, 1024], BF16) for _ in range(2)]
        Vs = [alloc([2, 1024], BF16) for _ in range(2)]
        KTs = alloc([8, 256], BF16)
        eTs = alloc([4, 2, 8], BF16)
        rZs = alloc([4, 8])
        for sq in range(NSEQ):
            ks, vs = Ks[sq % 2], Vs[sq % 2]
            dma('pool', ks.ap, I['ck'][sq].rearrange("(mc p) d -> p mc d", p=128), wr=[ks.res])
            dma('pool', vs.ap, I['cv'][sq].rearrange("(mc p) d -> p mc d", p=128), wr=[vs.res])
            for mc in range(2):
                b = nb()
                for dch in range(8):
                    trp(bankb(b)[:, dch * 128:(dch + 1) * 128], ks.ap[:, mc, dch * 128:(dch + 1) * 128], identb.ap, [ks.res, identb.res], [PB[b]])
                vcopy(KTs.ap[:, :, mc * 128:(mc + 1) * 128], bankb(b).rearrange("p (j c) -> p j c", j=8, c=128), [PB[b]], [KTs.res], eng=evq())
            tk = slice(NPR + sq * 8, NPR + sq * 8 + 8)
            b = nb()
            for h in range(4):
                for mc in range(2):
                    o = bank(b)[:, (h * 2 + mc) * 8:(h * 2 + mc + 1) * 8]
                    for dc in range(2):
                        mm(o, KTs.ap[:, 2 * h + dc, mc * 128:(mc + 1) * 128], qT.ap[:, 2 * h + dc, tk], dc == 0, dc == 1, [KTs.res, qT.res], [PB[b]])
            act(eTs.ap.rearrange("p h m t -> p (h m t)"), bank(b)[:, 0:64], ACT.Exp, [PB[b]], [eTs.res], scale=SCL)
            b = nb()
            for h in range(4):
                for mc in range(2):
                    mm(bank(b)[:, h * 8:(h + 1) * 8], onesb.ap, eTs.ap[:, h, mc, :], mc == 0, mc == 1, [onesb.res, eTs.res], [PB[b]])
            vrecip(rZs.ap.rearrange("p h t -> p (h t)"), bank(b)[:, 0:32], [PB[b]], [rZs.res])
            b = nb()
            for h in range(4):
                for dc in range(2):
                    o = bank(b)[:, (h * 2 + dc) * 8:(h * 2 + dc + 1) * 8]
                    for mc in range(2):
                        mm(o, vs.ap[:, mc, (2 * h + dc) * 128:(2 * h + dc + 1) * 128], eTs.ap[:, h, mc, :], mc == 0, mc == 1, [vs.res, eTs.res], [PB[b]])
            rz4 = rZs.ap
            rzb = bass.AP(rz4.tensor, rz4.offset, [list(rz4.ap[0]), list(rz4.ap[1]), [0, 2], list(rz4.ap[2])])
            vtt(oT.ap[:, :, tk].rearrange("p (h c) t -> p h c t", h=4, c=2), bank(b)[:, 0:64].rearrange("p (h c t) -> p h c t", h=4, c=2, t=8),
                rzb, ALU.mult, [PB[b], rZs.res], [oT.res])
        tap('oT', oT.ap[:, 0, :], [oT.res], [128, NT])
        gated_merge(I['xa_wo'], 1024, oT, 2, False)
        tap('merged3', merged.ap[:, 0, :], [merged.res], [128, NT])
        reset('C')
        if stop <= 6:
            return done()

        reset('E')
        reset('B')
        reset('F')
        x1 = alloc([9, D], R='C')
        xn2T = alloc([16, NT], BF16, R='E')
        xr_t = [alloc([512], R='B') for _ in range(2)]
        rtmps = [alloc([512], BF16, R='B') for _ in range(3)]
        xsb2 = [alloc([D], BF16, R='F') for _ in range(2)]
        load_g('norm2_g')
        for cg in range(4):
            panel, prl = wpanel_big(I['w_out'], 0, D, cg * 512, 512)
            for tt in range(9):
                xrt = xr_t[(cg * 9 + tt) % 2]
                dma('sp', xrt.ap, I['xm'][tt * 128:(tt + 1) * 128, cg * 512:(cg + 1) * 512], wr=[xrt.res])
                b = nb()
                for kc in range(16):
                    mm(bank(b), merged.ap[:, kc, tt * 128:(tt + 1) * 128], panel.ap[:, kc, :], kc == 0, kc == 15, [merged.res] + prl, [PB[b]])
                vtt(x1.ap[:, tt, cg * 512:(cg + 1) * 512], bank(b), xrt.ap, ALU.add, [PB[b], xrt.res], [x1.res])
                if cg == 3:
                    norm_ops(Buf(x1.ap[:, tt, :], x1.res), tt, None, xsb2[tt % 2], None)
                    if tt >= 1:
                        norm_T(xsb2[(tt - 1) % 2], xn2T, (tt - 1) * 128)
        norm_T(xsb2[8 % 2], xn2T, 8 * 128)
        tap('x1', x1.ap[:, 0, :], [x1.res], [128, D])
        reset('D')
        reset('F')
        if stop <= 7:
            return done()

        hT = alloc([16, NT], BF16, R='D')
        junkf = alloc([D], BF16, R='F')
        rctr = [0]
        for fq in range(4):
            def ev_h(mc, t0, n, b):
                rctr[0] += 1
                rt = rtmps[rctr[0] % 3]
                act(rt.ap[:, 0:n], bank(b)[:, 0:n], ACT.Relu, [PB[b]], [rt.res])
                vtt(hT.ap[:, mc, t0:t0 + n], rt.ap[:, 0:n], rt.ap[:, 0:n], ALU.mult, [rt.res], [hT.res])
            linear_fm(I['w1'], 0, D, fq * 2048, 2048, xn2T, TG, ev_h)
            if fq == 3:
                load_g('final_g')
            for cg in range(4):
                panel, prl = wpanel_big(I['w2'], fq * 2048, 2048, cg * 512, 512)
                for tt in range(9):
                    b = nb()
                    for kc in range(16):
                        mm(bank(b), hT.ap[:, kc, tt * 128:(tt + 1) * 128], panel.ap[:, kc, :], kc == 0, kc == 15, [hT.res] + prl, [PB[b]])
                    last = (fq == 3 and cg == 3)
                    xres = Res() if last else x1.res
                    vtt(x1.ap[:, tt, cg * 512:(cg + 1) * 512], x1.ap[:, tt, cg * 512:(cg + 1) * 512], bank(b), ALU.add, [PB[b], x1.res], [xres])
                    if last:
                        ss = ss_list[tt % 4]
                        xt_ap = x1.ap[:, tt, :]
                        vset(ss.ap[:, 0:1], 0.0, [ss.res])
                        act(junkf.ap, xt_ap, ACT.Square, [xres, ss.res], [junkf.res, ss.res], accum_out=ss.ap[:, 0:1])
                        vts(ss.ap[:, 1:2], ss.ap[:, 0:1], 1.0 / D, EPS, ALU.mult, ALU.add, [ss.res], [ss.res])
                        act(ss.ap[:, 1:2], ss.ap[:, 1:2], ACT.Sqrt, [ss.res], [ss.res])
                        vrecip(ss.ap[:, 2:3], ss.ap[:, 1:2], [ss.res], [ss.res])
                        vstt(xt_ap, xt_ap, ss.ap[:, 2:3], gbc.ap, ALU.mult, ALU.mult, [xres, ss.res, gbc.res], [xres])
                        dma('sp', O['y'][tt * 128:(tt + 1) * 128, :], xt_ap, rd=[xres])
        return done()


_WIN = (2, 4, 8, 16)


def _consts():
    ident = np.eye(128, dtype=np.float32)
    sel = np.zeros((64, 128, 128), np.float32)
    for a in range(8):
        for b in range(8):
            for q in range(16):
                sel[a * 8 + b, a * 16 + q, b * 16 + q] = 1.0
    mask = np.zeros((128, 128), np.float32)
    for s in range(8):
        for sp in range(s, 8):
            mask[s * 16:(s + 1) * 16, sp * 16:(sp + 1) * 16] = 1.0
    return ident, sel, mask


def make_in_maps(inp):
    f = lambda a: np.ascontiguousarray(np.asarray(a, dtype=np.float32))
    ident, sel, mask = _consts()
    shared = {
        'norm1_g': f(inp['norm1_g']).reshape(1, D), 'w_in': f(inp['w_in'][0]), 'b_gate': f(inp['b_gate']).reshape(6144),
        'pool_w': f(inp['pool_w']).reshape(1024, 256), 'pool_scale': f(inp['pool_scale']).reshape(1024),
        'pool_proj': f(inp['pool_proj'][0]), 'A_re': f(inp['ssm_A_re'][0]), 'A_im': f(inp['ssm_A_im'][0]),
        'log_dt': f(inp['ssm_log_dt']).reshape(1, 64), 'B_re': f(inp['ssm_B_re'][0]), 'B_im': f(inp['ssm_B_im'][0]),
        'C_re': f(inp['ssm_C_re']).reshape(1024, 64), 'C_im': f(inp['ssm_C_im']).reshape(1024, 64),
        'ssm_D': f(inp['ssm_D']).reshape(1024), 'glu_w': f(inp['ssm_glu_w'][0]), 'glu_b': f(inp['ssm_glu_b']).reshape(1024),
        'ssm_proj': f(inp['ssm_proj'][0]), 'mem_norm_g': f(inp['mem_norm_g']).reshape(1, D),
        'xa_wk': f(inp['xa_wk'][0]), 'xa_wv': f(inp['xa_wv'][0]), 'xa_wo': f(inp['xa_wo'][0]), 'w_out': f(inp['w_out'][0]),
        'norm2_g': f(inp['norm2_g']).reshape(1, D), 'w1': f(inp['mlp_w1'][0]), 'w2': f(inp['mlp_w2'][0]),
        'final_g': f(inp['final_norm_g']).reshape(1, D), 'ident': ident, 'sel': sel, 'mask': mask,
        'ciota': np.arange(128, dtype=np.float32).reshape(1, 128),
    }
    xpr, xsm = inp['x_prompt'], inp['x_sample']
    maps = []
    for c in range(8):
        b, h = c // 2, c % 2
        sq = slice(16 * c, 16 * c + 16)
        m = dict(shared)
        m['xm'] = f(np.concatenate([xpr[b, h * NPR:(h + 1) * NPR], xsm[sq].reshape(NSM, D)], axis=0))
        m['xp'] = f(xpr[b, 0:NPR]) if h == 1 else np.zeros((NPR, D), np.float32)
        m['spool'] = f(inp['state_pool'][0, sq]).reshape(NSEQ * 15, 1024)
        m['sre'] = f(inp['state_ssm_re'][0, sq]).reshape(NSEQ * 64, 64)
        m['sim'] = f(inp['state_ssm_im'][0, sq]).reshape(NSEQ * 64, 64)
        m['ck'] = f(inp['cache_mem_k'][0, sq]).reshape(NSEQ, 256, 1024)
        m['cv'] = f(inp['cache_mem_v'][0, sq]).reshape(NSEQ, 256, 1024)
        m['mem'] = f(inp['mem_prompt'][b])
        ic = np.zeros((4, 16), np.float32)
        for k, w in enumerate(_WIN):
            for t in range(16):
                ic[k, t] = 1.0 / (min(t + 1, w) if h == 0 else w)
        m['icnt'] = ic
        maps.append(m)
    return maps


def assemble(res):
    y_prompt = np.zeros((4, 2048, D), np.float32)
    y_sample = np.zeros((128, 8, D), np.float32)
    pool_p = np.zeros((1, 4, 15, 1024), np.float32)
    re_p = np.zeros((1, 4, 64, 64), np.float32)
    im_p = np.zeros((1, 4, 64, 64), np.float32)
    mk_p = np.zeros((1, 4, 256, 4, 256), np.float32)
    mv_p = np.zeros((1, 4, 256, 4, 256), np.float32)
    pool_s = np.zeros((1, 128, 15, 1024), np.float32)
    re_s = np.zeros((1, 128, 64, 64), np.float32)
    im_s = np.zeros((1, 128, 64, 64), np.float32)
    for c in range(8):
        r = res[c]
        b, h = c // 2, c % 2
        sq = slice(16 * c, 16 * c + 16)
        y_prompt[b, h * NPR:(h + 1) * NPR] = r['y'][0:NPR]
        y_sample[sq] = r['y'][NPR:NT].reshape(NSEQ, 8, D)
        pool_s[0, sq] = r['pool_s']
        re_s[0, sq] = r['hs_re'].reshape(NSEQ, 64, 64)
        im_s[0, sq] = r['hs_im'].reshape(NSEQ, 64, 64)
        if h == 1:
            pool_p[0, b] = r['pool_p'][1:16]
            re_p[0, b] = r['hp_re']
            im_p[0, b] = r['hp_im']
        else:
            mk_p[0, b] = r['mk'].reshape(256, 4, 256)
            mv_p[0, b] = r['mv'].reshape(256, 4, 256)
    return (y_prompt, y_sample, pool_p, re_p, im_p, mk_p, mv_p, pool_s, re_s, im_s)


_NC_CACHE = {}


def kernel(**inputs):
    if 'nc' not in _NC_CACHE:
        _NC_CACHE['nc'] = build_program()[0]
    nc = _NC_CACHE['nc']
    in_maps = make_in_maps(inputs)
    res = run_bass_kernel_spmd(nc, in_maps, core_ids=list(range(8)))
    return assemble(res.results)
```

```python
import math
import numpy as np
import concourse.bass as bass
import concourse.mybir as mybir
from concourse.bass_utils import run_bass_kernel_spmd
from contextlib import ExitStack

F32 = mybir.dt.float32
BF16 = mybir.dt.bfloat16
ACT = mybir.ActivationFunctionType
ALU = mybir.AluOpType

ENGS = ['pe', 'act', 'dve', 'pool', 'sp']
SAME_ENGINE_SYNC = True

D = 2048
NPR = 1024
NSM = 128
NT = NPR + NSM
NSEQ = 16
DFF = 8192
EPS = 1e-6
TG = [(0, 512), (512, 512), (1024, 128)]
TGP = [(0, 512), (512, 512)]
PI = float(np.pi)


class Res:
    __slots__ = ('name', 'w', 'r', 'excl')

    def __init__(self, name='', excl=False):
        self.name = name
        self.w = None
        self.r = {}
        self.excl = excl


class Sched:
    def __init__(self, ndsem=48):
        self.ops = {e: [] for e in ENGS}
        self.seen = {e: {} for e in ENGS}
        self.ndsem = ndsem
        self.dsem_cnt = [0] * ndsem
        self.dsem_next = 0
        self.dsem_next_sw = 0

    def _deps(self, eng, reads, writes):
        deps = []
        for r in reads:
            if r.w is not None:
                deps.append(r.w)
            if r.excl:
                deps.extend(t for k, t in r.r.items() if k != eng)
        for w in writes:
            if w.w is not None:
                deps.append(w.w)
            deps.extend(w.r.values())
        return deps

    def _add_waits(self, eng, deps):
        seen = self.seen[eng]
        best = {}
        for d in deps:
            key = (d[0], d[1])
            if d[0] == 'e' and d[1] == eng and (eng == 'pe' or not SAME_ENGINE_SYNC):
                continue
            if seen.get(key, -1) >= d[2]:
                continue
            if best.get(key, -1) < d[2]:
                best[key] = d[2]
        waits = []
        for key, v in best.items():
            seen[key] = v
            waits.append((key[0], key[1], v))
            if key[0] == 'e':
                self.ops[key[1]][v][3] = True
        return waits

    def op(self, eng, fn, reads=(), writes=()):
        waits = self._add_waits(eng, self._deps(eng, reads, writes))
        idx = len(self.ops[eng])
        self.ops[eng].append(['c', fn, waits, False, None])
        tag = ('e', eng, idx)
        for r in reads:
            r.r[eng] = tag
        for w in writes:
            w.w = tag
            w.r = {}
        return tag

    def dma(self, q, fn, reads=(), writes=()):
        half = self.ndsem // 2
        if q == 'pool':
            s = half + self.dsem_next_sw
            self.dsem_next_sw = (self.dsem_next_sw + 1) % (self.ndsem - half)
        else:
            s = self.dsem_next
            self.dsem_next = (s + 1) % half
        deps = self._deps(q, reads, writes)
        if self.dsem_cnt[s] > 0:
            deps.append(('d', s, self.dsem_cnt[s]))
        waits = self._add_waits(q, deps)
        self.dsem_cnt[s] += 16
        tag = ('d', s, self.dsem_cnt[s])
        self.ops[q].append(['d', fn, waits, False, s])
        for r in reads:
            r.r[('dma', s)] = tag
        for w in writes:
            w.w = tag
            w.r = {}
        return tag

    def barrier(self):
        tags = []
        for e in ENGS:
            for idx in range(len(self.ops[e]) - 1, -1, -1):
                if self.ops[e][idx][0] == 'c':
                    tags.append(('e', e, idx))
                    break
        for s in range(self.ndsem):
            if self.dsem_cnt[s] > 0:
                tags.append(('d', s, self.dsem_cnt[s]))
        for e in ENGS:
            waits = self._add_waits(e, tags)
            if waits:
                self.ops[e].append(['w', None, waits, False, None])

    def finish(self):
        self.barrier()

    def emit(self, nc, es):
        SEMMAX = 30000
        rank = {}
        nsem = {}
        for e in ENGS:
            c = 0
            rk = []
            for o in self.ops[e]:
                if o[3]:
                    c += 1
                rk.append(c)
            rank[e] = rk
            nsem[e] = max(1, (c + SEMMAX - 1) // SEMMAX)
        esem = {e: [es.enter_context(nc.semaphore('es_%s_%d' % (e, i))) for i in range(nsem[e])] for e in ENGS}
        dsem = [es.enter_context(nc.semaphore('ds_%d' % i)) for i in range(self.ndsem)]
        ops = self.ops
        block = es.enter_context(nc.Block())

        def semval(a, r):
            return esem[a][(r - 1) // SEMMAX], (r - 1) % SEMMAX + 1

        def run(e, E):
            for i, (kind, fn, waits, marked, ds) in enumerate(ops[e]):
                for (k, a, v) in waits:
                    if k == 'e':
                        sm, val = semval(a, rank[a][v])
                        E.wait_ge(sm, val)
                    else:
                        E.wait_ge(dsem[a], v)
                if kind == 'w':
                    continue
                inst = fn(E)
                if kind == 'd':
                    inst.then_inc(dsem[ds], 16)
                elif marked:
                    sm, val = semval(e, rank[e][i])
                    inst.then_inc(sm, 1)

        @block.tensor
        def _(E):
            run('pe', E)

        @block.scalar
        def _(E):
            run('act', E)

        @block.vector
        def _(E):
            run('dve', E)

        @block.gpsimd
        def _(E):
            run('pool', E)

        @block.sync
        def _(E):
            run('sp', E)


class Buf:
    __slots__ = ('ap', 'res')

    def __init__(self, ap, res):
        self.ap = ap
        self.res = res


def bc(ap, n):
    return bass.AP(ap.tensor, ap.offset, [list(x) for x in ap.ap] + [[0, n]])


def bc_mid(ap, n):
    a = [list(x) for x in ap.ap]
    return bass.AP(ap.tensor, ap.offset, [a[0], [0, n]] + a[1:])


LAST_S = None


def build_program(stop=99, taps=()):
    global LAST_S
    nc = bass.Bass("TRN2", target_bir_lowering=False)
    S = Sched()
    LAST_S = S
    taps = set(taps)
    tap_out = {}

    def din(name, shape):
        return nc.dram_tensor(name, list(shape), F32, kind="ExternalInput").ap()

    def dout(name, shape):
        return nc.dram_tensor(name, list(shape), F32, kind="ExternalOutput").ap()

    I = {}
    for name, shape in [
        ('xm', [NT, D]), ('xp', [NPR, D]), ('spool', [NSEQ * 15, 1024]), ('sre', [NSEQ * 64, 64]), ('sim', [NSEQ * 64, 64]),
        ('ck', [NSEQ, 256, 1024]), ('cv', [NSEQ, 256, 1024]), ('mem', [256, D]),
        ('norm1_g', [1, D]), ('w_in', [D, 9216]), ('b_gate', [6144]), ('pool_w', [1024, 256]), ('pool_scale', [1024]),
        ('pool_proj', [1024, D]), ('A_re', [64, 64]), ('A_im', [64, 64]), ('log_dt', [1, 64]),
        ('B_re', [64, 64, 16]), ('B_im', [64, 64, 16]), ('C_re', [1024, 64]), ('C_im', [1024, 64]), ('ssm_D', [1024]),
        ('glu_w', [1024, 1024]), ('glu_b', [1024]), ('ssm_proj', [1024, D]), ('mem_norm_g', [1, D]),
        ('xa_wk', [D, 1024]), ('xa_wv', [D, 1024]), ('xa_wo', [1024, D]), ('w_out', [D, D]), ('norm2_g', [1, D]),
        ('w1', [D, DFF]), ('w2', [DFF, D]), ('final_g', [1, D]),
        ('ident', [128, 128]), ('sel', [64, 128, 128]), ('mask', [128, 128]), ('icnt', [4, 16]), ('ciota', [1, 128]),
    ]:
        I[name] = din(name, shape)
    O = {}
    for name, shape in [
        ('y', [NT, D]), ('pool_p', [16, 1024]), ('hp_re', [64, 64]), ('hp_im', [64, 64]), ('mk', [256, 1024]), ('mv', [256, 1024]),
        ('pool_s', [NSEQ, 15, 1024]), ('hs_re', [NSEQ * 64, 64]), ('hs_im', [NSEQ * 64, 64]),
    ]:
        O[name] = dout(name, shape)

    with ExitStack() as es:
        es.enter_context(nc.allow_non_contiguous_dma(reason="small strided param loads"))
        ARENA_BYTES = 207 * 1024 + 512
        arena_t = es.enter_context(nc.sbuf_tensor("arena", [128, ARENA_BYTES // 4], F32))
        ps_t = es.enter_context(nc.psum_tensor("ps", [128, 4096], F32))
        PB = [Res('bank%d' % i, excl=True) for i in range(8)]
        st = {'bank': 0, 'ev': 0, 'ring': 0, 'big': 0}

        KB = 256
        REG = {}

        def region(name, base_kb, size_kb):
            REG[name] = {'base': int(base_kb * KB), 'top': int(base_kb * KB), 'lim': int((base_kb + size_kb) * KB), 'peak': 0}
        region('A', 0, 44)
        region('B', 44, 9)
        region('C', 53, 72)
        region('D', 125, 36)
        region('E', 161, 36)
        region('F', 197, 9)
        region('S', 71, 136.5)
        cur = {'R': 'A'}

        def use(name):
            cur['R'] = name

        def alloc(shape, dtype=F32, parts=128, R=None):
            rg = REG[R or cur['R']]
            n = 1
            for s_ in shape:
                n *= s_
            nb_ = n * (2 if dtype == BF16 else 4)
            nw = (nb_ + 3) // 4
            nw = (nw + 7) // 8 * 8
            w0 = rg['top']
            st['last_w0'] = w0
            rg['top'] += nw
            rg['peak'] = max(rg['peak'], rg['top'])
            assert rg['top'] <= rg['lim'], "region %s overflow: need %d KiB more" % (R or cur['R'], (rg['top'] - rg['lim']) // KB + 1)
            ap = arena_t[0:parts, w0:w0 + nw]
            if dtype == BF16:
                ap = ap.bitcast(BF16)
            ap = ap[:, 0:n]
            if len(shape) == 2:
                ap = ap.rearrange("p (a b) -> p a b", a=shape[0], b=shape[1])
            elif len(shape) == 3:
                ap = ap.rearrange("p (a b c) -> p a b c", a=shape[0], b=shape[1], c=shape[2])
            elif len(shape) == 4:
                ap = ap.rearrange("p (a b c d) -> p a b c d", a=shape[0], b=shape[1], c=shape[2], d=shape[3])
            return Buf(ap, Res())

        def mark(R=None):
            return (R or cur['R'], REG[R or cur['R']]['top'])

        def release(m):
            S.barrier()
            REG[m[0]]['top'] = m[1]

        def reset(R):
            S.barrier()
            REG[R]['top'] = REG[R]['base']

        def bank(i):
            return ps_t[:, i * 512:(i + 1) * 512]

        def bankb(i):
            return ps_t[:, i * 512:(i + 1) * 512].bitcast(BF16)

        def nb():
            b = st['bank'] % st.get('nbanks', 8)
            st['bank'] = (b + 1) % st.get('nbanks', 8)
            return b

        def evq():
            st['ev'] ^= 1
            return 'act' if st['ev'] else 'dve'

        def mm(out, lhsT, rhs, start, stop, rd, wr):
            S.op('pe', lambda E: E.matmul(out, lhsT, rhs, start=start, stop=stop), reads=rd, writes=wr)

        def trp(out, in_, ident, rd, wr):
            S.op('pe', lambda E: E.transpose(out, in_, ident), reads=rd, writes=wr)

        def vtt(out, a, b, op, rd, wr, eng='dve'):
            S.op(eng, lambda E: E.tensor_tensor(out, a, b, op), reads=rd, writes=wr)

        def vts(out, a, s1, s2, op0, op1, rd, wr, eng='dve'):
            S.op(eng, lambda E: E.tensor_scalar(out, a, s1, s2, op0, op1), reads=rd, writes=wr)

        def vstt(out, in0, scalar, in1, op0, op1, rd, wr, eng='dve'):
            S.op(eng, lambda E: E.scalar_tensor_tensor(out, in0, scalar, in1, op0, op1), reads=rd, writes=wr)

        def vcopy(out, a, rd, wr, eng='dve'):
            if eng == 'act':
                S.op('act', lambda E: E.copy(out, a), reads=rd, writes=wr)
            else:
                S.op(eng, lambda E: E.tensor_copy(out, a), reads=rd, writes=wr)

        def vset(out, val, wr, eng='dve'):
            S.op(eng, lambda E: E.memset(out, val), writes=wr)

        def act(out, in_, func, rd, wr, bias=None, scale=None, accum_out=None):
            kw = {}
            if bias is not None:
                kw['bias'] = bias
            if scale is not None:
                kw['scale'] = scale
            if accum_out is not None:
                kw['accum_out'] = accum_out
            S.op('act', lambda E: E.activation(out, in_, func, **kw), reads=rd, writes=wr)

        def dma(q, out, in_, rd=(), wr=()):
            S.dma(q, lambda E: E.dma_start(out=out, in_=in_), reads=rd, writes=wr)

        def tap(name, ap, rd, shape):
            if name in taps:
                t = dout('tap_' + name, shape)
                tap_out[name] = shape
                dma('sp' if ap.dtype == F32 else 'pool', t, ap, rd=rd)

        def vsadd(out, a, c, rd, wr, eng='dve'):
            S.op(eng, lambda E: E.tensor_scalar_add(out, a, c), reads=rd, writes=wr)

        def vsmul(out, a, c, rd, wr, eng='dve'):
            S.op(eng, lambda E: E.tensor_scalar_mul(out, a, c), reads=rd, writes=wr)

        def vrecip(out, a, rd, wr):
            S.op('dve', lambda E: E.reciprocal(out, a), reads=rd, writes=wr)

        def done():
            import os
            if os.environ.get('PEAKS'):
                for k_, v_ in REG.items():
                    print('region', k_, 'peak KiB', (v_['peak'] - v_['base']) / KB, 'size', (v_['lim'] - v_['base']) / KB)
            S.finish()
            S.emit(nc, es)
            return nc, tap_out

        use('A')
        identf = alloc([128])
        dma('sp', identf.ap, I['ident'], wr=[identf.res])
        identb = alloc([128], BF16)
        dma('pool', identb.ap, I['ident'], wr=[identb.res])
        onesb = alloc([128], BF16)
        vset(onesb.ap, 1.0, [onesb.res])
        gbc = alloc([D])
        small = {}

        def load_cols(name, n):
            b = alloc([n // 128])
            dma('sp', b.ap, I[name].rearrange("(c p) -> p c", p=128), wr=[b.res])
            small[name] = b
        load_cols('b_gate', 6144)
        load_cols('pool_scale', 1024)
        load_cols('ssm_D', 1024)
        load_cols('glu_b', 1024)
        icnt = alloc([4, 16])
        dma('sp', icnt.ap, bass.AP(I['icnt'].tensor, 0, [[0, 128], [16, 4], [1, 16]]), wr=[icnt.res])
        ss_list = [alloc([8]) for _ in range(4)]
        ss = ss_list[0]
        ring_all = alloc([4, 4096], BF16)
        ring_res = [Res() for _ in range(4)]

        def load_g(name):
            dma('sp', gbc.ap, bass.AP(I[name].tensor, 0, [[0, 128], [1, D]]), wr=[gbc.res])

        def wpanel(w2d, r0, K, c0, ncols, slots=(0, 1, 2, 3), key='ring'):
            KC = K // 128
            assert KC * ncols <= 4096
            si = slots[st.setdefault(key, 0) % len(slots)]
            st[key] += 1
            view = ring_all.ap[:, si, 0:KC * ncols].rearrange("p (k m) -> p k m", k=KC, m=ncols)
            src = w2d[r0:r0 + K, c0:c0 + ncols].rearrange("(kc p) m -> p kc m", p=128)
            dma('pool', view, src, wr=[ring_res[si]])
            return Buf(view, ring_res[si]), [ring_res[si]]

        def wpanel_big(w2d, r0, K, c0, ncols):
            KC = K // 128
            assert KC * ncols <= 8192
            bi = st['big'] % 2
            st['big'] += 1
            flat = ring_all.ap[:, 2 * bi:2 * bi + 2, :].rearrange("p a b -> p (a b)")
            view = flat[:, 0:KC * ncols].rearrange("p (k m) -> p k m", k=KC, m=ncols)
            src = w2d[r0:r0 + K, c0:c0 + ncols].rearrange("(kc p) m -> p kc m", p=128)
            rl = [ring_res[2 * bi], ring_res[2 * bi + 1]]
            dma('pool', view, src, wr=rl)
            return Buf(view, None), rl

        def norm_ops(s, tt, xslots, xs_b, junk, g=None):
            gb = g if g is not None else gbc
            if isinstance(s, Buf):
                xt = s
            else:
                xt = xslots[tt % len(xslots)]
                dma('sp', xt.ap, s, wr=[xt.res])
            ss = ss_list[tt % 4]
            jk = junk if junk is not None else xs_b
            vset(ss.ap[:, 0:1], 0.0, [ss.res])
            act(jk.ap, xt.ap, ACT.Square, [xt.res, ss.res], [jk.res, ss.res], accum_out=ss.ap[:, 0:1])
            vts(ss.ap[:, 1:2], ss.ap[:, 0:1], 1.0 / D, EPS, ALU.mult, ALU.add, [ss.res], [ss.res])
            act(ss.ap[:, 1:2], ss.ap[:, 1:2], ACT.Sqrt, [ss.res], [ss.res])
            vrecip(ss.ap[:, 2:3], ss.ap[:, 1:2], [ss.res], [ss.res])
            vstt(xs_b.ap, xt.ap, ss.ap[:, 2:3], gb.ap, ALU.mult, ALU.mult, [xt.res, ss.res, gb.res], [xs_b.res])

        def norm_T(xs_b, dst, c0):
            for h in range(2):
                b = nb()
                for j in range(8):
                    c = h * 8 + j
                    trp(bankb(b)[:, j * 128:(j + 1) * 128], xs_b.ap[:, c * 128:(c + 1) * 128], identb.ap,
                        [xs_b.res, identb.res], [PB[b]])
                vcopy(dst.ap[:, h * 8:(h + 1) * 8, c0:c0 + 128],
                      bankb(b).rearrange("p (j c) -> p j c", j=8, c=128), [PB[b]], [dst.res], eng=evq())

        def norm_transpose(src_fn, ntiles, dst, col0, xslots, xs_b_in, junk):
            for tt in range(ntiles):
                xs_b = xs_b_in[tt % len(xs_b_in)] if isinstance(xs_b_in, list) else xs_b_in
                norm_ops(src_fn(tt), tt, xslots, xs_b, junk)
                norm_T(xs_b, dst, col0 + tt * 128)

        def linear_fm(w2d, r0, K, c0, M, rhsT, tgroups, evac):
            KC = K // 128
            pc = min(M, (4096 // KC) // 128 * 128)
            for p0 in range(0, M, pc):
                panel, prl = wpanel(w2d, r0, K, c0 + p0, pc)
                for j in range(pc // 128):
                    mc = p0 // 128 + j
                    for (t0, n) in tgroups:
                        b = nb()
                        for kc in range(KC):
                            mm(bank(b)[:, 0:n], panel.ap[:, kc, j * 128:(j + 1) * 128], rhsT.ap[:, kc, t0:t0 + n],
                               kc == 0, kc == KC - 1, prl + [rhsT.res], [PB[b]])
                        evac(mc, t0, n, b)

        use('B')
        KT = alloc([8, 256], BF16)
        Vb = alloc([2, 1024], BF16)
        halo = alloc([8, 16])
        use('S')

        def sc():
            return alloc([64], F32, parts=64)
        maskf = alloc([128], R='B')
        dma('sp', maskf.ap, I['mask'], wr=[maskf.res])
        Pst_r = alloc([9, 64], F32, parts=64)
        Pst_i = alloc([9, 64], F32, parts=64)
        Qst_r = alloc([8, 64], F32, parts=64)
        Qst_i = alloc([8, 64], F32, parts=64)
        cst_r = alloc([8, 64], F32, parts=64)
        cst_i = alloc([8, 64], F32, parts=64)
        P_r = [Buf(Pst_r.ap[:, k, :], Pst_r.res) for k in range(9)]
        P_i = [Buf(Pst_i.ap[:, k, :], Pst_i.res) for k in range(9)]
        Q_r = [Buf(Qst_r.ap[:, 7 - k, :], Qst_r.res) for k in range(8)]
        Q_i = [Buf(Qst_i.ap[:, 7 - k, :], Qst_i.res) for k in range(8)]
        cs_r = [Buf(cst_r.ap[:, k, :], cst_r.res) for k in range(8)]
        cs_i = [Buf(cst_i.ap[:, k, :], cst_i.res) for k in range(8)]
        rho8 = sc()
        phin = sc()
        ciota = alloc([128], F32, parts=64)
        dma('sp', ciota.ap, bass.AP(I['ciota'].tensor, 0, [[0, 64], [1, 128]]), wr=[ciota.res])
        H0 = alloc([2, 64], F32, parts=64)
        HpF = alloc([2, 64], F32, parts=64)
        h0T = alloc([2, NSEQ, 64], F32, parts=64)
        sel = alloc([64, 128], BF16)
        dma('pool', sel.ap, I['sel'].rearrange("j p m -> p j m"), wr=[sel.res])
        upreT = alloc([8, NPR], BF16)
        umT = alloc([8, NT], BF16)
        m0 = mark()
        ar, ai, dtb = sc(), sc(), sc()
        tmpA = alloc([64], F32, parts=64)
        for (nm, dst) in (('A_re', ar), ('A_im', ai)):
            dma('sp', tmpA.ap, I[nm], wr=[tmpA.res])
            b = nb()
            trp(bank(b)[0:64, 0:64], tmpA.ap, identf.ap[0:64, 0:64], [tmpA.res, identf.res], [PB[b]])
            vcopy(dst.ap, bank(b)[0:64, 0:64], [PB[b]], [dst.res])
        dma('sp', dtb.ap, bass.AP(I['log_dt'].tensor, 0, [[0, 64], [1, 64]]), wr=[dtb.res])
        act(dtb.ap, dtb.ap, ACT.Exp, [dtb.res], [dtb.res])
        xsl = [alloc([D]) for _ in range(2)]
        xsb4 = [alloc([D], BF16) for _ in range(4)]
        xg = alloc([16, 512], BF16)
        xhalo = alloc([16, 128], BF16)
        gmem = alloc([D], R='C')
        mnT = alloc([16, 256], BF16, R='C')
        dma('sp', gmem.ap, bass.AP(I['mem_norm_g'].tensor, 0, [[0, 128], [1, D]]), wr=[gmem.res])
        for tt in range(2):
            norm_ops(I['mem'][tt * 128:(tt + 1) * 128, :], tt, xsl, xsb4[tt], None, g=gmem)
            norm_T(xsb4[tt], mnT, tt * 128)
        osb = alloc([512], R='C')
        load_g('norm1_g')
        glist = [(I['xp'], upreT, 0, 512), (I['xp'], upreT, 512, 512)] + [(I['xm'], umT, t0, n) for (t0, n) in TG]
        tile_ctr = [0]

        def p2_norm_ops(i, k):
            src, dstT, t0, n = glist[i]
            norm_ops(src[t0 + k * 128:t0 + (k + 1) * 128, :], tile_ctr[0], xsl, xsb4[k], None)
            tile_ctr[0] += 1

        for k in range(glist[0][3] // 128):
            p2_norm_ops(0, k)
        for (wn, on, isk) in (('xa_wk', 'mk', True), ('xa_wv', 'mv', False)):
            for half in range(2):
                panel, prl = wpanel_big(I[wn], 0, D, half * 512, 512)
                if isk:
                    for j in range(4):
                        mc = half * 4 + j
                        b = nb()
                        for kc in range(16):
                            mm(bank(b)[:, 0:256], panel.ap[:, kc, j * 128:(j + 1) * 128], mnT.ap[:, kc, :], kc == 0, kc == 15,
                               prl + [mnT.res], [PB[b]])
                        vcopy(KT.ap[:, mc, :], bank(b)[:, 0:256], [PB[b]], [KT.res], eng='act')
                for mt in range(2):
                    b = nb()
                    for kc in range(16):
                        mm(bank(b), mnT.ap[:, kc, mt * 128:(mt + 1) * 128], panel.ap[:, kc, :], kc == 0, kc == 15,
                           prl + [mnT.res], [PB[b]])
                    vcopy(osb.ap, bank(b), [PB[b]], [osb.res], eng='act')
                    if not isk:
                        vcopy(Vb.ap[:, mt, half * 512:(half + 1) * 512], bank(b), [PB[b]], [Vb.res], eng='act')
                    dma('act', O[on][mt * 128:(mt + 1) * 128, half * 512:(half + 1) * 512], osb.ap, rd=[osb.res])
        wres = [wpanel(I['w_in'], 0, D, 1024 + p_ * 256, 256, slots=(p_,), key='p2r%d' % p_) for p_ in range(4)]
        def p2_T(i):
            src, dstT, t0, n = glist[i]
            for k in range(n // 128):
                norm_T(xsb4[k], xg, k * 128)
            if i == 1:
                vcopy(xhalo.ap, xg.ap[:, :, 384:512], [xg.res], [xhalo.res], eng='act')

        def p2_unit(i, mc):
            src, dstT, t0, n = glist[i]
            panel, prl = wres[mc // 2]
            j = mc % 2
            b = nb()
            for kc in range(16):
                mm(bank(b)[:, 0:n], panel.ap[:, kc, j * 128:(j + 1) * 128], xg.ap[:, kc, 0:n], kc == 0, kc == 15,
                   prl + [xg.res], [PB[b]])
            vcopy(dstT.ap[:, mc, t0:t0 + n], bank(b)[:, 0:n], [PB[b]], [dstT.res], eng=evq())
        lr, th, Em, t1, t2, t3 = sc(), sc(), sc(), sc(), sc(), sc()
        vtt(lr.ap, dtb.ap, ar.ap, ALU.mult, [dtb.res, ar.res], [lr.res])
        vtt(th.ap, dtb.ap, ai.ap, ALU.mult, [dtb.res, ai.res], [th.res])
        act(Em.ap, lr.ap, ACT.Exp, [lr.res], [Em.res])

        def sin_of(dst, src, shift):
            vsadd(t1.ap, src.ap, shift, [src.res], [t1.res])
            vcopy(t2.ap, t1.ap, [t1.res], [t2.res])
            for kk in (1, 3, 5, 7, 9):
                vts(t3.ap, t1.ap, kk * PI, -2 * PI, ALU.is_ge, ALU.mult, [t1.res], [t3.res])
                vtt(t2.ap, t2.ap, t3.ap, ALU.add, [t2.res, t3.res], [t2.res])
            vts(t2.ap, t2.ap, -3.1415925, 3.1415925, ALU.max, ALU.min, [t2.res], [t2.res])
            act(dst.ap, t2.ap, ACT.Sin, [t2.res], [dst.res])
        sn, cs_ = sc(), sc()
        sin_of(sn, th, 0.0)
        sin_of(cs_, th, PI / 2)
        vset(P_r[0].ap, 1.0, [P_r[0].res])
        vset(P_i[0].ap, 0.0, [P_i[0].res])
        vset(Q_r[0].ap, 1.0, [Q_r[0].res])
        vset(Q_i[0].ap, 0.0, [Q_i[0].res])
        vtt(P_r[1].ap, Em.ap, cs_.ap, ALU.mult, [Em.res, cs_.res], [P_r[1].res])
        vtt(P_i[1].ap, Em.ap, sn.ap, ALU.mult, [Em.res, sn.res], [P_i[1].res])

        def cmul(o_r, o_i, a_r, a_i, b_r, b_i, rd, wr_r, wr_i, ta, tb, conj_b=False, eng='dve'):
            vtt(ta.ap, a_r, b_r, ALU.mult, rd, [ta.res], eng)
            vtt(tb.ap, a_i, b_i, ALU.mult, rd, [tb.res], eng)
            vtt(o_r, ta.ap, tb.ap, ALU.add if conj_b else ALU.subtract, [ta.res, tb.res], wr_r, eng)
            vtt(ta.ap, a_i, b_r, ALU.mult, rd, [ta.res], eng)
            vtt(tb.ap, a_r, b_i, ALU.mult, rd, [tb.res], eng)
            vtt(o_i, ta.ap, tb.ap, ALU.subtract if conj_b else ALU.add, [ta.res, tb.res], wr_i, eng)

        def cmul_s(o_r, o_i, a_r, a_i, b_r, b_i, conj_b=False):
            cmul(o_r.ap, o_i.ap, a_r.ap, a_i.ap, b_r.ap, b_i.ap, [a_r.res, a_i.res, b_r.res, b_i.res],
                 [o_r.res], [o_i.res], t1, t2, conj_b)
        for k in range(2, 9):
            cmul_s(P_r[k], P_i[k], P_r[k - 1], P_i[k - 1], P_r[1], P_i[1])
        den, fr, fi, xr = sc(), sc(), sc(), sc()
        vtt(den.ap, ar.ap, ar.ap, ALU.mult, [ar.res], [den.res])
        vtt(t3.ap, ai.ap, ai.ap, ALU.mult, [ai.res], [t3.res])
        vtt(den.ap, den.ap, t3.ap, ALU.add, [den.res, t3.res], [den.res])
        vrecip(den.ap, den.ap, [den.res], [den.res])
        vsadd(xr.ap, P_r[1].ap, -1.0, [P_r[1].res], [xr.res])
        cmul_s(fr, fi, xr, P_i[1], ar, ai, conj_b=True)
        vtt(fr.ap, fr.ap, den.ap, ALU.mult, [fr.res, den.res], [fr.res])
        vtt(fi.ap, fi.ap, den.ap, ALU.mult, [fi.res, den.res], [fi.res])
        einv, irho8 = sc(), sc()
        act(einv.ap, lr.ap, ACT.Exp, [lr.res], [einv.res], scale=-2.0)
        vtt(Q_r[1].ap, P_r[1].ap, einv.ap, ALU.mult, [P_r[1].res, einv.res], [Q_r[1].res])
        vstt(Q_i[1].ap, P_i[1].ap, -1.0, einv.ap, ALU.mult, ALU.mult, [P_i[1].res, einv.res], [Q_i[1].res])
        for k in range(2, 8):
            cmul_s(Q_r[k], Q_i[k], Q_r[k - 1], Q_i[k - 1], Q_r[1], Q_i[1])
        for s_ in range(8):
            cmul_s(cs_r[s_], cs_i[s_], P_r[7 - s_], P_i[7 - s_], fr, fi)
        act(rho8.ap, lr.ap, ACT.Exp, [lr.res], [rho8.res], scale=8.0)
        I32 = mybir.dt.int32
        vsmul(t1.ap, th.ap, 4.0 / PI, [th.res], [t1.res])
        vcopy(t2.ap.bitcast(I32), t1.ap, [t1.res], [t2.res])
        vcopy(t3.ap, t2.ap.bitcast(I32), [t2.res], [t3.res])
        vtt(phin.ap, t1.ap, t3.ap, ALU.subtract, [t1.res, t3.res], [phin.res])
        tap('P8r', P_r[8].ap, [P_r[8].res], [64, 64])
        tap('P8i', P_i[8].ap, [P_i[8].res], [64, 64])
        tap('fr', fr.ap, [fr.res], [64, 64])
        tap('fi', fi.ap, [fi.res], [64, 64])
        tmpS = alloc([64])
        for ri, nm in ((0, 'sre'), (1, 'sim')):
            for j in range(8):
                dma('sp', tmpS.ap, I[nm][j * 128:(j + 1) * 128, :], wr=[tmpS.res])
                b = nb()
                trp(bank(b)[0:64, 0:128], tmpS.ap, identf.ap, [tmpS.res, identf.res], [PB[b]])
                vcopy(h0T.ap[:, ri, 2 * j:2 * j + 2, :], bank(b)[0:64, 0:128].rearrange("p (s g) -> p s g", s=2, g=64),
                      [PB[b]], [h0T.res], eng='act')

        p2_T(0)
        for i in range(len(glist)):
            nxt = glist[i + 1][3] // 128 if i + 1 < len(glist) else 0
            for mc in range(8):
                if mc % 2 == 0 and mc // 2 < nxt:
                    p2_norm_ops(i + 1, mc // 2)
                p2_unit(i, mc)
            if nxt:
                p2_T(i + 1)

        def evh(mc, tt0, nn, b):
            vcopy(halo.ap[:, mc, :], bank(b)[:, 112:128], [PB[b]], [halo.res], eng=evq())
        linear_fm(I['w_in'], 0, D, 0, 1024, xhalo, [(0, 128)], evh)
        release(m0)
        REG['C']['top'] = REG['C']['base']
        tap('umT', umT.ap[:, 0, :], [umT.res], [128, NT])
        tap('upreT', upreT.ap[:, 0, :], [upreT.res], [128, NPR])
        if stop <= 2:
            return done()

        zg = alloc([8, NT], BF16, R='C')
        G8 = 8
        Brb = alloc([G8, 16], F32, parts=64)
        Bib = alloc([G8, 16], F32, parts=64)
        Crb = alloc([G8, 16], F32, parts=64)
        Cib = alloc([G8, 16], F32, parts=64)
        tmpC = alloc([64], R='A')
        bufA = alloc([2, G8, 128], F32)
        bufB = alloc([2, G8, 128], F32)
        tq1 = alloc([G8, 128], F32, parts=64)
        tq2 = alloc([G8, 128], F32, parts=64)
        Wa = alloc([G8, 2, 64], BF16)
        w0_wa = st['last_w0']
        Wb = alloc([G8, 128], BF16)
        Wc = alloc([G8, 2, 128], BF16, parts=64)
        ytmp_ap = None
        YR = None
        Rr = alloc([G8, 128], F32, parts=64)
        Ri = alloc([G8, 128], F32, parts=64)
        rhot = alloc([G8, 128], F32, parts=64)
        um = [alloc([2, G8], F32, parts=64, R='A') for _ in range(2)]
        U = alloc([G8, 144], BF16)
        Sb = alloc([2, G8, 144], F32, parts=64)
        Hprev = alloc([2, G8, 144], BF16, parts=64)
        Ysb = U
        ts8a = Buf(tq1.ap[:, :, 0:16], tq1.res)
        ts8b = Buf(tq2.ap[:, :, 0:16], tq2.res)
        ts8c = alloc([2, G8, 16], F32, parts=64, R='A')
        Dcol = small['ssm_D']
        bAf = bufA.ap.rearrange("p r g c -> p (r g c)")
        bBf = bufB.ap.rearrange("p r g c -> p (r g c)")
        gt = Buf(bAf[:, 0:NT], bufA.res)
        sg = gt
        ytmp_ap = bBf[:, 0:NT]
        YR = [bufB.res]
        A64 = Buf(bufA.ap[0:64], bufA.res)
        B64 = Buf(bufB.ap[0:64], bufB.res)

        def ring64(si):
            return ring_all.ap[0:64, si, :]
        WaTp = Buf(ring64(0).bitcast(F32).rearrange("p (r g c) -> p r g c", r=2, g=G8, c=128), Res())
        Wcppp = Buf(ring64(1).bitcast(F32).rearrange("p (r g c) -> p r g c", r=2, g=G8, c=128), Res())
        Wc2 = [Buf(ring64(2)[:, hh * 2048:(hh + 1) * 2048].rearrange("p (g r c) -> p g r c", g=G8, r=2, c=128), Res()) for hh in range(2)]
        s3 = ring64(3).bitcast(F32)
        tq1p = Buf(s3[:, 0:1024].rearrange("p (g c) -> p g c", g=G8, c=128), Res())
        tq2p = Buf(s3[:, 1024:2048].rearrange("p (g c) -> p g c", g=G8, c=128), Res())
        negone = alloc([1], F32, parts=64)
        vset(negone.ap, -1.0, [negone.res], eng='pool')
        v4 = lambda ap3: ap3.rearrange("p g (s q) -> p g s q", s=8, q=16)
        St, Ht, Hun = A64, B64, A64
        qa = Buf(tq1.ap, tq1.res)
        qb = Buf(tq2.ap, tq2.res)
        GRP = [(0, 3), (3, 3), (6, 2)]

        def ssm_loads(ch):
            g0 = ch * G8
            dma('sp', Brb.ap, I['B_re'][g0:g0 + G8].rearrange("g n q -> n g q"), wr=[Brb.res])
            dma('sp', Bib.ap, I['B_im'][g0:g0 + G8].rearrange("g n q -> n g q"), wr=[Bib.res])
            for (nm, dst) in (('C_re', Crb), ('C_im', Cib)):
                dma('sp', tmpC.ap, I[nm][ch * 128:(ch + 1) * 128, :], wr=[tmpC.res])
                b = nb()
                trp(bank(b)[0:64, 0:128], tmpC.ap, identf.ap, [tmpC.res, identf.res], [PB[b]])
                vcopy(dst.ap.rearrange("p g q -> p (g q)"), bank(b)[0:64, 0:128], [PB[b]], [dst.res], eng='act')

        def ssm_matrices(ch, part=3):
            g0 = ch * G8
            Wc = Wc2[ch % 2]

            def gsq(buf, off_elems):
                a = buf.ap
                return bass.AP(a.tensor, a.offset + off_elems + g0, [list(a.ap[0]), [1, G8], [64, 8], [0, 16]])

            def gq_s(buf):
                a = buf.ap
                return bass.AP(a.tensor, a.offset, [list(a.ap[0]), [16, G8], [0, 8], [1, 16]])
            q4a = Buf(v4(tq1p.ap), tq1p.res)
            q4b = Buf(v4(tq2p.ap), tq2p.res)
            if part & 1:
                cmul(v4(WaTp.ap[:, 0]), v4(WaTp.ap[:, 1]), gsq(cst_r, 0), gsq(cst_i, 0), gq_s(Brb), gq_s(Bib),
                     [cst_r.res, cst_i.res, Brb.res, Bib.res], [WaTp.res], [WaTp.res], q4a, q4b, eng='pool')
            todo = ([(Wcppp, Qst_r, Qst_i, False)] if part & 1 else []) + ([(Wc, Pst_r, Pst_i, True)] if part & 2 else [])
            for (dstb, pw_r, pw_i, isWc) in todo:
                off = 64 if isWc else 0
                pr, pi_ = gsq(pw_r, off), gsq(pw_i, off)
                cr, ci = gq_s(Crb), gq_s(Cib)
                rdl = [pw_r.res, pw_i.res, Crb.res, Cib.res]
                if isWc:
                    o_re, o_im = v4(dstb.ap[:, :, 0, :]), v4(dstb.ap[:, :, 1, :])
                else:
                    o_re, o_im = v4(dstb.ap[:, 0]), v4(dstb.ap[:, 1])
                vtt(q4a.ap, pr, cr, ALU.mult, rdl, [q4a.res], 'pool')
                vtt(q4b.ap, pi_, ci, ALU.mult, rdl, [q4b.res], 'pool')
                vtt(o_re, q4a.ap, q4b.ap, ALU.subtract, [q4a.res, q4b.res], [dstb.res], 'pool')
                vtt(q4a.ap, pi_, cr, ALU.mult, rdl, [q4a.res], 'pool')
                vtt(q4b.ap, pr, ci, ALU.mult, rdl, [q4b.res], 'pool')
                vtt(q4a.ap, q4a.ap, q4b.ap, ALU.add, [q4a.res, q4b.res], [q4a.res], 'pool')
                na = negone.ap
                nbc = bass.AP(na.tensor, na.offset, [list(na.ap[0]), [0, G8], [0, 8], [0, 16]])
                vtt(o_im, q4a.ap, nbc, ALU.mult, [q4a.res, negone.res], [dstb.res], 'pool')

        def ssm_tables(ch):
            g0 = ch * G8
            gs = slice(g0, g0 + G8)
            I32 = mybir.dt.int32
            SC = 2.0 * PI * 0.999999
            T_, Ti_ = tq1, tq2
            vtt(T_.ap, bc(phin.ap[:, gs], 128), bc_mid(ciota.ap, G8), ALU.mult, [phin.res, ciota.res], [T_.res])
            vcopy(Ti_.ap.bitcast(I32), T_.ap, [T_.res], [Ti_.res])
            vcopy(Ri.ap, Ti_.ap.bitcast(I32), [Ti_.res], [Ri.res])
            vtt(Ri.ap, T_.ap, Ri.ap, ALU.subtract, [T_.res, Ri.res], [Ri.res])
            act(Ri.ap, Ri.ap, ACT.Sin, [Ri.res], [Ri.res], scale=-SC)
            vsadd(T_.ap, T_.ap, 0.25, [T_.res], [T_.res])
            vcopy(Ti_.ap.bitcast(I32), T_.ap, [T_.res], [Ti_.res])
            vcopy(Rr.ap, Ti_.ap.bitcast(I32), [Ti_.res], [Rr.res])
            vtt(Rr.ap, T_.ap, Rr.ap, ALU.subtract, [T_.res, Rr.res], [Rr.res])
            act(Rr.ap, Rr.ap, ACT.Sin, [Rr.res], [Rr.res], scale=SC)
            vcopy(rhot.ap, bc(rho8.ap[:, gs], 128), [rho8.res], [rhot.res], eng='pool')
            vset(rhot.ap[:, :, 0:1], 0.0, [rhot.res], eng='pool')

        def ssm_wa_wb(ch):
            for gq in range(2):
                b = nb()
                for gg in range(4):
                    g = gq * 4 + gg
                    for ri in range(2):
                        trp(bank(b)[:, gg * 128 + ri * 64:gg * 128 + (ri + 1) * 64], WaTp.ap[:, ri, g, :], identf.ap[0:64, 0:64],
                            [WaTp.res, identf.res], [PB[b]])
                vcopy(Wa.ap[:, gq * 4:gq * 4 + 4].rearrange("p g r n -> p (g r n)"), bank(b), [PB[b]], [Wa.res], eng='act')
            for gq in range(2):
                b = nb()
                for gg in range(4):
                    g = gq * 4 + gg
                    o = bank(b)[:, gg * 128:(gg + 1) * 128]
                    mm(o, WaTp.ap[:, 0, g, :], Wcppp.ap[:, 0, g, :], True, False, [WaTp.res, Wcppp.res], [PB[b]])
                    mm(o, WaTp.ap[:, 1, g, :], Wcppp.ap[:, 1, g, :], False, True, [WaTp.res, Wcppp.res], [PB[b]])
                vtt(Wb.ap[:, gq * 4:gq * 4 + 4, :], bank(b).rearrange("p (g m) -> p g m", g=4, m=128), bc_mid(maskf.ap, 4), ALU.mult,
                    [PB[b], maskf.res], [Wb.res])

        def rotate_scan():
            cmul(St.ap[:, 0], St.ap[:, 1], Sb.ap[:, 0, :, 0:128], Sb.ap[:, 1, :, 0:128], Rr.ap, Ri.ap,
                 [Sb.res, Rr.res, Ri.res], [St.res], [St.res], qa, qb)
            for ri in range(2):
                S.op('dve', lambda E, ri=ri: E.tensor_tensor_scan(
                    Ht.ap[:, ri].rearrange("p g c -> p (g c)"), rhot.ap.rearrange("p g c -> p (g c)"),
                    St.ap[:, ri].rearrange("p g c -> p (g c)"), 0.0, ALU.mult, ALU.add),
                    reads=[St.res, rhot.res], writes=[Ht.res])

        def shuffle_in(ch, uT, ncol):
            for (ga, gn) in GRP:
                b = nb()
                for gg in range(gn):
                    g = ga + gg
                    for s_ in range(8):
                        mm(bank(b)[:, gg * ncol:(gg + 1) * ncol], sel.ap[:, g * 8 + s_, :], uT.ap[:, ch, s_:8 * ncol:8], s_ == 0, s_ == 7,
                           [sel.res, uT.res], [PB[b]])
                vcopy(U.ap[:, ga:ga + gn, 0:ncol], bank(b)[:, 0:gn * ncol].rearrange("p (g c) -> p g c", g=gn, c=ncol),
                      [PB[b]], [U.res], eng='act')

        def map_a(ncol):
            for ri in range(2):
                for (ga, gn) in GRP:
                    b = nb()
                    for gg in range(gn):
                        g = ga + gg
                        mm(bank(b)[0:64, gg * ncol:(gg + 1) * ncol], Wa.ap[:, g, ri, :], U.ap[:, g, 0:ncol], True, True, [Wa.res, U.res], [PB[b]])
                    vcopy(Sb.ap[:, ri, ga:ga + gn, 0:ncol], bank(b)[0:64, 0:gn * ncol].rearrange("p (g c) -> p g c", g=gn, c=ncol),
                          [PB[b]], [Sb.res] + ([SbS_res] if ncol > 128 else []), eng='act')

        qs1 = Buf(ts8a.ap[:, :, 0], ts8a.res)
        qs2 = Buf(ts8b.ap[:, :, 0], ts8b.res)
        p8r, p8i = P_r[8], P_i[8]
        CC = alloc([2, 64], F32, parts=64)
        SbS_res = Res()
        h0T_ch = [Res() for _ in range(8)]
        YB = [5, 6, 7]

        def ssm_pre(ch):
            gs = slice(ch * G8, ch * G8 + G8)
            shuffle_in(ch, upreT, 128)
            map_a(128)

        def ssm_pre_dve(ch):
            gs = slice(ch * G8, ch * G8 + G8)
            rotate_scan()
            cmul(H0.ap[:, 0, gs], H0.ap[:, 1, gs], Ht.ap[:, 0, :, 127], Ht.ap[:, 1, :, 127], Rr.ap[:, :, 127], Ri.ap[:, :, 127],
                 [Ht.res, Rr.res, Ri.res], [H0.res], [H0.res], qs1, qs2, conj_b=True)
            ca = Buf(tq1p.ap[:, :, 16], tq1p.res)
            cb = Buf(tq2p.ap[:, :, 16], tq2p.res)
            cmul(CC.ap[:, 0, gs], CC.ap[:, 1, gs], p8r.ap[:, gs], p8i.ap[:, gs], H0.ap[:, 0, gs], H0.ap[:, 1, gs],
                 [p8r.res, p8i.res, H0.res], [CC.res], [CC.res], ca, cb, eng='pool')

        def ssm_main(ch):
            g0 = ch * G8
            gs = slice(g0, g0 + G8)
            Wc = Wc2[ch % 2]
            shuffle_in(ch, umT, 144)
            map_a(144)
            for bi, (ga, gn) in enumerate(GRP):
                b = YB[bi]
                for gg in range(gn):
                    g = ga + gg
                    mm(bank(b)[:, gg * 144:(gg + 1) * 144], Wb.ap[:, g, :], U.ap[:, g, :], gg == 0, False, [Wb.res, U.res], [PB[b]])
            vcopy(Hprev.ap[:, :, :, 128:144], h0T.ap[:, :, :, gs].rearrange("p r s g -> p r g s"), [h0T.res], [Hprev.res, h0T_ch[ch]], eng='act')
            h0v_r = h0T.ap[:, 0, :, gs].rearrange("p s g -> p g s")
            h0v_i = h0T.ap[:, 1, :, gs].rearrange("p s g -> p g s")
            tsa = Buf(tq1p.ap[:, :, 0:16], tq1p.res)
            tsb = Buf(tq2p.ap[:, :, 0:16], tq2p.res)
            cmul(ts8c.ap[:, 0], ts8c.ap[:, 1], bc(p8r.ap[:, gs], 16), bc(p8i.ap[:, gs], 16), h0v_r, h0v_i,
                 [p8r.res, p8i.res, h0T_ch[ch]], [ts8c.res], [ts8c.res], tsa, tsb, eng='pool')
            vtt(h0v_r, ts8c.ap[:, 0], Sb.ap[:, 0, :, 128:144], ALU.add, [ts8c.res, SbS_res], [h0T_ch[ch]], 'pool')
            vtt(h0v_i, ts8c.ap[:, 1], Sb.ap[:, 1, :, 128:144], ALU.add, [ts8c.res, SbS_res], [h0T_ch[ch]], 'pool')
            vtt(Sb.ap[:, :, :, 0], Sb.ap[:, :, :, 0], CC.ap[:, :, gs], ALU.add, [Sb.res, CC.res], [Sb.res])
            rotate_scan()
            cmul(Hun.ap[:, 0], Hun.ap[:, 1], Ht.ap[:, 0], Ht.ap[:, 1], Rr.ap, Ri.ap,
                 [Ht.res, Rr.res, Ri.res], [Hun.res], [Hun.res], qa, qb, conj_b=True)
            vcopy(Hprev.ap[:, :, :, 1:128], Hun.ap[:, :, :, 0:127], [Hun.res], [Hprev.res], eng='act')
            vcopy(Hprev.ap[:, :, :, 0], H0.ap[:, :, gs], [H0.res], [Hprev.res], eng='act')
            vcopy(HpF.ap[:, :, gs], Hun.ap[:, :, :, 127], [Hun.res], [HpF.res], eng='act')
            if ch < 7:
                ssm_wa_wb(ch + 1)
                if ch < 6:
                    ssm_loads(ch + 2)
                    ssm_matrices(ch + 2, part=1)
                ssm_pre(ch + 1)
            for bi, (ga, gn) in enumerate(GRP):
                b = YB[bi]
                for gg in range(gn):
                    g = ga + gg
                    o = bank(b)[:, gg * 144:(gg + 1) * 144]
                    mm(o, Wc.ap[:, g, 0, :], Hprev.ap[:, 0, g, :], False, False, [Wc.res, Hprev.res], [PB[b]])
                    mm(o, Wc.ap[:, g, 1, :], Hprev.ap[:, 1, g, :], False, True, [Wc.res, Hprev.res], [PB[b]])
                vcopy(Ysb.ap[:, ga:ga + gn, :], bank(b)[:, 0:gn * 144].rearrange("p (g c) -> p g c", g=gn, c=144), [PB[b]], [Ysb.res], eng='act')
            if ch < 6:
                ssm_matrices(ch + 2, part=2)
            if ch < 7:
                ssm_tables(ch + 1)
                ssm_pre_dve(ch + 1)
            ua = umT.ap[:, ch, :]
            for (sa, sn) in GRP:
                b = nb()
                for sj in range(sn):
                    s_ = sa + sj
                    for g in range(G8):
                        mm(bank(b)[:, sj * 144:(sj + 1) * 144], sel.ap[:, s_ * 8 + g, :], Ysb.ap[:, g, :], g == 0, g == G8 - 1, [sel.res, Ysb.res], [PB[b]])
                yv = bass.AP(ytmp_ap.tensor, ytmp_ap.offset + sa, [list(ytmp_ap.ap[0]), [1, sn], [8, 144]])
                uv = bass.AP(ua.tensor, ua.offset + sa, [list(ua.ap[0]), [1, sn], [8, 144]])
                vstt(yv, uv, Dcol.ap[:, ch:ch + 1], bank(b)[:, 0:sn * 144].rearrange("p (s c) -> p s c", s=sn, c=144), ALU.mult, ALU.add,
                     [umT.res, Dcol.res, PB[b]], YR)
            if ch == 0:
                tap('y0', ytmp_ap, YR, [128, NT])
            vtt(gt.ap, ytmp_ap, ytmp_ap, ALU.mult, YR, [gt.res])
            vts(gt.ap, gt.ap, 0.044715, 1.0, ALU.mult, ALU.add, [gt.res], [gt.res])
            vtt(gt.ap, gt.ap, ytmp_ap, ALU.mult, [gt.res] + YR, [gt.res])
            act(sg.ap, gt.ap, ACT.Sigmoid, [gt.res], [sg.res], scale=1.5957691216057308)
            vtt(zg.ap[:, ch, :], ytmp_ap, sg.ap, ALU.mult, YR + [sg.res], [zg.res])

        st['nbanks'] = 5
        st['bank'] = 0
        ssm_loads(0)
        ssm_matrices(0)
        ssm_tables(0)
        ssm_wa_wb(0)
        ssm_loads(1)
        ssm_matrices(1)
        ssm_pre(0)
        ssm_pre_dve(0)
        for ch in range(8):
            ssm_main(ch)
        st['nbanks'] = 8

        tap('HpF', HpF.ap.rearrange("p r g -> p (r g)"), [HpF.res], [64, 128])
        tap('H0', H0.ap.rearrange("p r g -> p (r g)"), [H0.res], [64, 128])
        tap('h0T', h0T.ap[:, 0, 0, :], [h0T.res], [64, 64])
        stg = [Buf(bAf[:, k * 64:(k + 1) * 64], Res()) for k in range(18)] + [Buf(bBf[:, k * 64:(k + 1) * 64], Res()) for k in range(18)]
        si_ = 0
        S.barrier()
        for ri, on in ((0, 'hp_re'), (1, 'hp_im')):
            b = nb()
            trp(bank(b)[0:64, 0:64], HpF.ap[:, ri, :], identf.ap[0:64, 0:64], [HpF.res, identf.res], [PB[b]])
            o_ = stg[si_]; si_ += 1
            vcopy(o_.ap[0:64, :], bank(b)[0:64, 0:64], [PB[b]], [o_.res], eng=evq())
            dma('sp', O[on], o_.ap[0:64, :], rd=[o_.res])
        for ri, on in ((0, 'hs_re'), (1, 'hs_im')):
            for j in range(8):
                b = nb()
                trp(bank(b)[:, 0:64], h0T.ap[:, ri, 2 * j:2 * j + 2, :].rearrange("p s g -> p (s g)"), identf.ap[0:64, 0:64],
                    [h0T.res, identf.res], [PB[b]])
                o_ = stg[si_]; si_ += 1
                vcopy(o_.ap, bank(b)[:, 0:64], [PB[b]], [o_.res], eng=evq())
                dma('sp', O[on][j * 128:(j + 1) * 128, :], o_.ap, rd=[o_.res])
        tap('zg', zg.ap[:, 0, :], [zg.res], [128, NT])
        reset('S')
        if stop <= 3:
            return done()

        xnT = alloc([16, NT], BF16, R='E')
        merged = alloc([16, NT], BF16, R='D')
        gsbs = [alloc([512], R='F') for _ in range(3)]
        gtmp = alloc([512], R='F')
        rtmp = alloc([512], BF16, R='F')
        gctr = [0]

        def next_gsb():
            gctr[0] += 1
            return gsbs[gctr[0] % 3]
        use('C')
        zg2 = alloc([8, NT], BF16)
        glub = small['glu_b']
        m4 = mark()
        xsl = [alloc([D]) for _ in range(2)]
        xsb3 = [alloc([D], BF16) for _ in range(3)]
        glu_panels = {}

        def glu_unit(ui):
            mc, gi_ = ui // 3, ui % 3
            t0, n = TG[gi_]
            pk = mc // 4
            if pk not in glu_panels:
                glu_panels[pk] = wpanel(I['glu_w'], 0, 1024, pk * 512, 512)
            panel, prl = glu_panels[pk]
            j = mc % 4
            b = nb()
            for kc in range(8):
                mm(bank(b)[:, 0:n], panel.ap[:, kc, j * 128:(j + 1) * 128], zg.ap[:, kc, t0:t0 + n], kc == 0, kc == 7,
                   prl + [zg.res], [PB[b]])
            gsb = next_gsb()
            act(gsb.ap[:, 0:n], bank(b)[:, 0:n], ACT.Sigmoid, [PB[b], glub.res], [gsb.res], bias=glub.ap[:, mc:mc + 1])
            vtt(zg2.ap[:, mc, t0:t0 + n], gsb.ap[:, 0:n], zg.ap[:, mc, t0:t0 + n], ALU.mult, [gsb.res, zg.res], [zg2.res])
        ui = 0
        for k in range(9 + 1):
            if k < 9:
                norm_ops(I['xm'][k * 128:(k + 1) * 128, :], k, xsl, xsb3[k % 3], None)
            for _ in range(3 if k < 6 else 2):
                if ui < 24:
                    glu_unit(ui)
                    ui += 1
            if k >= 1:
                norm_T(xsb3[(k - 1) % 3], xnT, (k - 1) * 128)
        while ui < 24:
            glu_unit(ui)
            ui += 1
        release(m4)
        bgate = small['b_gate']

        def gated_merge(w2d, K, rhsT, gi, first):
            KCg = 16
            KCb = K // 128
            gpc = 256
            bpc = min(D, (4096 // KCb) // 128 * 128)
            gpanel = bpanel = None
            for mc in range(16):
                if (mc * 128) % gpc == 0:
                    gpanel, grl = wpanel(I['w_in'], 0, D, 3072 + gi * D + mc * 128, gpc, slots=(0, 1), key='rg')
                if (mc * 128) % bpc == 0:
                    bpanel, brl = wpanel(w2d, 0, K, mc * 128, bpc, slots=(2, 3), key='rb')
                jg = (mc * 128 % gpc) // 128
                jb = (mc * 128 % bpc) // 128
                for (t0, n) in TG:
                    b1 = nb()
                    for kc in range(KCg):
                        mm(bank(b1)[:, 0:n], gpanel.ap[:, kc, jg * 128:(jg + 1) * 128], xnT.ap[:, kc, t0:t0 + n], kc == 0, kc == KCg - 1,
                           grl + [xnT.res], [PB[b1]])
                    gsb = next_gsb()
                    act(gsb.ap[:, 0:n], bank(b1)[:, 0:n], ACT.Sigmoid, [PB[b1], bgate.res], [gsb.res],
                        bias=bgate.ap[:, gi * 16 + mc:gi * 16 + mc + 1])
                    b2 = nb()
                    for kc in range(KCb):
                        mm(bank(b2)[:, 0:n], bpanel.ap[:, kc, jb * 128:(jb + 1) * 128], rhsT.ap[:, kc, t0:t0 + n], kc == 0, kc == KCb - 1,
                           brl + [rhsT.res], [PB[b2]])
                    if first:
                        vtt(merged.ap[:, mc, t0:t0 + n], gsb.ap[:, 0:n], bank(b2)[:, 0:n], ALU.mult, [gsb.res, PB[b2]], [merged.res])
                    else:
                        vtt(gtmp.ap[:, 0:n], gsb.ap[:, 0:n], bank(b2)[:, 0:n], ALU.mult, [gsb.res, PB[b2]], [gtmp.res])
                        vtt(merged.ap[:, mc, t0:t0 + n], merged.ap[:, mc, t0:t0 + n], gtmp.ap[:, 0:n], ALU.add,
                            [merged.res, gtmp.res], [merged.res])

        gated_merge(I['ssm_proj'], 1024, zg2, 1, True)
        tap('merged1', merged.ap[:, 0, :], [merged.res], [128, NT])
        reset('C')
        if stop <= 4:
            return done()

        HL = 16
        upP = [alloc([HL + NPR]) for _ in range(2)]
        upS = [alloc([NSEQ, HL + 8]) for _ in range(2)]
        wsP = [alloc([HL + NPR]) for _ in range(2)]
        wsS = [alloc([NSEQ, HL + 8]) for _ in range(2)]
        pooled = alloc([2, NT], BF16)
        mixed = alloc([8, NT], BF16)
        spT = alloc([8, NSEQ, 15])
        usc = [alloc([128]) for _ in range(2)]
        sp_t = alloc([1024])
        po = alloc([1024])
        po2 = alloc([1024])
        pscale = small['pool_scale']
        for half in range(2):
            dma('sp', sp_t.ap[0:120, :], I['spool'][half * 120:(half + 1) * 120, :], wr=[sp_t.res])
            for c in range(8):
                b = nb()
                trp(bank(b)[:, 0:120], sp_t.ap[0:120, c * 128:(c + 1) * 128], identf.ap[0:120, 0:120], [sp_t.res, identf.res], [PB[b]])
                vcopy(spT.ap[:, c, half * 8:(half + 1) * 8, :], bank(b)[:, 0:120].rearrange("p (s r) -> p s r", s=8, r=15),
                      [PB[b]], [spT.res], eng=evq())
        for sq in range(NSEQ):
            dma('sp', O['pool_s'][sq, 0:7, :], I['spool'][sq * 15 + 8:sq * 15 + 15, :])

        def pool_chunk(c):
            uP, uS = upP[c % 2], upS[c % 2]
            k = c // 2
            nlev = k + 1
            wdw = 2 ** nlev
            b = nb()
            trp(bank(b)[0:16, 0:128], uP.ap[:, HL + NPR - 16:HL + NPR], identf.ap, [uP.res, identf.res], [PB[b]])
            vcopy(po.ap[0:16, c * 128:(c + 1) * 128], bank(b)[0:16, 0:128], [PB[b]], [po.res], eng=evq())
            b = nb()
            trp(bank(b)[:, 0:128], usc[c % 2].ap, identf.ap, [usc[c % 2].res, identf.res], [PB[b]])
            vcopy(po2.ap[:, c * 128:(c + 1) * 128], bank(b)[:, 0:128], [PB[b]], [po2.res], eng=evq())
            srcP, srcS = uP.ap, uS.ap
            rdP, rdS = [uP.res], [uS.res]
            for lev in range(nlev):
                sh = 2 ** lev
                lo = 2 ** (lev + 1) - 1
                dP, dS = wsP[lev % 2], wsS[lev % 2]
                vtt(dP.ap[:, lo:], srcP[:, lo:], srcP[:, lo - sh:HL + NPR - sh], ALU.add, rdP, [dP.res])
                vtt(dS.ap[:, :, lo:], srcS[:, :, lo:], srcS[:, :, lo - sh:HL + 8 - sh], ALU.add, rdS, [dS.res])
                srcP, srcS, rdP, rdS = dP.ap, dS.ap, [dP.res], [dS.res]
            j = c % 2
            vstt(pooled.ap[:, j, 0:NPR], srcP[:, HL:], 1.0 / wdw, uP.ap[:, HL:], ALU.mult, ALU.subtract, rdP + [uP.res], [pooled.res])
            vtt(gtmp.ap[:, 0:16], srcP[:, HL:HL + 16], icnt.ap[:, k, :], ALU.mult, rdP + [icnt.res], [gtmp.res])
            vtt(pooled.ap[:, j, 0:16], gtmp.ap[:, 0:16], uP.ap[:, HL:HL + 16], ALU.subtract, [gtmp.res, uP.res], [pooled.res])
            vstt(pooled.ap[:, j, NPR:NT].rearrange("p (s t) -> p s t", s=NSEQ, t=8), srcS[:, :, HL:], 1.0 / wdw, uS.ap[:, :, HL:],
                 ALU.mult, ALU.subtract, rdS + [uS.res], [pooled.res])

        def ev_up(mc, t0, n, b):
            uP, uS = upP[mc % 2], upS[mc % 2]
            if t0 == 0:
                vcopy(uP.ap[:, 0:HL], halo.ap[:, mc, :], [halo.res], [uP.res])
                vset(uS.ap[:, :, 0:1], 0.0, [uS.res])
                vcopy(uS.ap[:, :, 1:16], spT.ap[:, mc, :, :], [spT.res], [uS.res])
            npr = max(0, min(n, NPR - t0))
            if npr > 0:
                vcopy(uP.ap[:, HL + t0:HL + t0 + npr], bank(b)[:, 0:npr], [PB[b]], [uP.res], eng=evq())
            if t0 + n > NPR:
                vcopy(uS.ap[:, :, HL:HL + 8], bank(b)[:, npr:n].rearrange("p (s t) -> p s t", s=NSEQ, t=8), [PB[b]], [uS.res], eng=evq())
                vcopy(usc[mc % 2].ap, bank(b)[:, npr:n], [PB[b]], [usc[mc % 2].res], eng=evq())
                pool_chunk(mc)
                if mc % 2 == 1:
                    g = mc // 2

                    def ev_mix(mc2, t0_, n_, b_, g=g):
                        c_ = g * 2 + mc2
                        vsmul(mixed.ap[:, c_, t0_:t0_ + n_], bank(b_)[:, 0:n_], pscale.ap[:, c_:c_ + 1], [PB[b_], pscale.res], [mixed.res])
                    linear_fm(I['pool_w'], g * 256, 256, 0, 256, pooled, TG, ev_mix)
        linear_fm(I['w_in'], 0, D, 0, 1024, xnT, TG, ev_up)
        dma('sp', O['pool_p'], po.ap[0:16, :], rd=[po.res])
        for sq in range(NSEQ):
            dma('sp', O['pool_s'][sq, 7:15, :], po2.ap[sq * 8:(sq + 1) * 8, :], rd=[po2.res])
        gated_merge(I['pool_proj'], 1024, mixed, 0, False)
        tap('merged2', merged.ap[:, 0, :], [merged.res], [128, NT])
        reset('C')
        if stop <= 5:
            return done()

        qT = alloc([8, NT], BF16)
        oT = alloc([8, NT], BF16)

        def ev_q(mc, t0, n, b):
            vcopy(qT.ap[:, mc, t0:t0 + n], bank(b)[:, 0:n], [PB[b]], [qT.res], eng=evq())
        linear_fm(I['w_in'], 0, D, 2048, 1024, xnT, TG, ev_q)
        eT = alloc([2, 512], BF16)
        rZ = alloc([512])
        SCL = 1.0 / 16.0
        for h in range(4):
            for (t0, n) in TGP:
                for mc in range(2):
                    b = nb()
                    for dc in range(2):
                        mm(bank(b), KT.ap[:, 2 * h + dc, mc * 128:(mc + 1) * 128], qT.ap[:, 2 * h + dc, t0:t0 + n], dc == 0, dc == 1,
                           [KT.res, qT.res], [PB[b]])
                    act(eT.ap[:, mc, :], bank(b), ACT.Exp, [PB[b]], [eT.res], scale=SCL)
                b = nb()
                for mc in range(2):
                    mm(bank(b), onesb.ap, eT.ap[:, mc, :], mc == 0, mc == 1, [onesb.res, eT.res], [PB[b]])
                vrecip(rZ.ap, bank(b), [PB[b]], [rZ.res])
                for dc in range(2):
                    b = nb()
                    for mc in range(2):
                        mm(bank(b), Vb.ap[:, mc, (2 * h + dc) * 128:(2 * h + dc + 1) * 128], eT.ap[:, mc, :], mc == 0, mc == 1,
                           [Vb.res, eT.res], [PB[b]])
                    vtt(oT.ap[:, 2 * h + dc, t0:t0 + n], bank(b), rZ.ap, ALU.mult, [PB[b], rZ.res], [oT.res])
        Ks = [alloc([2, 1024], BF16) for _ in range(2)]
        Vs = [alloc([2, 1024], BF16) for _ in range(2)]
        KTs = alloc([8, 256], BF16)
        eTs = alloc([4, 2, 8], BF16)
        rZs = alloc([4, 8])
        for sq in range(NSEQ):
            ks, vs = Ks[sq % 2], Vs[sq % 2]
            dma('pool', ks.ap, I['ck'][sq].rearrange("(mc p) d -> p mc d", p=128), wr=[ks.res])
            dma('pool', vs.ap, I['cv'][sq].rearrange("(mc p) d -> p mc d", p=128), wr=[vs.res])
            for mc in range(2):
                b = nb()
                for dch in range(8):
                    trp(bankb(b)[:, dch * 128:(dch + 1) * 128], ks.ap[:, mc, dch * 128:(dch + 1) * 128], identb.ap, [ks.res, identb.res], [PB[b]])
                vcopy(KTs.ap[:, :, mc * 128:(mc + 1) * 128], bankb(b).rearrange("p (j c) -> p j c", j=8, c=128), [PB[b]], [KTs.res], eng=evq())
            tk = slice(NPR + sq * 8, NPR + sq * 8 + 8)
            b = nb()
            for h in range(4):
                for mc in range(2):
                    o = bank(b)[:, (h * 2 + mc) * 8:(h * 2 + mc + 1) * 8]
                    for dc in range(2):
                        mm(o, KTs.ap[:, 2 * h + dc, mc * 128:(mc + 1) * 128], qT.ap[:, 2 * h + dc, tk], dc == 0, dc == 1, [KTs.res, qT.res], [PB[b]])
            act(eTs.ap.rearrange("p h m t -> p (h m t)"), bank(b)[:, 0:64], ACT.Exp, [PB[b]], [eTs.res], scale=SCL)
            b = nb()
            for h in range(4):
                for mc in range(2):
                    mm(bank(b)[:, h * 8:(h + 1) * 8], onesb.ap, eTs.ap[:, h, mc, :], mc == 0, mc == 1, [onesb.res, eTs.res], [PB[b]])
            vrecip(rZs.ap.rearrange("p h t -> p (h t)"), bank(b)[:, 0:32], [PB[b]], [rZs.res])
            b = nb()
            for h in range(4):
                for dc in range(2):
                    o = bank(b)[:, (h * 2 + dc) * 8:(h * 2 + dc + 1) * 8]
                    for mc in range(2):
                        mm(o, vs.ap[:, mc, (2 * h + dc) * 128:(2 * h + dc + 1) * 128], eTs.ap[:, h, mc, :], mc == 0, mc == 1, [vs.res, eTs.res], [PB[b]])
            rz4 = rZs.ap
            rzb = bass.AP(rz4.tensor, rz4.offset, [list(rz4.ap[0]), list(rz4.ap[1]), [0, 2], list(rz4.ap[2])])
            vtt(oT.ap[:, :, tk].rearrange("p (h c) t -> p h c t", h=4, c=2), bank(b)[:, 0:64].rearrange("p (h c t) -> p h c t", h=4, c=2, t=8),
                rzb, ALU.mult, [PB[b], rZs.res], [oT.res])
        tap('oT', oT.ap[:, 0, :], [oT.res], [128, NT])
        gated_merge(I['xa_wo'], 1024, oT, 2, False)
        tap('merged3', merged.ap[:, 0, :], [merged.res], [128, NT])
        reset('C')
        if stop <= 6:
            return done()

        reset('E')
        reset('B')
        reset('F')
        x1 = alloc([9, D], R='C')
        xn2T = alloc([16, NT], BF16, R='E')
        xr_t = [alloc([512], R='B') for _ in range(2)]
        rtmps = [alloc([512], BF16, R='B') for _ in range(3)]
        xsb2 = [alloc([D], BF16, R='F') for _ in range(2)]
        load_g('norm2_g')
        for cg in range(4):
            panel, prl = wpanel_big(I['w_out'], 0, D, cg * 512, 512)
            for tt in range(9):
                xrt = xr_t[(cg * 9 + tt) % 2]
                dma('sp', xrt.ap, I['xm'][tt * 128:(tt + 1) * 128, cg * 512:(cg + 1) * 512], wr=[xrt.res])
                b = nb()
                for kc in range(16):
                    mm(bank(b), merged.ap[:, kc, tt * 128:(tt + 1) * 128], panel.ap[:, kc, :], kc == 0, kc == 15, [merged.res] + prl, [PB[b]])
                vtt(x1.ap[:, tt, cg * 512:(cg + 1) * 512], bank(b), xrt.ap, ALU.add, [PB[b], xrt.res], [x1.res])
                if cg == 3:
                    norm_ops(Buf(x1.ap[:, tt, :], x1.res), tt, None, xsb2[tt % 2], None)
                    if tt >= 1:
                        norm_T(xsb2[(tt - 1) % 2], xn2T, (tt - 1) * 128)
        norm_T(xsb2[8 % 2], xn2T, 8 * 128)
        tap('x1', x1.ap[:, 0, :], [x1.res], [128, D])
        reset('D')
        reset('F')
        if stop <= 7:
            return done()

        hT = alloc([16, NT], BF16, R='D')
        junkf = alloc([D], BF16, R='F')
        rctr = [0]
        for fq in range(4):
            def ev_h(mc, t0, n, b):
                rctr[0] += 1
                rt = rtmps[rctr[0] % 3]
                act(rt.ap[:, 0:n], bank(b)[:, 0:n], ACT.Relu, [PB[b]], [rt.res])
                vtt(hT.ap[:, mc, t0:t0 + n], rt.ap[:, 0:n], rt.ap[:, 0:n], ALU.mult, [rt.res], [hT.res])
            linear_fm(I['w1'], 0, D, fq * 2048, 2048, xn2T, TG, ev_h)
            if fq == 3:
                load_g('final_g')
            for cg in range(4):
                panel, prl = wpanel_big(I['w2'], fq * 2048, 2048, cg * 512, 512)
                for tt in range(9):
                    b = nb()
                    for kc in range(16):
                        mm(bank(b), hT.ap[:, kc, tt * 128:(tt + 1) * 128], panel.ap[:, kc, :], kc == 0, kc == 15, [hT.res] + prl, [PB[b]])
                    last = (fq == 3 and cg == 3)
                    xres = Res() if last else x1.res
                    vtt(x1.ap[:, tt, cg * 512:(cg + 1) * 512], x1.ap[:, tt, cg * 512:(cg + 1) * 512], bank(b), ALU.add, [PB[b], x1.res], [xres])
                    if last:
                        ss = ss_list[tt % 4]
                        xt_ap = x1.ap[:, tt, :]
                        vset(ss.ap[:, 0:1], 0.0, [ss.res])
                        act(junkf.ap, xt_ap, ACT.Square, [xres, ss.res], [junkf.res, ss.res], accum_out=ss.ap[:, 0:1])
                        vts(ss.ap[:, 1:2], ss.ap[:, 0:1], 1.0 / D, EPS, ALU.mult, ALU.add, [ss.res], [ss.res])
                        act(ss.ap[:, 1:2], ss.ap[:, 1:2], ACT.Sqrt, [ss.res], [ss.res])
                        vrecip(ss.ap[:, 2:3], ss.ap[:, 1:2], [ss.res], [ss.res])
                        vstt(xt_ap, xt_ap, ss.ap[:, 2:3], gbc.ap, ALU.mult, ALU.mult, [xres, ss.res, gbc.res], [xres])
                        dma('sp', O['y'][tt * 128:(tt + 1) * 128, :], xt_ap, rd=[xres])
        return done()


_WIN = (2, 4, 8, 16)


def _consts():
    ident = np.eye(128, dtype=np.float32)
    sel = np.zeros((64, 128, 128), np.float32)
    for a in range(8):
        for b in range(8):
            for q in range(16):
                sel[a * 8 + b, a * 16 + q, b * 16 + q] = 1.0
    mask = np.zeros((128, 128), np.float32)
    for s in range(8):
        for sp in range(s, 8):
            mask[s * 16:(s + 1) * 16, sp * 16:(sp + 1) * 16] = 1.0
    return ident, sel, mask


def make_in_maps(inp):
    f = lambda a: np.ascontiguousarray(np.asarray(a, dtype=np.float32))
    ident, sel, mask = _consts()
    shared = {
        'norm1_g': f(inp['norm1_g']).reshape(1, D), 'w_in': f(inp['w_in'][0]), 'b_gate': f(inp['b_gate']).reshape(6144),
        'pool_w': f(inp['pool_w']).reshape(1024, 256), 'pool_scale': f(inp['pool_scale']).reshape(1024),
        'pool_proj': f(inp['pool_proj'][0]), 'A_re': f(inp['ssm_A_re'][0]), 'A_im': f(inp['ssm_A_im'][0]),
        'log_dt': f(inp['ssm_log_dt']).reshape(1, 64), 'B_re': f(inp['ssm_B_re'][0]), 'B_im': f(inp['ssm_B_im'][0]),
        'C_re': f(inp['ssm_C_re']).reshape(1024, 64), 'C_im': f(inp['ssm_C_im']).reshape(1024, 64),
        'ssm_D': f(inp['ssm_D']).reshape(1024), 'glu_w': f(inp['ssm_glu_w'][0]), 'glu_b': f(inp['ssm_glu_b']).reshape(1024),
        'ssm_proj': f(inp['ssm_proj'][0]), 'mem_norm_g': f(inp['mem_norm_g']).reshape(1, D),
        'xa_wk': f(inp['xa_wk'][0]), 'xa_wv': f(inp['xa_wv'][0]), 'xa_wo': f(inp['xa_wo'][0]), 'w_out': f(inp['w_out'][0]),
        'norm2_g': f(inp['norm2_g']).reshape(1, D), 'w1': f(inp['mlp_w1'][0]), 'w2': f(inp['mlp_w2'][0]),
        'final_g': f(inp['final_norm_g']).reshape(1, D), 'ident': ident, 'sel': sel, 'mask': mask,
        'ciota': np.arange(128, dtype=np.float32).reshape(1, 128),
    }
    xpr, xsm = inp['x_prompt'], inp['x_sample']
    maps = []
    for c in range(8):
        b, h = c // 2, c % 2
        sq = slice(16 * c, 16 * c + 16)
        m = dict(shared)
        m['xm'] = f(np.concatenate([xpr[b, h * NPR:(h + 1) * NPR], xsm[sq].reshape(NSM, D)], axis=0))
        m['xp'] = f(xpr[b, 0:NPR]) if h == 1 else np.zeros((NPR, D), np.float32)
        m['spool'] = f(inp['state_pool'][0, sq]).reshape(NSEQ * 15, 1024)
        m['sre'] = f(inp['state_ssm_re'][0, sq]).reshape(NSEQ * 64, 64)
        m['sim'] = f(inp['state_ssm_im'][0, sq]).reshape(NSEQ * 64, 64)
        m['ck'] = f(inp['cache_mem_k'][0, sq]).reshape(NSEQ, 256, 1024)
        m['cv'] = f(inp['cache_mem_v'][0, sq]).reshape(NSEQ, 256, 1024)
        m['mem'] = f(inp['mem_prompt'][b])
        ic = np.zeros((4, 16), np.float32)
        for k, w in enumerate(_WIN):
            for t in range(16):
                ic[k, t] = 1.0 / (min(t + 1, w) if h == 0 else w)
        m['icnt'] = ic
        maps.append(m)
    return maps


def assemble(res):
    y_prompt = np.zeros((4, 2048, D), np.float32)
    y_sample = np.zeros((128, 8, D), np.float32)
    pool_p = np.zeros((1, 4, 15, 1024), np.float32)
    re_p = np.zeros((1, 4, 64, 64), np.float32)
    im_p = np.zeros((1, 4, 64, 64), np.float32)
    mk_p = np.zeros((1, 4, 256, 4, 256), np.float32)
    mv_p = np.zeros((1, 4, 256, 4, 256), np.float32)
    pool_s = np.zeros((1, 128, 15, 1024), np.float32)
    re_s = np.zeros((1, 128, 64, 64), np.float32)
    im_s = np.zeros((1, 128, 64, 64), np.float32)
    for c in range(8):
        r = res[c]
        b, h = c // 2, c % 2
        sq = slice(16 * c, 16 * c + 16)
        y_prompt[b, h * NPR:(h + 1) * NPR] = r['y'][0:NPR]
        y_sample[sq] = r['y'][NPR:NT].reshape(NSEQ, 8, D)
        pool_s[0, sq] = r['pool_s']
        re_s[0, sq] = r['hs_re'].reshape(NSEQ, 64, 64)
        im_s[0, sq] = r['hs_im'].reshape(NSEQ, 64, 64)
        if h == 1:
            pool_p[0, b] = r['pool_p'][1:16]
            re_p[0, b] = r['hp_re']
            im_p[0, b] = r['hp_im']
        else:
            mk_p[0, b] = r['mk'].reshape(256, 4, 256)
            mv_p[0, b] = r['mv'].reshape(256, 4, 256)
    return (y_prompt, y_sample, pool_p, re_p, im_p, mk_p, mv_p, pool_s, re_s, im_s)


_NC_CACHE = {}


def kernel(**inputs):
    if 'nc' not in _NC_CACHE:
        _NC_CACHE['nc'] = build_program()[0]
    nc = _NC_CACHE['nc']
    in_maps = make_in_maps(inputs)
    res = run_bass_kernel_spmd(nc, in_maps, core_ids=list(range(8)))
    return assemble(res.results)
```

```python
import math
import numpy as np
import concourse.bass as bass
import concourse.mybir as mybir
from concourse.bass_utils import run_bass_kernel_spmd
from contextlib import ExitStack

F32 = mybir.dt.float32
BF16 = mybir.dt.bfloat16
ACT = mybir.ActivationFunctionType
ALU = mybir.AluOpType

ENGS = ['pe', 'act', 'dve', 'pool', 'sp']
SAME_ENGINE_SYNC = True

D = 2048
NPR = 1024
NSM = 128
NT = NPR + NSM
NSEQ = 16
DFF = 8192
EPS = 1e-6
TG = [(0, 512), (512, 512), (1024, 128)]
TGP = [(0, 512), (512, 512)]
PI = float(np.pi)


class Res:
    __slots__ = ('name', 'w', 'r', 'excl')

    def __init__(self, name='', excl=False):
        self.name = name
        self.w = None
        self.r = {}
        self.excl = excl


class Sched:
    def __init__(self, ndsem=48):
        self.ops = {e: [] for e in ENGS}
        self.seen = {e: {} for e in ENGS}
        self.ndsem = ndsem
        self.dsem_cnt = [0] * ndsem
        self.dsem_next = 0
        self.dsem_next_sw = 0

    def _deps(self, eng, reads, writes):
        deps = []
        for r in reads:
            if r.w is not None:
                deps.append(r.w)
            if r.excl:
                deps.extend(t for k, t in r.r.items() if k != eng)
        for w in writes:
            if w.w is not None:
                deps.append(w.w)
            deps.extend(w.r.values())
        return deps

    def _add_waits(self, eng, deps):
        seen = self.seen[eng]
        best = {}
        for d in deps:
            key = (d[0], d[1])
            if d[0] == 'e' and d[1] == eng and (eng == 'pe' or not SAME_ENGINE_SYNC):
                continue
            if seen.get(key, -1) >= d[2]:
                continue
            if best.get(key, -1) < d[2]:
                best[key] = d[2]
        waits = []
        for key, v in best.items():
            seen[key] = v
            waits.append((key[0], key[1], v))
            if key[0] == 'e':
                self.ops[key[1]][v][3] = True
        return waits

    def op(self, eng, fn, reads=(), writes=()):
        waits = self._add_waits(eng, self._deps(eng, reads, writes))
        idx = len(self.ops[eng])
        self.ops[eng].append(['c', fn, waits, False, None])
        tag = ('e', eng, idx)
        for r in reads:
            r.r[eng] = tag
        for w in writes:
            w.w = tag
            w.r = {}
        return tag

    def dma(self, q, fn, reads=(), writes=()):
        half = self.ndsem // 2
        if q == 'pool':
            s = half + self.dsem_next_sw
            self.dsem_next_sw = (self.dsem_next_sw + 1) % (self.ndsem - half)
        else:
            s = self.dsem_next
            self.dsem_next = (s + 1) % half
        deps = self._deps(q, reads, writes)
        if self.dsem_cnt[s] > 0:
            deps.append(('d', s, self.dsem_cnt[s]))
        waits = self._add_waits(q, deps)
        self.dsem_cnt[s] += 16
        tag = ('d', s, self.dsem_cnt[s])
        self.ops[q].append(['d', fn, waits, False, s])
        for r in reads:
            r.r[('dma', s)] = tag
        for w in writes:
            w.w = tag
            w.r = {}
        return tag

    def barrier(self):
        tags = []
        for e in ENGS:
            for idx in range(len(self.ops[e]) - 1, -1, -1):
                if self.ops[e][idx][0] == 'c':
                    tags.append(('e', e, idx))
                    break
        for s in range(self.ndsem):
            if self.dsem_cnt[s] > 0:
                tags.append(('d', s, self.dsem_cnt[s]))
        for e in ENGS:
            waits = self._add_waits(e, tags)
            if waits:
                self.ops[e].append(['w', None, waits, False, None])

    def finish(self):
        self.barrier()

    def emit(self, nc, es):
        SEMMAX = 30000
        rank = {}
        nsem = {}
        for e in ENGS:
            c = 0
            rk = []
            for o in self.ops[e]:
                if o[3]:
                    c += 1
                rk.append(c)
            rank[e] = rk
            nsem[e] = max(1, (c + SEMMAX - 1) // SEMMAX)
        esem = {e: [es.enter_context(nc.semaphore('es_%s_%d' % (e, i))) for i in range(nsem[e])] for e in ENGS}
        dsem = [es.enter_context(nc.semaphore('ds_%d' % i)) for i in range(self.ndsem)]
        ops = self.ops
        block = es.enter_context(nc.Block())

        def semval(a, r):
            return esem[a][(r - 1) // SEMMAX], (r - 1) % SEMMAX + 1

        def run(e, E):
            for i, (kind, fn, waits, marked, ds) in enumerate(ops[e]):
                for (k, a, v) in waits:
                    if k == 'e':
                        sm, val = semval(a, rank[a][v])
                        E.wait_ge(sm, val)
                    else:
                        E.wait_ge(dsem[a], v)
                if kind == 'w':
                    continue
                inst = fn(E)
                if kind == 'd':
                    inst.then_inc(dsem[ds], 16)
                elif marked:
                    sm, val = semval(e, rank[e][i])
                    inst.then_inc(sm, 1)

        @block.tensor
        def _(E):
            run('pe', E)

        @block.scalar
        def _(E):
            run('act', E)

        @block.vector
        def _(E):
            run('dve', E)

        @block.gpsimd
        def _(E):
            run('pool', E)

        @block.sync
        def _(E):
            run('sp', E)


class Buf:
    __slots__ = ('ap', 'res')

    def __init__(self, ap, res):
        self.ap = ap
        self.res = res


def bc(ap, n):
    return bass.AP(ap.tensor, ap.offset, [list(x) for x in ap.ap] + [[0, n]])


def bc_mid(ap, n):
    a = [list(x) for x in ap.ap]
    return bass.AP(ap.tensor, ap.offset, [a[0], [0, n]] + a[1:])


LAST_S = None


def build_program(stop=99, taps=()):
    global LAST_S
    nc = bass.Bass("TRN2", target_bir_lowering=False)
    S = Sched()
    LAST_S = S
    taps = set(taps)
    tap_out = {}

    def din(name, shape):
        return nc.dram_tensor(name, list(shape), F32, kind="ExternalInput").ap()

    def dout(name, shape):
        return nc.dram_tensor(name, list(shape), F32, kind="ExternalOutput").ap()

    I = {}
    for name, shape in [
        ('xm', [NT, D]), ('xp', [NPR, D]), ('spool', [NSEQ * 15, 1024]), ('sre', [NSEQ * 64, 64]), ('sim', [NSEQ * 64, 64]),
        ('ck', [NSEQ, 256, 1024]), ('cv', [NSEQ, 256, 1024]), ('mem', [256, D]),
        ('norm1_g', [1, D]), ('w_in', [D, 9216]), ('b_gate', [6144]), ('pool_w', [1024, 256]), ('pool_scale', [1024]),
        ('pool_proj', [1024, D]), ('A_re', [64, 64]), ('A_im', [64, 64]), ('log_dt', [1, 64]),
        ('B_re', [64, 64, 16]), ('B_im', [64, 64, 16]), ('C_re', [1024, 64]), ('C_im', [1024, 64]), ('ssm_D', [1024]),
        ('glu_w', [1024, 1024]), ('glu_b', [1024]), ('ssm_proj', [1024, D]), ('mem_norm_g', [1, D]),
        ('xa_wk', [D, 1024]), ('xa_wv', [D, 1024]), ('xa_wo', [1024, D]), ('w_out', [D, D]), ('norm2_g', [1, D]),
        ('w1', [D, DFF]), ('w2', [DFF, D]), ('final_g', [1, D]),
        ('ident', [128, 128]), ('sel', [64, 128, 128]), ('mask', [128, 128]), ('icnt', [4, 16]), ('ciota', [1, 128]),
    ]:
        I[name] = din(name, shape)
    O = {}
    for name, shape in [
        ('y', [NT, D]), ('pool_p', [16, 1024]), ('hp_re', [64, 64]), ('hp_im', [64, 64]), ('mk', [256, 1024]), ('mv', [256, 1024]),
        ('pool_s', [NSEQ, 15, 1024]), ('hs_re', [NSEQ * 64, 64]), ('hs_im', [NSEQ * 64, 64]),
    ]:
        O[name] = dout(name, shape)

    with ExitStack() as es:
        es.enter_context(nc.allow_non_contiguous_dma(reason="small strided param loads"))
        ARENA_BYTES = 207 * 1024 + 512
        arena_t = es.enter_context(nc.sbuf_tensor("arena", [128, ARENA_BYTES // 4], F32))
        ps_t = es.enter_context(nc.psum_tensor("ps", [128, 4096], F32))
        PB = [Res('bank%d' % i, excl=True) for i in range(8)]
        st = {'bank': 0, 'ev': 0, 'ring': 0, 'big': 0}

        KB = 256
        REG = {}

        def region(name, base_kb, size_kb):
            REG[name] = {'base': int(base_kb * KB), 'top': int(base_kb * KB), 'lim': int((base_kb + size_kb) * KB), 'peak': 0}
        region('A', 0, 44)
        region('B', 44, 9)
        region('C', 53, 72)
        region('D', 125, 36)
        region('E', 161, 36)
        region('F', 197, 9)
        region('S', 71, 136.5)
        cur = {'R': 'A'}

        def use(name):
            cur['R'] = name

        def alloc(shape, dtype=F32, parts=128, R=None):
            rg = REG[R or cur['R']]
            n = 1
            for s_ in shape:
                n *= s_
            nb_ = n * (2 if dtype == BF16 else 4)
            nw = (nb_ + 3) // 4
            nw = (nw + 7) // 8 * 8
            w0 = rg['top']
            st['last_w0'] = w0
            rg['top'] += nw
            rg['peak'] = max(rg['peak'], rg['top'])
            assert rg['top'] <= rg['lim'], "region %s overflow: need %d KiB more" % (R or cur['R'], (rg['top'] - rg['lim']) // KB + 1)
            ap = arena_t[0:parts, w0:w0 + nw]
            if dtype == BF16:
                ap = ap.bitcast(BF16)
            ap = ap[:, 0:n]
            if len(shape) == 2:
                ap = ap.rearrange("p (a b) -> p a b", a=shape[0], b=shape[1])
            elif len(shape) == 3:
                ap = ap.rearrange("p (a b c) -> p a b c", a=shape[0], b=shape[1], c=shape[2])
            elif len(shape) == 4:
                ap = ap.rearrange("p (a b c d) -> p a b c d", a=shape[0], b=shape[1], c=shape[2], d=shape[3])
            return Buf(ap, Res())

        def mark(R=None):
            return (R or cur['R'], REG[R or cur['R']]['top'])

        def release(m):
            S.barrier()
            REG[m[0]]['top'] = m[1]

        def reset(R):
            S.barrier()
            REG[R]['top'] = REG[R]['base']

        def bank(i):
            return ps_t[:, i * 512:(i + 1) * 512]

        def bankb(i):
            return ps_t[:, i * 512:(i + 1) * 512].bitcast(BF16)

        def nb():
            b = st['bank'] % st.get('nbanks', 8)
            st['bank'] = (b + 1) % st.get('nbanks', 8)
            return b

        def evq():
            st['ev'] ^= 1
            return 'act' if st['ev'] else 'dve'

        def mm(out, lhsT, rhs, start, stop, rd, wr, skip=False):
            if skip:
                S.op('pe', lambda E: E.matmul(out, lhsT, rhs, start=start, stop=stop, skip_group_check=True), reads=rd, writes=wr)
            else:
                S.op('pe', lambda E: E.matmul(out, lhsT, rhs, start=start, stop=stop), reads=rd, writes=wr)

        def trp(out, in_, ident, rd, wr):
            S.op('pe', lambda E: E.transpose(out, in_, ident), reads=rd, writes=wr)

        def vtt(out, a, b, op, rd, wr, eng='dve'):
            S.op(eng, lambda E: E.tensor_tensor(out, a, b, op), reads=rd, writes=wr)

        def vts(out, a, s1, s2, op0, op1, rd, wr, eng='dve'):
            S.op(eng, lambda E: E.tensor_scalar(out, a, s1, s2, op0, op1), reads=rd, writes=wr)

        def vstt(out, in0, scalar, in1, op0, op1, rd, wr, eng='dve'):
            S.op(eng, lambda E: E.scalar_tensor_tensor(out, in0, scalar, in1, op0, op1), reads=rd, writes=wr)

        def vcopy(out, a, rd, wr, eng='dve'):
            if eng == 'act':
                S.op('act', lambda E: E.copy(out, a), reads=rd, writes=wr)
            else:
                S.op(eng, lambda E: E.tensor_copy(out, a), reads=rd, writes=wr)

        def vset(out, val, wr, eng='dve'):
            S.op(eng, lambda E: E.memset(out, val), writes=wr)

        def act(out, in_, func, rd, wr, bias=None, scale=None, accum_out=None):
            kw = {}
            if bias is not None:
                kw['bias'] = bias
            if scale is not None:
                kw['scale'] = scale
            if accum_out is not None:
                kw['accum_out'] = accum_out
            S.op('act', lambda E: E.activation(out, in_, func, **kw), reads=rd, writes=wr)

        def dma(q, out, in_, rd=(), wr=()):
            S.dma(q, lambda E: E.dma_start(out=out, in_=in_), reads=rd, writes=wr)

        def tap(name, ap, rd, shape):
            if name in taps:
                t = dout('tap_' + name, shape)
                tap_out[name] = shape
                dma('sp' if ap.dtype == F32 else 'pool', t, ap, rd=rd)

        def vsadd(out, a, c, rd, wr, eng='dve'):
            S.op(eng, lambda E: E.tensor_scalar_add(out, a, c), reads=rd, writes=wr)

        def vsmul(out, a, c, rd, wr, eng='dve'):
            S.op(eng, lambda E: E.tensor_scalar_mul(out, a, c), reads=rd, writes=wr)

        def vrecip(out, a, rd, wr):
            S.op('dve', lambda E: E.reciprocal(out, a), reads=rd, writes=wr)

        def done():
            import os
            if os.environ.get('PEAKS'):
                for k_, v_ in REG.items():
                    print('region', k_, 'peak KiB', (v_['peak'] - v_['base']) / KB, 'size', (v_['lim'] - v_['base']) / KB)
            S.finish()
            S.emit(nc, es)
            return nc, tap_out

        use('A')
        identf = alloc([128])
        dma('sp', identf.ap, I['ident'], wr=[identf.res])
        identb = alloc([128], BF16)
        dma('pool', identb.ap, I['ident'], wr=[identb.res])
        onesb = alloc([128], BF16)
        vset(onesb.ap, 1.0, [onesb.res])
        gbc = alloc([D])
        small = {}

        def load_cols(name, n):
            b = alloc([n // 128])
            dma('sp', b.ap, I[name].rearrange("(c p) -> p c", p=128), wr=[b.res])
            small[name] = b
        load_cols('b_gate', 6144)
        load_cols('pool_scale', 1024)
        load_cols('ssm_D', 1024)
        load_cols('glu_b', 1024)
        icnt = alloc([4, 16])
        dma('sp', icnt.ap, bass.AP(I['icnt'].tensor, 0, [[0, 128], [16, 4], [1, 16]]), wr=[icnt.res])
        ss_list = [alloc([8]) for _ in range(4)]
        ss = ss_list[0]
        ring_all = alloc([4, 4096], BF16)
        ring_res = [Res() for _ in range(4)]

        def load_g(name):
            dma('sp', gbc.ap, bass.AP(I[name].tensor, 0, [[0, 128], [1, D]]), wr=[gbc.res])

        def wpanel(w2d, r0, K, c0, ncols, slots=(0, 1, 2, 3), key='ring'):
            KC = K // 128
            assert KC * ncols <= 4096
            si = slots[st.setdefault(key, 0) % len(slots)]
            st[key] += 1
            view = ring_all.ap[:, si, 0:KC * ncols].rearrange("p (k m) -> p k m", k=KC, m=ncols)
            src = w2d[r0:r0 + K, c0:c0 + ncols].rearrange("(kc p) m -> p kc m", p=128)
            dma('pool', view, src, wr=[ring_res[si]])
            return Buf(view, ring_res[si]), [ring_res[si]]

        def wpanel_big(w2d, r0, K, c0, ncols):
            KC = K // 128
            assert KC * ncols <= 8192
            bi = st['big'] % 2
            st['big'] += 1
            flat = ring_all.ap[:, 2 * bi:2 * bi + 2, :].rearrange("p a b -> p (a b)")
            view = flat[:, 0:KC * ncols].rearrange("p (k m) -> p k m", k=KC, m=ncols)
            src = w2d[r0:r0 + K, c0:c0 + ncols].rearrange("(kc p) m -> p kc m", p=128)
            rl = [ring_res[2 * bi], ring_res[2 * bi + 1]]
            dma('pool', view, src, wr=rl)
            return Buf(view, None), rl

        def norm_ops(s, tt, xslots, xs_b, junk, g=None):
            gb = g if g is not None else gbc
            if isinstance(s, Buf):
                xt = s
            else:
                xt = xslots[tt % len(xslots)]
                dma('sp', xt.ap, s, wr=[xt.res])
            ss = ss_list[tt % 4]
            jk = junk if junk is not None else xs_b
            vset(ss.ap[:, 0:1], 0.0, [ss.res])
            act(jk.ap, xt.ap, ACT.Square, [xt.res, ss.res], [jk.res, ss.res], accum_out=ss.ap[:, 0:1])
            vts(ss.ap[:, 1:2], ss.ap[:, 0:1], 1.0 / D, EPS, ALU.mult, ALU.add, [ss.res], [ss.res])
            act(ss.ap[:, 1:2], ss.ap[:, 1:2], ACT.Sqrt, [ss.res], [ss.res])
            vrecip(ss.ap[:, 2:3], ss.ap[:, 1:2], [ss.res], [ss.res])
            vstt(xs_b.ap, xt.ap, ss.ap[:, 2:3], gb.ap, ALU.mult, ALU.mult, [xt.res, ss.res, gb.res], [xs_b.res])

        def norm_T(xs_b, dst, c0):
            for h in range(2):
                b = nb()
                for j in range(8):
                    c = h * 8 + j
                    trp(bankb(b)[:, j * 128:(j + 1) * 128], xs_b.ap[:, c * 128:(c + 1) * 128], identb.ap,
                        [xs_b.res, identb.res], [PB[b]])
                vcopy(dst.ap[:, h * 8:(h + 1) * 8, c0:c0 + 128],
                      bankb(b).rearrange("p (j c) -> p j c", j=8, c=128), [PB[b]], [dst.res], eng=evq())

        def norm_transpose(src_fn, ntiles, dst, col0, xslots, xs_b_in, junk):
            for tt in range(ntiles):
                xs_b = xs_b_in[tt % len(xs_b_in)] if isinstance(xs_b_in, list) else xs_b_in
                norm_ops(src_fn(tt), tt, xslots, xs_b, junk)
                norm_T(xs_b, dst, col0 + tt * 128)

        def linear_fm(w2d, r0, K, c0, M, rhsT, tgroups, evac):
            KC = K // 128
            pc = min(M, (4096 // KC) // 128 * 128)
            for p0 in range(0, M, pc):
                panel, prl = wpanel(w2d, r0, K, c0 + p0, pc)
                for j in range(pc // 128):
                    mc = p0 // 128 + j
                    for (t0, n) in tgroups:
                        b = nb()
                        for kc in range(KC):
                            mm(bank(b)[:, 0:n], panel.ap[:, kc, j * 128:(j + 1) * 128], rhsT.ap[:, kc, t0:t0 + n],
                               kc == 0, kc == KC - 1, prl + [rhsT.res], [PB[b]])
                        evac(mc, t0, n, b)

        use('B')
        KT = alloc([8, 256], BF16)
        Vb = alloc([2, 1024], BF16)
        halo = alloc([8, 16])
        use('S')

        def sc():
            return alloc([64], F32, parts=64)
        maskf = alloc([128], R='B')
        dma('sp', maskf.ap, I['mask'], wr=[maskf.res])
        Pst_r = alloc([9, 64], F32, parts=64)
        Pst_i = alloc([9, 64], F32, parts=64)
        Qst_r = alloc([8, 64], F32, parts=64)
        Qst_i = alloc([8, 64], F32, parts=64)
        cst_r = alloc([8, 64], F32, parts=64)
        cst_i = alloc([8, 64], F32, parts=64)
        P_r = [Buf(Pst_r.ap[:, k, :], Pst_r.res) for k in range(9)]
        P_i = [Buf(Pst_i.ap[:, k, :], Pst_i.res) for k in range(9)]
        Q_r = [Buf(Qst_r.ap[:, 7 - k, :], Qst_r.res) for k in range(8)]
        Q_i = [Buf(Qst_i.ap[:, 7 - k, :], Qst_i.res) for k in range(8)]
        cs_r = [Buf(cst_r.ap[:, k, :], cst_r.res) for k in range(8)]
        cs_i = [Buf(cst_i.ap[:, k, :], cst_i.res) for k in range(8)]
        rho8 = sc()
        phin = sc()
        ciota = alloc([128], F32, parts=64)
        dma('sp', ciota.ap, bass.AP(I['ciota'].tensor, 0, [[0, 64], [1, 128]]), wr=[ciota.res])
        H0 = alloc([2, 64], F32, parts=64)
        HpF = alloc([2, 64], F32, parts=64)
        h0T = alloc([2, NSEQ, 64], F32, parts=64)
        sel = alloc([64, 128], BF16)
        dma('pool', sel.ap, I['sel'].rearrange("j p m -> p j m"), wr=[sel.res])
        upreT = alloc([8, NPR], BF16)
        umT = alloc([8, NT], BF16)
        m0 = mark()
        ar, ai, dtb = sc(), sc(), sc()
        tmpA = alloc([64], F32, parts=64)
        for (nm, dst) in (('A_re', ar), ('A_im', ai)):
            dma('sp', tmpA.ap, I[nm], wr=[tmpA.res])
            b = nb()
            trp(bank(b)[0:64, 0:64], tmpA.ap, identf.ap[0:64, 0:64], [tmpA.res, identf.res], [PB[b]])
            vcopy(dst.ap, bank(b)[0:64, 0:64], [PB[b]], [dst.res])
        dma('sp', dtb.ap, bass.AP(I['log_dt'].tensor, 0, [[0, 64], [1, 64]]), wr=[dtb.res])
        act(dtb.ap, dtb.ap, ACT.Exp, [dtb.res], [dtb.res])
        xsl = [alloc([D]) for _ in range(2)]
        xsb4 = [alloc([D], BF16) for _ in range(4)]
        xg = alloc([16, 512], BF16)
        xhalo = alloc([16, 128], BF16)
        gmem = alloc([D], R='C')
        mnT = alloc([16, 256], BF16, R='C')
        dma('sp', gmem.ap, bass.AP(I['mem_norm_g'].tensor, 0, [[0, 128], [1, D]]), wr=[gmem.res])
        for tt in range(2):
            norm_ops(I['mem'][tt * 128:(tt + 1) * 128, :], tt, xsl, xsb4[tt], None, g=gmem)
            norm_T(xsb4[tt], mnT, tt * 128)
        osb = alloc([512], R='C')
        load_g('norm1_g')
        glist = [(I['xp'], upreT, 0, 512), (I['xp'], upreT, 512, 512)] + [(I['xm'], umT, t0, n) for (t0, n) in TG]
        tile_ctr = [0]

        def p2_norm_ops(i, k):
            src, dstT, t0, n = glist[i]
            norm_ops(src[t0 + k * 128:t0 + (k + 1) * 128, :], tile_ctr[0], xsl, xsb4[k], None)
            tile_ctr[0] += 1

        for k in range(glist[0][3] // 128):
            p2_norm_ops(0, k)
        for (wn, on, isk) in (('xa_wk', 'mk', True), ('xa_wv', 'mv', False)):
            for half in range(2):
                panel, prl = wpanel_big(I[wn], 0, D, half * 512, 512)
                if isk:
                    for j in range(4):
                        mc = half * 4 + j
                        b = nb()
                        for kc in range(16):
                            mm(bank(b)[:, 0:256], panel.ap[:, kc, j * 128:(j + 1) * 128], mnT.ap[:, kc, :], kc == 0, kc == 15,
                               prl + [mnT.res], [PB[b]])
                        vcopy(KT.ap[:, mc, :], bank(b)[:, 0:256], [PB[b]], [KT.res], eng='act')
                for mt in range(2):
                    b = nb()
                    for kc in range(16):
                        mm(bank(b), mnT.ap[:, kc, mt * 128:(mt + 1) * 128], panel.ap[:, kc, :], kc == 0, kc == 15,
                           prl + [mnT.res], [PB[b]])
                    vcopy(osb.ap, bank(b), [PB[b]], [osb.res], eng='act')
                    if not isk:
                        vcopy(Vb.ap[:, mt, half * 512:(half + 1) * 512], bank(b), [PB[b]], [Vb.res], eng='act')
                    dma('act', O[on][mt * 128:(mt + 1) * 128, half * 512:(half + 1) * 512], osb.ap, rd=[osb.res])
        wres = [wpanel(I['w_in'], 0, D, 1024 + p_ * 256, 256, slots=(p_,), key='p2r%d' % p_) for p_ in range(4)]
        def p2_T(i):
            src, dstT, t0, n = glist[i]
            for k in range(n // 128):
                norm_T(xsb4[k], xg, k * 128)
            if i == 1:
                vcopy(xhalo.ap, xg.ap[:, :, 384:512], [xg.res], [xhalo.res], eng='act')

        def p2_unit(i, mc):
            src, dstT, t0, n = glist[i]
            panel, prl = wres[mc // 2]
            j = mc % 2
            b = nb()
            for kc in range(16):
                mm(bank(b)[:, 0:n], panel.ap[:, kc, j * 128:(j + 1) * 128], xg.ap[:, kc, 0:n], kc == 0, kc == 15,
                   prl + [xg.res], [PB[b]])
            vcopy(dstT.ap[:, mc, t0:t0 + n], bank(b)[:, 0:n], [PB[b]], [dstT.res], eng=evq())
        lr, th, Em, t1, t2, t3 = sc(), sc(), sc(), sc(), sc(), sc()
        vtt(lr.ap, dtb.ap, ar.ap, ALU.mult, [dtb.res, ar.res], [lr.res])
        vtt(th.ap, dtb.ap, ai.ap, ALU.mult, [dtb.res, ai.res], [th.res])
        act(Em.ap, lr.ap, ACT.Exp, [lr.res], [Em.res])

        def sin_of(dst, src, shift):
            vsadd(t1.ap, src.ap, shift, [src.res], [t1.res])
            vcopy(t2.ap, t1.ap, [t1.res], [t2.res])
            for kk in (1, 3, 5, 7, 9):
                vts(t3.ap, t1.ap, kk * PI, -2 * PI, ALU.is_ge, ALU.mult, [t1.res], [t3.res])
                vtt(t2.ap, t2.ap, t3.ap, ALU.add, [t2.res, t3.res], [t2.res])
            vts(t2.ap, t2.ap, -3.1415925, 3.1415925, ALU.max, ALU.min, [t2.res], [t2.res])
            act(dst.ap, t2.ap, ACT.Sin, [t2.res], [dst.res])
        sn, cs_ = sc(), sc()
        sin_of(sn, th, 0.0)
        sin_of(cs_, th, PI / 2)
        vset(P_r[0].ap, 1.0, [P_r[0].res])
        vset(P_i[0].ap, 0.0, [P_i[0].res])
        vset(Q_r[0].ap, 1.0, [Q_r[0].res])
        vset(Q_i[0].ap, 0.0, [Q_i[0].res])
        vtt(P_r[1].ap, Em.ap, cs_.ap, ALU.mult, [Em.res, cs_.res], [P_r[1].res])
        vtt(P_i[1].ap, Em.ap, sn.ap, ALU.mult, [Em.res, sn.res], [P_i[1].res])

        def cmul(o_r, o_i, a_r, a_i, b_r, b_i, rd, wr_r, wr_i, ta, tb, conj_b=False, eng='dve'):
            vtt(ta.ap, a_r, b_r, ALU.mult, rd, [ta.res], eng)
            vtt(tb.ap, a_i, b_i, ALU.mult, rd, [tb.res], eng)
            vtt(o_r, ta.ap, tb.ap, ALU.add if conj_b else ALU.subtract, [ta.res, tb.res], wr_r, eng)
            vtt(ta.ap, a_i, b_r, ALU.mult, rd, [ta.res], eng)
            vtt(tb.ap, a_r, b_i, ALU.mult, rd, [tb.res], eng)
            vtt(o_i, ta.ap, tb.ap, ALU.subtract if conj_b else ALU.add, [ta.res, tb.res], wr_i, eng)

        def cmul_s(o_r, o_i, a_r, a_i, b_r, b_i, conj_b=False):
            cmul(o_r.ap, o_i.ap, a_r.ap, a_i.ap, b_r.ap, b_i.ap, [a_r.res, a_i.res, b_r.res, b_i.res],
                 [o_r.res], [o_i.res], t1, t2, conj_b)
        for k in range(2, 9):
            cmul_s(P_r[k], P_i[k], P_r[k - 1], P_i[k - 1], P_r[1], P_i[1])
        den, fr, fi, xr = sc(), sc(), sc(), sc()
        vtt(den.ap, ar.ap, ar.ap, ALU.mult, [ar.res], [den.res])
        vtt(t3.ap, ai.ap, ai.ap, ALU.mult, [ai.res], [t3.res])
        vtt(den.ap, den.ap, t3.ap, ALU.add, [den.res, t3.res], [den.res])
        vrecip(den.ap, den.ap, [den.res], [den.res])
        vsadd(xr.ap, P_r[1].ap, -1.0, [P_r[1].res], [xr.res])
        cmul_s(fr, fi, xr, P_i[1], ar, ai, conj_b=True)
        vtt(fr.ap, fr.ap, den.ap, ALU.mult, [fr.res, den.res], [fr.res])
        vtt(fi.ap, fi.ap, den.ap, ALU.mult, [fi.res, den.res], [fi.res])
        einv, irho8 = sc(), sc()
        act(einv.ap, lr.ap, ACT.Exp, [lr.res], [einv.res], scale=-2.0)
        vtt(Q_r[1].ap, P_r[1].ap, einv.ap, ALU.mult, [P_r[1].res, einv.res], [Q_r[1].res])
        vstt(Q_i[1].ap, P_i[1].ap, -1.0, einv.ap, ALU.mult, ALU.mult, [P_i[1].res, einv.res], [Q_i[1].res])
        for k in range(2, 8):
            cmul_s(Q_r[k], Q_i[k], Q_r[k - 1], Q_i[k - 1], Q_r[1], Q_i[1])
        for s_ in range(8):
            cmul_s(cs_r[s_], cs_i[s_], P_r[7 - s_], P_i[7 - s_], fr, fi)
        act(rho8.ap, lr.ap, ACT.Exp, [lr.res], [rho8.res], scale=8.0)
        I32 = mybir.dt.int32
        vsmul(t1.ap, th.ap, 4.0 / PI, [th.res], [t1.res])
        vcopy(t2.ap.bitcast(I32), t1.ap, [t1.res], [t2.res])
        vcopy(t3.ap, t2.ap.bitcast(I32), [t2.res], [t3.res])
        vtt(phin.ap, t1.ap, t3.ap, ALU.subtract, [t1.res, t3.res], [phin.res])
        tap('P8r', P_r[8].ap, [P_r[8].res], [64, 64])
        tap('P8i', P_i[8].ap, [P_i[8].res], [64, 64])
        tap('fr', fr.ap, [fr.res], [64, 64])
        tap('fi', fi.ap, [fi.res], [64, 64])
        tmpS = alloc([64])
        for ri, nm in ((0, 'sre'), (1, 'sim')):
            for j in range(8):
                dma('sp', tmpS.ap, I[nm][j * 128:(j + 1) * 128, :], wr=[tmpS.res])
                b = nb()
                trp(bank(b)[0:64, 0:128], tmpS.ap, identf.ap, [tmpS.res, identf.res], [PB[b]])
                vcopy(h0T.ap[:, ri, 2 * j:2 * j + 2, :], bank(b)[0:64, 0:128].rearrange("p (s g) -> p s g", s=2, g=64),
                      [PB[b]], [h0T.res], eng='act')

        p2_T(0)
        for i in range(len(glist)):
            nxt = glist[i + 1][3] // 128 if i + 1 < len(glist) else 0
            for mc in range(8):
                if mc % 2 == 0 and mc // 2 < nxt:
                    p2_norm_ops(i + 1, mc // 2)
                p2_unit(i, mc)
            if nxt:
                p2_T(i + 1)

        def evh(mc, tt0, nn, b):
            vcopy(halo.ap[:, mc, :], bank(b)[:, 112:128], [PB[b]], [halo.res], eng=evq())
        linear_fm(I['w_in'], 0, D, 0, 1024, xhalo, [(0, 128)], evh)
        release(m0)
        REG['C']['top'] = REG['C']['base']
        tap('umT', umT.ap[:, 0, :], [umT.res], [128, NT])
        tap('upreT', upreT.ap[:, 0, :], [upreT.res], [128, NPR])
        if stop <= 2:
            return done()

        zg = alloc([8, NT], BF16, R='C')
        G8 = 8
        Brb = alloc([G8, 16], F32, parts=64)
        Bib = alloc([G8, 16], F32, parts=64)
        Crb = alloc([G8, 16], F32, parts=64)
        Cib = alloc([G8, 16], F32, parts=64)
        tmpC = alloc([64], R='A')
        bufA = alloc([2, G8, 128], F32)
        bufB = alloc([2, G8, 128], F32)
        tq1 = alloc([G8, 128], F32, parts=64)
        tq2 = alloc([G8, 128], F32, parts=64)
        Wa = alloc([G8, 2, 64], BF16)
        w0_wa = st['last_w0']
        Wb = alloc([G8, 128], BF16)
        Wc = alloc([G8, 2, 128], BF16, parts=64)
        ytmp_ap = None
        YR = None
        Rr = alloc([G8, 128], F32, parts=64)
        Ri = alloc([G8, 128], F32, parts=64)
        rhot = alloc([G8, 128], F32, parts=64)
        um = [alloc([2, G8], F32, parts=64, R='A') for _ in range(2)]
        U = alloc([G8, 144], BF16)
        Sb = alloc([2, G8, 144], F32, parts=64)
        Hprev = alloc([2, G8, 144], BF16, parts=64)
        Ysb = U
        ts8a = Buf(tq1.ap[:, :, 0:16], tq1.res)
        ts8b = Buf(tq2.ap[:, :, 0:16], tq2.res)
        ts8c = alloc([2, G8, 16], F32, parts=64, R='A')
        Dcol = small['ssm_D']
        bAf = bufA.ap.rearrange("p r g c -> p (r g c)")
        bBf = bufB.ap.rearrange("p r g c -> p (r g c)")
        gt = Buf(bAf[:, 0:NT], bufA.res)
        sg = gt
        ytmp_ap = bBf[:, 0:NT]
        YR = [bufB.res]
        A64 = Buf(bufA.ap[0:64], bufA.res)
        B64 = Buf(bufB.ap[0:64], bufB.res)

        def ring64(si):
            return ring_all.ap[0:64, si, :]
        WaTp = Buf(ring64(0).bitcast(F32).rearrange("p (r g c) -> p r g c", r=2, g=G8, c=128), Res())
        Wcppp = Buf(ring64(1).bitcast(F32).rearrange("p (r g c) -> p r g c", r=2, g=G8, c=128), Res())
        Wc2 = [Buf(ring64(2)[:, hh * 2048:(hh + 1) * 2048].rearrange("p (g r c) -> p g r c", g=G8, r=2, c=128), Res()) for hh in range(2)]
        s3 = ring64(3).bitcast(F32)
        tq1p = Buf(s3[:, 0:1024].rearrange("p (g c) -> p g c", g=G8, c=128), Res())
        tq2p = Buf(s3[:, 1024:2048].rearrange("p (g c) -> p g c", g=G8, c=128), Res())
        negone = alloc([1], F32, parts=64)
        vset(negone.ap, -1.0, [negone.res], eng='pool')
        v4 = lambda ap3: ap3.rearrange("p g (s q) -> p g s q", s=8, q=16)
        St, Ht, Hun = A64, B64, A64
        qa = Buf(tq1.ap, tq1.res)
        qb = Buf(tq2.ap, tq2.res)
        GRP = [(0, 3), (3, 3), (6, 2)]

        def ssm_loads(ch):
            g0 = ch * G8
            dma('sp', Brb.ap, I['B_re'][g0:g0 + G8].rearrange("g n q -> n g q"), wr=[Brb.res])
            dma('sp', Bib.ap, I['B_im'][g0:g0 + G8].rearrange("g n q -> n g q"), wr=[Bib.res])
            for (nm, dst) in (('C_re', Crb), ('C_im', Cib)):
                dma('sp', tmpC.ap, I[nm][ch * 128:(ch + 1) * 128, :], wr=[tmpC.res])
                b = nb()
                trp(bank(b)[0:64, 0:128], tmpC.ap, identf.ap, [tmpC.res, identf.res], [PB[b]])
                vcopy(dst.ap.rearrange("p g q -> p (g q)"), bank(b)[0:64, 0:128], [PB[b]], [dst.res], eng='act')

        def ssm_matrices(ch, part=3):
            g0 = ch * G8
            Wc = Wc2[ch % 2]

            def gsq(buf, off_elems):
                a = buf.ap
                return bass.AP(a.tensor, a.offset + off_elems + g0, [list(a.ap[0]), [1, G8], [64, 8], [0, 16]])

            def gq_s(buf):
                a = buf.ap
                return bass.AP(a.tensor, a.offset, [list(a.ap[0]), [16, G8], [0, 8], [1, 16]])
            q4a = Buf(v4(tq1p.ap), tq1p.res)
            q4b = Buf(v4(tq2p.ap), tq2p.res)
            if part & 1:
                cmul(v4(WaTp.ap[:, 0]), v4(WaTp.ap[:, 1]), gsq(cst_r, 0), gsq(cst_i, 0), gq_s(Brb), gq_s(Bib),
                     [cst_r.res, cst_i.res, Brb.res, Bib.res], [WaTp.res], [WaTp.res], q4a, q4b, eng='pool')
            todo = ([(Wcppp, Qst_r, Qst_i, False)] if part & 1 else []) + ([(Wc, Pst_r, Pst_i, True)] if part & 2 else [])
            for (dstb, pw_r, pw_i, isWc) in todo:
                off = 64 if isWc else 0
                pr, pi_ = gsq(pw_r, off), gsq(pw_i, off)
                cr, ci = gq_s(Crb), gq_s(Cib)
                rdl = [pw_r.res, pw_i.res, Crb.res, Cib.res]
                if isWc:
                    o_re, o_im = v4(dstb.ap[:, :, 0, :]), v4(dstb.ap[:, :, 1, :])
                else:
                    o_re, o_im = v4(dstb.ap[:, 0]), v4(dstb.ap[:, 1])
                vtt(q4a.ap, pr, cr, ALU.mult, rdl, [q4a.res], 'pool')
                vtt(q4b.ap, pi_, ci, ALU.mult, rdl, [q4b.res], 'pool')
                vtt(o_re, q4a.ap, q4b.ap, ALU.subtract, [q4a.res, q4b.res], [dstb.res], 'pool')
                vtt(q4a.ap, pi_, cr, ALU.mult, rdl, [q4a.res], 'pool')
                vtt(q4b.ap, pr, ci, ALU.mult, rdl, [q4b.res], 'pool')
                vtt(q4a.ap, q4a.ap, q4b.ap, ALU.add, [q4a.res, q4b.res], [q4a.res], 'pool')
                na = negone.ap
                nbc = bass.AP(na.tensor, na.offset, [list(na.ap[0]), [0, G8], [0, 8], [0, 16]])
                vtt(o_im, q4a.ap, nbc, ALU.mult, [q4a.res, negone.res], [dstb.res], 'pool')

        def ssm_tables(ch):
            g0 = ch * G8
            gs = slice(g0, g0 + G8)
            I32 = mybir.dt.int32
            SC = 2.0 * PI * 0.999999
            T_, Ti_ = tq1, tq2
            vtt(T_.ap, bc(phin.ap[:, gs], 128), bc_mid(ciota.ap, G8), ALU.mult, [phin.res, ciota.res], [T_.res])
            vcopy(Ti_.ap.bitcast(I32), T_.ap, [T_.res], [Ti_.res])
            vcopy(Ri.ap, Ti_.ap.bitcast(I32), [Ti_.res], [Ri.res])
            vtt(Ri.ap, T_.ap, Ri.ap, ALU.subtract, [T_.res, Ri.res], [Ri.res])
            act(Ri.ap, Ri.ap, ACT.Sin, [Ri.res], [Ri.res], scale=-SC)
            vsadd(T_.ap, T_.ap, 0.25, [T_.res], [T_.res])
            vcopy(Ti_.ap.bitcast(I32), T_.ap, [T_.res], [Ti_.res])
            vcopy(Rr.ap, Ti_.ap.bitcast(I32), [Ti_.res], [Rr.res])
            vtt(Rr.ap, T_.ap, Rr.ap, ALU.subtract, [T_.res, Rr.res], [Rr.res])
            act(Rr.ap, Rr.ap, ACT.Sin, [Rr.res], [Rr.res], scale=SC)
            vcopy(rhot.ap, bc(rho8.ap[:, gs], 128), [rho8.res], [rhot.res], eng='pool')
            vset(rhot.ap[:, :, 0:1], 0.0, [rhot.res], eng='pool')

        def ssm_wa_wb(ch):
            for gq in range(2):
                b = nb()
                for gg in range(4):
                    g = gq * 4 + gg
                    for ri in range(2):
                        trp(bank(b)[:, gg * 128 + ri * 64:gg * 128 + (ri + 1) * 64], WaTp.ap[:, ri, g, :], identf.ap[0:64, 0:64],
                            [WaTp.res, identf.res], [PB[b]])
                vcopy(Wa.ap[:, gq * 4:gq * 4 + 4].rearrange("p g r n -> p (g r n)"), bank(b), [PB[b]], [Wa.res], eng='act')
            for gq in range(2):
                b = nb()
                for gg in range(4):
                    g = gq * 4 + gg
                    o = bank(b)[:, gg * 128:(gg + 1) * 128]
                    mm(o, WaTp.ap[:, 0, g, :], Wcppp.ap[:, 0, g, :], True, False, [WaTp.res, Wcppp.res], [PB[b]])
                    mm(o, WaTp.ap[:, 1, g, :], Wcppp.ap[:, 1, g, :], False, True, [WaTp.res, Wcppp.res], [PB[b]])
                vtt(Wb.ap[:, gq * 4:gq * 4 + 4, :], bank(b).rearrange("p (g m) -> p g m", g=4, m=128), bc_mid(maskf.ap, 4), ALU.mult,
                    [PB[b], maskf.res], [Wb.res])

        def rotate_scan():
            cmul(St.ap[:, 0], St.ap[:, 1], Sb.ap[:, 0, :, 0:128], Sb.ap[:, 1, :, 0:128], Rr.ap, Ri.ap,
                 [Sb.res, Rr.res, Ri.res], [St.res], [St.res], qa, qb)
            for ri in range(2):
                S.op('dve', lambda E, ri=ri: E.tensor_tensor_scan(
                    Ht.ap[:, ri].rearrange("p g c -> p (g c)"), rhot.ap.rearrange("p g c -> p (g c)"),
                    St.ap[:, ri].rearrange("p g c -> p (g c)"), 0.0, ALU.mult, ALU.add),
                    reads=[St.res, rhot.res], writes=[Ht.res])

        def shuffle_in(ch, uT, ncol):
            for (ga, gn) in GRP:
                b = nb()
                for gg in range(gn):
                    g = ga + gg
                    for s_ in range(8):
                        mm(bank(b)[:, gg * ncol:(gg + 1) * ncol], sel.ap[:, g * 8 + s_, :], uT.ap[:, ch, s_:8 * ncol:8], s_ == 0, s_ == 7,
                           [sel.res, uT.res], [PB[b]])
                vcopy(U.ap[:, ga:ga + gn, 0:ncol], bank(b)[:, 0:gn * ncol].rearrange("p (g c) -> p g c", g=gn, c=ncol),
                      [PB[b]], [U.res], eng='act')

        def map_a(ncol):
            for ri in range(2):
                for (ga, gn) in GRP:
                    b = nb()
                    for gg in range(gn):
                        g = ga + gg
                        mm(bank(b)[0:64, gg * ncol:(gg + 1) * ncol], Wa.ap[:, g, ri, :], U.ap[:, g, 0:ncol], True, True, [Wa.res, U.res], [PB[b]])
                    vcopy(Sb.ap[:, ri, ga:ga + gn, 0:ncol], bank(b)[0:64, 0:gn * ncol].rearrange("p (g c) -> p g c", g=gn, c=ncol),
                          [PB[b]], [Sb.res] + ([SbS_res] if ncol > 128 else []), eng='act')

        qs1 = Buf(ts8a.ap[:, :, 0], ts8a.res)
        qs2 = Buf(ts8b.ap[:, :, 0], ts8b.res)
        p8r, p8i = P_r[8], P_i[8]
        CC = alloc([2, 64], F32, parts=64)
        SbS_res = Res()
        h0T_ch = [Res() for _ in range(8)]
        YB = [5, 6, 7]

        def ssm_pre(ch):
            gs = slice(ch * G8, ch * G8 + G8)
            shuffle_in(ch, upreT, 128)
            map_a(128)

        def ssm_pre_dve(ch):
            gs = slice(ch * G8, ch * G8 + G8)
            rotate_scan()
            cmul(H0.ap[:, 0, gs], H0.ap[:, 1, gs], Ht.ap[:, 0, :, 127], Ht.ap[:, 1, :, 127], Rr.ap[:, :, 127], Ri.ap[:, :, 127],
                 [Ht.res, Rr.res, Ri.res], [H0.res], [H0.res], qs1, qs2, conj_b=True)
            ca = Buf(tq1p.ap[:, :, 16], tq1p.res)
            cb = Buf(tq2p.ap[:, :, 16], tq2p.res)
            cmul(CC.ap[:, 0, gs], CC.ap[:, 1, gs], p8r.ap[:, gs], p8i.ap[:, gs], H0.ap[:, 0, gs], H0.ap[:, 1, gs],
                 [p8r.res, p8i.res, H0.res], [CC.res], [CC.res], ca, cb, eng='pool')

        def ssm_main(ch):
            g0 = ch * G8
            gs = slice(g0, g0 + G8)
            Wc = Wc2[ch % 2]
            shuffle_in(ch, umT, 144)
            map_a(144)
            for bi, (ga, gn) in enumerate(GRP):
                b = YB[bi]
                for gg in range(gn):
                    g = ga + gg
                    mm(bank(b)[:, gg * 144:(gg + 1) * 144], Wb.ap[:, g, :], U.ap[:, g, :], gg == 0, False, [Wb.res, U.res], [PB[b]], skip=True)
            vcopy(Hprev.ap[:, :, :, 128:144], h0T.ap[:, :, :, gs].rearrange("p r s g -> p r g s"), [h0T.res], [Hprev.res, h0T_ch[ch]], eng='act')
            h0v_r = h0T.ap[:, 0, :, gs].rearrange("p s g -> p g s")
            h0v_i = h0T.ap[:, 1, :, gs].rearrange("p s g -> p g s")
            tsa = Buf(tq1p.ap[:, :, 0:16], tq1p.res)
            tsb = Buf(tq2p.ap[:, :, 0:16], tq2p.res)
            cmul(ts8c.ap[:, 0], ts8c.ap[:, 1], bc(p8r.ap[:, gs], 16), bc(p8i.ap[:, gs], 16), h0v_r, h0v_i,
                 [p8r.res, p8i.res, h0T_ch[ch]], [ts8c.res], [ts8c.res], tsa, tsb, eng='pool')
            vtt(h0v_r, ts8c.ap[:, 0], Sb.ap[:, 0, :, 128:144], ALU.add, [ts8c.res, SbS_res], [h0T_ch[ch]], 'pool')
            vtt(h0v_i, ts8c.ap[:, 1], Sb.ap[:, 1, :, 128:144], ALU.add, [ts8c.res, SbS_res], [h0T_ch[ch]], 'pool')
            vtt(Sb.ap[:, :, :, 0], Sb.ap[:, :, :, 0], CC.ap[:, :, gs], ALU.add, [Sb.res, CC.res], [Sb.res])
            rotate_scan()
            cmul(Hun.ap[:, 0], Hun.ap[:, 1], Ht.ap[:, 0], Ht.ap[:, 1], Rr.ap, Ri.ap,
                 [Ht.res, Rr.res, Ri.res], [Hun.res], [Hun.res], qa, qb, conj_b=True)
            vcopy(Hprev.ap[:, :, :, 1:128], Hun.ap[:, :, :, 0:127], [Hun.res], [Hprev.res], eng='act')
            vcopy(Hprev.ap[:, :, :, 0], H0.ap[:, :, gs], [H0.res], [Hprev.res], eng='act')
            vcopy(HpF.ap[:, :, gs], Hun.ap[:, :, :, 127], [Hun.res], [HpF.res], eng='act')
            if ch < 7:
                ssm_wa_wb(ch + 1)
                if ch < 6:
                    ssm_loads(ch + 2)
                    ssm_matrices(ch + 2, part=1)
                ssm_pre(ch + 1)
            for bi, (ga, gn) in enumerate(GRP):
                b = YB[bi]
                for gg in range(gn):
                    g = ga + gg
                    o = bank(b)[:, gg * 144:(gg + 1) * 144]
                    mm(o, Wc.ap[:, g, 0, :], Hprev.ap[:, 0, g, :], False, False, [Wc.res, Hprev.res], [PB[b]], skip=True)
                    mm(o, Wc.ap[:, g, 1, :], Hprev.ap[:, 1, g, :], False, True, [Wc.res, Hprev.res], [PB[b]], skip=True)
                vcopy(Ysb.ap[:, ga:ga + gn, :], bank(b)[:, 0:gn * 144].rearrange("p (g c) -> p g c", g=gn, c=144), [PB[b]], [Ysb.res], eng='act')
            if ch < 6:
                ssm_matrices(ch + 2, part=2)
            if ch < 7:
                ssm_tables(ch + 1)
                ssm_pre_dve(ch + 1)
            ua = umT.ap[:, ch, :]
            for (sa, sn) in GRP:
                b = nb()
                for sj in range(sn):
                    s_ = sa + sj
                    for g in range(G8):
                        mm(bank(b)[:, sj * 144:(sj + 1) * 144], sel.ap[:, s_ * 8 + g, :], Ysb.ap[:, g, :], g == 0, g == G8 - 1, [sel.res, Ysb.res], [PB[b]])
                yv = bass.AP(ytmp_ap.tensor, ytmp_ap.offset + sa, [list(ytmp_ap.ap[0]), [1, sn], [8, 144]])
                uv = bass.AP(ua.tensor, ua.offset + sa, [list(ua.ap[0]), [1, sn], [8, 144]])
                vstt(yv, uv, Dcol.ap[:, ch:ch + 1], bank(b)[:, 0:sn * 144].rearrange("p (s c) -> p s c", s=sn, c=144), ALU.mult, ALU.add,
                     [umT.res, Dcol.res, PB[b]], YR)
            if ch == 0:
                tap('y0', ytmp_ap, YR, [128, NT])
            vtt(gt.ap, ytmp_ap, ytmp_ap, ALU.mult, YR, [gt.res])
            vts(gt.ap, gt.ap, 0.044715, 1.0, ALU.mult, ALU.add, [gt.res], [gt.res])
            vtt(gt.ap, gt.ap, ytmp_ap, ALU.mult, [gt.res] + YR, [gt.res])
            act(sg.ap, gt.ap, ACT.Sigmoid, [gt.res], [sg.res], scale=1.5957691216057308)
            vtt(zg.ap[:, ch, :], ytmp_ap, sg.ap, ALU.mult, YR + [sg.res], [zg.res])

        st['nbanks'] = 5
        st['bank'] = 0
        ssm_loads(0)
        ssm_matrices(0)
        ssm_tables(0)
        ssm_wa_wb(0)
        ssm_loads(1)
        ssm_matrices(1)
        ssm_pre(0)
        ssm_pre_dve(0)
        for ch in range(8):
            ssm_main(ch)
        st['nbanks'] = 8

        tap('HpF', HpF.ap.rearrange("p r g -> p (r g)"), [HpF.res], [64, 128])
        tap('H0', H0.ap.rearrange("p r g -> p (r g)"), [H0.res], [64, 128])
        tap('h0T', h0T.ap[:, 0, 0, :], [h0T.res], [64, 64])
        stg = [Buf(bAf[:, k * 64:(k + 1) * 64], Res()) for k in range(18)] + [Buf(bBf[:, k * 64:(k + 1) * 64], Res()) for k in range(18)]
        si_ = 0
        S.barrier()
        for ri, on in ((0, 'hp_re'), (1, 'hp_im')):
            b = nb()
            trp(bank(b)[0:64, 0:64], HpF.ap[:, ri, :], identf.ap[0:64, 0:64], [HpF.res, identf.res], [PB[b]])
            o_ = stg[si_]; si_ += 1
            vcopy(o_.ap[0:64, :], bank(b)[0:64, 0:64], [PB[b]], [o_.res], eng=evq())
            dma('sp', O[on], o_.ap[0:64, :], rd=[o_.res])
        for ri, on in ((0, 'hs_re'), (1, 'hs_im')):
            for j in range(8):
                b = nb()
                trp(bank(b)[:, 0:64], h0T.ap[:, ri, 2 * j:2 * j + 2, :].rearrange("p s g -> p (s g)"), identf.ap[0:64, 0:64],
                    [h0T.res, identf.res], [PB[b]])
                o_ = stg[si_]; si_ += 1
                vcopy(o_.ap, bank(b)[:, 0:64], [PB[b]], [o_.res], eng=evq())
                dma('sp', O[on][j * 128:(j + 1) * 128, :], o_.ap, rd=[o_.res])
        tap('zg', zg.ap[:, 0, :], [zg.res], [128, NT])
        reset('S')
        if stop <= 3:
            return done()

        xnT = alloc([16, NT], BF16, R='E')
        merged = alloc([16, NT], BF16, R='D')
        gsbs = [alloc([512], R='F') for _ in range(3)]
        gtmp = alloc([512], R='F')
        rtmp = alloc([512], BF16, R='F')
        gctr = [0]

        def next_gsb():
            gctr[0] += 1
            return gsbs[gctr[0] % 3]
        use('C')
        zg2 = alloc([8, NT], BF16)
        glub = small['glu_b']
        m4 = mark()
        xsl = [alloc([D]) for _ in range(2)]
        xsb3 = [alloc([D], BF16) for _ in range(3)]
        glu_panels = {}

        def glu_unit(ui):
            mc, gi_ = ui // 3, ui % 3
            t0, n = TG[gi_]
            pk = mc // 4
            if pk not in glu_panels:
                glu_panels[pk] = wpanel(I['glu_w'], 0, 1024, pk * 512, 512)
            panel, prl = glu_panels[pk]
            j = mc % 4
            b = nb()
            for kc in range(8):
                mm(bank(b)[:, 0:n], panel.ap[:, kc, j * 128:(j + 1) * 128], zg.ap[:, kc, t0:t0 + n], kc == 0, kc == 7,
                   prl + [zg.res], [PB[b]])
            gsb = next_gsb()
            act(gsb.ap[:, 0:n], bank(b)[:, 0:n], ACT.Sigmoid, [PB[b], glub.res], [gsb.res], bias=glub.ap[:, mc:mc + 1])
            vtt(zg2.ap[:, mc, t0:t0 + n], gsb.ap[:, 0:n], zg.ap[:, mc, t0:t0 + n], ALU.mult, [gsb.res, zg.res], [zg2.res])
        ui = 0
        for k in range(9 + 1):
            if k < 9:
                norm_ops(I['xm'][k * 128:(k + 1) * 128, :], k, xsl, xsb3[k % 3], None)
            for _ in range(3 if k < 6 else 2):
                if ui < 24:
                    glu_unit(ui)
                    ui += 1
            if k >= 1:
                norm_T(xsb3[(k - 1) % 3], xnT, (k - 1) * 128)
        while ui < 24:
            glu_unit(ui)
            ui += 1
        release(m4)
        bgate = small['b_gate']

        def gated_merge(w2d, K, rhsT, gi, first):
            KCg = 16
            KCb = K // 128
            gpc = 256
            bpc = min(D, (4096 // KCb) // 128 * 128)
            gpanel = bpanel = None
            for mc in range(16):
                if (mc * 128) % gpc == 0:
                    gpanel, grl = wpanel(I['w_in'], 0, D, 3072 + gi * D + mc * 128, gpc, slots=(0, 1), key='rg')
                if (mc * 128) % bpc == 0:
                    bpanel, brl = wpanel(w2d, 0, K, mc * 128, bpc, slots=(2, 3), key='rb')
                jg = (mc * 128 % gpc) // 128
                jb = (mc * 128 % bpc) // 128
                for (t0, n) in TG:
                    b1 = nb()
                    for kc in range(KCg):
                        mm(bank(b1)[:, 0:n], gpanel.ap[:, kc, jg * 128:(jg + 1) * 128], xnT.ap[:, kc, t0:t0 + n], kc == 0, kc == KCg - 1,
                           grl + [xnT.res], [PB[b1]])
                    gsb = next_gsb()
                    act(gsb.ap[:, 0:n], bank(b1)[:, 0:n], ACT.Sigmoid, [PB[b1], bgate.res], [gsb.res],
                        bias=bgate.ap[:, gi * 16 + mc:gi * 16 + mc + 1])
                    b2 = nb()
                    for kc in range(KCb):
                        mm(bank(b2)[:, 0:n], bpanel.ap[:, kc, jb * 128:(jb + 1) * 128], rhsT.ap[:, kc, t0:t0 + n], kc == 0, kc == KCb - 1,
                           brl + [rhsT.res], [PB[b2]])
                    if first:
                        vtt(merged.ap[:, mc, t0:t0 + n], gsb.ap[:, 0:n], bank(b2)[:, 0:n], ALU.mult, [gsb.res, PB[b2]], [merged.res])
                    else:
                        vtt(gtmp.ap[:, 0:n], gsb.ap[:, 0:n], bank(b2)[:, 0:n], ALU.mult, [gsb.res, PB[b2]], [gtmp.res])
                        vtt(merged.ap[:, mc, t0:t0 + n], merged.ap[:, mc, t0:t0 + n], gtmp.ap[:, 0:n], ALU.add,
                            [merged.res, gtmp.res], [merged.res])

        gated_merge(I['ssm_proj'], 1024, zg2, 1, True)
        tap('merged1', merged.ap[:, 0, :], [merged.res], [128, NT])
        reset('C')
        if stop <= 4:
            return done()

        HL = 16
        upP = [alloc([HL + NPR]) for _ in range(2)]
        upS = [alloc([NSEQ, HL + 8]) for _ in range(2)]
        wsP = [alloc([HL + NPR]) for _ in range(2)]
        wsS = [alloc([NSEQ, HL + 8]) for _ in range(2)]
        pooled = alloc([2, NT], BF16)
        mixed = alloc([8, NT], BF16)
        spT = alloc([8, NSEQ, 15])
        usc = [alloc([128]) for _ in range(2)]
        sp_t = alloc([1024])
        po = alloc([1024])
        po2 = alloc([1024])
        pscale = small['pool_scale']
        for half in range(2):
            dma('sp', sp_t.ap[0:120, :], I['spool'][half * 120:(half + 1) * 120, :], wr=[sp_t.res])
            for c in range(8):
                b = nb()
                trp(bank(b)[:, 0:120], sp_t.ap[0:120, c * 128:(c + 1) * 128], identf.ap[0:120, 0:120], [sp_t.res, identf.res], [PB[b]])
                vcopy(spT.ap[:, c, half * 8:(half + 1) * 8, :], bank(b)[:, 0:120].rearrange("p (s r) -> p s r", s=8, r=15),
                      [PB[b]], [spT.res], eng=evq())
        for sq in range(NSEQ):
            dma('sp', O['pool_s'][sq, 0:7, :], I['spool'][sq * 15 + 8:sq * 15 + 15, :])

        def pool_chunk(c):
            uP, uS = upP[c % 2], upS[c % 2]
            k = c // 2
            nlev = k + 1
            wdw = 2 ** nlev
            b = nb()
            trp(bank(b)[0:16, 0:128], uP.ap[:, HL + NPR - 16:HL + NPR], identf.ap, [uP.res, identf.res], [PB[b]])
            vcopy(po.ap[0:16, c * 128:(c + 1) * 128], bank(b)[0:16, 0:128], [PB[b]], [po.res], eng=evq())
            b = nb()
            trp(bank(b)[:, 0:128], usc[c % 2].ap, identf.ap, [usc[c % 2].res, identf.res], [PB[b]])
            vcopy(po2.ap[:, c * 128:(c + 1) * 128], bank(b)[:, 0:128], [PB[b]], [po2.res], eng=evq())
            srcP, srcS = uP.ap, uS.ap
            rdP, rdS = [uP.res], [uS.res]
            for lev in range(nlev):
                sh = 2 ** lev
                lo = 2 ** (lev + 1) - 1
                dP, dS = wsP[lev % 2], wsS[lev % 2]
                vtt(dP.ap[:, lo:], srcP[:, lo:], srcP[:, lo - sh:HL + NPR - sh], ALU.add, rdP, [dP.res])
                vtt(dS.ap[:, :, lo:], srcS[:, :, lo:], srcS[:, :, lo - sh:HL + 8 - sh], ALU.add, rdS, [dS.res])
                srcP, srcS, rdP, rdS = dP.ap, dS.ap, [dP.res], [dS.res]
            j = c % 2
            vstt(pooled.ap[:, j, 0:NPR], srcP[:, HL:], 1.0 / wdw, uP.ap[:, HL:], ALU.mult, ALU.subtract, rdP + [uP.res], [pooled.res])
            vtt(gtmp.ap[:, 0:16], srcP[:, HL:HL + 16], icnt.ap[:, k, :], ALU.mult, rdP + [icnt.res], [gtmp.res])
            vtt(pooled.ap[:, j, 0:16], gtmp.ap[:, 0:16], uP.ap[:, HL:HL + 16], ALU.subtract, [gtmp.res, uP.res], [pooled.res])
            vstt(pooled.ap[:, j, NPR:NT].rearrange("p (s t) -> p s t", s=NSEQ, t=8), srcS[:, :, HL:], 1.0 / wdw, uS.ap[:, :, HL:],
                 ALU.mult, ALU.subtract, rdS + [uS.res], [pooled.res])

        def ev_up(mc, t0, n, b):
            uP, uS = upP[mc % 2], upS[mc % 2]
            if t0 == 0:
                vcopy(uP.ap[:, 0:HL], halo.ap[:, mc, :], [halo.res], [uP.res])
                vset(uS.ap[:, :, 0:1], 0.0, [uS.res])
                vcopy(uS.ap[:, :, 1:16], spT.ap[:, mc, :, :], [spT.res], [uS.res])
            npr = max(0, min(n, NPR - t0))
            if npr > 0:
                vcopy(uP.ap[:, HL + t0:HL + t0 + npr], bank(b)[:, 0:npr], [PB[b]], [uP.res], eng=evq())
            if t0 + n > NPR:
                vcopy(uS.ap[:, :, HL:HL + 8], bank(b)[:, npr:n].rearrange("p (s t) -> p s t", s=NSEQ, t=8), [PB[b]], [uS.res], eng=evq())
                vcopy(usc[mc % 2].ap, bank(b)[:, npr:n], [PB[b]], [usc[mc % 2].res], eng=evq())
                pool_chunk(mc)
                if mc % 2 == 1:
                    g = mc // 2

                    def ev_mix(mc2, t0_, n_, b_, g=g):
                        c_ = g * 2 + mc2
                        vsmul(mixed.ap[:, c_, t0_:t0_ + n_], bank(b_)[:, 0:n_], pscale.ap[:, c_:c_ + 1], [PB[b_], pscale.res], [mixed.res])
                    linear_fm(I['pool_w'], g * 256, 256, 0, 256, pooled, TG, ev_mix)
        linear_fm(I['w_in'], 0, D, 0, 1024, xnT, TG, ev_up)
        dma('sp', O['pool_p'], po.ap[0:16, :], rd=[po.res])
        for sq in range(NSEQ):
            dma('sp', O['pool_s'][sq, 7:15, :], po2.ap[sq * 8:(sq + 1) * 8, :], rd=[po2.res])
        gated_merge(I['pool_proj'], 1024, mixed, 0, False)
        tap('merged2', merged.ap[:, 0, :], [merged.res], [128, NT])
        reset('C')
        if stop <= 5:
            return done()

        qT = alloc([8, NT], BF16)
        oT = alloc([8, NT], BF16)

        def ev_q(mc, t0, n, b):
            vcopy(qT.ap[:, mc, t0:t0 + n], bank(b)[:, 0:n], [PB[b]], [qT.res], eng=evq())
        linear_fm(I['w_in'], 0, D, 2048, 1024, xnT, TG, ev_q)
        eT = alloc([2, 512], BF16)
        rZ = alloc([512])
        SCL = 1.0 / 16.0
        for h in range(4):
            for (t0, n) in TGP:
                for mc in range(2):
                    b = nb()
                    for dc in range(2):
                        mm(bank(b), KT.ap[:, 2 * h + dc, mc * 128:(mc + 1) * 128], qT.ap[:, 2 * h + dc, t0:t0 + n], dc == 0, dc == 1,
                           [KT.res, qT.res], [PB[b]])
                    act(eT.ap[:, mc, :], bank(b), ACT.Exp, [PB[b]], [eT.res], scale=SCL)
                b = nb()
                for mc in range(2):
                    mm(bank(b), onesb.ap, eT.ap[:, mc, :], mc == 0, mc == 1, [onesb.res, eT.res], [PB[b]])
                vrecip(rZ.ap, bank(b), [PB[b]], [rZ.res])
                for dc in range(2):
                    b = nb()
                    for mc in range(2):
                        mm(bank(b), Vb.ap[:, mc, (2 * h + dc) * 128:(2 * h + dc + 1) * 128], eT.ap[:, mc, :], mc == 0, mc == 1,
                           [Vb.res, eT.res], [PB[b]])
                    vtt(oT.ap[:, 2 * h + dc, t0:t0 + n], bank(b), rZ.ap, ALU.mult, [PB[b], rZ.res], [oT.res])
        Ks = [alloc([2, 1024], BF16) for _ in range(2)]
        Vs = [alloc([2, 1024], BF16) for _ in range(2)]
        KTs = alloc([8, 256], BF16)
        eTs = alloc([4, 2, 8], BF16)
        rZs = alloc([4, 8])
        for sq in range(NSEQ):
            ks, vs = Ks[sq % 2], Vs[sq % 2]
            dma('pool', ks.ap, I['ck'][sq].rearrange("(mc p) d -> p mc d", p=128), wr=[ks.res])
            dma('pool', vs.ap, I['cv'][sq].rearrange("(mc p) d -> p mc d", p=128), wr=[vs.res])
            for mc in range(2):
                b = nb()
                for dch in range(8):
                    trp(bankb(b)[:, dch * 128:(dch + 1) * 128], ks.ap[:, mc, dch * 128:(dch + 1) * 128], identb.ap, [ks.res, identb.res], [PB[b]])
                vcopy(KTs.ap[:, :, mc * 128:(mc + 1) * 128], bankb(b).rearrange("p (j c) -> p j c", j=8, c=128), [PB[b]], [KTs.res], eng=evq())
            tk = slice(NPR + sq * 8, NPR + sq * 8 + 8)
            b = nb()
            for h in range(4):
                for mc in range(2):
                    o = bank(b)[:, (h * 2 + mc) * 8:(h * 2 + mc + 1) * 8]
                    for dc in range(2):
                        mm(o, KTs.ap[:, 2 * h + dc, mc * 128:(mc + 1) * 128], qT.ap[:, 2 * h + dc, tk], dc == 0, dc == 1, [KTs.res, qT.res], [PB[b]])
            act(eTs.ap.rearrange("p h m t -> p (h m t)"), bank(b)[:, 0:64], ACT.Exp, [PB[b]], [eTs.res], scale=SCL)
            b = nb()
            for h in range(4):
                for mc in range(2):
                    mm(bank(b)[:, h * 8:(h + 1) * 8], onesb.ap, eTs.ap[:, h, mc, :], mc == 0, mc == 1, [onesb.res, eTs.res], [PB[b]])
            vrecip(rZs.ap.rearrange("p h t -> p (h t)"), bank(b)[:, 0:32], [PB[b]], [rZs.res])
            b = nb()
            for h in range(4):
                for dc in range(2):
                    o = bank(b)[:, (h * 2 + dc) * 8:(h * 2 + dc + 1) * 8]
                    for mc in range(2):
                        mm(o, vs.ap[:, mc, (2 * h + dc) * 128:(2 * h + dc + 1) * 128], eTs.ap[:, h, mc, :], mc == 0, mc == 1, [vs.res, eTs.res], [PB[b]])
            rz4 = rZs.ap
            rzb = bass.AP(rz4.tensor, rz4.offset, [list(rz4.ap[0]), list(rz4.ap[1]), [0, 2], list(rz4.ap[2])])
            vtt(oT.ap[:, :, tk].rearrange("p (h c) t -> p h c t", h=4, c=2), bank(b)[:, 0:64].rearrange("p (h c t) -> p h c t", h=4, c=2, t=8),
                rzb, ALU.mult, [PB[b], rZs.res], [oT.res])
        tap('oT', oT.ap[:, 0, :], [oT.res], [128, NT])
        gated_merge(I['xa_wo'], 1024, oT, 2, False)
        tap('merged3', merged.ap[:, 0, :], [merged.res], [128, NT])
        reset('C')
        if stop <= 6:
            return done()

        reset('E')
        reset('B')
        reset('F')
        x1 = alloc([9, D], R='C')
        xn2T = alloc([16, NT], BF16, R='E')
        xr_t = [alloc([512], R='B') for _ in range(2)]
        rtmps = [alloc([512], BF16, R='B') for _ in range(3)]
        xsb2 = [alloc([D], BF16, R='F') for _ in range(2)]
        load_g('norm2_g')
        for cg in range(4):
            panel, prl = wpanel_big(I['w_out'], 0, D, cg * 512, 512)
            for tt in range(9):
                xrt = xr_t[(cg * 9 + tt) % 2]
                dma('sp', xrt.ap, I['xm'][tt * 128:(tt + 1) * 128, cg * 512:(cg + 1) * 512], wr=[xrt.res])
                b = nb()
                for kc in range(16):
                    mm(bank(b), merged.ap[:, kc, tt * 128:(tt + 1) * 128], panel.ap[:, kc, :], kc == 0, kc == 15, [merged.res] + prl, [PB[b]])
                vtt(x1.ap[:, tt, cg * 512:(cg + 1) * 512], bank(b), xrt.ap, ALU.add, [PB[b], xrt.res], [x1.res])
                if cg == 3:
                    norm_ops(Buf(x1.ap[:, tt, :], x1.res), tt, None, xsb2[tt % 2], None)
                    if tt >= 1:
                        norm_T(xsb2[(tt - 1) % 2], xn2T, (tt - 1) * 128)
        norm_T(xsb2[8 % 2], xn2T, 8 * 128)
        tap('x1', x1.ap[:, 0, :], [x1.res], [128, D])
        reset('D')
        reset('F')
        if stop <= 7:
            return done()

        hT = alloc([16, NT], BF16, R='D')
        junkf = alloc([D], BF16, R='F')
        rctr = [0]
        for fq in range(4):
            def ev_h(mc, t0, n, b):
                rctr[0] += 1
                rt = rtmps[rctr[0] % 3]
                act(rt.ap[:, 0:n], bank(b)[:, 0:n], ACT.Relu, [PB[b]], [rt.res])
                vtt(hT.ap[:, mc, t0:t0 + n], rt.ap[:, 0:n], rt.ap[:, 0:n], ALU.mult, [rt.res], [hT.res])
            linear_fm(I['w1'], 0, D, fq * 2048, 2048, xn2T, TG, ev_h)
            if fq == 3:
                load_g('final_g')
            for cg in range(4):
                panel, prl = wpanel_big(I['w2'], fq * 2048, 2048, cg * 512, 512)
                for tt in range(9):
                    b = nb()
                    for kc in range(16):
                        mm(bank(b), hT.ap[:, kc, tt * 128:(tt + 1) * 128], panel.ap[:, kc, :], kc == 0, kc == 15, [hT.res] + prl, [PB[b]])
                    last = (fq == 3 and cg == 3)
                    xres = Res() if last else x1.res
                    vtt(x1.ap[:, tt, cg * 512:(cg + 1) * 512], x1.ap[:, tt, cg * 512:(cg + 1) * 512], bank(b), ALU.add, [PB[b], x1.res], [xres])
                    if last:
                        ss = ss_list[tt % 4]
                        xt_ap = x1.ap[:, tt, :]
                        vset(ss.ap[:, 0:1], 0.0, [ss.res])
                        act(junkf.ap, xt_ap, ACT.Square, [xres, ss.res], [junkf.res, ss.res], accum_out=ss.ap[:, 0:1])
                        vts(ss.ap[:, 1:2], ss.ap[:, 0:1], 1.0 / D, EPS, ALU.mult, ALU.add, [ss.res], [ss.res])
                        act(ss.ap[:, 1:2], ss.ap[:, 1:2], ACT.Sqrt, [ss.res], [ss.res])
                        vrecip(ss.ap[:, 2:3], ss.ap[:, 1:2], [ss.res], [ss.res])
                        vstt(xt_ap, xt_ap, ss.ap[:, 2:3], gbc.ap, ALU.mult, ALU.mult, [xres, ss.res, gbc.res], [xres])
                        dma('sp', O['y'][tt * 128:(tt + 1) * 128, :], xt_ap, rd=[xres])
        return done()


_WIN = (2, 4, 8, 16)


def _consts():
    ident = np.eye(128, dtype=np.float32)
    sel = np.zeros((64, 128, 128), np.float32)
    for a in range(8):
        for b in range(8):
            for q in range(16):
                sel[a * 8 + b, a * 16 + q, b * 16 + q] = 1.0
    mask = np.zeros((128, 128), np.float32)
    for s in range(8):
        for sp in range(s, 8):
            mask[s * 16:(s + 1) * 16, sp * 16:(sp + 1) * 16] = 1.0
    return ident, sel, mask


def make_in_maps(inp):
    f = lambda a: np.ascontiguousarray(np.asarray(a, dtype=np.float32))
    ident, sel, mask = _consts()
    shared = {
        'norm1_g': f(inp['norm1_g']).reshape(1, D), 'w_in': f(inp['w_in'][0]), 'b_gate': f(inp['b_gate']).reshape(6144),
        'pool_w': f(inp['pool_w']).reshape(1024, 256), 'pool_scale': f(inp['pool_scale']).reshape(1024),
        'pool_proj': f(inp['pool_proj'][0]), 'A_re': f(inp['ssm_A_re'][0]), 'A_im': f(inp['ssm_A_im'][0]),
        'log_dt': f(inp['ssm_log_dt']).reshape(1, 64), 'B_re': f(inp['ssm_B_re'][0]), 'B_im': f(inp['ssm_B_im'][0]),
        'C_re': f(inp['ssm_C_re']).reshape(1024, 64), 'C_im': f(inp['ssm_C_im']).reshape(1024, 64),
        'ssm_D': f(inp['ssm_D']).reshape(1024), 'glu_w': f(inp['ssm_glu_w'][0]), 'glu_b': f(inp['ssm_glu_b']).reshape(1024),
        'ssm_proj': f(inp['ssm_proj'][0]), 'mem_norm_g': f(inp['mem_norm_g']).reshape(1, D),
        'xa_wk': f(inp['xa_wk'][0]), 'xa_wv': f(inp['xa_wv'][0]), 'xa_wo': f(inp['xa_wo'][0]), 'w_out': f(inp['w_out'][0]),
        'norm2_g': f(inp['norm2_g']).reshape(1, D), 'w1': f(inp['mlp_w1'][0]), 'w2': f(inp['mlp_w2'][0]),
        'final_g': f(inp['final_norm_g']).reshape(1, D), 'ident': ident, 'sel': sel, 'mask': mask,
        'ciota': np.arange(128, dtype=np.float32).reshape(1, 128),
    }
    xpr, xsm = inp['x_prompt'], inp['x_sample']
    maps = []
    for c in range(8):
        b, h = c // 2, c % 2
        sq = slice(16 * c, 16 * c + 16)
        m = dict(shared)
        m['xm'] = f(np.concatenate([xpr[b, h * NPR:(h + 1) * NPR], xsm[sq].reshape(NSM, D)], axis=0))
        m['xp'] = f(xpr[b, 0:NPR]) if h == 1 else np.zeros((NPR, D), np.float32)
        m['spool'] = f(inp['state_pool'][0, sq]).reshape(NSEQ * 15, 1024)
        m['sre'] = f(inp['state_ssm_re'][0, sq]).reshape(NSEQ * 64, 64)
        m['sim'] = f(inp['state_ssm_im'][0, sq]).reshape(NSEQ * 64, 64)
        m['ck'] = f(inp['cache_mem_k'][0, sq]).reshape(NSEQ, 256, 1024)
        m['cv'] = f(inp['cache_mem_v'][0, sq]).reshape(NSEQ, 256, 1024)
        m['mem'] = f(inp['mem_prompt'][b])
        ic = np.zeros((4, 16), np.float32)
        for k, w in enumerate(_WIN):
            for t in range(16):
                ic[k, t] = 1.0 / (min(t + 1, w) if h == 0 else w)
        m['icnt'] = ic
        maps.append(m)
    return maps


def assemble(res):
    y_prompt = np.zeros((4, 2048, D), np.float32)
    y_sample = np.zeros((128, 8, D), np.float32)
    pool_p = np.zeros((1, 4, 15, 1024), np.float32)
    re_p = np.zeros((1, 4, 64, 64), np.float32)
    im_p = np.zeros((1, 4, 64, 64), np.float32)
    mk_p = np.zeros((1, 4, 256, 4, 256), np.float32)
    mv_p = np.zeros((1, 4, 256, 4, 256), np.float32)
    pool_s = np.zeros((1, 128, 15, 1024), np.float32)
    re_s = np.zeros((1, 128, 64, 64), np.float32)
    im_s = np.zeros((1, 128, 64, 64), np.float32)
    for c in range(8):
        r = res[c]
        b, h = c // 2, c % 2
        sq = slice(16 * c, 16 * c + 16)
        y_prompt[b, h * NPR:(h + 1) * NPR] = r['y'][0:NPR]
        y_sample[sq] = r['y'][NPR:NT].reshape(NSEQ, 8, D)
        pool_s[0, sq] = r['pool_s']
        re_s[0, sq] = r['hs_re'].reshape(NSEQ, 64, 64)
        im_s[0, sq] = r['hs_im'].reshape(NSEQ, 64, 64)
        if h == 1:
            pool_p[0, b] = r['pool_p'][1:16]
            re_p[0, b] = r['hp_re']
            im_p[0, b] = r['hp_im']
        else:
            mk_p[0, b] = r['mk'].reshape(256, 4, 256)
            mv_p[0, b] = r['mv'].reshape(256, 4, 256)
    return (y_prompt, y_sample, pool_p, re_p, im_p, mk_p, mv_p, pool_s, re_s, im_s)


_NC_CACHE = {}


def kernel(**inputs):
    if 'nc' not in _NC_CACHE:
        _NC_CACHE['nc'] = build_program()[0]
    nc = _NC_CACHE['nc']
    in_maps = make_in_maps(inputs)
    res = run_bass_kernel_spmd(nc, in_maps, core_ids=list(range(8)))
    return assemble(res.results)
```
